# Optimizing a Trainium2 kernel written in Bass

```python
import math
import jax, jax.numpy as jnp
from jax import lax
import numpy as np

D_MODEL = 1024
BATCH = 4
SEQ = 4096
DEPTH = 4

N_MIXERS = 2
N_MLA = (DEPTH + 1) // 2
N_SSD = DEPTH // 2
N_SUB = 3
EPS = 1e-6

D_FF = 2816

MLA_HEADS = 16
Q_LORA = 384
KV_LORA = 256
QK_NOPE = 64
QK_ROPE = 32
QK_HEAD = QK_NOPE + QK_ROPE
V_HEAD = 64
MLA_A_DIM = Q_LORA + KV_LORA + QK_ROPE
ROPE_THETA = 10000.0
Q_BLOCK = 128
MAX_POS_OFFSET = 1024

SSD_EXPAND = 2
D_INNER = SSD_EXPAND * D_MODEL
SSD_HEAD_DIM = 64
SSD_HEADS = D_INNER // SSD_HEAD_DIM
SSD_GROUPS = 4
SSD_STATE = 128
CONV_WIDTH = 4
CHUNK = 128
CONV_DIM = D_INNER + 2 * SSD_GROUPS * SSD_STATE
IN_PROJ_DIM = 2 * D_INNER + 2 * SSD_GROUPS * SSD_STATE + SSD_HEADS
DT_MIN = 0.001
DT_MAX = 0.1

kernel_name = 'hybrid_mla_ssd_macaron_adaln'


def rms_norm(x, gain):
    xf = x.astype(jnp.float32)
    y = xf * lax.rsqrt(jnp.mean(xf * xf, axis=-1, keepdims=True) + EPS)
    return (y * gain.astype(jnp.float32)).astype(x.dtype)


def modulate(x, gain, mod):
    return rms_norm(x, gain) * (1 + mod[:, None, 1]) + mod[:, None, 0]


def swiglu(h, w_gu, w_down):
    g, u = jnp.split(h @ w_gu, 2, axis=-1)
    return (jax.nn.silu(g) * u) @ w_down


def rope_tables(positions):
    inv = 1.0 / (ROPE_THETA ** (jnp.arange(0, QK_ROPE, 2, dtype=jnp.float32) / QK_ROPE))
    ang = positions.astype(jnp.float32)[..., None] * inv
    return jnp.cos(ang), jnp.sin(ang)


def apply_rope(x, cos, sin):
    x1, x2 = jnp.split(x, 2, axis=-1)
    cos = cos[:, :, None].astype(x.dtype)
    sin = sin[:, :, None].astype(x.dtype)
    return jnp.concatenate([x1 * cos - x2 * sin, x1 * sin + x2 * cos], axis=-1)


def causal_block_attention(q, k, v):
    Bn, S, H, Dh = q.shape
    Dv = v.shape[-1]
    nb = S // Q_BLOCK
    scale = Dh ** -0.5
    qb = q.reshape(Bn, nb, Q_BLOCK, H, Dh).transpose(1, 0, 3, 2, 4)
    kt = k.transpose(0, 2, 1, 3)
    vt = v.transpose(0, 2, 1, 3)
    k_pos = jnp.arange(S)

    def one_block(args):
        q_blk, i = args
        s = jnp.einsum('bhqd,bhkd->bhqk', q_blk, kt, preferred_element_type=jnp.float32) * scale
        q_pos = i * Q_BLOCK + jnp.arange(Q_BLOCK)
        s = jnp.where(k_pos[None, :] <= q_pos[:, None], s, -jnp.inf)
        p = jax.nn.softmax(s, axis=-1).astype(vt.dtype)
        return jnp.einsum('bhqk,bhkd->bhqd', p, vt)

    o = lax.map(one_block, (qb, jnp.arange(nb)))
    return o.transpose(1, 0, 3, 2, 4).reshape(Bn, S, H, Dv)


def mla_mixer(h, cos, sin, w_a, q_a_gain, kv_a_gain, w_qb, w_kvb, q_gain, k_gain, w_o):
    Bn, S, _ = h.shape
    q_lat, kv_lat, k_rope = jnp.split(h @ w_a, [Q_LORA, Q_LORA + KV_LORA], axis=-1)
    q = (rms_norm(q_lat, q_a_gain) @ w_qb).reshape(Bn, S, MLA_HEADS, QK_HEAD)
    kv = (rms_norm(kv_lat, kv_a_gain) @ w_kvb).reshape(Bn, S, MLA_HEADS, QK_NOPE + V_HEAD)
    k_nope, v = jnp.split(kv, [QK_NOPE], axis=-1)
    k_rope = jnp.broadcast_to(k_rope[:, :, None, :], (Bn, S, MLA_HEADS, QK_ROPE))
    k = jnp.concatenate([k_nope, k_rope], axis=-1)
    q = rms_norm(q, q_gain)
    k = rms_norm(k, k_gain)
    q = jnp.concatenate([q[..., :QK_NOPE], apply_rope(q[..., QK_NOPE:], cos, sin)], axis=-1)
    k = jnp.concatenate([k[..., :QK_NOPE], apply_rope(k[..., QK_NOPE:], cos, sin)], axis=-1)
    o = causal_block_attention(q, k, v)
    return o.reshape(Bn, S, MLA_HEADS * V_HEAD) @ w_o


def causal_depthwise_conv(u, w, b):
    out = lax.conv_general_dilated(
        u, w[:, None, :].astype(u.dtype), window_strides=(1,), padding=[(CONV_WIDTH - 1, 0)],
        dimension_numbers=('NWC', 'WIO', 'NWC'), feature_group_count=u.shape[-1])
    return out + b


def ssd_chunked_scan(x, dt, A, Bm, Cm):
    Bn, S, H, P = x.shape
    G, N = Bm.shape[2], Bm.shape[3]
    K = H // G
    nc = S // CHUNK
    f32 = jnp.float32
    xdt = (x.astype(f32) * dt[..., None]).reshape(Bn, nc, CHUNK, G, K, P)
    a = (dt * A).reshape(Bn, nc, CHUNK, G, K).transpose(0, 1, 3, 4, 2)
    Bc = Bm.astype(f32).reshape(Bn, nc, CHUNK, G, N)
    Cc = Cm.astype(f32).reshape(Bn, nc, CHUNK, G, N)
    a_cum = jnp.cumsum(a, axis=-1)
    seg = a_cum[..., :, None] - a_cum[..., None, :]
    causal = jnp.tril(jnp.ones((CHUNK, CHUNK), dtype=bool))
    decay_ls = jnp.exp(jnp.where(causal, seg, -jnp.inf))
    cb = jnp.einsum('bclgn,bcsgn->bcgls', Cc, Bc)
    y_diag = jnp.einsum('bcgls,bcgkls,bcsgkp->bclgkp', cb, decay_ls, xdt)
    decay_to_end = jnp.exp(a_cum[..., -1:] - a_cum)
    states = jnp.einsum('bclgn,bcgkl,bclgkp->bcgkpn', Bc, decay_to_end, xdt)
    chunk_decay = jnp.exp(a_cum[..., -1])

    def step(carry, inp):
        st, dec = inp
        return carry * dec[..., None, None] + st, carry

    init = jnp.zeros((Bn, G, K, P, N), f32)
    _, prev = lax.scan(step, init, (states.transpose(1, 0, 2, 3, 4, 5), chunk_decay.transpose(1, 0, 2, 3)))
    prev = prev.transpose(1, 0, 2, 3, 4, 5)
    y_off = jnp.einsum('bclgn,bcgkpn,bcgkl->bclgkp', Cc, prev, jnp.exp(a_cum))
    return (y_diag + y_off).reshape(Bn, S, H, P).astype(x.dtype)


def ssd_mixer(h, w_in, conv_w, conv_b, dt_bias, a_log, d_skip, norm_gain, w_out):
    Bn, S, _ = h.shape
    z, xbc, dt = jnp.split(h @ w_in, [D_INNER, D_INNER + CONV_DIM], axis=-1)
    xbc = jax.nn.silu(causal_depthwise_conv(xbc, conv_w, conv_b))
    xs, Bm, Cm = jnp.split(xbc, [D_INNER, D_INNER + SSD_GROUPS * SSD_STATE], axis=-1)
    xs = xs.reshape(Bn, S, SSD_HEADS, SSD_HEAD_DIM)
    Bm = Bm.reshape(Bn, S, SSD_GROUPS, SSD_STATE)
    Cm = Cm.reshape(Bn, S, SSD_GROUPS, SSD_STATE)
    dt = jax.nn.softplus(dt.astype(jnp.float32) + dt_bias.astype(jnp.float32))
    A = -jnp.exp(a_log.astype(jnp.float32))
    y = ssd_chunked_scan(xs, dt, A, Bm, Cm)
    y = (y + d_skip[:, None] * xs).reshape(Bn, S, D_INNER)
    g = (y * jax.nn.silu(z)).reshape(Bn, S, SSD_GROUPS, D_INNER // SSD_GROUPS)
    g = rms_norm(g, norm_gain.reshape(SSD_GROUPS, D_INNER // SSD_GROUPS)).reshape(Bn, S, D_INNER)
    return g @ w_out


def setup_inputs(seed: int = 0) -> dict:
    key = jax.random.key(seed)
    ks = jax.random.split(key, 32)
    f32 = jnp.float32

    def nrm(k, shape, fan_in, mult=1.0):
        return jax.random.normal(k, shape, f32) * (mult * fan_in ** -0.5)

    def gain(k, shape):
        return 1.0 + 0.02 * jax.random.normal(k, shape, f32)

    x = jax.random.normal(ks[0], (BATCH, SEQ, D_MODEL), f32)
    c = jax.random.normal(ks[1], (BATCH, D_MODEL), f32)
    positions = (jax.random.randint(ks[2], (BATCH, 1), 0, MAX_POS_OFFSET, dtype=jnp.int32)
                 + jnp.arange(SEQ, dtype=jnp.int32)[None, :])
    norm_gain = gain(ks[3], (DEPTH, N_SUB, D_MODEL))
    ada_w = nrm(ks[4], (DEPTH, D_MODEL, N_SUB * 3 * D_MODEL), D_MODEL, 0.5)
    ada_b = 0.02 * jax.random.normal(ks[5], (DEPTH, N_SUB * 3 * D_MODEL), f32)
    ffn_w_gu = nrm(ks[6], (DEPTH, 2, D_MODEL, 2 * D_FF), D_MODEL)
    ffn_w_down = nrm(ks[7], (DEPTH, 2, D_FF, D_MODEL), D_FF)
    mla_w_a = nrm(ks[8], (N_MLA, D_MODEL, MLA_A_DIM), D_MODEL)
    mla_q_a_gain = gain(ks[9], (N_MLA, Q_LORA))
    mla_kv_a_gain = gain(ks[10], (N_MLA, KV_LORA))
    mla_w_qb = nrm(ks[11], (N_MLA, Q_LORA, MLA_HEADS * QK_HEAD), Q_LORA)
    mla_w_kvb = nrm(ks[12], (N_MLA, KV_LORA, MLA_HEADS * (QK_NOPE + V_HEAD)), KV_LORA)
    mla_q_gain = gain(ks[13], (N_MLA, QK_HEAD))
    mla_k_gain = gain(ks[14], (N_MLA, QK_HEAD))
    mla_w_o = nrm(ks[15], (N_MLA, MLA_HEADS * V_HEAD, D_MODEL), MLA_HEADS * V_HEAD)
    ssd_w_in = nrm(ks[16], (N_SSD, D_MODEL, IN_PROJ_DIM), D_MODEL)
    ssd_conv_w = nrm(ks[17], (N_SSD, CONV_WIDTH, CONV_DIM), CONV_WIDTH)
    ssd_conv_b = 0.02 * jax.random.normal(ks[18], (N_SSD, CONV_DIM), f32)
    dt0 = jnp.exp(jax.random.uniform(ks[19], (N_SSD, SSD_HEADS), f32, math.log(DT_MIN), math.log(DT_MAX)))
    ssd_dt_bias = dt0 + jnp.log(-jnp.expm1(-dt0))
    ssd_a_log = jnp.log(jax.random.uniform(ks[20], (N_SSD, SSD_HEADS), f32, 1.0, 16.0))
    ssd_d = gain(ks[21], (N_SSD, SSD_HEADS))
    ssd_norm_gain = gain(ks[22], (N_SSD, D_INNER))
    ssd_w_out = nrm(ks[23], (N_SSD, D_INNER, D_MODEL), D_INNER)
    return {
        'x': x, 'c': c, 'positions': positions,
        'norm_gain': norm_gain, 'ada_w': ada_w, 'ada_b': ada_b,
        'ffn_w_gu': ffn_w_gu, 'ffn_w_down': ffn_w_down,
        'mla_w_a': mla_w_a, 'mla_q_a_gain': mla_q_a_gain, 'mla_kv_a_gain': mla_kv_a_gain,
        'mla_w_qb': mla_w_qb, 'mla_w_kvb': mla_w_kvb, 'mla_q_gain': mla_q_gain,
        'mla_k_gain': mla_k_gain, 'mla_w_o': mla_w_o,
        'ssd_w_in': ssd_w_in, 'ssd_conv_w': ssd_conv_w, 'ssd_conv_b': ssd_conv_b,
        'ssd_dt_bias': ssd_dt_bias, 'ssd_a_log': ssd_a_log, 'ssd_d': ssd_d,
        'ssd_norm_gain': ssd_norm_gain, 'ssd_w_out': ssd_w_out,
    }


def reference(x, c, positions, norm_gain, ada_w, ada_b, ffn_w_gu, ffn_w_down,
              mla_w_a, mla_q_a_gain, mla_kv_a_gain, mla_w_qb, mla_w_kvb, mla_q_gain,
              mla_k_gain, mla_w_o, ssd_w_in, ssd_conv_w, ssd_conv_b, ssd_dt_bias,
              ssd_a_log, ssd_d, ssd_norm_gain, ssd_w_out):
    Bn = x.shape[0]
    cos, sin = rope_tables(positions)
    mods = jnp.einsum('bd,lde->lbe', jax.nn.silu(c), ada_w) + ada_b[:, None, :]
    mods = mods.reshape(DEPTH, Bn, N_SUB, 3, D_MODEL)
    for i in range(DEPTH):
        m = mods[i]
        j = i // N_MIXERS
        h = modulate(x, norm_gain[i, 0], m[:, 0])
        x = x + 0.5 * m[:, None, 0, 2] * swiglu(h, ffn_w_gu[i, 0], ffn_w_down[i, 0])
        h = modulate(x, norm_gain[i, 1], m[:, 1])
        if i % N_MIXERS == 0:
            y = mla_mixer(h, cos, sin, mla_w_a[j], mla_q_a_gain[j], mla_kv_a_gain[j], mla_w_qb[j],
                          mla_w_kvb[j], mla_q_gain[j], mla_k_gain[j], mla_w_o[j])
        else:
            y = ssd_mixer(h, ssd_w_in[j], ssd_conv_w[j], ssd_conv_b[j], ssd_dt_bias[j], ssd_a_log[j],
                          ssd_d[j], ssd_norm_gain[j], ssd_w_out[j])
        x = x + m[:, None, 1, 2] * y
        h = modulate(x, norm_gain[i, 2], m[:, 2])
        x = x + 0.5 * m[:, None, 2, 2] * swiglu(h, ffn_w_gu[i, 1], ffn_w_down[i, 1])
    return x
```

```python
import numpy as np
from contextlib import ExitStack
import concourse.bass as bass
import concourse.mybir as mybir
from concourse.bass_utils import run_bass_kernel_spmd


F32 = mybir.dt.float32
BF16 = mybir.dt.bfloat16
I32 = mybir.dt.int32
AF = mybir.ActivationFunctionType
ALU = mybir.AluOpType
AX = mybir.AxisListType

ENGS = ("pe", "act", "dve", "pool", "sp")


class Buf:
    __slots__ = ("w", "r", "name")

    def __init__(self, name=""):
        self.w = None
        self.r = []
        self.name = name


class KB:
    def __init__(self, nc, ctx):
        self.nc = nc
        self.ctx = ctx
        self.q = {e: [] for e in ENGS}
        self.cnt = {}
        self.semh = {}
        self.seen = {e: {} for e in ENGS}
        self.pe_pending = []
        for e in ENGS:
            self.new_sem("E_" + e)
        self.ndma = 0
        self.NPOOL = 64
        self.buf_key = {}
        self.pool_i = 0

    def phase_reset(self):
        self.buf_key = {}
        self.pool_i = 0

    def _dma_key(self, reads, writes):
        prim = (list(writes) + list(reads))[0]
        k = self.buf_key.get(id(prim))
        if k is None:
            assert self.pool_i < self.NPOOL, "DMA semaphore pool exhausted in this phase"
            k = "DS%d" % self.pool_i
            self.pool_i += 1
            self.buf_key[id(prim)] = k
        return k

    def new_sem(self, key):
        self.semh[key] = self.ctx.enter_context(self.nc.semaphore(key))
        self.cnt[key] = 0
        return key

    def sb(self, name, shape, dtype):
        return self.ctx.enter_context(self.nc.sbuf_tensor(name, list(shape), dtype))

    def ps(self, name, shape, dtype=F32):
        return self.ctx.enter_context(self.nc.psum_tensor(name, list(shape), dtype))

    def _waits(self, eng, toks):
        ws = []
        best = {}
        for t in toks:
            if t is None:
                continue
            s, v = t
            if self.seen[eng].get(s, 0) >= v:
                continue
            if best.get(s, 0) < v:
                best[s] = v
        for s, v in best.items():
            self.seen[eng][s] = v
            ws.append((s, v))
        return ws

    def op(self, eng, fn, reads=(), writes=(), inc=True, extra=()):
        toks = list(extra)
        for b in reads:
            toks.append(b.w)
        for b in writes:
            toks.append(b.w)
            toks.extend(b.r)
        if eng == "pe":
            toks = [t for t in toks if t is not None and t[0] != "E_pe"]
        ws = self._waits(eng, toks)
        key = "E_" + eng
        if eng == "pe" and not inc:
            self.pe_pending.append((tuple(reads), tuple(writes)))
            tok = None
        else:
            self.cnt[key] += 1
            tok = (key, self.cnt[key])
            allrw = [(tuple(reads), tuple(writes))]
            if eng == "pe":
                allrw += self.pe_pending
                self.pe_pending = []
            for rs, wr in allrw:
                for b in rs:
                    b.r.append(tok)
                for b in wr:
                    b.w = tok
                    b.r = []
        semh = self.semh

        def emit(e, fn=fn, ws=ws, tok=tok):
            for s, v in ws:
                e.wait_ge(semh[s], v)
            ins = fn(e)
            if tok is not None:
                ins.then_inc(semh[tok[0]], 1)
        self.q[eng].append(emit)
        return tok

    def dma(self, eng, out, in_, reads=(), writes=(), extra=(), key=None, **kw):
        toks = list(extra)
        for b in reads:
            toks.append(b.w)
        for b in writes:
            toks.append(b.w)
            toks.extend(b.r)
        ws = self._waits(eng, toks)
        key = self._dma_key(reads, writes)
        if key not in self.semh:
            self.new_sem(key)
        self.cnt[key] += 16
        tok = (key, self.cnt[key])
        for b in reads:
            b.r.append(tok)
        for b in writes:
            b.w = tok
            b.r = []
        semh = self.semh

        def emit(e, ws=ws, tok=tok, out=out, in_=in_, kw=kw):
            for s, v in ws:
                e.wait_ge(semh[s], v)
            try:
                e.dma_start(out=out, in_=in_, **kw).then_inc(semh[tok[0]], 16)
            except Exception:
                print("DMA FAIL out", out.shape, out.ap, "in", in_.shape, in_.ap, flush=True)
                raise
        self.q[eng].append(emit)
        return tok

    def wait_all(self, eng, toks):
        ws = self._waits(eng, toks)
        semh = self.semh

        def emit(e, ws=ws):
            for s, v in ws:
                e.wait_ge(semh[s], v)
        self.q[eng].append(emit)

    def emit_all(self):
        nc = self.nc
        q = self.q
        with nc.Block() as block:
            @block.sync
            def _(e):
                for f in q["sp"]:
                    f(e)

            @block.tensor
            def _(e):
                for f in q["pe"]:
                    f(e)

            @block.scalar
            def _(e):
                for f in q["act"]:
                    f(e)

            @block.vector
            def _(e):
                for f in q["dve"]:
                    f(e)

            @block.gpsimd
            def _(e):
                for f in q["pool"]:
                    f(e)


CC_INC = 1


def _kb_cc(self, kind, ins, outs, groups, reads=(), writes=(), extra=()):
    toks = list(extra)
    for b in reads:
        toks.append(b.w)
    for b in writes:
        toks.append(b.w)
        toks.extend(b.r)
    ws = self._waits("pool", toks)
    key = "CC"
    if key not in self.semh:
        self.new_sem(key)
    self.cnt[key] += CC_INC
    tok = (key, self.cnt[key])
    for b in reads:
        b.r.append(tok)
    for b in writes:
        b.w = tok
        b.r = []
    semh = self.semh

    def emit(e, ws=ws, tok=tok):
        for s, v in ws:
            e.wait_ge(semh[s], v)
        e.collective_compute(kind, ALU.bypass, replica_groups=groups, ins=ins, outs=outs).then_inc(semh[tok[0]], CC_INC)
    self.q["pool"].append(emit)
    return tok


KB.cc = _kb_cc


def _kb_dma_if(self, eng, cond, out, in_true, in_false, reads=(), writes=(), extra=()):
    toks = list(extra)
    for b in reads:
        toks.append(b.w)
    for b in writes:
        toks.append(b.w)
        toks.extend(b.r)
    ws = self._waits(eng, toks)
    key = self._dma_key(reads, writes)
    if key not in self.semh:
        self.new_sem(key)
    self.cnt[key] += 16
    tok = (key, self.cnt[key])
    for b in reads:
        b.r.append(tok)
    for b in writes:
        b.w = tok
        b.r = []
    semh = self.semh

    def emit(e, ws=ws, tok=tok):
        for s, v in ws:
            e.wait_ge(semh[s], v)
        with e.If(cond):
            e.dma_start(out=out, in_=in_true).then_inc(semh[tok[0]], 16)
        with e.Else():
            e.dma_start(out=out, in_=in_false).then_inc(semh[tok[0]], 16)
    self.q[eng].append(emit)
    return tok


KB.dma_if = _kb_dma_if


D = 1024
KC = 8
T = 2048
S = 4096
TG = 512
NG = T // TG
DFF = 2816
EPS = 1e-6
NH = 16
DH = 96
QL = 384
KVL = 256
SC96 = 96 ** -0.5
GROUPS = [[0, 1], [2, 3], [4, 5], [6, 7]]


class GBuf:
    def __init__(self, gat_ap, rows, R, snd_ap=None):
        self.gat, self.nrows, self.R, self.snd = gat_ap, rows, R, snd_ap

    def rows(self, s, q0, n):
        R = self.R
        k = q0 // R
        assert (q0 + n - 1) // R == k, (q0, n, R)
        base = k * 2 * R + s * R + q0 % R
        return self.gat[base:base + n, :]

    def gather(self, kb, toks, groups):
        R = self.R
        for k in range(self.nrows // R):
            kb.cc("AllGather", [self.snd[k * R:(k + 1) * R, :]], [self.gat[k * 2 * R:(k + 1) * 2 * R, :]], groups, extra=toks)


class Prog:
    def __init__(self, steps, name="p"):
        self.steps = steps
        self.nc = bass.Bass("TRN2", target_bir_lowering=False)
        self.ctx = ExitStack()
        self.kb = KB(self.nc, self.ctx)
        self.debug = False
        self.dbg_toks = []
        self.inputs = {}
        self.outputs = {}
        self.rot_i = 0

    def din(self, name, shape, dt=F32):
        self.inputs[name] = (tuple(shape), dt)
        return self.nc.dram_tensor(name, list(shape), dt, kind="ExternalInput").ap()

    def dout(self, name, shape, dt=F32):
        self.outputs[name] = (tuple(shape), dt)
        return self.nc.dram_tensor(name, list(shape), dt, kind="ExternalOutput").ap()

    def dint(self, name, shape, dt=F32):
        return self.nc.dram_tensor(name, list(shape), dt).ap()

    def setup(self):
        kb = self.kb
        self.x = kb.sb("x", [128, KC, T], F32)
        self.xb = [[Buf() for g in range(NG)] for k in range(KC)]
        self.SCR = 35328
        self.scratch = kb.sb("scratch", [128, self.SCR], F32)
        self.scr_off = 0
        self.ones = kb.sb("ones", [128, 128], BF16); self.ones_b = Buf()
        self.onesf = kb.sb("onesf", [128, 128], F32); self.onesf_b = Buf()
        kb.op("pool", lambda e: e.memset(self.ones[:], 1.0), writes=[self.ones_b])
        kb.op("pool", lambda e: e.memset(self.onesf[:], 1.0), writes=[self.onesf_b])
        self.c_sb = kb.sb("c_sb", [128, KC], F32); self.c_b = Buf()
        self.sc = kb.sb("sc", [128, KC], F32); self.sc_b = Buf()
        cT = self.din("cT", [128, KC])
        kb.dma("sp", self.c_sb[:], cT, writes=[self.c_b])
        kb.op("act", lambda e: e.activation(self.sc[:], self.c_sb[:], AF.Silu), reads=[self.c_b], writes=[self.sc_b])
        self.pb = [(kb.ps("pb%d" % i, [128, 512]), Buf()) for i in range(8)]
        self.mods = kb.sb("mods", [128, 24], F32); self.mods_b = Buf()
        self.A = kb.sb("A", [128, KC], F32); self.A_b = Buf()
        self.hg = kb.sb("hg", [128, KC], F32); self.hg_b = Buf()
        self.gain_sb = kb.sb("gain_sb", [128, KC], F32); self.gain_b = Buf()
        self.adab_sb = kb.sb("adab_sb", [128, 24], F32); self.adab_b = Buf()
        self.aw_i = 0

    def carve(self, shape, dt, p0=0):
        n = int(np.prod(shape[1:]))
        words = n if dt == F32 or dt == I32 else (n + 1) // 2
        assert self.scr_off + words <= self.SCR, ("scratch overflow", self.scr_off, words)
        v = self.scratch[:, self.scr_off:self.scr_off + words]
        self.scr_off += words
        if dt != F32:
            v = v.bitcast(dt)
        if len(shape) == 3:
            v = v.rearrange("p (a b) -> p a b", b=shape[2])
        elif len(shape) == 4:
            v = v.rearrange("p (a b c) -> p a b c", b=shape[2], c=shape[3])
        return v[p0:p0 + shape[0]] if shape[0] < 128 else v

    def phase_begin(self):
        self.barrier()
        self.kb.phase_reset()
        self.scr_off = 0

    def norm_scratch(self):
        self.h = self.carve([128, KC, T], BF16)
        self.hb = [Buf() for g in range(NG)]
        self.sq = self.carve([128, KC, TG], BF16); self.sq_b = Buf()
        self.sd = self.carve([128, TG], F32); self.sd_b = Buf()
        self.rstd = self.carve([128, TG], F32); self.rstd_b = Buf()
        self.tmp = [(self.carve([128, TG], F32), Buf()) for i in range(2)]

    def dbg(self, name, ap, bufs, dt=F32):
        if not getattr(self, "debug", False):
            return
        shp = list(ap.shape)
        d = self.dout("dbg_" + name, shp, dt)
        self.dbg_toks.append(self.kb.dma("sp", d, ap, reads=bufs))

    def dsel(self, out, in0, in1, reads=(), writes=()):
        if not hasattr(self, "cond0"):
            rr = self.nc.sync.partition_id() % 2
            self.cond0 = (rr == 0)
        return self.kb.dma_if("sp", self.cond0, out, in0, in1, reads=reads, writes=writes)

    def rot(self, lo=0, n=2):
        i = lo + (self.rot_i % n)
        self.rot_i += 1
        return self.pb[i]

    def load_x(self, xT):
        for k in range(KC):
            self.kb.dma("sp", self.x[:, k, :], xT[k * 128:(k + 1) * 128, :], writes=self.xb[k])

    def store_x(self, yT):
        toks = []
        for k in range(KC):
            toks.append(self.kb.dma("sp", yT[k * 128:(k + 1) * 128, :], self.x[:, k, :], reads=self.xb[k]))
        return toks

    def barrier(self):
        kb = self.kb
        toks = [(k, v) for k, v in kb.cnt.items() if v > 0]
        for e in ENGS:
            kb.wait_all(e, toks)

    def emit_mods(self, tag, gate_scale):
        kb = self.kb
        save_off = self.scr_off
        self.aw = [(self.carve([128, KC, 256], F32), Buf()) for i in range(2)]
        adaw = self.din(tag + "_adaw", [D, 3 * D])
        adab = self.din(tag + "_adab", [128, 24])
        gain = self.din(tag + "_gain", [128, KC])
        kb.dma("sp", self.gain_sb[:], gain, writes=[self.gain_b])
        kb.dma("sp", self.adab_sb[:], adab, writes=[self.adab_b])
        ps_mod, ps_mod_b = self.pb[7]
        wv = adaw.rearrange("(k p) n -> p k n", p=128)
        sc = self.sc
        for blk in range(12):
            wt, wb = self.aw[self.aw_i % 2]
            slot = self.aw_i % 2
            self.aw_i += 1
            kb.dma("sp", wt[:], wv[:, :, blk * 256:(blk + 1) * 256], writes=[wb], key="aw%d" % slot)
            for j in range(2):
                col = blk * 2 + j
                for k in range(KC):
                    kb.op("pe", lambda e, col=col, k=k, j=j, wt=wt: e.matmul(
                        ps_mod[:, col:col + 1], lhsT=wt[:, k, j * 128:(j + 1) * 128], rhs=sc[:, k:k + 1],
                        start=(k == 0), stop=(k == KC - 1)),
                        reads=[wb, self.sc_b], writes=[ps_mod_b], inc=(k == KC - 1 and j == 1))
        mods, A, hg = self.mods, self.A, self.hg
        kb.op("dve", lambda e: e.tensor_tensor(mods[:], ps_mod[:, 0:24], self.adab_sb[:], op=ALU.add),
              reads=[ps_mod_b, self.adab_b], writes=[self.mods_b])
        kb.op("dve", lambda e: e.scalar_tensor_tensor(A[:], mods[:, 8:16], 1.0, self.gain_sb[:], op0=ALU.add, op1=ALU.mult),
              reads=[self.mods_b, self.gain_b], writes=[self.A_b])
        kb.op("dve", lambda e: e.tensor_scalar(hg[:], mods[:, 16:24], gate_scale, None, op0=ALU.mult),
              reads=[self.mods_b], writes=[self.hg_b])
        self.barrier()
        self.scr_off = save_off

    def rstd_from_ps(self, ps, psb, npart, dim, out, outb):
        kb = self.kb
        sd, sd_b = self.sd, self.sd_b
        kb.op("dve", lambda e: e.tensor_scalar(sd[0:npart, :], ps[0:npart, :], 1.0 / dim, EPS, op0=ALU.mult, op1=ALU.add),
              reads=[psb], writes=[sd_b])
        kb.op("act", lambda e: e.activation(sd[0:npart, :], sd[0:npart, :], AF.Sqrt), reads=[sd_b], writes=[sd_b])
        kb.op("dve", lambda e: e.reciprocal(out[0:npart, :], sd[0:npart, :]), reads=[sd_b], writes=[outb])

    def emit_h(self):
        kb = self.kb
        x, h, sq, ones = self.x, self.h, self.sq, self.ones
        rstd, A, mods = self.rstd, self.A, self.mods
        ps_ssq, ps_ssq_b = self.pb[6]
        for g in range(NG):
            gs = slice(g * TG, (g + 1) * TG)
            kb.op("act", lambda e, gs=gs: e.activation(sq[:], x[:, :, gs], AF.Square),
                  reads=[self.xb[k][g] for k in range(KC)], writes=[self.sq_b])
            for k in range(KC):
                kb.op("pe", lambda e, k=k: e.matmul(ps_ssq[:], lhsT=ones[:], rhs=sq[:, k, :], start=(k == 0), stop=(k == KC - 1)),
                      reads=[self.ones_b, self.sq_b], writes=[ps_ssq_b], inc=(k == KC - 1))
            self.rstd_from_ps(ps_ssq, ps_ssq_b, 128, D, self.rstd, self.rstd_b)
            for k in range(KC):
                tt, tb = self.tmp[k % 2]
                kb.op("dve", lambda e, k=k, gs=gs, tt=tt: e.scalar_tensor_tensor(
                    tt[:], x[:, k, gs], A[:, k:k + 1], rstd[:], op0=ALU.mult, op1=ALU.mult),
                    reads=[self.xb[k][g], self.A_b, self.rstd_b], writes=[tb])
                kb.op("act", lambda e, k=k, gs=gs, tt=tt: e.activation(
                    h[:, k, gs], tt[:], AF.Identity, bias=mods[:, k:k + 1], scale=1.0),
                    reads=[tb, self.mods_b], writes=[self.hb[g]])

    def ffn(self, tag):
        kb = self.kb
        self.phase_begin()
        self.emit_mods(tag, 0.5)
        self.norm_scratch()
        self.emit_h()
        wgu = self.din(tag + "_wgu", [D, 2 * DFF])
        wdn = self.din(tag + "_wdn", [DFF, D])
        x, h, hg = self.x, self.h, self.hg
        if True:
            sbl = lambda n, s, d: self.carve(s, d)
            blocks = [(0, 4), (4, 4), (8, 4), (12, 4), (16, 4), (20, 2)]
            NBMAX = 4
            wg = [(sbl(tag + "wg%d" % i, [128, KC, NBMAX * 128], BF16), Buf()) for i in range(2)]
            wu = [(sbl(tag + "wu%d" % i, [128, KC, NBMAX * 128], BF16), Buf()) for i in range(2)]
            wd = [(sbl(tag + "wd%d" % i, [128, NBMAX, D], BF16), Buf()) for i in range(2)]
            act = [(sbl(tag + "act%d" % i, [128, NBMAX, TG], BF16), Buf()) for i in range(2)]
            sg = [(sbl(tag + "sg%d" % i, [128, TG], F32), Buf()) for i in range(2)]
            wguv = wgu.rearrange("(k p) n -> p k n", p=128)
            wdnv = wdn.rearrange("(j p) n -> p j n", p=128)
            it = 0
            ci = 0
            for bi, (j0, nb) in enumerate(blocks):
                s = bi % 2
                wgt, wgb = wg[s]; wut, wub = wu[s]; wdt, wdb = wd[s]
                kb.dma("pool", wgt[:, :, 0:nb * 128], wguv[:, :, j0 * 128:(j0 + nb) * 128], writes=[wgb], key=tag + "wg%d" % s)
                kb.dma("pool", wut[:, :, 0:nb * 128], wguv[:, :, DFF + j0 * 128:DFF + (j0 + nb) * 128], writes=[wub], key=tag + "wu%d" % s)
                kb.dma("pool", wdt[:, 0:nb, :], wdnv[:, j0:j0 + nb, :], writes=[wdb], key=tag + "wd%d" % s)
                for g in range(NG):
                    gs = slice(g * TG, (g + 1) * TG)
                    at, ab = act[it % 2]
                    it += 1
                    for jj in range(nb):
                        pg, pgb = self.pb[ci % 2]; pu, pub = self.pb[2 + ci % 2]
                        sgt, sgb = sg[ci % 2]
                        ci += 1
                        for k in range(KC):
                            kb.op("pe", lambda e, k=k, jj=jj, pg=pg, wgt=wgt, gs=gs: e.matmul(
                                pg[:], lhsT=wgt[:, k, jj * 128:(jj + 1) * 128], rhs=h[:, k, gs], start=(k == 0), stop=(k == KC - 1)),
                                reads=[wgb, self.hb[g]], writes=[pgb], inc=(k == KC - 1))
                        for k in range(KC):
                            kb.op("pe", lambda e, k=k, jj=jj, pu=pu, wut=wut, gs=gs: e.matmul(
                                pu[:], lhsT=wut[:, k, jj * 128:(jj + 1) * 128], rhs=h[:, k, gs], start=(k == 0), stop=(k == KC - 1)),
                                reads=[wub, self.hb[g]], writes=[pub], inc=(k == KC - 1))
                        kb.op("act", lambda e, pg=pg, sgt=sgt: e.activation(sgt[:], pg[:], AF.Silu), reads=[pgb], writes=[sgb])
                        kb.op("dve", lambda e, jj=jj, at=at, sgt=sgt, pu=pu: e.tensor_tensor(at[:, jj, :], sgt[:], pu[:], op=ALU.mult),
                              reads=[sgb, pub], writes=[ab])
                    for m in range(KC):
                        py, pyb = self.pb[4 + m % 2]
                        for jj in range(nb):
                            kb.op("pe", lambda e, m=m, jj=jj, py=py, wdt=wdt, at=at: e.matmul(
                                py[:], lhsT=wdt[:, jj, m * 128:(m + 1) * 128], rhs=at[:, jj, :], start=(jj == 0), stop=(jj == nb - 1)),
                                reads=[wdb, ab], writes=[pyb], inc=(jj == nb - 1))
                        kb.op("dve", lambda e, m=m, gs=gs, py=py: e.scalar_tensor_tensor(
                            x[:, m, gs], py[:], hg[:, m:m + 1], x[:, m, gs], op0=ALU.mult, op1=ALU.add),
                            reads=[pyb, self.hg_b, self.xb[m][g]], writes=[self.xb[m][g]])

    def rope_tables(self, ropec_sb, ropec_b, pos_d, gs, cos2, sinS, cs_b, W):
        kb = self.kb
        TWO_PI = float(2 * np.pi)
        C1 = 6.28125
        C2 = float(np.float32(2 * np.pi - 6.28125))
        pi_f = float(np.pi)
        pos_i, ang, kf, ki, r, m = W["pos_i"], W["ang"], W["kf"], W["ki"], W["r"], W["m"]
        wb = W["b"]
        kb.dma("sp", pos_i[:], pos_d[0:1, gs].broadcast_to([32, TG]), writes=[wb])
        kb.op("dve", lambda e: e.tensor_copy(ang[:], pos_i[:]), reads=[wb], writes=[wb])
        kb.op("dve", lambda e: e.tensor_scalar(ang[:], ang[:], ropec_sb[64:96, 0:1], None, op0=ALU.mult), reads=[wb, ropec_b], writes=[wb])
        kb.op("dve", lambda e: e.tensor_scalar(kf[:], ang[:], 1.0 / TWO_PI, None, op0=ALU.mult), reads=[wb], writes=[wb])
        kb.op("dve", lambda e: e.tensor_copy(ki[:], kf[:]), reads=[wb], writes=[wb])
        kb.op("dve", lambda e: e.tensor_copy(kf[:], ki[:]), reads=[wb], writes=[wb])
        kb.op("dve", lambda e: e.scalar_tensor_tensor(r[:], kf[:], -C1, ang[:], op0=ALU.mult, op1=ALU.add), reads=[wb], writes=[wb])
        kb.op("dve", lambda e: e.scalar_tensor_tensor(r[:], kf[:], -C2, r[:], op0=ALU.mult, op1=ALU.add), reads=[wb], writes=[wb])
        kb.op("dve", lambda e: e.tensor_scalar(m[:], r[:], pi_f, TWO_PI, op0=ALU.is_gt, op1=ALU.mult), reads=[wb], writes=[wb])
        kb.op("dve", lambda e: e.tensor_tensor(r[:], r[:], m[:], op=ALU.subtract), reads=[wb], writes=[wb])
        kb.op("dve", lambda e: e.tensor_scalar(m[:], r[:], -pi_f, TWO_PI, op0=ALU.is_lt, op1=ALU.mult), reads=[wb], writes=[wb])
        kb.op("dve", lambda e: e.tensor_tensor(r[:], r[:], m[:], op=ALU.add), reads=[wb], writes=[wb])
        kb.op("act", lambda e: e.activation(sinS[:], r[:], AF.Sin), reads=[wb], writes=[cs_b])
        kb.op("dve", lambda e: e.tensor_scalar(sinS[:], sinS[:], ropec_sb[64:96, 1:2], None, op0=ALU.mult), reads=[cs_b, ropec_b], writes=[cs_b])
        kb.op("dve", lambda e: e.tensor_scalar(r[:], r[:], pi_f / 2, None, op0=ALU.add), reads=[wb, cs_b], writes=[wb])
        kb.op("dve", lambda e: e.tensor_scalar(m[:], r[:], pi_f, TWO_PI, op0=ALU.is_gt, op1=ALU.mult), reads=[wb], writes=[wb])
        kb.op("dve", lambda e: e.tensor_tensor(r[:], r[:], m[:], op=ALU.subtract), reads=[wb], writes=[wb])
        kb.op("act", lambda e: e.activation(cos2[:], r[:], AF.Sin), reads=[wb], writes=[cs_b])

    def mla_p1(self, tag, snd):
        kb = self.kb
        self.phase_begin()
        self.emit_mods(tag, 1.0)
        self.norm_scratch()
        self.emit_h()
        x, h, ones = self.x, self.h, self.ones
        wa_d = self.din(tag + "_wa", [D, 704])
        wq_d = self.din(tag + "_wq", [QL, 16 * 128])
        wkn_d = self.din(tag + "_wkn", [KVL, 1024])
        wv_d = self.din(tag + "_wv", [KVL, 1024])
        mg_d = self.din(tag + "_mg", [128, 12])
        if "ropec" not in self.inputs:
            self.ropec_d = self.din("ropec", [32, 2])
            self.pos_d = self.din("pos", [1, T], I32)
        C = self.carve
        wa = C([128, KC, 704], BF16); wa_b = Buf()
        wq = C([128, 3, 2048], BF16); wq_b = Buf()
        wkn = C([128, 2, 1024], BF16); wkn_b = Buf()
        wv = C([128, 2, 1024], BF16); wv_b = Buf()
        mg = C([128, 12], F32); mg_b = Buf()
        ropec = C([128, 2], F32); ropec_b = Buf()
        sel2 = C([128, 2], BF16); sel2_b = Buf()
        alat = C([128, 3, TG], F32); alat_b = Buf()
        sqa = C([128, 3, TG], BF16); sqa_b = Buf()
        qn = C([128, 3, TG], BF16); qn_b = Buf()
        kvn = C([128, 2, TG], BF16); kvn_b = Buf()
        cos2 = C([32, TG], F32, 64); sinS = C([32, TG], F32, 64); cs_b = Buf()
        RW = {"pos_i": C([32, TG], I32, 64), "ang": C([32, TG], F32, 64), "kf": C([32, TG], F32, 64), "ki": C([32, TG], I32, 64),
              "r": C([32, TG], F32, 64), "m": C([32, TG], F32, 64), "b": Buf()}
        krsq = C([32, TG], BF16, 64); krsq_b = Buf()
        t1 = C([32, TG], F32, 64); t1_b = Buf()
        t2 = C([32, TG], F32, 64); t2_b = Buf()
        krf = C([32, TG], BF16, 64); krf_b = Buf()
        kst = [(C([128, TG], BF16), Buf()) for i in range(2)]
        ksq = [(C([128, TG], BF16), Buf()) for i in range(2)]
        v_sb = C([128, 4, 1024], BF16); v_b = Buf()
        qsq = C([96, TG], BF16); qsq_b = Buf()
        rq96 = C([96, TG], F32); rq96_b = Buf()
        qst = [(C([96, TG], BF16), Buf()) for i in range(2)]
        rsum = C([128, 4], F32); rsum_b = Buf()
        s1 = C([128, 4, 16], F32); s1_b = Buf()
        rstd, rstd_b = self.rstd, self.rstd_b
        kb.dma("pool", wa[:], wa_d.rearrange("(k p) n -> p k n", p=128), writes=[wa_b])
        kb.dma("pool", wq[:], wq_d.rearrange("(k p) n -> p k n", p=128), writes=[wq_b])
        kb.dma("pool", wkn[:], wkn_d.rearrange("(k p) n -> p k n", p=128), writes=[wkn_b])
        kb.dma("pool", wv[:], wv_d.rearrange("(k p) n -> p k n", p=128), writes=[wv_b])
        kb.dma("sp", mg[:], mg_d, writes=[mg_b])
        kb.dma("sp", ropec[64:96, :], self.ropec_d, writes=[ropec_b])
        kb.op("pool", lambda e: e.memset(sel2[:], 0.0), writes=[sel2_b])
        kb.op("pool", lambda e: e.memset(sel2[0:64, 0:1], 1.0), writes=[sel2_b])
        kb.op("pool", lambda e: e.memset(sel2[64:128, 1:2], 1.0), writes=[sel2_b])
        QT, KT, V, RK = snd["QT"], snd["KT"], snd["V"], snd["RK"]
        pb = self.pb
        p_ssq, p_ssq_b = pb[6]
        p_kr, p_kr_b = pb[3]
        p_krs, p_krs_b = pb[4]
        p_st, p_st_b = pb[5]
        out_toks = []
        qi = 0
        for g in range(NG):
            gs = slice(g * TG, (g + 1) * TG)
            self.rope_tables(ropec, ropec_b, self.pos_d, gs, cos2, sinS, cs_b, RW)

            def a_chunks(c0, n):
                for ci in range(n):
                    ps, psb = self.rot()
                    for k in range(KC):
                        kb.op("pe", lambda e, k=k, ci=ci, ps=ps, gs=gs, c0=c0: e.matmul(
                            ps[:], lhsT=wa[:, k, (c0 + ci) * 128:(c0 + ci + 1) * 128], rhs=h[:, k, gs], start=(k == 0), stop=(k == KC - 1)),
                            reads=[wa_b, self.hb[g]], writes=[psb], inc=(k == KC - 1))
                    kb.op("act", lambda e, ci=ci, ps=ps: e.activation(alat[:, ci, :], ps[:], AF.Copy), reads=[psb], writes=[alat_b])
                    kb.op("act", lambda e, ci=ci, ps=ps: e.activation(sqa[:, ci, :], ps[:], AF.Square), reads=[psb], writes=[sqa_b])
                for ci in range(n):
                    kb.op("pe", lambda e, ci=ci, n=n: e.matmul(p_ssq[:], lhsT=ones[:], rhs=sqa[:, ci, :], start=(ci == 0), stop=(ci == n - 1)),
                          reads=[self.ones_b, sqa_b], writes=[p_ssq_b], inc=(ci == n - 1))
                self.rstd_from_ps(p_ssq, p_ssq_b, 128, n * 128, rstd, rstd_b)
            a_chunks(0, 3)
            for ci in range(3):
                kb.op("dve", lambda e, ci=ci: e.scalar_tensor_tensor(qn[:, ci, :], alat[:, ci, :], mg[:, 5 + ci:6 + ci], rstd[:], op0=ALU.mult, op1=ALU.mult),
                      reads=[alat_b, mg_b, rstd_b], writes=[qn_b])
            if g == 0:
                self.dbg("h", h[:, :, 0:TG], self.hb, BF16)
                self.dbg("mods", self.mods[:], [self.mods_b])
                self.dbg("qn", qn[:], [qn_b], BF16)
                self.dbg("alat_q", alat[:], [alat_b])
            a_chunks(3, 2)
            for ci in range(2):
                kb.op("dve", lambda e, ci=ci: e.scalar_tensor_tensor(kvn[:, ci, :], alat[:, ci, :], mg[:, 3 + ci:4 + ci], rstd[:], op0=ALU.mult, op1=ALU.mult),
                      reads=[alat_b, mg_b, rstd_b], writes=[kvn_b])
            if g == 0:
                self.dbg("kvn", kvn[:], [kvn_b], BF16)
                self.dbg("cos2", cos2[:], [cs_b])
                self.dbg("sinS", sinS[:], [cs_b])
            for k in range(KC):
                kb.op("pe", lambda e, k=k, gs=gs: e.matmul(p_kr[64:96, :], lhsT=wa[:, k, 640:672], rhs=h[:, k, gs], start=(k == 0), stop=(k == KC - 1)),
                      reads=[wa_b, self.hb[g]], writes=[p_kr_b], inc=(k == KC - 1))
            for k in range(KC):
                kb.op("pe", lambda e, k=k, gs=gs: e.matmul(p_krs[64:96, :], lhsT=wa[:, k, 672:704], rhs=h[:, k, gs], start=(k == 0), stop=(k == KC - 1)),
                      reads=[wa_b, self.hb[g]], writes=[p_krs_b], inc=(k == KC - 1))
            kb.op("act", lambda e: e.activation(krsq[:], p_kr[64:96, :], AF.Square), reads=[p_kr_b], writes=[krsq_b])
            kb.op("dve", lambda e: e.scalar_tensor_tensor(t1[:], p_kr[64:96, :], mg[64:96, 1:2], cos2[:], op0=ALU.mult, op1=ALU.mult),
                  reads=[p_kr_b, mg_b, cs_b], writes=[t1_b])
            kb.op("dve", lambda e: e.scalar_tensor_tensor(t2[:], p_krs[64:96, :], mg[64:96, 2:3], sinS[:], op0=ALU.mult, op1=ALU.mult),
                  reads=[p_krs_b, mg_b, cs_b], writes=[t2_b])
            kb.op("pool", lambda e: e.tensor_tensor(krf[:], t1[:], t2[:], op=ALU.add), reads=[t1_b, t2_b], writes=[krf_b])
            KTv = KT.rearrange("(h p) t -> p h t", p=DH)
            out_toks.append(kb.dma("sp", KTv[64:96, :, gs], krf[:].unsqueeze(1).broadcast_to([32, NH, TG]), reads=[krf_b]))
            for tt in range(4):
                kb.op("pe", lambda e, tt=tt: e.matmul(p_st[:, 64 + tt:65 + tt], lhsT=krsq[:, tt * 128:(tt + 1) * 128], rhs=ones[64:96, 0:1],
                                                      start=True, stop=True),
                      reads=[krsq_b, self.ones_b], writes=[p_st_b], inc=(tt == 3))
            for j in range(8):
                ps, psb = self.rot()
                for kc in range(2):
                    kb.op("pe", lambda e, kc=kc, j=j, ps=ps: e.matmul(ps[:], lhsT=wkn[:, kc, j * 128:(j + 1) * 128], rhs=kvn[:, kc, :],
                                                                  start=(kc == 0), stop=(kc == 1)),
                          reads=[wkn_b, kvn_b], writes=[psb], inc=(kc == 1))
                kt_, ktb = kst[j % 2]
                kq_, kqb = ksq[j % 2]
                kb.op("act", lambda e, ps=ps, kt_=kt_: e.activation(kt_[:], ps[:], AF.Identity, scale=mg[:, 0:1]), reads=[psb, mg_b], writes=[ktb])
                kb.op("act", lambda e, ps=ps, kq_=kq_: e.activation(kq_[:], ps[:], AF.Square), reads=[psb], writes=[kqb])
                for hh in range(2):
                    hq = 2 * j + hh
                    out_toks.append(kb.dma("sp", KT[hq * DH:hq * DH + 64, gs], kt_[hh * 64:(hh + 1) * 64, :], reads=[ktb]))
                for tt in range(4):
                    kb.op("pe", lambda e, tt=tt, j=j, kq_=kq_: e.matmul(p_st[:, tt * 16 + 2 * j:tt * 16 + 2 * j + 2],
                                                                     lhsT=kq_[:, tt * 128:(tt + 1) * 128], rhs=sel2[:], start=True, stop=True),
                          reads=[kqb, sel2_b], writes=[p_st_b], inc=(tt == 3))
            kb.op("act", lambda e: e.activation(rsum[:], p_st[:, 64:68], AF.Copy), reads=[p_st_b], writes=[rsum_b])
            kb.op("dve", lambda e: e.tensor_tensor(s1[:], p_st[:, 0:64].rearrange("p (a b) -> p a b", b=16),
                                                   rsum[:].unsqueeze(2).broadcast_to([128, 4, 16]), op=ALU.add),
                  reads=[p_st_b, rsum_b], writes=[s1_b])
            kb.op("dve", lambda e: e.tensor_scalar(s1[:], s1[:], 1.0 / DH, EPS, op0=ALU.mult, op1=ALU.add), reads=[s1_b], writes=[s1_b])
            kb.op("act", lambda e: e.activation(s1[:], s1[:], AF.Sqrt), reads=[s1_b], writes=[s1_b])
            kb.op("dve", lambda e: e.reciprocal(s1[:], s1[:]), reads=[s1_b], writes=[s1_b])
            kb.op("dve", lambda e: e.tensor_scalar(s1[:], s1[:], SC96, None, op0=ALU.mult), reads=[s1_b], writes=[s1_b])
            out_toks.append(kb.dma("sp", RK[g * TG:(g + 1) * TG, :].rearrange("(a p) h -> p a h", p=128), s1[:], reads=[s1_b]))
            for tt in range(4):
                for half in range(2):
                    ps, psb = self.rot()
                    for kc in range(2):
                        kb.op("pe", lambda e, kc=kc, tt=tt, half=half, ps=ps: e.matmul(
                            ps[:], lhsT=kvn[:, kc, tt * 128:(tt + 1) * 128], rhs=wv[:, kc, half * 512:(half + 1) * 512],
                            start=(kc == 0), stop=(kc == 1)),
                            reads=[wv_b, kvn_b], writes=[psb], inc=(kc == 1))
                    if half == 0:
                        kb.op("act", lambda e, tt=tt, ps=ps: e.activation(v_sb[:, tt, 0:512], ps[:], AF.Copy), reads=[psb], writes=[v_b])
                    else:
                        kb.op("dve", lambda e, tt=tt, ps=ps: e.tensor_copy(v_sb[:, tt, 512:1024], ps[:]), reads=[psb], writes=[v_b])
            out_toks.append(kb.dma("sp", V[g * TG:(g + 1) * TG, :].rearrange("(a p) n -> p a n", p=128), v_sb[:], reads=[v_b]))
            for hq in range(NH):
                ps, psb = self.rot()
                for kc in range(3):
                    kb.op("pe", lambda e, kc=kc, hq=hq, ps=ps: e.matmul(ps[0:96, :], lhsT=wq[:, kc, hq * 128:hq * 128 + 96], rhs=qn[:, kc, :],
                                                                    start=(kc == 0), stop=(kc == 2)),
                          reads=[wq_b, qn_b], writes=[psb], inc=(kc == 2))
                for kc in range(3):
                    kb.op("pe", lambda e, kc=kc, hq=hq: e.matmul(p_krs[64:96, :], lhsT=wq[:, kc, hq * 128 + 96:hq * 128 + 128], rhs=qn[:, kc, :],
                                                              start=(kc == 0), stop=(kc == 2)),
                          reads=[wq_b, qn_b], writes=[p_krs_b], inc=(kc == 2))
                kb.op("act", lambda e, ps=ps: e.activation(qsq[:], ps[0:96, :], AF.Square), reads=[psb], writes=[qsq_b])
                kb.op("pe", lambda e: e.matmul(p_ssq[0:96, :], lhsT=ones[0:96, 0:96], rhs=qsq[:], start=True, stop=True),
                      reads=[self.ones_b, qsq_b], writes=[p_ssq_b])
                self.rstd_from_ps(p_ssq, p_ssq_b, 96, DH, rq96, rq96_b)
                qs_, qsb = qst[qi % 2]
                qi += 1
                kb.op("dve", lambda e, ps=ps, qs_=qs_: e.scalar_tensor_tensor(qs_[0:64, :], ps[0:64, :], mg[0:64, 8:9], rq96[0:64, :],
                                                                            op0=ALU.mult, op1=ALU.mult),
                      reads=[psb, mg_b, rq96_b], writes=[qsb])
                kb.op("dve", lambda e, ps=ps: e.scalar_tensor_tensor(t1[:], ps[64:96, :], mg[64:96, 8:9], cos2[:], op0=ALU.mult, op1=ALU.mult),
                      reads=[psb, mg_b, cs_b], writes=[t1_b])
                kb.op("dve", lambda e: e.scalar_tensor_tensor(t2[:], p_krs[64:96, :], mg[64:96, 9:10], sinS[:], op0=ALU.mult, op1=ALU.mult),
                      reads=[p_krs_b, mg_b, cs_b], writes=[t2_b])
                kb.op("pool", lambda e: e.tensor_tensor(t1[:], t1[:], t2[:], op=ALU.add), reads=[t1_b, t2_b], writes=[t1_b])
                kb.op("pool", lambda e, qs_=qs_: e.tensor_tensor(qs_[64:96, :], t1[:], rq96[64:96, :], op=ALU.mult),
                      reads=[t1_b, rq96_b], writes=[qsb])
                out_toks.append(kb.dma("sp", QT[hq * DH:(hq + 1) * DH, gs], qs_[:], reads=[qsb]))
        return out_toks

    def mla_p2(self, tag, gat, snd_o):
        kb = self.kb
        nc = self.nc
        self.phase_begin()
        C = self.carve
        GQ, GK, GV, GRK = gat["QT"], gat["KT"], gat["V"], gat["RK"]
        rk_all = C([128, 32, 8], F32); rk_b = Buf()
        v_all = C([128, 32, 512], BF16); v_b = Buf()
        qT = [(C([96, S], BF16), Buf()) for i in range(2)]
        kT = [(C([96, S], BF16), Buf()) for i in range(2)]
        pt = [(C([128, TG], BF16), Buf()) for i in range(3)]
        ost = [(C([64, S], BF16), Buf()) for i in range(2)]
        rec = C([64, TG], F32); rec_b = Buf()
        ones = self.ones
        for s in range(2):
            rkr = GRK.rows(s, 0, T)
            self.dsel(rk_all[:, s * 16:(s + 1) * 16, :],
                      rkr[:, 0:8].rearrange("(a p) h -> p a h", p=128),
                      rkr[:, 8:16].rearrange("(a p) h -> p a h", p=128), writes=[rk_b])
            for a in range(2):
                vr = GV.rows(s, a * 1024, 1024)
                self.dsel(v_all[:, s * 16 + a * 8:s * 16 + (a + 1) * 8, :],
                          vr[:, 0:512].rearrange("(a p) n -> p a n", p=128),
                          vr[:, 512:1024].rearrange("(a p) n -> p a n", p=128), writes=[v_b])
        pb = self.pb
        out_toks = []
        ti = 0
        gi = 0
        for hh in range(8):
            qt_, qb = qT[hh % 2]
            kt_, kbf = kT[hh % 2]
            for s in range(2):
                self.dsel(qt_[:, s * T:(s + 1) * T], GQ.rows(s, hh * DH, DH), GQ.rows(s, (8 + hh) * DH, DH), writes=[qb])
                self.dsel(kt_[:, s * T:(s + 1) * T], GK.rows(s, hh * DH, DH), GK.rows(s, (8 + hh) * DH, DH), writes=[kbf])
            os_, osb = ost[hh % 2]
            for gq in range(8):
                qs = slice(gq * TG, (gq + 1) * TG)
                po, pob = pb[3 + gi % 2]
                pd, pdb = pb[5 + gi % 2]
                gi += 1
                nkt = 4 * (gq + 1)
                for kt in range(nkt):
                    ps, psb = pb[ti % 3]
                    p_, p_b = pt[ti % 3]
                    ti += 1
                    kb.op("pe", lambda e, kt=kt, ps=ps, kt_=kt_, qt_=qt_, qs=qs: e.matmul(
                        ps[:], lhsT=kt_[:, kt * 128:(kt + 1) * 128], rhs=qt_[:, qs], start=True, stop=True),
                        reads=[kbf, qb], writes=[psb])
                    kb.op("act", lambda e, kt=kt, hh=hh, ps=ps, p_=p_: e.activation(p_[:], ps[:], AF.Exp, scale=rk_all[:, kt, hh:hh + 1]),
                          reads=[psb, rk_b], writes=[p_b])
                    if kt >= 4 * gq:
                        base = gq * TG - kt * 128
                        kb.op("pool", lambda e, p_=p_, base=base: e.affine_select(
                            out=p_[:], in_=p_[:], pattern=[[1, TG]], compare_op=ALU.is_ge, fill=0.0, base=base, channel_multiplier=-1),
                            reads=[p_b], writes=[p_b])
                    kb.op("pe", lambda e, kt=kt, hh=hh, po=po, p_=p_, nkt=nkt: e.matmul(
                        po[0:64, :], lhsT=v_all[:, kt, hh * 64:(hh + 1) * 64], rhs=p_[:], start=(kt == 0), stop=(kt == nkt - 1)),
                        reads=[v_b, p_b], writes=[pob], inc=(kt == nkt - 1))
                    kb.op("pe", lambda e, kt=kt, pd=pd, p_=p_, nkt=nkt: e.matmul(
                        pd[0:64, :], lhsT=ones[:, 0:64], rhs=p_[:], start=(kt == 0), stop=(kt == nkt - 1)),
                        reads=[self.ones_b, p_b], writes=[pdb], inc=(kt == nkt - 1))
                kb.op("dve", lambda e, pd=pd: e.reciprocal(rec[:], pd[0:64, :]), reads=[pdb], writes=[rec_b])
                kb.op("dve", lambda e, po=po, os_=os_, qs=qs: e.tensor_tensor(os_[:, qs], po[0:64, :], rec[:], op=ALU.mult),
                      reads=[pob, rec_b], writes=[osb])
            out_toks.append(kb.dma("sp", snd_o[hh * 64:(hh + 1) * 64, :], os_[:], reads=[osb]))
        return out_toks

    def mixer_p3(self, tag, gat_o, nk, recompute_mods, wname, gate_scale=1.0):
        kb = self.kb
        nc = self.nc
        self.phase_begin()
        if recompute_mods:
            self.emit_mods(tag, gate_scale)
        C = self.carve
        wo_d = self.din(tag + wname, [nk * 128, D])
        wo = C([128, nk, D], BF16); wo_b = Buf()
        kb.dma("pool", wo[:], wo_d.rearrange("(k p) n -> p k n", p=128), writes=[wo_b])
        osb = [(C([128, nk, TG], BF16), Buf()) for i in range(2)]
        x, hg = self.x, self.hg
        for g in range(NG):
            gs = slice(g * TG, (g + 1) * TG)
            o_, ob = osb[g % 2]
            R = gat_o.R
            for s_ in range(2):
                for kk in range(gat_o.nrows // R):
                    rr_ = gat_o.rows(s_, kk * R, R)
                    c0_ = (s_ * gat_o.nrows + kk * R) // 128
                    self.dsel(o_[:, c0_:c0_ + R // 128, :], rr_[:, g * TG:(g + 1) * TG].rearrange("(k p) t -> p k t", p=128),
                              rr_[:, T + g * TG:T + (g + 1) * TG].rearrange("(k p) t -> p k t", p=128), writes=[ob])
            for m in range(KC):
                py, pyb = self.pb[m % 2]
                for k in range(nk):
                    kb.op("pe", lambda e, m=m, k=k, py=py, o_=o_: e.matmul(py[:], lhsT=wo[:, k, m * 128:(m + 1) * 128], rhs=o_[:, k, :],
                                                                       start=(k == 0), stop=(k == nk - 1)),
                          reads=[wo_b, ob], writes=[pyb], inc=(k == nk - 1))
                kb.op("dve", lambda e, m=m, gs=gs, py=py: e.scalar_tensor_tensor(
                    x[:, m, gs], py[:], hg[:, m:m + 1], x[:, m, gs], op0=ALU.mult, op1=ALU.add),
                    reads=[pyb, self.hg_b, self.xb[m][g]], writes=[self.xb[m][g]])

    def ssd_p1(self, tag, snd):
        kb = self.kb
        self.phase_begin()
        self.emit_mods(tag, 1.0)
        self.norm_scratch()
        self.emit_h()
        h = self.h
        win_d = self.din(tag + "_win", [D, 5152])
        wv = win_d.rearrange("(k p) n -> p k n", p=128)
        C = self.carve
        wblk = [(C([128, KC, 512], BF16), Buf()) for i in range(2)]
        wdt = C([128, KC, 32], BF16); wdt_b = Buf()
        stg = [(C([128, TG], BF16), Buf()) for i in range(4)]
        dts = C([128, 16, 32], F32); dts_b = Buf()
        XBC, Z, DT = snd["XBC"], snd["Z"], snd["DT"]
        out_toks = []
        kb.dma("pool", wdt[:], wv[:, :, 5120:5152], writes=[wdt_b])
        bi = 0
        si = 0
        for blk in range(6):
            wt, wb = wblk[bi % 2]; bi += 1
            kb.dma("pool", wt[:], wv[:, :, 2048 + blk * 512:2048 + (blk + 1) * 512], writes=[wb])
            for cc in range(4):
                ch = blk * 4 + cc
                for g in range(NG):
                    gs = slice(g * TG, (g + 1) * TG)
                    ps, psb = self.rot(0, 4)
                    for k in range(KC):
                        kb.op("pe", lambda e, k=k, cc=cc, ps=ps, wt=wt, gs=gs: e.matmul(
                            ps[:], lhsT=wt[:, k, cc * 128:(cc + 1) * 128], rhs=h[:, k, gs], start=(k == 0), stop=(k == KC - 1)),
                            reads=[wb, self.hb[g]], writes=[psb], inc=(k == KC - 1))
                    st, stb = stg[si % 4]; si += 1
                    if si % 2 == 0:
                        kb.op("act", lambda e, ps=ps, st=st: e.activation(st[:], ps[:], AF.Copy), reads=[psb], writes=[stb])
                    else:
                        kb.op("dve", lambda e, ps=ps, st=st: e.tensor_copy(st[:], ps[:]), reads=[psb], writes=[stb])
                    out_toks.append(kb.dma("sp", XBC[ch * 128:(ch + 1) * 128, gs], st[:], reads=[stb]))
        for blk in range(4):
            wt, wb = wblk[bi % 2]; bi += 1
            kb.dma("pool", wt[:], wv[:, :, blk * 512:(blk + 1) * 512], writes=[wb])
            for tt in range(16):
                ps, psb = self.rot(0, 4)
                for k in range(KC):
                    kb.op("pe", lambda e, k=k, tt=tt, ps=ps, wt=wt: e.matmul(
                        ps[:], lhsT=h[:, k, tt * 128:(tt + 1) * 128], rhs=wt[:, k, :], start=(k == 0), stop=(k == KC - 1)),
                        reads=[wb, self.hb[tt // 4]], writes=[psb], inc=(k == KC - 1))
                st, stb = stg[si % 4]; si += 1
                if si % 2 == 0:
                    kb.op("act", lambda e, ps=ps, st=st: e.activation(st[:], ps[:], AF.Copy), reads=[psb], writes=[stb])
                else:
                    kb.op("dve", lambda e, ps=ps, st=st: e.tensor_copy(st[:], ps[:]), reads=[psb], writes=[stb])
                out_toks.append(kb.dma("sp", Z[tt * 128:(tt + 1) * 128, blk * 512:(blk + 1) * 512], st[:], reads=[stb]))
        pdt, pdt_b = self.pb[4]
        for tt in range(16):
            for k in range(KC):
                kb.op("pe", lambda e, k=k, tt=tt: e.matmul(pdt[:, tt * 32:(tt + 1) * 32], lhsT=h[:, k, tt * 128:(tt + 1) * 128], rhs=wdt[:, k, :],
                                                        start=(k == 0), stop=(k == KC - 1)),
                      reads=[wdt_b, self.hb[tt // 4]], writes=[pdt_b], inc=(k == KC - 1))
        kb.op("dve", lambda e: e.tensor_copy(dts[:], pdt[:].rearrange("p (a b) -> p a b", b=32)), reads=[pdt_b], writes=[dts_b])
        out_toks.append(kb.dma("sp", DT.rearrange("(a p) h -> p a h", p=128), dts[:], reads=[dts_b]))
        return out_toks

    def ssd_p2(self, tag, gat, snd_g):
        kb = self.kb
        nc = self.nc
        self.phase_begin()
        C = self.carve
        GX, GZ, GDT = gat["XBC"], gat["Z"], gat["DT"]
        cw_d = self.din(tag + "_cw", [128, 48])
        cb_d = self.din(tag + "_cb", [128, 12])
        cbrow_d = self.din(tag + "_cbrow", [1, 1280])
        hp_d = self.din(tag + "_hp", [128, 48])
        ng_d = self.din(tag + "_ng", [128, 1024])
        cw = C([128, 48], F32); cw_b = Buf()
        cb = C([128, 12], F32); cb_b = Buf()
        cbrow = C([1, 1280], F32); cbrow_b = Buf()
        cbhi = C([1, 1280], BF16); cblo = C([1, 1280], BF16); cbf = C([1, 1280], F32); cbhl_b = Buf()
        hp = C([128, 48], F32); hp_b = Buf()
        ng = C([128, 1024], F32); ng_b = Buf()
        identb = C([128, 128], BF16); Tb = C([128, 128], BF16)
        U = C([128, 128], F32); Tm = C([128, 128], F32); cst_b = Buf()
        diag = C([128, 48, 128], BF16); diag_b = Buf()
        Aneg = C([128, 16], F32); Aneg_b = Buf()
        u = C([128, 12, TG + 4], BF16); u_b = Buf()
        zt = C([128, 4, 1024], BF16); zt_b = Buf()
        dtr = C([128, 4, 16], F32); dtr_b = Buf()
        xs = C([128, 4, 1024], F32); xs_b = Buf()
        Btok = C([128, 4, 256], BF16); Btok_b = Buf()
        BT = C([128, 2, TG], BF16); CT = C([128, 2, TG], BF16); bct_b = Buf()
        dtv = C([128, 4, 16], F32); av = C([128, 4, 16], F32); dtv_b = Buf()
        acum = C([128, 16], F32); ea = C([128, 16], F32); dte = C([128, 16], F32); cd = C([128, 16], F32); sm_b = Buf()
        aU = C([128, 16, 128], F32); aU_b = Buf()
        dec = [(C([128, 8, 128], BF16), Buf()) for i in range(2)]
        cbm = [(C([128, 128], BF16), Buf()) for i in range(2)]
        MT = [(C([128, 8, 128], BF16), Buf()) for i in range(2)]
        xdt = C([128, 1024], BF16); xdt_b = Buf()
        Bdec = C([128, 16, 128], BF16); Bdec_b = Buf()
        Sf = C([128, 1024], F32); Sf_b = Buf()
        Sb = C([128, 1024], BF16); Sb_b = Buf()
        t1 = C([128, 1024], F32); t1_b = Buf()
        t3 = C([128, 1024], F32); t3_b = Buf()
        yv = C([128, 1024], F32); yv_b = Buf()
        sz = t3; sz_b = t3_b
        ssq = C([128, 2], F32); ssq_b = Buf()
        junk = C([128, 512], BF16); junk_b = Buf()
        gn = C([128, 1024], BF16); gn_b = Buf()
        gT = [(C([128, 8, TG], BF16), Buf()) for i in range(1)]
        ones, onesf = self.ones, self.onesf
        pb = self.pb
        kb.dma("sp", cw[:], cw_d, writes=[cw_b])
        kb.dma("sp", cb[:], cb_d, writes=[cb_b])
        kb.dma("sp", cbrow[:], cbrow_d, writes=[cbrow_b])
        kb.dma("sp", hp[:], hp_d, writes=[hp_b])
        kb.dma("sp", ng[:], ng_d, writes=[ng_b])
        kb.op("dve", lambda e: e.tensor_copy(cbhi[:], cbrow[:]), reads=[cbrow_b], writes=[cbhl_b])
        kb.op("dve", lambda e: e.tensor_copy(cbf[:], cbhi[:]), reads=[cbhl_b], writes=[cbhl_b])
        kb.op("dve", lambda e: e.tensor_tensor(cbf[:], cbrow[:], cbf[:], op=ALU.subtract), reads=[cbhl_b, cbrow_b], writes=[cbhl_b])
        kb.op("dve", lambda e: e.tensor_copy(cblo[:], cbf[:]), reads=[cbhl_b], writes=[cbhl_b])
        for (tile_, pat, base, cm, cmp_) in ((Tm, [[1, 128]], 0, -1, ALU.is_ge), (U, [[-1, 128]], -1, 1, ALU.is_ge)):
            kb.op("pool", lambda e, tile_=tile_: e.memset(tile_[:], 1.0), writes=[cst_b])
            kb.op("pool", lambda e, tile_=tile_, pat=pat, base=base, cm=cm, cmp_=cmp_: e.affine_select(
                out=tile_[:], in_=tile_[:], pattern=pat, compare_op=cmp_, fill=0.0, base=base, channel_multiplier=cm),
                reads=[cst_b], writes=[cst_b])
        kb.op("pool", lambda e: e.memset(identb[:], 1.0), writes=[cst_b])
        kb.op("pool", lambda e: e.affine_select(out=identb[:], in_=identb[:], pattern=[[-1, 128]], compare_op=ALU.is_equal, fill=0.0,
                                                base=0, channel_multiplier=1), reads=[cst_b], writes=[cst_b])
        kb.op("dve", lambda e: e.tensor_copy(Tb[:], Tm[:]), reads=[cst_b], writes=[cst_b])
        for c in range(12):
            for j in range(4):
                kb.op("dve" if (c + j) % 2 else "pool", lambda e, c=c, j=j: e.tensor_scalar(
                    diag[:, c * 4 + j, :], identb[:], cw[:, c * 4 + j:c * 4 + j + 1], None, op0=ALU.mult),
                    reads=[cst_b, cw_b], writes=[diag_b])
        kb.op("act", lambda e: e.activation(Aneg[:], hp[:, 16:32], AF.Exp), reads=[hp_b], writes=[Aneg_b])
        kb.op("dve", lambda e: e.tensor_scalar(Aneg[:], Aneg[:], -1.0, None, op0=ALU.mult), reads=[Aneg_b], writes=[Aneg_b])
        kb.op("pool", lambda e: e.memset(Sf[:], 0.0), writes=[Sf_b])
        kb.op("pool", lambda e: e.memset(u[:, :, 0:4], 0.0), writes=[u_b])
        out_toks = []
        for G in range(8):
            s = G // 4
            gl = G % 4
            t0 = gl * TG
            if G > 0:
                kb.op("dve", lambda e: e.tensor_copy(u[:, :, 0:4], u[:, :, TG:TG + 4]), reads=[u_b], writes=[u_b])
            for (c0, nch, r0, r1) in ((0, 4, 0, 1024), (4, 4, 512, 1536), (8, 2, 2048, 2048 + 256), (10, 2, 2560, 2560 + 256)):
                self.dsel(u[:, c0:c0 + nch, 4:4 + TG],
                          GX.rows(s, r0, nch * 128)[:, t0:t0 + TG].rearrange("(c p) t -> p c t", p=128),
                          GX.rows(s, r1, nch * 128)[:, t0:t0 + TG].rearrange("(c p) t -> p c t", p=128), writes=[u_b])
            zr = GZ.rows(s, t0, TG)
            self.dsel(zt[:], zr[:, 0:1024].rearrange("(a p) n -> p a n", p=128),
                      zr[:, 1024:2048].rearrange("(a p) n -> p a n", p=128), writes=[zt_b])
            dr = GDT.rows(s, t0, TG)
            self.dsel(dtr[:], dr[:, 0:16].rearrange("(a p) h -> p a h", p=128),
                      dr[:, 16:32].rearrange("(a p) h -> p a h", p=128), writes=[dtr_b])
            for c in range(8, 12):
                ps, psb = pb[7]
                for j in range(4):
                    kb.op("pe", lambda e, c=c, j=j, ps=ps: e.matmul(ps[:], lhsT=diag[:, c * 4 + j, :], rhs=u[:, c, 1 + j:1 + j + TG],
                                                              start=(j == 0), stop=(j == 3)),
                          reads=[diag_b, u_b], writes=[psb], inc=(j == 3))
                dst = BT if c < 10 else CT
                kb.op("act", lambda e, c=c, ps=ps, dst=dst: e.activation(dst[:, c % 2, :], ps[:], AF.Silu, bias=cb[:, c:c + 1]),
                      reads=[psb, cb_b], writes=[bct_b])
            for tt in range(4):
                for half in range(2):
                    ps, psb = pb[4 + half]
                    for cc in range(4):
                        c = half * 4 + cc
                        for j in range(4):
                            kb.op("pe", lambda e, c=c, cc=cc, j=j, tt=tt, ps=ps: e.matmul(
                                ps[:, cc * 128:(cc + 1) * 128], lhsT=u[:, c, 1 + j + tt * 128:1 + j + tt * 128 + 128], rhs=diag[:, c * 4 + j, :],
                                start=(cc == 0 and j == 0), stop=False, skip_group_check=True),
                                reads=[diag_b, u_b], writes=[psb], inc=False)
                    kb.op("pe", lambda e, half=half, ps=ps: e.matmul(ps[:], lhsT=ones[0:1, 0:128], rhs=cbhi[0:1, half * 512:(half + 1) * 512],
                                                                  start=False, stop=False, skip_group_check=True),
                          reads=[self.ones_b, cbhl_b], writes=[psb], inc=False)
                    kb.op("pe", lambda e, half=half, ps=ps: e.matmul(ps[:], lhsT=ones[0:1, 0:128], rhs=cblo[0:1, half * 512:(half + 1) * 512],
                                                                  start=False, stop=True, skip_group_check=True),
                          reads=[self.ones_b, cbhl_b], writes=[psb], inc=True)
                    kb.op("act", lambda e, half=half, tt=tt, ps=ps: e.activation(xs[:, tt, half * 512:(half + 1) * 512], ps[:], AF.Silu),
                          reads=[psb], writes=[xs_b])
                ps, psb = pb[7]
                for cc in range(2):
                    c = 8 + cc
                    for j in range(4):
                        kb.op("pe", lambda e, c=c, cc=cc, j=j, tt=tt, ps=ps: e.matmul(
                            ps[:, cc * 128:(cc + 1) * 128], lhsT=u[:, c, 1 + j + tt * 128:1 + j + tt * 128 + 128], rhs=diag[:, c * 4 + j, :],
                            start=(cc == 0 and j == 0), stop=False, skip_group_check=True),
                            reads=[diag_b, u_b], writes=[psb], inc=False)
                kb.op("pe", lambda e, ps=ps: e.matmul(ps[:, 0:256], lhsT=ones[0:1, 0:128], rhs=cbhi[0:1, 1024:1280], start=False, stop=False,
                                                      skip_group_check=True), reads=[self.ones_b, cbhl_b], writes=[psb], inc=False)
                kb.op("pe", lambda e, ps=ps: e.matmul(ps[:, 0:256], lhsT=ones[0:1, 0:128], rhs=cblo[0:1, 1024:1280], start=False, stop=True,
                                                      skip_group_check=True), reads=[self.ones_b, cbhl_b], writes=[psb], inc=True)
                kb.op("act", lambda e, tt=tt, ps=ps: e.activation(Btok[:, tt, :], ps[:, 0:256], AF.Silu), reads=[psb], writes=[Btok_b])
            kb.op("dve", lambda e: e.tensor_tensor(dtv[:], dtr[:], hp[:, 0:16].unsqueeze(1).broadcast_to([128, 4, 16]), op=ALU.add),
                  reads=[dtr_b, hp_b], writes=[dtv_b])
            kb.op("act", lambda e: e.activation(dtv[:], dtv[:], AF.Exp), reads=[dtv_b], writes=[dtv_b])
            kb.op("act", lambda e: e.activation(dtv[:], dtv[:], AF.Ln, bias=1.0), reads=[dtv_b], writes=[dtv_b])
            kb.op("dve", lambda e: e.tensor_tensor(av[:], dtv[:], Aneg[:].unsqueeze(1).broadcast_to([128, 4, 16]), op=ALU.mult),
                  reads=[dtv_b, Aneg_b], writes=[dtv_b])
            gt_, gtb = gT[0]
            for tt in range(4):
                ts_ = slice(tt * 128, (tt + 1) * 128)
                pst, pstb = pb[6]
                kb.op("pe", lambda e, tt=tt: e.matmul(pst[:, 0:16], lhsT=Tm[:], rhs=av[:, tt, :], start=True, stop=True),
                      reads=[cst_b, dtv_b], writes=[pstb])
                kb.op("pe", lambda e, tt=tt: e.matmul(pst[:, 16:32], lhsT=onesf[:], rhs=av[:, tt, :], start=True, stop=True),
                      reads=[self.onesf_b, dtv_b], writes=[pstb])
                kb.op("act", lambda e: e.activation(ea[:], pst[:, 0:16], AF.Exp), reads=[pstb], writes=[sm_b])
                kb.op("act", lambda e: e.activation(cd[:], pst[:, 16:32], AF.Exp), reads=[pstb], writes=[sm_b])
                kb.op("act", lambda e: e.activation(acum[:], pst[:, 0:16], AF.Copy), reads=[pstb], writes=[sm_b])
                kb.op("dve", lambda e: e.tensor_tensor(dte[:], pst[:, 16:32], acum[:], op=ALU.subtract), reads=[pstb, sm_b], writes=[sm_b])
                kb.op("act", lambda e: e.activation(dte[:], dte[:], AF.Exp), reads=[sm_b], writes=[sm_b])
                kb.op("dve", lambda e, tt=tt: e.tensor_tensor(xdt[:].rearrange("p (h d) -> p h d", d=64), xs[:, tt, :].rearrange("p (h d) -> p h d", d=64),
                                                            dtv[:, tt, :].unsqueeze(2).broadcast_to([128, 16, 64]), op=ALU.mult),
                      reads=[xs_b, dtv_b], writes=[xdt_b])
                kb.op("pool", lambda e, tt=tt: e.tensor_tensor(aU[:], U[:].unsqueeze(1).broadcast_to([128, 16, 128]),
                                                             av[:, tt, :].unsqueeze(2).broadcast_to([128, 16, 128]), op=ALU.mult),
                      reads=[cst_b, dtv_b], writes=[aU_b])
                for gg in range(2):
                    kb.op("pool", lambda e, tt=tt, gg=gg: e.tensor_tensor(
                        Bdec[:, gg * 8:(gg + 1) * 8, :], Btok[:, tt, gg * 128:(gg + 1) * 128].unsqueeze(1).broadcast_to([128, 8, 128]),
                        dte[:, gg * 8:(gg + 1) * 8].unsqueeze(2).broadcast_to([128, 8, 128]), op=ALU.mult),
                        reads=[Btok_b, sm_b], writes=[Bdec_b])
                kb.op("dve", lambda e: e.tensor_copy(Sb[:], Sf[:]), reads=[Sf_b], writes=[Sb_b])
                py0, py0b = pb[2]
                py1, py1b = pb[3]
                pys = [(py0, py0b), (py1, py1b)]
                for gg in range(2):
                    cm_, cmb = cbm[gg]
                    kb.op("pe", lambda e, gg=gg, ts_=ts_: e.matmul(pst[:, 32 + gg * 128:32 + (gg + 1) * 128], lhsT=BT[:, gg, ts_], rhs=CT[:, gg, ts_],
                                                                start=True, stop=True), reads=[bct_b], writes=[pstb])
                    kb.op("dve", lambda e, gg=gg, cm_=cm_: e.tensor_tensor(cm_[:], pst[:, 32 + gg * 128:32 + (gg + 1) * 128], Tb[:], op=ALU.mult),
                          reads=[pstb, cst_b], writes=[cmb])
                    for half in range(2):
                        ps, psb = pb[half]
                        for hh in range(4):
                            hd = gg * 8 + half * 4 + hh
                            kb.op("pe", lambda e, hd=hd, hh=hh, ps=ps: e.matmul(ps[:, hh * 128:(hh + 1) * 128], lhsT=aU[:, hd, :], rhs=Tm[:],
                                                                             start=True, stop=True),
                                  reads=[aU_b, cst_b], writes=[psb])
                    dc, dcb = dec[gg]
                    for half in range(2):
                        ps, psb = pb[half]
                        kb.op("act", lambda e, half=half, ps=ps, dc=dc: e.activation(
                            dc[:, half * 4:(half + 1) * 4, :], ps[:].rearrange("p (a b) -> p a b", b=128), AF.Exp), reads=[psb], writes=[dcb])
                    mt, mtb = MT[gg]
                    kb.op("dve" if gg == 0 else "pool", lambda e, mt=mt, dc=dc, cm_=cm_: e.tensor_tensor(
                        mt[:], dc[:], cm_[:].unsqueeze(1).broadcast_to([128, 8, 128]), op=ALU.mult), reads=[dcb, cmb], writes=[mtb])
                    py, pyb = pys[gg]
                    for hh in range(8):
                        hd = gg * 8 + hh
                        kb.op("pe", lambda e, hd=hd, hh=hh, py=py, mt=mt: e.matmul(py[:, hh * 64:(hh + 1) * 64], lhsT=mt[:, hh, :],
                                                                              rhs=xdt[:, hd * 64:(hd + 1) * 64], start=True, stop=True),
                              reads=[mtb, xdt_b], writes=[pyb], inc=(hh == 7))
                for gg in range(2):
                    ps, psb = pb[gg]
                    kb.op("pe", lambda e, gg=gg, ps=ps, ts_=ts_: e.matmul(ps[:], lhsT=CT[:, gg, ts_], rhs=Sb[:, gg * 512:(gg + 1) * 512], start=True, stop=True),
                          reads=[bct_b, Sb_b], writes=[psb])
                    kb.op("dve", lambda e, gg=gg, ps=ps: e.tensor_tensor(
                        t1[:, gg * 512:(gg + 1) * 512].rearrange("p (h d) -> p h d", d=64), ps[:].rearrange("p (h d) -> p h d", d=64),
                        ea[:, gg * 8:(gg + 1) * 8].unsqueeze(2).broadcast_to([128, 8, 64]), op=ALU.mult),
                        reads=[psb, sm_b], writes=[t1_b])
                kb.op("pool", lambda e, tt=tt: e.tensor_tensor(t3[:].rearrange("p (h d) -> p h d", d=64), xs[:, tt, :].rearrange("p (h d) -> p h d", d=64),
                                                             hp[:, 32:48].unsqueeze(2).broadcast_to([128, 16, 64]), op=ALU.mult),
                      reads=[xs_b, hp_b], writes=[t3_b])
                kb.op("pool", lambda e: e.tensor_tensor(t1[:], t1[:], t3[:], op=ALU.add), reads=[t1_b, t3_b], writes=[t1_b])
                for gg in range(2):
                    py, pyb = pys[gg]
                    kb.op("dve", lambda e, gg=gg, py=py: e.tensor_tensor(yv[:, gg * 512:(gg + 1) * 512], py[:], t1[:, gg * 512:(gg + 1) * 512], op=ALU.add),
                          reads=[pyb, t1_b], writes=[yv_b])
                kb.op("act", lambda e, tt=tt: e.activation(sz[:], zt[:, tt, :], AF.Silu), reads=[zt_b], writes=[sz_b])
                kb.op("dve", lambda e: e.tensor_tensor(yv[:], yv[:], sz[:], op=ALU.mult), reads=[yv_b, sz_b], writes=[yv_b])
                for gg in range(2):
                    kb.op("act", lambda e, gg=gg: e.activation(junk[:], yv[:, gg * 512:(gg + 1) * 512], AF.Square, accum_out=ssq[:, gg:gg + 1]),
                          reads=[yv_b], writes=[junk_b, ssq_b])
                kb.op("dve", lambda e: e.tensor_scalar(ssq[:], ssq[:], 1.0 / 512, EPS, op0=ALU.mult, op1=ALU.add), reads=[ssq_b], writes=[ssq_b])
                kb.op("act", lambda e: e.activation(ssq[:], ssq[:], AF.Sqrt), reads=[ssq_b], writes=[ssq_b])
                kb.op("dve", lambda e: e.reciprocal(ssq[:], ssq[:]), reads=[ssq_b], writes=[ssq_b])
                for gg in range(2):
                    kb.op("dve", lambda e, gg=gg: e.scalar_tensor_tensor(gn[:, gg * 512:(gg + 1) * 512], yv[:, gg * 512:(gg + 1) * 512], ssq[:, gg:gg + 1],
                                                                       ng[:, gg * 512:(gg + 1) * 512], op0=ALU.mult, op1=ALU.mult),
                          reads=[yv_b, ssq_b, ng_b], writes=[gn_b])
                for half in range(2):
                    ps, psb = pb[4 + half]
                    for fc in range(4):
                        f = half * 4 + fc
                        kb.op("pe", lambda e, f=f, fc=fc, ps=ps: e.matmul(ps[:, fc * 128:(fc + 1) * 128], lhsT=gn[:, f * 128:(f + 1) * 128], rhs=identb[:],
                                                                       start=True, stop=True), reads=[gn_b, cst_b], writes=[psb], inc=(fc == 3))
                    kb.op("act" if half == 0 else "dve", (lambda e, half=half, ps=ps, gt_=gt_, ts_=ts_: e.activation(
                        gt_[:, half * 4:(half + 1) * 4, ts_], ps[:].rearrange("p (a b) -> p a b", b=128), AF.Copy)) if half == 0 else
                        (lambda e, half=half, ps=ps, gt_=gt_, ts_=ts_: e.tensor_copy(gt_[:, half * 4:(half + 1) * 4, ts_], ps[:].rearrange("p (a b) -> p a b", b=128))),
                        reads=[psb], writes=[gtb])
                for gg in range(2):
                    ps, psb = pb[gg]
                    for hh in range(8):
                        hd = gg * 8 + hh
                        kb.op("pe", lambda e, hd=hd, hh=hh, ps=ps: e.matmul(ps[:, hh * 64:(hh + 1) * 64], lhsT=Bdec[:, hd, :], rhs=xdt[:, hd * 64:(hd + 1) * 64],
                                                                         start=True, stop=True), reads=[Bdec_b, xdt_b], writes=[psb], inc=(hh == 7))
                kb.op("pool", lambda e: e.tensor_tensor(Sf[:].rearrange("p (h d) -> p h d", d=64), Sf[:].rearrange("p (h d) -> p h d", d=64),
                                                        cd[:].unsqueeze(2).broadcast_to([128, 16, 64]), op=ALU.mult), reads=[Sf_b, sm_b, Sb_b], writes=[Sf_b])
                for gg in range(2):
                    ps, psb = pb[gg]
                    kb.op("dve", lambda e, gg=gg, ps=ps: e.tensor_tensor(Sf[:, gg * 512:(gg + 1) * 512], Sf[:, gg * 512:(gg + 1) * 512], ps[:], op=ALU.add),
                          reads=[psb, Sf_b], writes=[Sf_b])
            col0 = s * T + t0
            out_toks.append(kb.dma("sp", snd_g[:, col0:col0 + TG].rearrange("(c p) t -> p c t", p=128), gt_[:], reads=[gtb]))
        return out_toks


D = 1024; T = 2048; S = 4096; DFF = 2816


def fm(v):
    v = np.asarray(v, np.float32)
    return np.ascontiguousarray(v.reshape(-1, 128).T)


def ropec():
    inv = (1.0 / (10000.0 ** (np.arange(0, 32, 2, dtype=np.float32) / 32))).astype(np.float32)
    c = np.zeros((32, 2), np.float32)
    c[:16, 0] = inv; c[16:, 0] = inv
    c[:16, 1] = -1.0; c[16:, 1] = 1.0
    return c


def prep_mods(I, i, sub, tag):
    return {tag + "_adaw": np.ascontiguousarray(I["ada_w"][i][:, sub * 3072:(sub + 1) * 3072]),
            tag + "_adab": fm(I["ada_b"][i][sub * 3072:(sub + 1) * 3072]),
            tag + "_gain": fm(I["norm_gain"][i, sub])}


def prep_ffn(I, i, which, tag):
    d = prep_mods(I, i, 0 if which == 0 else 2, tag)
    d[tag + "_wgu"] = I["ffn_w_gu"][i, which]
    d[tag + "_wdn"] = I["ffn_w_down"][i, which]
    return d


def prep_mla(I, i, tag):
    j = i // 2
    d = prep_mods(I, i, 1, tag)
    wa = I["mla_w_a"][j]
    kr = wa[:, 640:672]
    d[tag + "_wa"] = np.ascontiguousarray(np.concatenate([wa, kr[:, 16:], kr[:, :16]], 1))
    wqb = I["mla_w_qb"][j].reshape(384, 16, 96)
    nope, rp = wqb[:, :, :64], wqb[:, :, 64:]
    wq = np.concatenate([nope, rp, rp[:, :, 16:], rp[:, :, :16]], 2)
    d[tag + "_wq"] = np.ascontiguousarray(wq.reshape(384, 2048))
    wkv = I["mla_w_kvb"][j].reshape(256, 16, 128)
    d[tag + "_wkn"] = np.ascontiguousarray(wkv[:, :, :64].reshape(256, 1024))
    d[tag + "_wv"] = np.ascontiguousarray(wkv[:, :, 64:].reshape(256, 1024))
    mg = np.zeros((128, 12), np.float32)
    gk = I["mla_k_gain"][j]; gq = I["mla_q_gain"][j]
    mg[:64, 0] = gk[:64]; mg[64:, 0] = gk[:64]
    mg[64:96, 1] = gk[64:]
    mg[64:80, 2] = gk[80:]; mg[80:96, 2] = gk[64:80]
    mg[:, 3:5] = fm(I["mla_kv_a_gain"][j])
    mg[:, 5:8] = fm(I["mla_q_a_gain"][j])
    mg[:96, 8] = gq
    mg[64:80, 9] = gq[80:]; mg[80:96, 9] = gq[64:80]
    d[tag + "_mg"] = mg
    d[tag + "_wo"] = I["mla_w_o"][j]
    return d


def prep_ssd(I, i, tag):
    j = i // 2
    d = prep_mods(I, i, 1, tag)
    d[tag + "_win"] = I["ssd_w_in"][j]
    d[tag + "_wout"] = I["ssd_w_out"][j]
    return d


def prep_ssd_rank(I, i, tag, r):
    j = i // 2
    cwf = I["ssd_conv_w"][j]
    cbf = I["ssd_conv_b"][j]
    chans = np.concatenate([np.arange(1024 * r, 1024 * r + 1024), 2048 + 256 * r + np.arange(256), 2560 + 256 * r + np.arange(256)])
    cw = cwf[:, chans].reshape(4, 12, 128).transpose(2, 1, 0).reshape(128, 48)
    cb = cbf[chans].reshape(12, 128).T
    cbrow = cbf[chans[:1280]][None, :]
    hp = np.concatenate([I["ssd_dt_bias"][j][16 * r:16 * r + 16], I["ssd_a_log"][j][16 * r:16 * r + 16], I["ssd_d"][j][16 * r:16 * r + 16]])
    hp = np.broadcast_to(hp[None, :], (128, 48))
    ng = np.broadcast_to(I["ssd_norm_gain"][j][1024 * r:1024 * r + 1024][None, :], (128, 1024))
    f = lambda a: np.ascontiguousarray(a, dtype=np.float32)
    return {tag + "_cw": f(cw), tag + "_cb": f(cb), tag + "_cbrow": f(cbrow), tag + "_hp": f(hp), tag + "_ng": f(ng)}


import ml_dtypes
_BF = ml_dtypes.bfloat16
_PROGS = {}


def _build(kind):
    if kind in _PROGS:
        return _PROGS[kind]
    ph, mixer = kind
    P = Prog(None)
    P.setup()
    toks = []
    if ph == "A":
        xT = P.din("xT", [D, T]); P.load_x(xT)
        P.ffn("F0")
        yT = P.dout("yT", [D, T])
        toks += P.store_x(yT)
        if mixer == "mla":
            snd = {"QT": P.dout("QT", [16 * 96, T], BF16), "KT": P.dout("KT", [16 * 96, T], BF16),
                   "V": P.dout("V", [T, 1024], BF16), "RK": P.dout("RK", [T, 16], F32)}
            toks += P.mla_p1("M", snd)
        else:
            snd = {"XBC": P.dout("XBC", [3072, T], BF16), "Z": P.dout("Z", [T, 2048], BF16), "DT": P.dout("DT", [T, 32], F32)}
            toks += P.ssd_p1("M", snd)
    elif ph == "B":
        if mixer == "mla":
            g_ap = {"QT": GBuf(P.din("gQT", [2 * 16 * 96, T], BF16), 1536, 1536), "KT": GBuf(P.din("gKT", [2 * 16 * 96, T], BF16), 1536, 1536),
                    "V": GBuf(P.din("gV", [2 * T, 1024], BF16), T, T), "RK": GBuf(P.din("gRK", [2 * T, 16], F32), T, T)}
            snd_o = P.dout("O", [512, S], BF16)
            toks += P.mla_p2("M", g_ap, snd_o)
        else:
            g_ap = {"XBC": GBuf(P.din("gXBC", [2 * 3072, T], BF16), 3072, 3072), "Z": GBuf(P.din("gZ", [2 * T, 2048], BF16), T, T),
                    "DT": GBuf(P.din("gDT", [2 * T, 32], F32), T, T)}
            snd_g = P.dout("O", [1024, S], BF16)
            toks += P.ssd_p2("M", g_ap, snd_g)
    else:
        xT = P.din("xT", [D, T]); P.load_x(xT)
        if mixer == "mla":
            gO = GBuf(P.din("gO", [1024, S], BF16), 512, 512)
            P.mixer_p3("M", gO, 8, True, "_wo")
        else:
            gO = GBuf(P.din("gO", [2048, S], BF16), 1024, 1024)
            P.mixer_p3("M", gO, 16, True, "_wout")
        P.ffn("F1")
        yT = P.dout("yT", [D, T])
        toks += P.store_x(yT)
    P.kb.wait_all("sp", toks)
    P.kb.emit_all()
    _PROGS[kind] = P
    return P


def _launch(P, cands):
    ims = []
    for c in range(8):
        d = {}
        for n in P.inputs:
            for src in cands[c]:
                if n in src:
                    d[n] = src[n]
                    break
            else:
                raise KeyError(n)
        ims.append(d)
    res = run_bass_kernel_spmd(P.nc, ims, core_ids=list(range(8)))
    return res.results


def kernel(**I):
    I = {k: np.asarray(v) for k, v in I.items()}
    x = I["x"].astype(np.float32)
    B = x.shape[0]
    rc = ropec()
    core_common = []
    for c in range(8):
        b, r = c // 2, c % 2
        core_common.append({"cT": fm(I["c"][b]), "ropec": rc,
                            "pos": np.ascontiguousarray(I["positions"][b][None, r * T:(r + 1) * T]).astype(np.int32)})
    xs = [np.ascontiguousarray(x[c // 2, (c % 2) * T:(c % 2 + 1) * T].T) for c in range(8)]
    for i in range(4):
        mixer = "mla" if i % 2 == 0 else "ssd"
        W0 = prep_ffn(I, i, 0, "F0")
        W1 = prep_ffn(I, i, 1, "F1")
        WM = prep_mla(I, i, "M") if mixer == "mla" else prep_ssd(I, i, "M")
        WR = [prep_ssd_rank(I, i, "M", r) for r in range(2)] if mixer == "ssd" else [{}, {}]
        P = _build(("A", mixer))
        res = _launch(P, [[{"xT": xs[c]}, core_common[c], W0, WM] for c in range(8)])
        xs = [res[c]["yT"] for c in range(8)]
        names = ["QT", "KT", "V", "RK"] if mixer == "mla" else ["XBC", "Z", "DT"]
        gat = []
        for pr in range(4):
            g = {"g" + n: np.concatenate([res[2 * pr][n], res[2 * pr + 1][n]], 0) for n in names}
            gat += [g, g]
        del res
        P = _build(("B", mixer))
        res = _launch(P, [[gat[c], core_common[c], WM, WR[c % 2]] for c in range(8)])
        gO = []
        for pr in range(4):
            g = {"gO": np.concatenate([res[2 * pr]["O"], res[2 * pr + 1]["O"]], 0)}
            gO += [g, g]
        del res, gat
        P = _build(("C", mixer))
        res = _launch(P, [[{"xT": xs[c]}, gO[c], core_common[c], W1, WM] for c in range(8)])
        xs = [res[c]["yT"] for c in range(8)]
        del res
    out = np.empty_like(x)
    for c in range(8):
        out[c // 2, (c % 2) * T:(c % 2 + 1) * T] = xs[c].T
    return out


def _build_fused():
    if "fused" in _PROGS:
        return _PROGS["fused"]
    P = Prog(None)
    P.setup()
    kb = P.kb
    xT = P.din("xT", [D, T]); P.load_x(xT)
    for i in range(4):
        mixer = "mla" if i % 2 == 0 else "ssd"
        L = "L%d" % i
        P.ffn(L + "F0")
        if mixer == "mla":
            shapes = {"QT": ([16 * 96, T], BF16, 384), "KT": ([16 * 96, T], BF16, 384), "V": ([T, 1024], BF16, 1024), "RK": ([T, 16], F32, T)}
        else:
            shapes = {"XBC": ([3072, T], BF16, 512), "Z": ([T, 2048], BF16, 512), "DT": ([T, 32], F32, T)}
        snd = {n: P.dint(L + "s" + n, shp, dt) for n, (shp, dt, R) in shapes.items()}
        gat = {n: GBuf(P.dint(L + "g" + n, [2 * shp[0], shp[1]], dt), shp[0], R, snd[n]) for n, (shp, dt, R) in shapes.items()}
        toks = P.mla_p1(L + "M", snd) if mixer == "mla" else P.ssd_p1(L + "M", snd)
        for n in shapes:
            gat[n].gather(kb, toks, GROUPS)
        if mixer == "mla":
            snd_o = P.dint(L + "sO", [512, S], BF16)
            gat_o = GBuf(P.dint(L + "gO", [1024, S], BF16), 512, 256, snd_o)
            toks = P.mla_p2(L + "M", gat, snd_o)
        else:
            snd_o = P.dint(L + "sO", [1024, S], BF16)
            gat_o = GBuf(P.dint(L + "gO", [2048, S], BF16), 1024, 256, snd_o)
            toks = P.ssd_p2(L + "M", gat, snd_o)
        gat_o.gather(kb, toks, GROUPS)
        if mixer == "mla":
            P.mixer_p3(L + "M", gat_o, 8, False, "_wo")
        else:
            P.mixer_p3(L + "M", gat_o, 16, False, "_wout")
        P.ffn(L + "F1")
    yT = P.dout("yT", [D, T])
    toks = P.store_x(yT)
    kb.wait_all("sp", toks)
    kb.emit_all()
    _PROGS["fused"] = P
    return P


def kernel_fused(**I):
    I = {k: np.asarray(v) for k, v in I.items()}
    x = I["x"].astype(np.float32)
    rc = ropec()
    P = _build_fused()
    Wall = {}
    WR = [{}, {}]
    for i in range(4):
        L = "L%d" % i
        Wall.update(prep_ffn(I, i, 0, L + "F0"))
        Wall.update(prep_ffn(I, i, 1, L + "F1"))
        if i % 2 == 0:
            Wall.update(prep_mla(I, i, L + "M"))
        else:
            Wall.update(prep_ssd(I, i, L + "M"))
            for r in range(2):
                WR[r].update(prep_ssd_rank(I, i, L + "M", r))
    cands = []
    for c in range(8):
        b, r = c // 2, c % 2
        cc = {"cT": fm(I["c"][b]), "ropec": rc, "pos": np.ascontiguousarray(I["positions"][b][None, r * T:(r + 1) * T]).astype(np.int32),
              "xT": np.ascontiguousarray(x[b, r * T:(r + 1) * T].T)}
        cands.append([cc, Wall, WR[r]])
    res = _launch(P, cands)
    out = np.empty_like(x)
    for c in range(8):
        out[c // 2, (c % 2) * T:(c % 2 + 1) * T] = res[c]["yT"].T
    return out


kernel_multi = kernel
kernel = kernel_fused
```

```python
import numpy as np
from contextlib import ExitStack
import concourse.bass as bass
import concourse.mybir as mybir
from concourse.bass_utils import run_bass_kernel_spmd


F32 = mybir.dt.float32
BF16 = mybir.dt.bfloat16
I32 = mybir.dt.int32
AF = mybir.ActivationFunctionType
ALU = mybir.AluOpType
AX = mybir.AxisListType

ENGS = ("pe", "act", "dve", "pool", "sp")


class Buf:
    __slots__ = ("w", "r", "name")

    def __init__(self, name=""):
        self.w = None
        self.r = []
        self.name = name


class KB:
    def __init__(self, nc, ctx):
        self.nc = nc
        self.ctx = ctx
        self.q = {e: [] for e in ENGS}
        self.cnt = {}
        self.semh = {}
        self.seen = {e: {} for e in ENGS}
        self.pe_pending = []
        for e in ENGS:
            self.new_sem("E_" + e)
        self.ndma = 0
        self.NPOOL = 64
        self.buf_key = {}
        self.pool_i = 0

    def phase_reset(self):
        self.buf_key = {}
        self.pool_i = 0

    def _dma_key(self, reads, writes):
        prim = (list(writes) + list(reads))[0]
        k = self.buf_key.get(id(prim))
        if k is None:
            assert self.pool_i < self.NPOOL, "DMA semaphore pool exhausted in this phase"
            k = "DS%d" % self.pool_i
            self.pool_i += 1
            self.buf_key[id(prim)] = k
        return k

    def new_sem(self, key):
        self.semh[key] = self.ctx.enter_context(self.nc.semaphore(key))
        self.cnt[key] = 0
        return key

    def sb(self, name, shape, dtype):
        return self.ctx.enter_context(self.nc.sbuf_tensor(name, list(shape), dtype))

    def ps(self, name, shape, dtype=F32):
        return self.ctx.enter_context(self.nc.psum_tensor(name, list(shape), dtype))

    def _waits(self, eng, toks):
        ws = []
        best = {}
        for t in toks:
            if t is None:
                continue
            s, v = t
            if self.seen[eng].get(s, 0) >= v:
                continue
            if best.get(s, 0) < v:
                best[s] = v
        for s, v in best.items():
            self.seen[eng][s] = v
            ws.append((s, v))
        return ws

    def op(self, eng, fn, reads=(), writes=(), inc=True, extra=()):
        toks = list(extra)
        for b in reads:
            toks.append(b.w)
        for b in writes:
            toks.append(b.w)
            toks.extend(b.r)
        if eng == "pe":
            toks = [t for t in toks if t is not None and t[0] != "E_pe"]
        ws = self._waits(eng, toks)
        key = "E_" + eng
        if eng == "pe" and not inc:
            self.pe_pending.append((tuple(reads), tuple(writes)))
            tok = None
        else:
            self.cnt[key] += 1
            tok = (key, self.cnt[key])
            allrw = [(tuple(reads), tuple(writes))]
            if eng == "pe":
                allrw += self.pe_pending
                self.pe_pending = []
            for rs, wr in allrw:
                for b in rs:
                    b.r.append(tok)
                for b in wr:
                    b.w = tok
                    b.r = []
        semh = self.semh

        def emit(e, fn=fn, ws=ws, tok=tok):
            for s, v in ws:
                e.wait_ge(semh[s], v)
            ins = fn(e)
            if tok is not None:
                ins.then_inc(semh[tok[0]], 1)
        self.q[eng].append(emit)
        return tok

    def dma(self, eng, out, in_, reads=(), writes=(), extra=(), key=None, **kw):
        toks = list(extra)
        for b in reads:
            toks.append(b.w)
        for b in writes:
            toks.append(b.w)
            toks.extend(b.r)
        ws = self._waits(eng, toks)
        key = self._dma_key(reads, writes)
        if key not in self.semh:
            self.new_sem(key)
        self.cnt[key] += 16
        tok = (key, self.cnt[key])
        for b in reads:
            b.r.append(tok)
        for b in writes:
            b.w = tok
            b.r = []
        semh = self.semh

        def emit(e, ws=ws, tok=tok, out=out, in_=in_, kw=kw):
            for s, v in ws:
                e.wait_ge(semh[s], v)
            try:
                e.dma_start(out=out, in_=in_, **kw).then_inc(semh[tok[0]], 16)
            except Exception:
                print("DMA FAIL out", out.shape, out.ap, "in", in_.shape, in_.ap, flush=True)
                raise
        self.q[eng].append(emit)
        return tok

    def wait_all(self, eng, toks):
        ws = self._waits(eng, toks)
        semh = self.semh

        def emit(e, ws=ws):
            for s, v in ws:
                e.wait_ge(semh[s], v)
        self.q[eng].append(emit)

    def emit_all(self):
        nc = self.nc
        q = self.q
        with nc.Block() as block:
            @block.sync
            def _(e):
                for f in q["sp"]:
                    f(e)

            @block.tensor
            def _(e):
                for f in q["pe"]:
                    f(e)

            @block.scalar
            def _(e):
                for f in q["act"]:
                    f(e)

            @block.vector
            def _(e):
                for f in q["dve"]:
                    f(e)

            @block.gpsimd
            def _(e):
                for f in q["pool"]:
                    f(e)


CC_INC = 1


def _kb_cc(self, kind, ins, outs, groups, reads=(), writes=(), extra=()):
    toks = list(extra)
    for b in reads:
        toks.append(b.w)
    for b in writes:
        toks.append(b.w)
        toks.extend(b.r)
    ws = self._waits("pool", toks)
    key = "CC"
    if key not in self.semh:
        self.new_sem(key)
    self.cnt[key] += CC_INC
    tok = (key, self.cnt[key])
    for b in reads:
        b.r.append(tok)
    for b in writes:
        b.w = tok
        b.r = []
    semh = self.semh

    def emit(e, ws=ws, tok=tok):
        for s, v in ws:
            e.wait_ge(semh[s], v)
        e.collective_compute(kind, ALU.bypass, replica_groups=groups, ins=ins, outs=outs).then_inc(semh[tok[0]], CC_INC)
    self.q["pool"].append(emit)
    return tok


KB.cc = _kb_cc


def _kb_dma_if(self, eng, cond, out, in_true, in_false, reads=(), writes=(), extra=()):
    toks = list(extra)
    for b in reads:
        toks.append(b.w)
    for b in writes:
        toks.append(b.w)
        toks.extend(b.r)
    ws = self._waits(eng, toks)
    key = self._dma_key(reads, writes)
    if key not in self.semh:
        self.new_sem(key)
    self.cnt[key] += 16
    tok = (key, self.cnt[key])
    for b in reads:
        b.r.append(tok)
    for b in writes:
        b.w = tok
        b.r = []
    semh = self.semh

    def emit(e, ws=ws, tok=tok):
        for s, v in ws:
            e.wait_ge(semh[s], v)
        with e.If(cond):
            e.dma_start(out=out, in_=in_true).then_inc(semh[tok[0]], 16)
        with e.Else():
            e.dma_start(out=out, in_=in_false).then_inc(semh[tok[0]], 16)
    self.q[eng].append(emit)
    return tok


KB.dma_if = _kb_dma_if


D = 1024
KC = 8
T = 2048
S = 4096
TG = 512
NG = T // TG
DFF = 2816
EPS = 1e-6
NH = 16
DH = 96
QL = 384
KVL = 256
SC96 = 96 ** -0.5
GROUPS = [[0, 1], [2, 3], [4, 5], [6, 7]]


class GBuf:
    def __init__(self, gat_ap, rows, R, snd_ap=None):
        self.gat, self.nrows, self.R, self.snd = gat_ap, rows, R, snd_ap

    def rows(self, s, q0, n):
        R = self.R
        k = q0 // R
        assert (q0 + n - 1) // R == k, (q0, n, R)
        base = k * 2 * R + s * R + q0 % R
        return self.gat[base:base + n, :]

    def gather(self, kb, toks, groups):
        R = self.R
        for k in range(self.nrows // R):
            kb.cc("AllGather", [self.snd[k * R:(k + 1) * R, :]], [self.gat[k * 2 * R:(k + 1) * 2 * R, :]], groups, extra=toks)


class Prog:
    def __init__(self, steps, name="p"):
        self.steps = steps
        self.nc = bass.Bass("TRN2", target_bir_lowering=False)
        self.ctx = ExitStack()
        self.kb = KB(self.nc, self.ctx)
        self.debug = False
        self.dbg_toks = []
        self.inputs = {}
        self.outputs = {}
        self.rot_i = 0

    def din(self, name, shape, dt=F32):
        self.inputs[name] = (tuple(shape), dt)
        return self.nc.dram_tensor(name, list(shape), dt, kind="ExternalInput").ap()

    def dout(self, name, shape, dt=F32):
        self.outputs[name] = (tuple(shape), dt)
        return self.nc.dram_tensor(name, list(shape), dt, kind="ExternalOutput").ap()

    def dint(self, name, shape, dt=F32):
        return self.nc.dram_tensor(name, list(shape), dt).ap()

    def setup(self):
        kb = self.kb
        self.x = kb.sb("x", [128, KC, T], F32)
        self.xb = [[Buf() for g in range(NG)] for k in range(KC)]
        self.SCR = 35328
        self.scratch = kb.sb("scratch", [128, self.SCR], F32)
        self.scr_off = 0
        self.ones = kb.sb("ones", [128, 128], BF16); self.ones_b = Buf()
        self.onesf = kb.sb("onesf", [128, 128], F32); self.onesf_b = Buf()
        kb.op("pool", lambda e: e.memset(self.ones[:], 1.0), writes=[self.ones_b])
        kb.op("pool", lambda e: e.memset(self.onesf[:], 1.0), writes=[self.onesf_b])
        self.c_sb = kb.sb("c_sb", [128, KC], F32); self.c_b = Buf()
        self.sc = kb.sb("sc", [128, KC], F32); self.sc_b = Buf()
        cT = self.din("cT", [128, KC])
        kb.dma("sp", self.c_sb[:], cT, writes=[self.c_b])
        kb.op("act", lambda e: e.activation(self.sc[:], self.c_sb[:], AF.Silu), reads=[self.c_b], writes=[self.sc_b])
        self.pb = [(kb.ps("pb%d" % i, [128, 512]), Buf()) for i in range(8)]
        self.mods = kb.sb("mods", [128, 24], F32); self.mods_b = Buf()
        self.A = kb.sb("A", [128, KC], F32); self.A_b = Buf()
        self.hg = kb.sb("hg", [128, KC], F32); self.hg_b = Buf()
        self.gain_sb = kb.sb("gain_sb", [128, KC], F32); self.gain_b = Buf()
        self.adab_sb = kb.sb("adab_sb", [128, 24], F32); self.adab_b = Buf()
        self.aw_i = 0

    def carve(self, shape, dt, p0=0):
        n = int(np.prod(shape[1:]))
        words = n if dt == F32 or dt == I32 else (n + 1) // 2
        assert self.scr_off + words <= self.SCR, ("scratch overflow", self.scr_off, words)
        v = self.scratch[:, self.scr_off:self.scr_off + words]
        self.scr_off += words
        if dt != F32:
            v = v.bitcast(dt)
        if len(shape) == 3:
            v = v.rearrange("p (a b) -> p a b", b=shape[2])
        elif len(shape) == 4:
            v = v.rearrange("p (a b c) -> p a b c", b=shape[2], c=shape[3])
        return v[p0:p0 + shape[0]] if shape[0] < 128 else v

    def phase_begin(self):
        self.barrier()
        self.kb.phase_reset()
        self.scr_off = 0

    def norm_scratch(self):
        self.h = self.carve([128, KC, T], BF16)
        self.hb = [Buf() for g in range(NG)]
        self.sq = self.carve([128, KC, TG], BF16); self.sq_b = Buf()
        self.sd = self.carve([128, TG], F32); self.sd_b = Buf()
        self.rstd = self.carve([128, TG], F32); self.rstd_b = Buf()
        self.tmp = [(self.carve([128, TG], F32), Buf()) for i in range(2)]

    def dbg(self, name, ap, bufs, dt=F32):
        if not getattr(self, "debug", False):
            return
        shp = list(ap.shape)
        d = self.dout("dbg_" + name, shp, dt)
        self.dbg_toks.append(self.kb.dma("sp", d, ap, reads=bufs))

    def dsel(self, out, in0, in1, reads=(), writes=()):
        if not hasattr(self, "cond0"):
            rr = self.nc.sync.partition_id() % 2
            self.cond0 = (rr == 0)
        return self.kb.dma_if("sp", self.cond0, out, in0, in1, reads=reads, writes=writes)

    def rot(self, lo=0, n=2):
        i = lo + (self.rot_i % n)
        self.rot_i += 1
        return self.pb[i]

    def load_x(self, xT):
        for k in range(KC):
            self.kb.dma("sp", self.x[:, k, :], xT[k * 128:(k + 1) * 128, :], writes=self.xb[k])

    def store_x(self, yT):
        toks = []
        for k in range(KC):
            toks.append(self.kb.dma("sp", yT[k * 128:(k + 1) * 128, :], self.x[:, k, :], reads=self.xb[k]))
        return toks

    def barrier(self):
        kb = self.kb
        toks = [(k, v) for k, v in kb.cnt.items() if v > 0]
        for e in ENGS:
            kb.wait_all(e, toks)

    def emit_mods(self, tag, gate_scale):
        kb = self.kb
        save_off = self.scr_off
        self.aw = [(self.carve([128, KC, 256], F32), Buf()) for i in range(2)]
        adaw = self.din(tag + "_adaw", [D, 3 * D])
        adab = self.din(tag + "_adab", [128, 24])
        gain = self.din(tag + "_gain", [128, KC])
        kb.dma("sp", self.gain_sb[:], gain, writes=[self.gain_b])
        kb.dma("sp", self.adab_sb[:], adab, writes=[self.adab_b])
        ps_mod, ps_mod_b = self.pb[7]
        wv = adaw.rearrange("(k p) n -> p k n", p=128)
        sc = self.sc
        for blk in range(12):
            wt, wb = self.aw[self.aw_i % 2]
            slot = self.aw_i % 2
            self.aw_i += 1
            kb.dma("sp", wt[:], wv[:, :, blk * 256:(blk + 1) * 256], writes=[wb], key="aw%d" % slot)
            for j in range(2):
                col = blk * 2 + j
                for k in range(KC):
                    kb.op("pe", lambda e, col=col, k=k, j=j, wt=wt: e.matmul(
                        ps_mod[:, col:col + 1], lhsT=wt[:, k, j * 128:(j + 1) * 128], rhs=sc[:, k:k + 1],
                        start=(k == 0), stop=(k == KC - 1)),
                        reads=[wb, self.sc_b], writes=[ps_mod_b], inc=(k == KC - 1 and j == 1))
        mods, A, hg = self.mods, self.A, self.hg
        kb.op("dve", lambda e: e.tensor_tensor(mods[:], ps_mod[:, 0:24], self.adab_sb[:], op=ALU.add),
              reads=[ps_mod_b, self.adab_b], writes=[self.mods_b])
        kb.op("dve", lambda e: e.scalar_tensor_tensor(A[:], mods[:, 8:16], 1.0, self.gain_sb[:], op0=ALU.add, op1=ALU.mult),
              reads=[self.mods_b, self.gain_b], writes=[self.A_b])
        kb.op("dve", lambda e: e.tensor_scalar(hg[:], mods[:, 16:24], gate_scale, None, op0=ALU.mult),
              reads=[self.mods_b], writes=[self.hg_b])
        self.barrier()
        self.scr_off = save_off

    def rstd_from_ps(self, ps, psb, npart, dim, out, outb, sdt=None):
        kb = self.kb
        sd, sd_b = sdt if sdt is not None else (self.sd, self.sd_b)
        kb.op("dve", lambda e: e.tensor_scalar(sd[0:npart, :], ps[0:npart, :], 1.0 / dim, EPS, op0=ALU.mult, op1=ALU.add),
              reads=[psb], writes=[sd_b])
        kb.op("act", lambda e: e.activation(sd[0:npart, :], sd[0:npart, :], AF.Sqrt), reads=[sd_b], writes=[sd_b])
        kb.op("dve", lambda e: e.reciprocal(out[0:npart, :], sd[0:npart, :]), reads=[sd_b], writes=[outb])

    def emit_h(self):
        kb = self.kb
        x, h, sq, ones = self.x, self.h, self.sq, self.ones
        rstd, A, mods = self.rstd, self.A, self.mods
        ps_ssq, ps_ssq_b = self.pb[6]
        for g in range(NG):
            gs = slice(g * TG, (g + 1) * TG)
            kb.op("act", lambda e, gs=gs: e.activation(sq[:], x[:, :, gs], AF.Square),
                  reads=[self.xb[k][g] for k in range(KC)], writes=[self.sq_b])
            for k in range(KC):
                kb.op("pe", lambda e, k=k: e.matmul(ps_ssq[:], lhsT=ones[:], rhs=sq[:, k, :], start=(k == 0), stop=(k == KC - 1)),
                      reads=[self.ones_b, self.sq_b], writes=[ps_ssq_b], inc=(k == KC - 1))
            self.rstd_from_ps(ps_ssq, ps_ssq_b, 128, D, self.rstd, self.rstd_b)
            for k in range(KC):
                tt, tb = self.tmp[k % 2]
                kb.op("dve", lambda e, k=k, gs=gs, tt=tt: e.scalar_tensor_tensor(
                    tt[:], x[:, k, gs], A[:, k:k + 1], rstd[:], op0=ALU.mult, op1=ALU.mult),
                    reads=[self.xb[k][g], self.A_b, self.rstd_b], writes=[tb])
                kb.op("act", lambda e, k=k, gs=gs, tt=tt: e.activation(
                    h[:, k, gs], tt[:], AF.Identity, bias=mods[:, k:k + 1], scale=1.0),
                    reads=[tb, self.mods_b], writes=[self.hb[g]])

    def ffn(self, tag):
        kb = self.kb
        self.phase_begin()
        self.emit_mods(tag, 0.5)
        self.norm_scratch()
        self.emit_h()
        wgu = self.din(tag + "_wgu", [D, 2 * DFF])
        wdn = self.din(tag + "_wdn", [DFF, D])
        x, h, hg = self.x, self.h, self.hg
        if True:
            sbl = lambda n, s, d: self.carve(s, d)
            blocks = [(0, 4), (4, 4), (8, 4), (12, 4), (16, 4), (20, 2)]
            NBMAX = 4
            wg = [(sbl(tag + "wg%d" % i, [128, KC, NBMAX * 128], BF16), Buf()) for i in range(2)]
            wu = [(sbl(tag + "wu%d" % i, [128, KC, NBMAX * 128], BF16), Buf()) for i in range(2)]
            wd = [(sbl(tag + "wd%d" % i, [128, NBMAX, D], BF16), Buf()) for i in range(2)]
            act = [(sbl(tag + "act%d" % i, [128, NBMAX, TG], BF16), Buf()) for i in range(2)]
            sg = [(sbl(tag + "sg%d" % i, [128, TG], F32), Buf()) for i in range(2)]
            wguv = wgu.rearrange("(k p) n -> p k n", p=128)
            wdnv = wdn.rearrange("(j p) n -> p j n", p=128)
            it = 0
            ci = 0
            for bi, (j0, nb) in enumerate(blocks):
                s = bi % 2
                wgt, wgb = wg[s]; wut, wub = wu[s]; wdt, wdb = wd[s]
                kb.dma("pool", wgt[:, :, 0:nb * 128], wguv[:, :, j0 * 128:(j0 + nb) * 128], writes=[wgb], key=tag + "wg%d" % s)
                kb.dma("pool", wut[:, :, 0:nb * 128], wguv[:, :, DFF + j0 * 128:DFF + (j0 + nb) * 128], writes=[wub], key=tag + "wu%d" % s)
                kb.dma("pool", wdt[:, 0:nb, :], wdnv[:, j0:j0 + nb, :], writes=[wdb], key=tag + "wd%d" % s)
                for g in range(NG):
                    gs = slice(g * TG, (g + 1) * TG)
                    at, ab = act[it % 2]
                    it += 1
                    for jj in range(nb):
                        pg, pgb = self.pb[ci % 2]; pu, pub = self.pb[2 + ci % 2]
                        sgt, sgb = sg[ci % 2]
                        ci += 1
                        for k in range(KC):
                            kb.op("pe", lambda e, k=k, jj=jj, pg=pg, wgt=wgt, gs=gs: e.matmul(
                                pg[:], lhsT=wgt[:, k, jj * 128:(jj + 1) * 128], rhs=h[:, k, gs], start=(k == 0), stop=(k == KC - 1)),
                                reads=[wgb, self.hb[g]], writes=[pgb], inc=(k == KC - 1))
                        for k in range(KC):
                            kb.op("pe", lambda e, k=k, jj=jj, pu=pu, wut=wut, gs=gs: e.matmul(
                                pu[:], lhsT=wut[:, k, jj * 128:(jj + 1) * 128], rhs=h[:, k, gs], start=(k == 0), stop=(k == KC - 1)),
                                reads=[wub, self.hb[g]], writes=[pub], inc=(k == KC - 1))
                        kb.op("act", lambda e, pg=pg, sgt=sgt: e.activation(sgt[:], pg[:], AF.Silu), reads=[pgb], writes=[sgb])
                        kb.op("dve", lambda e, jj=jj, at=at, sgt=sgt, pu=pu: e.tensor_tensor(at[:, jj, :], sgt[:], pu[:], op=ALU.mult),
                              reads=[sgb, pub], writes=[ab])
                    for m in range(KC):
                        py, pyb = self.pb[4 + m % 2]
                        for jj in range(nb):
                            kb.op("pe", lambda e, m=m, jj=jj, py=py, wdt=wdt, at=at: e.matmul(
                                py[:], lhsT=wdt[:, jj, m * 128:(m + 1) * 128], rhs=at[:, jj, :], start=(jj == 0), stop=(jj == nb - 1)),
                                reads=[wdb, ab], writes=[pyb], inc=(jj == nb - 1))
                        kb.op("dve", lambda e, m=m, gs=gs, py=py: e.scalar_tensor_tensor(
                            x[:, m, gs], py[:], hg[:, m:m + 1], x[:, m, gs], op0=ALU.mult, op1=ALU.add),
                            reads=[pyb, self.hg_b, self.xb[m][g]], writes=[self.xb[m][g]])

    def rope_tables(self, ropec_sb, ropec_b, pos_d, gs, cos2, sinS, cs_b, W):
        kb = self.kb
        TWO_PI = float(2 * np.pi)
        C1 = 6.28125
        C2 = float(np.float32(2 * np.pi - 6.28125))
        pi_f = float(np.pi)
        pos_i, ang, kf, ki, r, m = W["pos_i"], W["ang"], W["kf"], W["ki"], W["r"], W["m"]
        wb = W["b"]
        kb.dma("sp", pos_i[:], pos_d[0:1, gs].broadcast_to([32, TG]), writes=[wb])
        kb.op("dve", lambda e: e.tensor_copy(ang[:], pos_i[:]), reads=[wb], writes=[wb])
        kb.op("dve", lambda e: e.tensor_scalar(ang[:], ang[:], ropec_sb[64:96, 0:1], None, op0=ALU.mult), reads=[wb, ropec_b], writes=[wb])
        kb.op("dve", lambda e: e.tensor_scalar(kf[:], ang[:], 1.0 / TWO_PI, None, op0=ALU.mult), reads=[wb], writes=[wb])
        kb.op("dve", lambda e: e.tensor_copy(ki[:], kf[:]), reads=[wb], writes=[wb])
        kb.op("dve", lambda e: e.tensor_copy(kf[:], ki[:]), reads=[wb], writes=[wb])
        kb.op("dve", lambda e: e.scalar_tensor_tensor(r[:], kf[:], -C1, ang[:], op0=ALU.mult, op1=ALU.add), reads=[wb], writes=[wb])
        kb.op("dve", lambda e: e.scalar_tensor_tensor(r[:], kf[:], -C2, r[:], op0=ALU.mult, op1=ALU.add), reads=[wb], writes=[wb])
        kb.op("dve", lambda e: e.tensor_scalar(m[:], r[:], pi_f, TWO_PI, op0=ALU.is_gt, op1=ALU.mult), reads=[wb], writes=[wb])
        kb.op("dve", lambda e: e.tensor_tensor(r[:], r[:], m[:], op=ALU.subtract), reads=[wb], writes=[wb])
        kb.op("dve", lambda e: e.tensor_scalar(m[:], r[:], -pi_f, TWO_PI, op0=ALU.is_lt, op1=ALU.mult), reads=[wb], writes=[wb])
        kb.op("dve", lambda e: e.tensor_tensor(r[:], r[:], m[:], op=ALU.add), reads=[wb], writes=[wb])
        kb.op("act", lambda e: e.activation(sinS[:], r[:], AF.Sin), reads=[wb], writes=[cs_b])
        kb.op("dve", lambda e: e.tensor_scalar(sinS[:], sinS[:], ropec_sb[64:96, 1:2], None, op0=ALU.mult), reads=[cs_b, ropec_b], writes=[cs_b])
        kb.op("dve", lambda e: e.tensor_scalar(r[:], r[:], pi_f / 2, None, op0=ALU.add), reads=[wb, cs_b], writes=[wb])
        kb.op("dve", lambda e: e.tensor_scalar(m[:], r[:], pi_f, TWO_PI, op0=ALU.is_gt, op1=ALU.mult), reads=[wb], writes=[wb])
        kb.op("dve", lambda e: e.tensor_tensor(r[:], r[:], m[:], op=ALU.subtract), reads=[wb], writes=[wb])
        kb.op("act", lambda e: e.activation(cos2[:], r[:], AF.Sin), reads=[wb], writes=[cs_b])

    def mla_p1(self, tag, snd):
        kb = self.kb
        self.phase_begin()
        self.emit_mods(tag, 1.0)
        self.norm_scratch()
        self.emit_h()
        x, h, ones = self.x, self.h, self.ones
        wa_d = self.din(tag + "_wa", [D, 704])
        wq_d = self.din(tag + "_wq", [QL, 16 * 128])
        wkn_d = self.din(tag + "_wkn", [KVL, 1024])
        wv_d = self.din(tag + "_wv", [KVL, 1024])
        mg_d = self.din(tag + "_mg", [128, 12])
        if "ropec" not in self.inputs:
            self.ropec_d = self.din("ropec", [32, 2])
            self.pos_d = self.din("pos", [1, T], I32)
        C = self.carve
        wa = C([128, KC, 704], BF16); wa_b = Buf()
        wq = C([128, 3, 2048], BF16); wq_b = Buf()
        wkn = C([128, 2, 1024], BF16); wkn_b = Buf()
        wv = C([128, 2, 1024], BF16); wv_b = Buf()
        mg = C([128, 12], F32); mg_b = Buf()
        ropec = C([128, 2], F32); ropec_b = Buf()
        sel2 = C([128, 2], BF16); sel2_b = Buf()
        alat = C([128, 3, TG], F32); alat_b = Buf()
        sqa = C([128, 3, TG], BF16); sqa_b = Buf()
        qn = C([128, 3, TG], BF16); qn_b = Buf()
        kvn = C([128, 2, TG], BF16); kvn_b = Buf()
        cos2 = C([32, TG], F32, 64); sinS = C([32, TG], F32, 64); cs_b = Buf()
        RW = {"pos_i": C([32, TG], I32, 64), "ang": C([32, TG], F32, 64), "kf": C([32, TG], F32, 64),
              "r": C([32, TG], F32, 64), "b": Buf()}
        RW["ki"] = RW["pos_i"]
        RW["m"] = RW["kf"]
        krsq = C([32, TG], BF16, 64); krsq_b = Buf()
        t1 = C([32, TG], F32, 64); t1_b = Buf()
        t2 = C([32, TG], F32, 64); t2_b = Buf()
        krf = C([32, TG], BF16, 64); krf_b = Buf()
        kst = [(C([128, TG], BF16), Buf()) for i in range(2)]
        ksq = [(C([128, TG], BF16), Buf()) for i in range(2)]
        v_sbs = [(C([128, 1024], BF16), Buf()) for i in range(2)]
        qsqs = [(C([96, TG], BF16), Buf()) for i in range(2)]
        rq96s = [(C([96, TG], F32), Buf()) for i in range(2)]
        sds = [(C([96, TG], F32), Buf()) for i in range(2)]
        t1s = [(t1, t1_b), (C([32, TG], F32, 64), Buf())]
        t2s = [(t2, t2_b), (C([32, TG], F32, 64), Buf())]
        qst = [(C([96, TG], BF16), Buf()) for i in range(2)]
        rsum = C([128, 4], F32); rsum_b = Buf()
        s1 = C([128, 4, 16], F32); s1_b = Buf()
        rstd, rstd_b = self.rstd, self.rstd_b
        kb.dma("pool", wa[:], wa_d.rearrange("(k p) n -> p k n", p=128), writes=[wa_b])
        kb.dma("pool", wq[:], wq_d.rearrange("(k p) n -> p k n", p=128), writes=[wq_b])
        kb.dma("pool", wkn[:], wkn_d.rearrange("(k p) n -> p k n", p=128), writes=[wkn_b])
        kb.dma("pool", wv[:], wv_d.rearrange("(k p) n -> p k n", p=128), writes=[wv_b])
        kb.dma("sp", mg[:], mg_d, writes=[mg_b])
        kb.dma("sp", ropec[64:96, :], self.ropec_d, writes=[ropec_b])
        kb.op("pool", lambda e: e.memset(sel2[:], 0.0), writes=[sel2_b])
        kb.op("pool", lambda e: e.memset(sel2[0:64, 0:1], 1.0), writes=[sel2_b])
        kb.op("pool", lambda e: e.memset(sel2[64:128, 1:2], 1.0), writes=[sel2_b])
        QT, KT, V, RK = snd["QT"], snd["KT"], snd["V"], snd["RK"]
        pb = self.pb
        p_ssq, p_ssq_b = pb[6]
        p_kr, p_kr_b = pb[3]
        p_krs, p_krs_b = pb[4]
        p_st, p_st_b = pb[5]
        out_toks = []
        qi = 0
        for g in range(NG):
            gs = slice(g * TG, (g + 1) * TG)
            self.rope_tables(ropec, ropec_b, self.pos_d, gs, cos2, sinS, cs_b, RW)

            def a_chunks(c0, n):
                for ci in range(n):
                    ps, psb = self.rot()
                    for k in range(KC):
                        kb.op("pe", lambda e, k=k, ci=ci, ps=ps, gs=gs, c0=c0: e.matmul(
                            ps[:], lhsT=wa[:, k, (c0 + ci) * 128:(c0 + ci + 1) * 128], rhs=h[:, k, gs], start=(k == 0), stop=(k == KC - 1)),
                            reads=[wa_b, self.hb[g]], writes=[psb], inc=(k == KC - 1))
                    kb.op("act", lambda e, ci=ci, ps=ps: e.activation(alat[:, ci, :], ps[:], AF.Copy), reads=[psb], writes=[alat_b])
                    kb.op("act", lambda e, ci=ci, ps=ps: e.activation(sqa[:, ci, :], ps[:], AF.Square), reads=[psb], writes=[sqa_b])
                for ci in range(n):
                    kb.op("pe", lambda e, ci=ci, n=n: e.matmul(p_ssq[:], lhsT=ones[:], rhs=sqa[:, ci, :], start=(ci == 0), stop=(ci == n - 1)),
                          reads=[self.ones_b, sqa_b], writes=[p_ssq_b], inc=(ci == n - 1))
                self.rstd_from_ps(p_ssq, p_ssq_b, 128, n * 128, rstd, rstd_b)
            a_chunks(0, 3)
            for ci in range(3):
                kb.op("dve", lambda e, ci=ci: e.scalar_tensor_tensor(qn[:, ci, :], alat[:, ci, :], mg[:, 5 + ci:6 + ci], rstd[:], op0=ALU.mult, op1=ALU.mult),
                      reads=[alat_b, mg_b, rstd_b], writes=[qn_b])
            if g == 0:
                self.dbg("h", h[:, :, 0:TG], self.hb, BF16)
                self.dbg("mods", self.mods[:], [self.mods_b])
                self.dbg("qn", qn[:], [qn_b], BF16)
                self.dbg("alat_q", alat[:], [alat_b])
            a_chunks(3, 2)
            for ci in range(2):
                kb.op("dve", lambda e, ci=ci: e.scalar_tensor_tensor(kvn[:, ci, :], alat[:, ci, :], mg[:, 3 + ci:4 + ci], rstd[:], op0=ALU.mult, op1=ALU.mult),
                      reads=[alat_b, mg_b, rstd_b], writes=[kvn_b])
            if g == 0:
                self.dbg("kvn", kvn[:], [kvn_b], BF16)
                self.dbg("cos2", cos2[:], [cs_b])
                self.dbg("sinS", sinS[:], [cs_b])
            for k in range(KC):
                kb.op("pe", lambda e, k=k, gs=gs: e.matmul(p_kr[64:96, :], lhsT=wa[:, k, 640:672], rhs=h[:, k, gs], start=(k == 0), stop=(k == KC - 1)),
                      reads=[wa_b, self.hb[g]], writes=[p_kr_b], inc=(k == KC - 1))
            for k in range(KC):
                kb.op("pe", lambda e, k=k, gs=gs: e.matmul(p_krs[64:96, :], lhsT=wa[:, k, 672:704], rhs=h[:, k, gs], start=(k == 0), stop=(k == KC - 1)),
                      reads=[wa_b, self.hb[g]], writes=[p_krs_b], inc=(k == KC - 1))
            kb.op("act", lambda e: e.activation(krsq[:], p_kr[64:96, :], AF.Square), reads=[p_kr_b], writes=[krsq_b])
            kb.op("dve", lambda e: e.scalar_tensor_tensor(t1[:], p_kr[64:96, :], mg[64:96, 1:2], cos2[:], op0=ALU.mult, op1=ALU.mult),
                  reads=[p_kr_b, mg_b, cs_b], writes=[t1_b])
            kb.op("dve", lambda e: e.scalar_tensor_tensor(t2[:], p_krs[64:96, :], mg[64:96, 2:3], sinS[:], op0=ALU.mult, op1=ALU.mult),
                  reads=[p_krs_b, mg_b, cs_b], writes=[t2_b])
            kb.op("pool", lambda e: e.tensor_tensor(krf[:], t1[:], t2[:], op=ALU.add), reads=[t1_b, t2_b], writes=[krf_b])
            KTv = KT.rearrange("(h p) t -> p h t", p=DH)
            out_toks.append(kb.dma("sp", KTv[64:96, :, gs], krf[:].unsqueeze(1).broadcast_to([32, NH, TG]), reads=[krf_b]))
            for tt in range(4):
                kb.op("pe", lambda e, tt=tt: e.matmul(p_st[:, 64 + tt:65 + tt], lhsT=krsq[:, tt * 128:(tt + 1) * 128], rhs=ones[64:96, 0:1],
                                                      start=True, stop=True),
                      reads=[krsq_b, self.ones_b], writes=[p_st_b], inc=(tt == 3))
            for j in range(8):
                ps, psb = self.rot()
                for kc in range(2):
                    kb.op("pe", lambda e, kc=kc, j=j, ps=ps: e.matmul(ps[:], lhsT=wkn[:, kc, j * 128:(j + 1) * 128], rhs=kvn[:, kc, :],
                                                                  start=(kc == 0), stop=(kc == 1)),
                          reads=[wkn_b, kvn_b], writes=[psb], inc=(kc == 1))
                kt_, ktb = kst[j % 2]
                kq_, kqb = ksq[j % 2]
                kb.op("act", lambda e, ps=ps, kt_=kt_: e.activation(kt_[:], ps[:], AF.Identity, scale=mg[:, 0:1]), reads=[psb, mg_b], writes=[ktb])
                kb.op("act", lambda e, ps=ps, kq_=kq_: e.activation(kq_[:], ps[:], AF.Square), reads=[psb], writes=[kqb])
                for hh in range(2):
                    hq = 2 * j + hh
                    out_toks.append(kb.dma("sp", KT[hq * DH:hq * DH + 64, gs], kt_[hh * 64:(hh + 1) * 64, :], reads=[ktb]))
                for tt in range(4):
                    kb.op("pe", lambda e, tt=tt, j=j, kq_=kq_: e.matmul(p_st[:, tt * 16 + 2 * j:tt * 16 + 2 * j + 2],
                                                                     lhsT=kq_[:, tt * 128:(tt + 1) * 128], rhs=sel2[:], start=True, stop=True),
                          reads=[kqb, sel2_b], writes=[p_st_b], inc=(tt == 3))
            kb.op("act", lambda e: e.activation(rsum[:], p_st[:, 64:68], AF.Copy), reads=[p_st_b], writes=[rsum_b])
            kb.op("dve", lambda e: e.tensor_tensor(s1[:], p_st[:, 0:64].rearrange("p (a b) -> p a b", b=16),
                                                   rsum[:].unsqueeze(2).broadcast_to([128, 4, 16]), op=ALU.add),
                  reads=[p_st_b, rsum_b], writes=[s1_b])
            kb.op("dve", lambda e: e.tensor_scalar(s1[:], s1[:], 1.0 / DH, EPS, op0=ALU.mult, op1=ALU.add), reads=[s1_b], writes=[s1_b])
            kb.op("act", lambda e: e.activation(s1[:], s1[:], AF.Sqrt), reads=[s1_b], writes=[s1_b])
            kb.op("dve", lambda e: e.reciprocal(s1[:], s1[:]), reads=[s1_b], writes=[s1_b])
            kb.op("dve", lambda e: e.tensor_scalar(s1[:], s1[:], SC96, None, op0=ALU.mult), reads=[s1_b], writes=[s1_b])
            out_toks.append(kb.dma("sp", RK[g * TG:(g + 1) * TG, :].rearrange("(a p) h -> p a h", p=128), s1[:], reads=[s1_b]))
            for tt in range(4):
                v_sb, v_b = v_sbs[tt % 2]
                for half in range(2):
                    ps, psb = self.rot()
                    for kc in range(2):
                        kb.op("pe", lambda e, kc=kc, tt=tt, half=half, ps=ps: e.matmul(
                            ps[:], lhsT=kvn[:, kc, tt * 128:(tt + 1) * 128], rhs=wv[:, kc, half * 512:(half + 1) * 512],
                            start=(kc == 0), stop=(kc == 1)),
                            reads=[wv_b, kvn_b], writes=[psb], inc=(kc == 1))
                    if half == 0:
                        kb.op("act", lambda e, ps=ps, v_sb=v_sb: e.activation(v_sb[:, 0:512], ps[:], AF.Copy), reads=[psb], writes=[v_b])
                    else:
                        kb.op("dve", lambda e, ps=ps, v_sb=v_sb: e.tensor_copy(v_sb[:, 512:1024], ps[:]), reads=[psb], writes=[v_b])
                r0 = g * TG + tt * 128
                out_toks.append(kb.dma("sp", V[r0:r0 + 128, :], v_sb[:], reads=[v_b]))
            for hq in range(NH):
                ps, psb = self.rot()
                pks, pks_b = pb[3 + hq % 2]
                pss, pss_b = pb[6 + hq % 2]
                qsq, qsq_b = qsqs[hq % 2]
                rq96, rq96_b = rq96s[hq % 2]
                t1q, t1q_b = t1s[hq % 2]
                t2q, t2q_b = t2s[hq % 2]
                for kc in range(3):
                    kb.op("pe", lambda e, kc=kc, hq=hq, ps=ps: e.matmul(ps[0:96, :], lhsT=wq[:, kc, hq * 128:hq * 128 + 96], rhs=qn[:, kc, :],
                                                                    start=(kc == 0), stop=(kc == 2)),
                          reads=[wq_b, qn_b], writes=[psb], inc=(kc == 2))
                for kc in range(3):
                    kb.op("pe", lambda e, kc=kc, hq=hq, pks=pks: e.matmul(pks[64:96, :], lhsT=wq[:, kc, hq * 128 + 96:hq * 128 + 128], rhs=qn[:, kc, :],
                                                                       start=(kc == 0), stop=(kc == 2)),
                          reads=[wq_b, qn_b], writes=[pks_b], inc=(kc == 2))
                kb.op("act", lambda e, ps=ps, qsq=qsq: e.activation(qsq[:], ps[0:96, :], AF.Square), reads=[psb], writes=[qsq_b])
                kb.op("pe", lambda e, pss=pss, qsq=qsq: e.matmul(pss[0:96, :], lhsT=ones[0:96, 0:96], rhs=qsq[:], start=True, stop=True),
                      reads=[self.ones_b, qsq_b], writes=[pss_b])
                self.rstd_from_ps(pss, pss_b, 96, DH, rq96, rq96_b, sds[hq % 2])
                qs_, qsb = qst[qi % 2]
                qi += 1
                kb.op("dve", lambda e, ps=ps, qs_=qs_, rq96=rq96: e.scalar_tensor_tensor(qs_[0:64, :], ps[0:64, :], mg[0:64, 8:9], rq96[0:64, :],
                                                                                       op0=ALU.mult, op1=ALU.mult),
                      reads=[psb, mg_b, rq96_b], writes=[qsb])
                kb.op("dve", lambda e, ps=ps, t1q=t1q: e.scalar_tensor_tensor(t1q[:], ps[64:96, :], mg[64:96, 8:9], cos2[:], op0=ALU.mult, op1=ALU.mult),
                      reads=[psb, mg_b, cs_b], writes=[t1q_b])
                kb.op("dve", lambda e, pks=pks, t2q=t2q: e.scalar_tensor_tensor(t2q[:], pks[64:96, :], mg[64:96, 9:10], sinS[:], op0=ALU.mult, op1=ALU.mult),
                      reads=[pks_b, mg_b, cs_b], writes=[t2q_b])
                kb.op("pool", lambda e, t1q=t1q, t2q=t2q: e.tensor_tensor(t1q[:], t1q[:], t2q[:], op=ALU.add), reads=[t1q_b, t2q_b], writes=[t1q_b])
                kb.op("pool", lambda e, qs_=qs_, t1q=t1q, rq96=rq96: e.tensor_tensor(qs_[64:96, :], t1q[:], rq96[64:96, :], op=ALU.mult),
                      reads=[t1q_b, rq96_b], writes=[qsb])
                out_toks.append(kb.dma("sp", QT[hq * DH:(hq + 1) * DH, gs], qs_[:], reads=[qsb]))
        return out_toks

    def mla_p2(self, tag, gat, snd_o):
        kb = self.kb
        nc = self.nc
        self.phase_begin()
        C = self.carve
        GQ, GK, GV, GRK = gat["QT"], gat["KT"], gat["V"], gat["RK"]
        rk_all = C([128, 32, 8], F32); rk_b = Buf()
        v_all = C([128, 32, 512], BF16); v_b = Buf()
        qT = [(C([96, S], BF16), Buf()) for i in range(2)]
        kT = [(C([96, S], BF16), Buf()) for i in range(2)]
        pt = [(C([128, TG], BF16), Buf()) for i in range(6)]
        ost = [(C([64, S], BF16), Buf()) for i in range(2)]
        rec = C([64, TG], F32); rec_b = Buf()
        ones = self.ones
        for s in range(2):
            rkr = GRK.rows(s, 0, T)
            self.dsel(rk_all[:, s * 16:(s + 1) * 16, :],
                      rkr[:, 0:8].rearrange("(a p) h -> p a h", p=128),
                      rkr[:, 8:16].rearrange("(a p) h -> p a h", p=128), writes=[rk_b])
            for a in range(2):
                vr = GV.rows(s, a * 1024, 1024)
                self.dsel(v_all[:, s * 16 + a * 8:s * 16 + (a + 1) * 8, :],
                          vr[:, 0:512].rearrange("(a p) n -> p a n", p=128),
                          vr[:, 512:1024].rearrange("(a p) n -> p a n", p=128), writes=[v_b])
        pb = self.pb
        out_toks = []
        LAG = 2
        NPT = len(pt)
        pend = []
        ti = 0
        gi = 0
        for hh in range(8):
            qt_, qb = qT[hh % 2]
            kt_, kbf = kT[hh % 2]
            for s in range(2):
                self.dsel(qt_[:, s * T:(s + 1) * T], GQ.rows(s, hh * DH, DH), GQ.rows(s, (8 + hh) * DH, DH), writes=[qb])
                self.dsel(kt_[:, s * T:(s + 1) * T], GK.rows(s, hh * DH, DH), GK.rows(s, (8 + hh) * DH, DH), writes=[kbf])
            os_, osb = ost[hh % 2]
            for gq in range(8):
                qs = slice(gq * TG, (gq + 1) * TG)
                po, pob = pb[4 + gi % 2]
                pd, pdb = pb[6 + gi % 2]
                gi += 1
                nkt = 4 * (gq + 1)
                for kt in range(nkt):
                    ps, psb = pb[ti % 4]
                    p_, p_b = pt[ti % NPT]
                    ti += 1
                    kb.op("pe", lambda e, kt=kt, ps=ps, kt_=kt_, qt_=qt_, qs=qs: e.matmul(
                        ps[:], lhsT=kt_[:, kt * 128:(kt + 1) * 128], rhs=qt_[:, qs], start=True, stop=True),
                        reads=[kbf, qb], writes=[psb])
                    kb.op("act", lambda e, kt=kt, hh=hh, ps=ps, p_=p_: e.activation(p_[:], ps[:], AF.Exp, scale=rk_all[:, kt, hh:hh + 1]),
                          reads=[psb, rk_b], writes=[p_b])
                    if kt >= 4 * gq:
                        base = gq * TG - kt * 128
                        kb.op("pool", lambda e, p_=p_, base=base: e.affine_select(
                            out=p_[:], in_=p_[:], pattern=[[1, TG]], compare_op=ALU.is_ge, fill=0.0, base=base, channel_multiplier=-1),
                            reads=[p_b], writes=[p_b])

                    def emit_pv(kt=kt, hh=hh, po=po, pob=pob, pd=pd, pdb=pdb, p_=p_, p_b=p_b, nkt=nkt, os_=os_, osb=osb, qs=qs):
                        kb.op("pe", lambda e: e.matmul(po[0:64, :], lhsT=v_all[:, kt, hh * 64:(hh + 1) * 64], rhs=p_[:],
                                                       start=(kt == 0), stop=(kt == nkt - 1)),
                              reads=[v_b, p_b], writes=[pob], inc=(kt == nkt - 1))
                        kb.op("pe", lambda e: e.matmul(pd[0:64, :], lhsT=ones[:, 0:64], rhs=p_[:], start=(kt == 0), stop=(kt == nkt - 1)),
                              reads=[self.ones_b, p_b], writes=[pdb], inc=True)
                        if kt == nkt - 1:
                            kb.op("dve", lambda e: e.reciprocal(rec[:], pd[0:64, :]), reads=[pdb], writes=[rec_b])
                            kb.op("dve", lambda e: e.tensor_tensor(os_[:, qs], po[0:64, :], rec[:], op=ALU.mult),
                                  reads=[pob, rec_b], writes=[osb])
                            if qs.stop == S:
                                out_toks.append(kb.dma("sp", snd_o[hh * 64:(hh + 1) * 64, :], os_[:], reads=[osb]))
                    pend.append(emit_pv)
                    if len(pend) > LAG:
                        pend.pop(0)()
        while pend:
            pend.pop(0)()
        return out_toks

    def mixer_p3(self, tag, gat_o, nk, recompute_mods, wname, gate_scale=1.0):
        kb = self.kb
        nc = self.nc
        self.phase_begin()
        if recompute_mods:
            self.emit_mods(tag, gate_scale)
        C = self.carve
        wo_d = self.din(tag + wname, [nk * 128, D])
        wo = C([128, nk, D], BF16); wo_b = Buf()
        kb.dma("pool", wo[:], wo_d.rearrange("(k p) n -> p k n", p=128), writes=[wo_b])
        osb = [(C([128, nk, TG], BF16), Buf()) for i in range(2)]
        x, hg = self.x, self.hg
        for g in range(NG):
            gs = slice(g * TG, (g + 1) * TG)
            o_, ob = osb[g % 2]
            R = gat_o.R
            for s_ in range(2):
                for kk in range(gat_o.nrows // R):
                    rr_ = gat_o.rows(s_, kk * R, R)
                    c0_ = (s_ * gat_o.nrows + kk * R) // 128
                    self.dsel(o_[:, c0_:c0_ + R // 128, :], rr_[:, g * TG:(g + 1) * TG].rearrange("(k p) t -> p k t", p=128),
                              rr_[:, T + g * TG:T + (g + 1) * TG].rearrange("(k p) t -> p k t", p=128), writes=[ob])
            for m in range(KC):
                py, pyb = self.pb[m % 2]
                for k in range(nk):
                    kb.op("pe", lambda e, m=m, k=k, py=py, o_=o_: e.matmul(py[:], lhsT=wo[:, k, m * 128:(m + 1) * 128], rhs=o_[:, k, :],
                                                                       start=(k == 0), stop=(k == nk - 1)),
                          reads=[wo_b, ob], writes=[pyb], inc=(k == nk - 1))
                kb.op("dve", lambda e, m=m, gs=gs, py=py: e.scalar_tensor_tensor(
                    x[:, m, gs], py[:], hg[:, m:m + 1], x[:, m, gs], op0=ALU.mult, op1=ALU.add),
                    reads=[pyb, self.hg_b, self.xb[m][g]], writes=[self.xb[m][g]])

    def ssd_p1(self, tag, snd):
        kb = self.kb
        self.phase_begin()
        self.emit_mods(tag, 1.0)
        self.norm_scratch()
        self.emit_h()
        h = self.h
        win_d = self.din(tag + "_win", [D, 5152])
        wv = win_d.rearrange("(k p) n -> p k n", p=128)
        C = self.carve
        wblk = [(C([128, KC, 512], BF16), Buf()) for i in range(2)]
        wdt = C([128, KC, 32], BF16); wdt_b = Buf()
        stg = [(C([128, TG], BF16), Buf()) for i in range(4)]
        dts = C([128, 16, 32], F32); dts_b = Buf()
        XBC, Z, DT = snd["XBC"], snd["Z"], snd["DT"]
        out_toks = []
        kb.dma("pool", wdt[:], wv[:, :, 5120:5152], writes=[wdt_b])
        bi = 0
        si = 0
        for blk in range(6):
            wt, wb = wblk[bi % 2]; bi += 1
            kb.dma("pool", wt[:], wv[:, :, 2048 + blk * 512:2048 + (blk + 1) * 512], writes=[wb])
            for cc in range(4):
                ch = blk * 4 + cc
                for g in range(NG):
                    gs = slice(g * TG, (g + 1) * TG)
                    ps, psb = self.rot(0, 4)
                    for k in range(KC):
                        kb.op("pe", lambda e, k=k, cc=cc, ps=ps, wt=wt, gs=gs: e.matmul(
                            ps[:], lhsT=wt[:, k, cc * 128:(cc + 1) * 128], rhs=h[:, k, gs], start=(k == 0), stop=(k == KC - 1)),
                            reads=[wb, self.hb[g]], writes=[psb], inc=(k == KC - 1))
                    st, stb = stg[si % 4]; si += 1
                    if si % 2 == 0:
                        kb.op("act", lambda e, ps=ps, st=st: e.activation(st[:], ps[:], AF.Copy), reads=[psb], writes=[stb])
                    else:
                        kb.op("dve", lambda e, ps=ps, st=st: e.tensor_copy(st[:], ps[:]), reads=[psb], writes=[stb])
                    out_toks.append(kb.dma("sp", XBC[ch * 128:(ch + 1) * 128, gs], st[:], reads=[stb]))
        for blk in range(4):
            wt, wb = wblk[bi % 2]; bi += 1
            kb.dma("pool", wt[:], wv[:, :, blk * 512:(blk + 1) * 512], writes=[wb])
            for tt in range(16):
                ps, psb = self.rot(0, 4)
                for k in range(KC):
                    kb.op("pe", lambda e, k=k, tt=tt, ps=ps, wt=wt: e.matmul(
                        ps[:], lhsT=h[:, k, tt * 128:(tt + 1) * 128], rhs=wt[:, k, :], start=(k == 0), stop=(k == KC - 1)),
                        reads=[wb, self.hb[tt // 4]], writes=[psb], inc=(k == KC - 1))
                st, stb = stg[si % 4]; si += 1
                if si % 2 == 0:
                    kb.op("act", lambda e, ps=ps, st=st: e.activation(st[:], ps[:], AF.Copy), reads=[psb], writes=[stb])
                else:
                    kb.op("dve", lambda e, ps=ps, st=st: e.tensor_copy(st[:], ps[:]), reads=[psb], writes=[stb])
                out_toks.append(kb.dma("sp", Z[tt * 128:(tt + 1) * 128, blk * 512:(blk + 1) * 512], st[:], reads=[stb]))
        pdt, pdt_b = self.pb[4]
        for tt in range(16):
            for k in range(KC):
                kb.op("pe", lambda e, k=k, tt=tt: e.matmul(pdt[:, tt * 32:(tt + 1) * 32], lhsT=h[:, k, tt * 128:(tt + 1) * 128], rhs=wdt[:, k, :],
                                                        start=(k == 0), stop=(k == KC - 1)),
                      reads=[wdt_b, self.hb[tt // 4]], writes=[pdt_b], inc=(k == KC - 1))
        kb.op("dve", lambda e: e.tensor_copy(dts[:], pdt[:].rearrange("p (a b) -> p a b", b=32)), reads=[pdt_b], writes=[dts_b])
        out_toks.append(kb.dma("sp", DT.rearrange("(a p) h -> p a h", p=128), dts[:], reads=[dts_b]))
        return out_toks

    def ssd_p2(self, tag, gat, snd_g):
        kb = self.kb
        nc = self.nc
        self.phase_begin()
        C = self.carve
        GX, GZ, GDT = gat["XBC"], gat["Z"], gat["DT"]
        cw_d = self.din(tag + "_cw", [128, 48])
        cb_d = self.din(tag + "_cb", [128, 12])
        cbrow_d = self.din(tag + "_cbrow", [1, 1280])
        hp_d = self.din(tag + "_hp", [128, 48])
        ng_d = self.din(tag + "_ng", [128, 1024])
        cw = C([128, 48], F32); cw_b = Buf()
        cb = C([128, 12], F32); cb_b = Buf()
        cbrow = C([1, 1280], F32); cbrow_b = Buf()
        cbhi = C([1, 1280], BF16); cblo = C([1, 1280], BF16); cbf = C([1, 1280], F32); cbhl_b = Buf()
        hp = C([128, 48], F32); hp_b = Buf()
        ng = C([128, 1024], F32); ng_b = Buf()
        identb = C([128, 128], BF16); Tb = C([128, 128], BF16)
        U = C([128, 128], F32); Tm = C([128, 128], F32); cst_b = Buf()
        diag = C([128, 48, 128], BF16); diag_b = Buf()
        Aneg = C([128, 16], F32); Aneg_b = Buf()
        u = C([128, 12, TG + 4], BF16); u_b = Buf()
        zt = C([128, 4, 1024], BF16); zt_b = Buf()
        dtr = C([128, 4, 16], F32); dtr_b = Buf()
        xs = C([128, 4, 1024], F32); xs_b = Buf()
        Btok = C([128, 4, 256], BF16); Btok_b = Buf()
        BT = C([128, 2, TG], BF16); CT = C([128, 2, TG], BF16); bct_b = Buf()
        dtv = C([128, 4, 16], F32); av = C([128, 4, 16], F32); dtv_b = Buf()
        acum = C([128, 16], F32); ea = C([128, 16], F32); dte = C([128, 16], F32); cd = C([128, 16], F32); sm_b = Buf()
        aU = C([128, 16, 128], F32); aU_b = Buf()
        dec = [(C([128, 8, 128], BF16), Buf()) for i in range(2)]
        cbm = [(C([128, 128], BF16), Buf()) for i in range(2)]
        MT = [(C([128, 8, 128], BF16), Buf()) for i in range(2)]
        xdt = C([128, 1024], BF16); xdt_b = Buf()
        Bdec = C([128, 16, 128], BF16); Bdec_b = Buf()
        Sf = C([128, 1024], F32); Sf_b = Buf()
        Sb = C([128, 1024], BF16); Sb_b = Buf()
        t1 = C([128, 1024], F32); t1_b = Buf()
        t3 = C([128, 1024], F32); t3_b = Buf()
        yv = C([128, 1024], F32); yv_b = Buf()
        sz = t3; sz_b = t3_b
        ssq = C([128, 2], F32); ssq_b = Buf()
        junk = C([128, 512], BF16); junk_b = Buf()
        gn = C([128, 1024], BF16); gn_b = Buf()
        gT = [(C([128, 8, TG], BF16), Buf()) for i in range(1)]
        ones, onesf = self.ones, self.onesf
        pb = self.pb
        kb.dma("sp", cw[:], cw_d, writes=[cw_b])
        kb.dma("sp", cb[:], cb_d, writes=[cb_b])
        kb.dma("sp", cbrow[:], cbrow_d, writes=[cbrow_b])
        kb.dma("sp", hp[:], hp_d, writes=[hp_b])
        kb.dma("sp", ng[:], ng_d, writes=[ng_b])
        kb.op("dve", lambda e: e.tensor_copy(cbhi[:], cbrow[:]), reads=[cbrow_b], writes=[cbhl_b])
        kb.op("dve", lambda e: e.tensor_copy(cbf[:], cbhi[:]), reads=[cbhl_b], writes=[cbhl_b])
        kb.op("dve", lambda e: e.tensor_tensor(cbf[:], cbrow[:], cbf[:], op=ALU.subtract), reads=[cbhl_b, cbrow_b], writes=[cbhl_b])
        kb.op("dve", lambda e: e.tensor_copy(cblo[:], cbf[:]), reads=[cbhl_b], writes=[cbhl_b])
        for (tile_, pat, base, cm, cmp_) in ((Tm, [[1, 128]], 0, -1, ALU.is_ge), (U, [[-1, 128]], -1, 1, ALU.is_ge)):
            kb.op("pool", lambda e, tile_=tile_: e.memset(tile_[:], 1.0), writes=[cst_b])
            kb.op("pool", lambda e, tile_=tile_, pat=pat, base=base, cm=cm, cmp_=cmp_: e.affine_select(
                out=tile_[:], in_=tile_[:], pattern=pat, compare_op=cmp_, fill=0.0, base=base, channel_multiplier=cm),
                reads=[cst_b], writes=[cst_b])
        kb.op("pool", lambda e: e.memset(identb[:], 1.0), writes=[cst_b])
        kb.op("pool", lambda e: e.affine_select(out=identb[:], in_=identb[:], pattern=[[-1, 128]], compare_op=ALU.is_equal, fill=0.0,
                                                base=0, channel_multiplier=1), reads=[cst_b], writes=[cst_b])
        kb.op("dve", lambda e: e.tensor_copy(Tb[:], Tm[:]), reads=[cst_b], writes=[cst_b])
        for c in range(12):
            for j in range(4):
                kb.op("dve" if (c + j) % 2 else "pool", lambda e, c=c, j=j: e.tensor_scalar(
                    diag[:, c * 4 + j, :], identb[:], cw[:, c * 4 + j:c * 4 + j + 1], None, op0=ALU.mult),
                    reads=[cst_b, cw_b], writes=[diag_b])
        kb.op("act", lambda e: e.activation(Aneg[:], hp[:, 16:32], AF.Exp), reads=[hp_b], writes=[Aneg_b])
        kb.op("dve", lambda e: e.tensor_scalar(Aneg[:], Aneg[:], -1.0, None, op0=ALU.mult), reads=[Aneg_b], writes=[Aneg_b])
        kb.op("pool", lambda e: e.memset(Sf[:], 0.0), writes=[Sf_b])
        kb.op("pool", lambda e: e.memset(u[:, :, 0:4], 0.0), writes=[u_b])
        out_toks = []
        for G in range(8):
            s = G // 4
            gl = G % 4
            t0 = gl * TG
            if G > 0:
                kb.op("dve", lambda e: e.tensor_copy(u[:, :, 0:4], u[:, :, TG:TG + 4]), reads=[u_b], writes=[u_b])
            for (c0, nch, r0, r1) in ((0, 4, 0, 1024), (4, 4, 512, 1536), (8, 2, 2048, 2048 + 256), (10, 2, 2560, 2560 + 256)):
                self.dsel(u[:, c0:c0 + nch, 4:4 + TG],
                          GX.rows(s, r0, nch * 128)[:, t0:t0 + TG].rearrange("(c p) t -> p c t", p=128),
                          GX.rows(s, r1, nch * 128)[:, t0:t0 + TG].rearrange("(c p) t -> p c t", p=128), writes=[u_b])
            zr = GZ.rows(s, t0, TG)
            self.dsel(zt[:], zr[:, 0:1024].rearrange("(a p) n -> p a n", p=128),
                      zr[:, 1024:2048].rearrange("(a p) n -> p a n", p=128), writes=[zt_b])
            dr = GDT.rows(s, t0, TG)
            self.dsel(dtr[:], dr[:, 0:16].rearrange("(a p) h -> p a h", p=128),
                      dr[:, 16:32].rearrange("(a p) h -> p a h", p=128), writes=[dtr_b])
            for c in range(8, 12):
                ps, psb = pb[7]
                for j in range(4):
                    kb.op("pe", lambda e, c=c, j=j, ps=ps: e.matmul(ps[:], lhsT=diag[:, c * 4 + j, :], rhs=u[:, c, 1 + j:1 + j + TG],
                                                              start=(j == 0), stop=(j == 3)),
                          reads=[diag_b, u_b], writes=[psb], inc=(j == 3))
                dst = BT if c < 10 else CT
                kb.op("act", lambda e, c=c, ps=ps, dst=dst: e.activation(dst[:, c % 2, :], ps[:], AF.Silu, bias=cb[:, c:c + 1]),
                      reads=[psb, cb_b], writes=[bct_b])
            for tt in range(4):
                for half in range(2):
                    ps, psb = pb[4 + half]
                    for cc in range(4):
                        c = half * 4 + cc
                        for j in range(4):
                            kb.op("pe", lambda e, c=c, cc=cc, j=j, tt=tt, ps=ps: e.matmul(
                                ps[:, cc * 128:(cc + 1) * 128], lhsT=u[:, c, 1 + j + tt * 128:1 + j + tt * 128 + 128], rhs=diag[:, c * 4 + j, :],
                                start=(cc == 0 and j == 0), stop=False, skip_group_check=True),
                                reads=[diag_b, u_b], writes=[psb], inc=False)
                    kb.op("pe", lambda e, half=half, ps=ps: e.matmul(ps[:], lhsT=ones[0:1, 0:128], rhs=cbhi[0:1, half * 512:(half + 1) * 512],
                                                                  start=False, stop=False, skip_group_check=True),
                          reads=[self.ones_b, cbhl_b], writes=[psb], inc=False)
                    kb.op("pe", lambda e, half=half, ps=ps: e.matmul(ps[:], lhsT=ones[0:1, 0:128], rhs=cblo[0:1, half * 512:(half + 1) * 512],
                                                                  start=False, stop=True, skip_group_check=True),
                          reads=[self.ones_b, cbhl_b], writes=[psb], inc=True)
                    kb.op("act", lambda e, half=half, tt=tt, ps=ps: e.activation(xs[:, tt, half * 512:(half + 1) * 512], ps[:], AF.Silu),
                          reads=[psb], writes=[xs_b])
                ps, psb = pb[7]
                for cc in range(2):
                    c = 8 + cc
                    for j in range(4):
                        kb.op("pe", lambda e, c=c, cc=cc, j=j, tt=tt, ps=ps: e.matmul(
                            ps[:, cc * 128:(cc + 1) * 128], lhsT=u[:, c, 1 + j + tt * 128:1 + j + tt * 128 + 128], rhs=diag[:, c * 4 + j, :],
                            start=(cc == 0 and j == 0), stop=False, skip_group_check=True),
                            reads=[diag_b, u_b], writes=[psb], inc=False)
                kb.op("pe", lambda e, ps=ps: e.matmul(ps[:, 0:256], lhsT=ones[0:1, 0:128], rhs=cbhi[0:1, 1024:1280], start=False, stop=False,
                                                      skip_group_check=True), reads=[self.ones_b, cbhl_b], writes=[psb], inc=False)
                kb.op("pe", lambda e, ps=ps: e.matmul(ps[:, 0:256], lhsT=ones[0:1, 0:128], rhs=cblo[0:1, 1024:1280], start=False, stop=True,
                                                      skip_group_check=True), reads=[self.ones_b, cbhl_b], writes=[psb], inc=True)
                kb.op("act", lambda e, tt=tt, ps=ps: e.activation(Btok[:, tt, :], ps[:, 0:256], AF.Silu), reads=[psb], writes=[Btok_b])
            kb.op("dve", lambda e: e.tensor_tensor(dtv[:], dtr[:], hp[:, 0:16].unsqueeze(1).broadcast_to([128, 4, 16]), op=ALU.add),
                  reads=[dtr_b, hp_b], writes=[dtv_b])
            kb.op("act", lambda e: e.activation(dtv[:], dtv[:], AF.Exp), reads=[dtv_b], writes=[dtv_b])
            kb.op("act", lambda e: e.activation(dtv[:], dtv[:], AF.Ln, bias=1.0), reads=[dtv_b], writes=[dtv_b])
            kb.op("dve", lambda e: e.tensor_tensor(av[:], dtv[:], Aneg[:].unsqueeze(1).broadcast_to([128, 4, 16]), op=ALU.mult),
                  reads=[dtv_b, Aneg_b], writes=[dtv_b])
            gt_, gtb = gT[0]
            for tt in range(4):
                ts_ = slice(tt * 128, (tt + 1) * 128)
                pst, pstb = pb[6]
                kb.op("pe", lambda e, tt=tt: e.matmul(pst[:, 0:16], lhsT=Tm[:], rhs=av[:, tt, :], start=True, stop=True),
                      reads=[cst_b, dtv_b], writes=[pstb])
                kb.op("pe", lambda e, tt=tt: e.matmul(pst[:, 16:32], lhsT=onesf[:], rhs=av[:, tt, :], start=True, stop=True),
                      reads=[self.onesf_b, dtv_b], writes=[pstb])
                kb.op("act", lambda e: e.activation(ea[:], pst[:, 0:16], AF.Exp), reads=[pstb], writes=[sm_b])
                kb.op("act", lambda e: e.activation(cd[:], pst[:, 16:32], AF.Exp), reads=[pstb], writes=[sm_b])
                kb.op("act", lambda e: e.activation(acum[:], pst[:, 0:16], AF.Copy), reads=[pstb], writes=[sm_b])
                kb.op("dve", lambda e: e.tensor_tensor(dte[:], pst[:, 16:32], acum[:], op=ALU.subtract), reads=[pstb, sm_b], writes=[sm_b])
                kb.op("act", lambda e: e.activation(dte[:], dte[:], AF.Exp), reads=[sm_b], writes=[sm_b])
                kb.op("dve", lambda e, tt=tt: e.tensor_tensor(xdt[:].rearrange("p (h d) -> p h d", d=64), xs[:, tt, :].rearrange("p (h d) -> p h d", d=64),
                                                            dtv[:, tt, :].unsqueeze(2).broadcast_to([128, 16, 64]), op=ALU.mult),
                      reads=[xs_b, dtv_b], writes=[xdt_b])
                kb.op("pool", lambda e, tt=tt: e.tensor_tensor(aU[:], U[:].unsqueeze(1).broadcast_to([128, 16, 128]),
                                                             av[:, tt, :].unsqueeze(2).broadcast_to([128, 16, 128]), op=ALU.mult),
                      reads=[cst_b, dtv_b], writes=[aU_b])
                for gg in range(2):
                    kb.op("pool", lambda e, tt=tt, gg=gg: e.tensor_tensor(
                        Bdec[:, gg * 8:(gg + 1) * 8, :], Btok[:, tt, gg * 128:(gg + 1) * 128].unsqueeze(1).broadcast_to([128, 8, 128]),
                        dte[:, gg * 8:(gg + 1) * 8].unsqueeze(2).broadcast_to([128, 8, 128]), op=ALU.mult),
                        reads=[Btok_b, sm_b], writes=[Bdec_b])
                kb.op("dve", lambda e: e.tensor_copy(Sb[:], Sf[:]), reads=[Sf_b], writes=[Sb_b])
                py0, py0b = pb[2]
                py1, py1b = pb[3]
                pys = [(py0, py0b), (py1, py1b)]
                for gg in range(2):
                    cm_, cmb = cbm[gg]
                    kb.op("pe", lambda e, gg=gg, ts_=ts_: e.matmul(pst[:, 32 + gg * 128:32 + (gg + 1) * 128], lhsT=BT[:, gg, ts_], rhs=CT[:, gg, ts_],
                                                                start=True, stop=True), reads=[bct_b], writes=[pstb])
                    kb.op("dve", lambda e, gg=gg, cm_=cm_: e.tensor_tensor(cm_[:], pst[:, 32 + gg * 128:32 + (gg + 1) * 128], Tb[:], op=ALU.mult),
                          reads=[pstb, cst_b], writes=[cmb])
                    for half in range(2):
                        ps, psb = pb[half]
                        for hh in range(4):
                            hd = gg * 8 + half * 4 + hh
                            kb.op("pe", lambda e, hd=hd, hh=hh, ps=ps: e.matmul(ps[:, hh * 128:(hh + 1) * 128], lhsT=aU[:, hd, :], rhs=Tm[:],
                                                                             start=True, stop=True),
                                  reads=[aU_b, cst_b], writes=[psb])
                    dc, dcb = dec[gg]
                    for half in range(2):
                        ps, psb = pb[half]
                        kb.op("act", lambda e, half=half, ps=ps, dc=dc: e.activation(
                            dc[:, half * 4:(half + 1) * 4, :], ps[:].rearrange("p (a b) -> p a b", b=128), AF.Exp), reads=[psb], writes=[dcb])
                    mt, mtb = MT[gg]
                    kb.op("dve" if gg == 0 else "pool", lambda e, mt=mt, dc=dc, cm_=cm_: e.tensor_tensor(
                        mt[:], dc[:], cm_[:].unsqueeze(1).broadcast_to([128, 8, 128]), op=ALU.mult), reads=[dcb, cmb], writes=[mtb])
                    py, pyb = pys[gg]
                    for hh in range(8):
                        hd = gg * 8 + hh
                        kb.op("pe", lambda e, hd=hd, hh=hh, py=py, mt=mt: e.matmul(py[:, hh * 64:(hh + 1) * 64], lhsT=mt[:, hh, :],
                                                                              rhs=xdt[:, hd * 64:(hd + 1) * 64], start=True, stop=True),
                              reads=[mtb, xdt_b], writes=[pyb], inc=(hh == 7))
                for gg in range(2):
                    ps, psb = pb[gg]
                    kb.op("pe", lambda e, gg=gg, ps=ps, ts_=ts_: e.matmul(ps[:], lhsT=CT[:, gg, ts_], rhs=Sb[:, gg * 512:(gg + 1) * 512], start=True, stop=True),
                          reads=[bct_b, Sb_b], writes=[psb])
                    kb.op("dve", lambda e, gg=gg, ps=ps: e.tensor_tensor(
                        t1[:, gg * 512:(gg + 1) * 512].rearrange("p (h d) -> p h d", d=64), ps[:].rearrange("p (h d) -> p h d", d=64),
                        ea[:, gg * 8:(gg + 1) * 8].unsqueeze(2).broadcast_to([128, 8, 64]), op=ALU.mult),
                        reads=[psb, sm_b], writes=[t1_b])
                kb.op("pool", lambda e, tt=tt: e.tensor_tensor(t3[:].rearrange("p (h d) -> p h d", d=64), xs[:, tt, :].rearrange("p (h d) -> p h d", d=64),
                                                             hp[:, 32:48].unsqueeze(2).broadcast_to([128, 16, 64]), op=ALU.mult),
                      reads=[xs_b, hp_b], writes=[t3_b])
                kb.op("pool", lambda e: e.tensor_tensor(t1[:], t1[:], t3[:], op=ALU.add), reads=[t1_b, t3_b], writes=[t1_b])
                for gg in range(2):
                    py, pyb = pys[gg]
                    kb.op("dve", lambda e, gg=gg, py=py: e.tensor_tensor(yv[:, gg * 512:(gg + 1) * 512], py[:], t1[:, gg * 512:(gg + 1) * 512], op=ALU.add),
                          reads=[pyb, t1_b], writes=[yv_b])
                kb.op("act", lambda e, tt=tt: e.activation(sz[:], zt[:, tt, :], AF.Silu), reads=[zt_b], writes=[sz_b])
                kb.op("dve", lambda e: e.tensor_tensor(yv[:], yv[:], sz[:], op=ALU.mult), reads=[yv_b, sz_b], writes=[yv_b])
                for gg in range(2):
                    kb.op("act", lambda e, gg=gg: e.activation(junk[:], yv[:, gg * 512:(gg + 1) * 512], AF.Square, accum_out=ssq[:, gg:gg + 1]),
                          reads=[yv_b], writes=[junk_b, ssq_b])
                kb.op("dve", lambda e: e.tensor_scalar(ssq[:], ssq[:], 1.0 / 512, EPS, op0=ALU.mult, op1=ALU.add), reads=[ssq_b], writes=[ssq_b])
                kb.op("act", lambda e: e.activation(ssq[:], ssq[:], AF.Sqrt), reads=[ssq_b], writes=[ssq_b])
                kb.op("dve", lambda e: e.reciprocal(ssq[:], ssq[:]), reads=[ssq_b], writes=[ssq_b])
                for gg in range(2):
                    kb.op("dve", lambda e, gg=gg: e.scalar_tensor_tensor(gn[:, gg * 512:(gg + 1) * 512], yv[:, gg * 512:(gg + 1) * 512], ssq[:, gg:gg + 1],
                                                                       ng[:, gg * 512:(gg + 1) * 512], op0=ALU.mult, op1=ALU.mult),
                          reads=[yv_b, ssq_b, ng_b], writes=[gn_b])
                for half in range(2):
                    ps, psb = pb[4 + half]
                    for fc in range(4):
                        f = half * 4 + fc
                        kb.op("pe", lambda e, f=f, fc=fc, ps=ps: e.matmul(ps[:, fc * 128:(fc + 1) * 128], lhsT=gn[:, f * 128:(f + 1) * 128], rhs=identb[:],
                                                                       start=True, stop=True), reads=[gn_b, cst_b], writes=[psb], inc=(fc == 3))
                    kb.op("act" if half == 0 else "dve", (lambda e, half=half, ps=ps, gt_=gt_, ts_=ts_: e.activation(
                        gt_[:, half * 4:(half + 1) * 4, ts_], ps[:].rearrange("p (a b) -> p a b", b=128), AF.Copy)) if half == 0 else
                        (lambda e, half=half, ps=ps, gt_=gt_, ts_=ts_: e.tensor_copy(gt_[:, half * 4:(half + 1) * 4, ts_], ps[:].rearrange("p (a b) -> p a b", b=128))),
                        reads=[psb], writes=[gtb])
                for gg in range(2):
                    ps, psb = pb[gg]
                    for hh in range(8):
                        hd = gg * 8 + hh
                        kb.op("pe", lambda e, hd=hd, hh=hh, ps=ps: e.matmul(ps[:, hh * 64:(hh + 1) * 64], lhsT=Bdec[:, hd, :], rhs=xdt[:, hd * 64:(hd + 1) * 64],
                                                                         start=True, stop=True), reads=[Bdec_b, xdt_b], writes=[psb], inc=(hh == 7))
                kb.op("pool", lambda e: e.tensor_tensor(Sf[:].rearrange("p (h d) -> p h d", d=64), Sf[:].rearrange("p (h d) -> p h d", d=64),
                                                        cd[:].unsqueeze(2).broadcast_to([128, 16, 64]), op=ALU.mult), reads=[Sf_b, sm_b, Sb_b], writes=[Sf_b])
                for gg in range(2):
                    ps, psb = pb[gg]
                    kb.op("dve", lambda e, gg=gg, ps=ps: e.tensor_tensor(Sf[:, gg * 512:(gg + 1) * 512], Sf[:, gg * 512:(gg + 1) * 512], ps[:], op=ALU.add),
                          reads=[psb, Sf_b], writes=[Sf_b])
            col0 = s * T + t0
            out_toks.append(kb.dma("sp", snd_g[:, col0:col0 + TG].rearrange("(c p) t -> p c t", p=128), gt_[:], reads=[gtb]))
        return out_toks


D = 1024; T = 2048; S = 4096; DFF = 2816


def fm(v):
    v = np.asarray(v, np.float32)
    return np.ascontiguousarray(v.reshape(-1, 128).T)


def ropec():
    inv = (1.0 / (10000.0 ** (np.arange(0, 32, 2, dtype=np.float32) / 32))).astype(np.float32)
    c = np.zeros((32, 2), np.float32)
    c[:16, 0] = inv; c[16:, 0] = inv
    c[:16, 1] = -1.0; c[16:, 1] = 1.0
    return c


def prep_mods(I, i, sub, tag):
    return {tag + "_adaw": np.ascontiguousarray(I["ada_w"][i][:, sub * 3072:(sub + 1) * 3072]),
            tag + "_adab": fm(I["ada_b"][i][sub * 3072:(sub + 1) * 3072]),
            tag + "_gain": fm(I["norm_gain"][i, sub])}


def prep_ffn(I, i, which, tag):
    d = prep_mods(I, i, 0 if which == 0 else 2, tag)
    d[tag + "_wgu"] = I["ffn_w_gu"][i, which]
    d[tag + "_wdn"] = I["ffn_w_down"][i, which]
    return d


def prep_mla(I, i, tag):
    j = i // 2
    d = prep_mods(I, i, 1, tag)
    wa = I["mla_w_a"][j]
    kr = wa[:, 640:672]
    d[tag + "_wa"] = np.ascontiguousarray(np.concatenate([wa, kr[:, 16:], kr[:, :16]], 1))
    wqb = I["mla_w_qb"][j].reshape(384, 16, 96)
    nope, rp = wqb[:, :, :64], wqb[:, :, 64:]
    wq = np.concatenate([nope, rp, rp[:, :, 16:], rp[:, :, :16]], 2)
    d[tag + "_wq"] = np.ascontiguousarray(wq.reshape(384, 2048))
    wkv = I["mla_w_kvb"][j].reshape(256, 16, 128)
    d[tag + "_wkn"] = np.ascontiguousarray(wkv[:, :, :64].reshape(256, 1024))
    d[tag + "_wv"] = np.ascontiguousarray(wkv[:, :, 64:].reshape(256, 1024))
    mg = np.zeros((128, 12), np.float32)
    gk = I["mla_k_gain"][j]; gq = I["mla_q_gain"][j]
    mg[:64, 0] = gk[:64]; mg[64:, 0] = gk[:64]
    mg[64:96, 1] = gk[64:]
    mg[64:80, 2] = gk[80:]; mg[80:96, 2] = gk[64:80]
    mg[:, 3:5] = fm(I["mla_kv_a_gain"][j])
    mg[:, 5:8] = fm(I["mla_q_a_gain"][j])
    mg[:96, 8] = gq
    mg[64:80, 9] = gq[80:]; mg[80:96, 9] = gq[64:80]
    d[tag + "_mg"] = mg
    d[tag + "_wo"] = I["mla_w_o"][j]
    return d


def prep_ssd(I, i, tag):
    j = i // 2
    d = prep_mods(I, i, 1, tag)
    d[tag + "_win"] = I["ssd_w_in"][j]
    d[tag + "_wout"] = I["ssd_w_out"][j]
    return d


def prep_ssd_rank(I, i, tag, r):
    j = i // 2
    cwf = I["ssd_conv_w"][j]
    cbf = I["ssd_conv_b"][j]
    chans = np.concatenate([np.arange(1024 * r, 1024 * r + 1024), 2048 + 256 * r + np.arange(256), 2560 + 256 * r + np.arange(256)])
    cw = cwf[:, chans].reshape(4, 12, 128).transpose(2, 1, 0).reshape(128, 48)
    cb = cbf[chans].reshape(12, 128).T
    cbrow = cbf[chans[:1280]][None, :]
    hp = np.concatenate([I["ssd_dt_bias"][j][16 * r:16 * r + 16], I["ssd_a_log"][j][16 * r:16 * r + 16], I["ssd_d"][j][16 * r:16 * r + 16]])
    hp = np.broadcast_to(hp[None, :], (128, 48))
    ng = np.broadcast_to(I["ssd_norm_gain"][j][1024 * r:1024 * r + 1024][None, :], (128, 1024))
    f = lambda a: np.ascontiguousarray(a, dtype=np.float32)
    return {tag + "_cw": f(cw), tag + "_cb": f(cb), tag + "_cbrow": f(cbrow), tag + "_hp": f(hp), tag + "_ng": f(ng)}


import ml_dtypes
_BF = ml_dtypes.bfloat16
_PROGS = {}


def _build(kind):
    if kind in _PROGS:
        return _PROGS[kind]
    ph, mixer = kind
    P = Prog(None)
    P.setup()
    toks = []
    if ph == "A":
        xT = P.din("xT", [D, T]); P.load_x(xT)
        P.ffn("F0")
        yT = P.dout("yT", [D, T])
        toks += P.store_x(yT)
        if mixer == "mla":
            snd = {"QT": P.dout("QT", [16 * 96, T], BF16), "KT": P.dout("KT", [16 * 96, T], BF16),
                   "V": P.dout("V", [T, 1024], BF16), "RK": P.dout("RK", [T, 16], F32)}
            toks += P.mla_p1("M", snd)
        else:
            snd = {"XBC": P.dout("XBC", [3072, T], BF16), "Z": P.dout("Z", [T, 2048], BF16), "DT": P.dout("DT", [T, 32], F32)}
            toks += P.ssd_p1("M", snd)
    elif ph == "B":
        if mixer == "mla":
            g_ap = {"QT": GBuf(P.din("gQT", [2 * 16 * 96, T], BF16), 1536, 1536), "KT": GBuf(P.din("gKT", [2 * 16 * 96, T], BF16), 1536, 1536),
                    "V": GBuf(P.din("gV", [2 * T, 1024], BF16), T, T), "RK": GBuf(P.din("gRK", [2 * T, 16], F32), T, T)}
            snd_o = P.dout("O", [512, S], BF16)
            toks += P.mla_p2("M", g_ap, snd_o)
        else:
            g_ap = {"XBC": GBuf(P.din("gXBC", [2 * 3072, T], BF16), 3072, 3072), "Z": GBuf(P.din("gZ", [2 * T, 2048], BF16), T, T),
                    "DT": GBuf(P.din("gDT", [2 * T, 32], F32), T, T)}
            snd_g = P.dout("O", [1024, S], BF16)
            toks += P.ssd_p2("M", g_ap, snd_g)
    else:
        xT = P.din("xT", [D, T]); P.load_x(xT)
        if mixer == "mla":
            gO = GBuf(P.din("gO", [1024, S], BF16), 512, 512)
            P.mixer_p3("M", gO, 8, True, "_wo")
        else:
            gO = GBuf(P.din("gO", [2048, S], BF16), 1024, 1024)
            P.mixer_p3("M", gO, 16, True, "_wout")
        P.ffn("F1")
        yT = P.dout("yT", [D, T])
        toks += P.store_x(yT)
    P.kb.wait_all("sp", toks)
    P.kb.emit_all()
    _PROGS[kind] = P
    return P


def _launch(P, cands):
    ims = []
    for c in range(8):
        d = {}
        for n in P.inputs:
            for src in cands[c]:
                if n in src:
                    d[n] = src[n]
                    break
            else:
                raise KeyError(n)
        ims.append(d)
    res = run_bass_kernel_spmd(P.nc, ims, core_ids=list(range(8)))
    return res.results


def kernel(**I):
    I = {k: np.asarray(v) for k, v in I.items()}
    x = I["x"].astype(np.float32)
    B = x.shape[0]
    rc = ropec()
    core_common = []
    for c in range(8):
        b, r = c // 2, c % 2
        core_common.append({"cT": fm(I["c"][b]), "ropec": rc,
                            "pos": np.ascontiguousarray(I["positions"][b][None, r * T:(r + 1) * T]).astype(np.int32)})
    xs = [np.ascontiguousarray(x[c // 2, (c % 2) * T:(c % 2 + 1) * T].T) for c in range(8)]
    for i in range(4):
        mixer = "mla" if i % 2 == 0 else "ssd"
        W0 = prep_ffn(I, i, 0, "F0")
        W1 = prep_ffn(I, i, 1, "F1")
        WM = prep_mla(I, i, "M") if mixer == "mla" else prep_ssd(I, i, "M")
        WR = [prep_ssd_rank(I, i, "M", r) for r in range(2)] if mixer == "ssd" else [{}, {}]
        P = _build(("A", mixer))
        res = _launch(P, [[{"xT": xs[c]}, core_common[c], W0, WM] for c in range(8)])
        xs = [res[c]["yT"] for c in range(8)]
        names = ["QT", "KT", "V", "RK"] if mixer == "mla" else ["XBC", "Z", "DT"]
        gat = []
        for pr in range(4):
            g = {"g" + n: np.concatenate([res[2 * pr][n], res[2 * pr + 1][n]], 0) for n in names}
            gat += [g, g]
        del res
        P = _build(("B", mixer))
        res = _launch(P, [[gat[c], core_common[c], WM, WR[c % 2]] for c in range(8)])
        gO = []
        for pr in range(4):
            g = {"gO": np.concatenate([res[2 * pr]["O"], res[2 * pr + 1]["O"]], 0)}
            gO += [g, g]
        del res, gat
        P = _build(("C", mixer))
        res = _launch(P, [[{"xT": xs[c]}, gO[c], core_common[c], W1, WM] for c in range(8)])
        xs = [res[c]["yT"] for c in range(8)]
        del res
    out = np.empty_like(x)
    for c in range(8):
        out[c // 2, (c % 2) * T:(c % 2 + 1) * T] = xs[c].T
    return out


def _build_fused():
    if "fused" in _PROGS:
        return _PROGS["fused"]
    P = Prog(None)
    P.setup()
    kb = P.kb
    xT = P.din("xT", [D, T]); P.load_x(xT)
    for i in range(4):
        mixer = "mla" if i % 2 == 0 else "ssd"
        L = "L%d" % i
        P.ffn(L + "F0")
        if mixer == "mla":
            shapes = {"QT": ([16 * 96, T], BF16, 384), "KT": ([16 * 96, T], BF16, 384), "V": ([T, 1024], BF16, 1024), "RK": ([T, 16], F32, T)}
        else:
            shapes = {"XBC": ([3072, T], BF16, 512), "Z": ([T, 2048], BF16, 512), "DT": ([T, 32], F32, T)}
        snd = {n: P.dint(L + "s" + n, shp, dt) for n, (shp, dt, R) in shapes.items()}
        gat = {n: GBuf(P.dint(L + "g" + n, [2 * shp[0], shp[1]], dt), shp[0], R, snd[n]) for n, (shp, dt, R) in shapes.items()}
        toks = P.mla_p1(L + "M", snd) if mixer == "mla" else P.ssd_p1(L + "M", snd)
        for n in shapes:
            gat[n].gather(kb, toks, GROUPS)
        if mixer == "mla":
            snd_o = P.dint(L + "sO", [512, S], BF16)
            gat_o = GBuf(P.dint(L + "gO", [1024, S], BF16), 512, 256, snd_o)
            toks = P.mla_p2(L + "M", gat, snd_o)
        else:
            snd_o = P.dint(L + "sO", [1024, S], BF16)
            gat_o = GBuf(P.dint(L + "gO", [2048, S], BF16), 1024, 256, snd_o)
            toks = P.ssd_p2(L + "M", gat, snd_o)
        gat_o.gather(kb, toks, GROUPS)
        if mixer == "mla":
            P.mixer_p3(L + "M", gat_o, 8, False, "_wo")
        else:
            P.mixer_p3(L + "M", gat_o, 16, False, "_wout")
        P.ffn(L + "F1")
    yT = P.dout("yT", [D, T])
    toks = P.store_x(yT)
    kb.wait_all("sp", toks)
    kb.emit_all()
    _PROGS["fused"] = P
    return P


def kernel_fused(**I):
    I = {k: np.asarray(v) for k, v in I.items()}
    x = I["x"].astype(np.float32)
    rc = ropec()
    P = _build_fused()
    Wall = {}
    WR = [{}, {}]
    for i in range(4):
        L = "L%d" % i
        Wall.update(prep_ffn(I, i, 0, L + "F0"))
        Wall.update(prep_ffn(I, i, 1, L + "F1"))
        if i % 2 == 0:
            Wall.update(prep_mla(I, i, L + "M"))
        else:
            Wall.update(prep_ssd(I, i, L + "M"))
            for r in range(2):
                WR[r].update(prep_ssd_rank(I, i, L + "M", r))
    cands = []
    for c in range(8):
        b, r = c // 2, c % 2
        cc = {"cT": fm(I["c"][b]), "ropec": rc, "pos": np.ascontiguousarray(I["positions"][b][None, r * T:(r + 1) * T]).astype(np.int32),
              "xT": np.ascontiguousarray(x[b, r * T:(r + 1) * T].T)}
        cands.append([cc, Wall, WR[r]])
    res = _launch(P, cands)
    out = np.empty_like(x)
    for c in range(8):
        out[c // 2, (c % 2) * T:(c % 2 + 1) * T] = res[c]["yT"].T
    return out


kernel_multi = kernel
kernel = kernel_fused
```

```python
import numpy as np
from contextlib import ExitStack
import concourse.bass as bass
import concourse.mybir as mybir
from concourse.bass_utils import run_bass_kernel_spmd


F32 = mybir.dt.float32
BF16 = mybir.dt.bfloat16
I32 = mybir.dt.int32
AF = mybir.ActivationFunctionType
ALU = mybir.AluOpType
AX = mybir.AxisListType

ENGS = ("pe", "act", "dve", "pool", "sp")


class Buf:
    __slots__ = ("w", "r", "name")

    def __init__(self, name=""):
        self.w = None
        self.r = []
        self.name = name


class KB:
    def __init__(self, nc, ctx):
        self.nc = nc
        self.ctx = ctx
        self.q = {e: [] for e in ENGS}
        self.cnt = {}
        self.semh = {}
        self.seen = {e: {} for e in ENGS}
        self.pe_pending = []
        for e in ENGS:
            self.new_sem("E_" + e)
        self.ndma = 0
        self.NPOOL = 64
        self.buf_key = {}
        self.pool_i = 0

    def phase_reset(self):
        self.buf_key = {}
        self.pool_i = 0

    def _dma_key(self, reads, writes):
        prim = (list(writes) + list(reads))[0]
        k = self.buf_key.get(id(prim))
        if k is None:
            assert self.pool_i < self.NPOOL, "DMA semaphore pool exhausted in this phase"
            k = "DS%d" % self.pool_i
            self.pool_i += 1
            self.buf_key[id(prim)] = k
        return k

    def new_sem(self, key):
        self.semh[key] = self.ctx.enter_context(self.nc.semaphore(key))
        self.cnt[key] = 0
        return key

    def sb(self, name, shape, dtype):
        return self.ctx.enter_context(self.nc.sbuf_tensor(name, list(shape), dtype))

    def ps(self, name, shape, dtype=F32):
        return self.ctx.enter_context(self.nc.psum_tensor(name, list(shape), dtype))

    def _waits(self, eng, toks):
        ws = []
        best = {}
        for t in toks:
            if t is None:
                continue
            s, v = t
            if self.seen[eng].get(s, 0) >= v:
                continue
            if best.get(s, 0) < v:
                best[s] = v
        for s, v in best.items():
            self.seen[eng][s] = v
            ws.append((s, v))
        return ws

    def op(self, eng, fn, reads=(), writes=(), inc=True, extra=()):
        toks = list(extra)
        for b in reads:
            toks.append(b.w)
        for b in writes:
            toks.append(b.w)
            toks.extend(b.r)
        if eng == "pe":
            toks = [t for t in toks if t is not None and t[0] != "E_pe"]
        ws = self._waits(eng, toks)
        key = "E_" + eng
        if eng == "pe" and not inc:
            self.pe_pending.append((tuple(reads), tuple(writes)))
            tok = None
        else:
            self.cnt[key] += 1
            tok = (key, self.cnt[key])
            allrw = [(tuple(reads), tuple(writes))]
            if eng == "pe":
                allrw += self.pe_pending
                self.pe_pending = []
            for rs, wr in allrw:
                for b in rs:
                    b.r.append(tok)
                for b in wr:
                    b.w = tok
                    b.r = []
        semh = self.semh

        def emit(e, fn=fn, ws=ws, tok=tok):
            for s, v in ws:
                e.wait_ge(semh[s], v)
            ins = fn(e)
            if tok is not None:
                ins.then_inc(semh[tok[0]], 1)
        self.q[eng].append(emit)
        return tok

    def dma(self, eng, out, in_, reads=(), writes=(), extra=(), key=None, **kw):
        toks = list(extra)
        for b in reads:
            toks.append(b.w)
        for b in writes:
            toks.append(b.w)
            toks.extend(b.r)
        ws = self._waits(eng, toks)
        key = self._dma_key(reads, writes)
        if key not in self.semh:
            self.new_sem(key)
        self.cnt[key] += 16
        tok = (key, self.cnt[key])
        for b in reads:
            b.r.append(tok)
        for b in writes:
            b.w = tok
            b.r = []
        semh = self.semh

        def emit(e, ws=ws, tok=tok, out=out, in_=in_, kw=kw):
            for s, v in ws:
                e.wait_ge(semh[s], v)
            try:
                e.dma_start(out=out, in_=in_, **kw).then_inc(semh[tok[0]], 16)
            except Exception:
                print("DMA FAIL out", out.shape, out.ap, "in", in_.shape, in_.ap, flush=True)
                raise
        self.q[eng].append(emit)
        return tok

    def wait_all(self, eng, toks):
        ws = self._waits(eng, toks)
        semh = self.semh

        def emit(e, ws=ws):
            for s, v in ws:
                e.wait_ge(semh[s], v)
        self.q[eng].append(emit)

    def emit_all(self):
        nc = self.nc
        q = self.q
        with nc.Block() as block:
            @block.sync
            def _(e):
                for f in q["sp"]:
                    f(e)

            @block.tensor
            def _(e):
                for f in q["pe"]:
                    f(e)

            @block.scalar
            def _(e):
                for f in q["act"]:
                    f(e)

            @block.vector
            def _(e):
                for f in q["dve"]:
                    f(e)

            @block.gpsimd
            def _(e):
                for f in q["pool"]:
                    f(e)


CC_INC = 1


def _kb_cc(self, kind, ins, outs, groups, reads=(), writes=(), extra=()):
    toks = list(extra)
    for b in reads:
        toks.append(b.w)
    for b in writes:
        toks.append(b.w)
        toks.extend(b.r)
    ws = self._waits("pool", toks)
    key = "CC"
    if key not in self.semh:
        self.new_sem(key)
    self.cnt[key] += CC_INC
    tok = (key, self.cnt[key])
    for b in reads:
        b.r.append(tok)
    for b in writes:
        b.w = tok
        b.r = []
    semh = self.semh

    def emit(e, ws=ws, tok=tok):
        for s, v in ws:
            e.wait_ge(semh[s], v)
        e.collective_compute(kind, ALU.bypass, replica_groups=groups, ins=ins, outs=outs).then_inc(semh[tok[0]], CC_INC)
    self.q["pool"].append(emit)
    return tok


KB.cc = _kb_cc


def _kb_dma_if(self, eng, cond, out, in_true, in_false, reads=(), writes=(), extra=()):
    toks = list(extra)
    for b in reads:
        toks.append(b.w)
    for b in writes:
        toks.append(b.w)
        toks.extend(b.r)
    ws = self._waits(eng, toks)
    key = self._dma_key(reads, writes)
    if key not in self.semh:
        self.new_sem(key)
    self.cnt[key] += 16
    tok = (key, self.cnt[key])
    for b in reads:
        b.r.append(tok)
    for b in writes:
        b.w = tok
        b.r = []
    semh = self.semh

    def emit(e, ws=ws, tok=tok):
        for s, v in ws:
            e.wait_ge(semh[s], v)
        with e.If(cond):
            e.dma_start(out=out, in_=in_true).then_inc(semh[tok[0]], 16)
        with e.Else():
            e.dma_start(out=out, in_=in_false).then_inc(semh[tok[0]], 16)
    self.q[eng].append(emit)
    return tok


KB.dma_if = _kb_dma_if


D = 1024
KC = 8
T = 2048
S = 4096
TG = 512
NG = T // TG
DFF = 2816
EPS = 1e-6
NH = 16
DH = 96
QL = 384
KVL = 256
SC96 = 96 ** -0.5
GROUPS = [[0, 1], [2, 3], [4, 5], [6, 7]]


class GBuf:
    def __init__(self, gat_ap, rows, R, snd_ap=None):
        self.gat, self.nrows, self.R, self.snd = gat_ap, rows, R, snd_ap

    def rows(self, s, q0, n):
        R = self.R
        k = q0 // R
        assert (q0 + n - 1) // R == k, (q0, n, R)
        base = k * 2 * R + s * R + q0 % R
        return self.gat[base:base + n, :]

    def gather(self, kb, toks, groups):
        R = self.R
        for k in range(self.nrows // R):
            kb.cc("AllGather", [self.snd[k * R:(k + 1) * R, :]], [self.gat[k * 2 * R:(k + 1) * 2 * R, :]], groups, extra=toks)


class Prog:
    def __init__(self, steps, name="p"):
        self.steps = steps
        self.nc = bass.Bass("TRN2", target_bir_lowering=False)
        self.ctx = ExitStack()
        self.kb = KB(self.nc, self.ctx)
        self.debug = False
        self.dbg_toks = []
        self.inputs = {}
        self.outputs = {}
        self.rot_i = 0

    def din(self, name, shape, dt=F32):
        self.inputs[name] = (tuple(shape), dt)
        return self.nc.dram_tensor(name, list(shape), dt, kind="ExternalInput").ap()

    def dout(self, name, shape, dt=F32):
        self.outputs[name] = (tuple(shape), dt)
        return self.nc.dram_tensor(name, list(shape), dt, kind="ExternalOutput").ap()

    def dint(self, name, shape, dt=F32):
        return self.nc.dram_tensor(name, list(shape), dt).ap()

    def setup(self):
        kb = self.kb
        self.x = kb.sb("x", [128, KC, T], F32)
        self.xb = [[Buf() for g in range(NG)] for k in range(KC)]
        self.SCR = 35328
        self.scratch = kb.sb("scratch", [128, self.SCR], F32)
        self.scr_off = 0
        self.ones = kb.sb("ones", [128, 128], BF16); self.ones_b = Buf()
        self.onesf = kb.sb("onesf", [128, 128], F32); self.onesf_b = Buf()
        kb.op("pool", lambda e: e.memset(self.ones[:], 1.0), writes=[self.ones_b])
        kb.op("pool", lambda e: e.memset(self.onesf[:], 1.0), writes=[self.onesf_b])
        self.c_sb = kb.sb("c_sb", [128, KC], F32); self.c_b = Buf()
        self.sc = kb.sb("sc", [128, KC], F32); self.sc_b = Buf()
        cT = self.din("cT", [128, KC])
        kb.dma("sp", self.c_sb[:], cT, writes=[self.c_b])
        kb.op("act", lambda e: e.activation(self.sc[:], self.c_sb[:], AF.Silu), reads=[self.c_b], writes=[self.sc_b])
        self.pb = [(kb.ps("pb%d" % i, [128, 512]), Buf()) for i in range(8)]
        self.mods = kb.sb("mods", [128, 24], F32); self.mods_b = Buf()
        self.A = kb.sb("A", [128, KC], F32); self.A_b = Buf()
        self.hg = kb.sb("hg", [128, KC], F32); self.hg_b = Buf()
        self.gain_sb = kb.sb("gain_sb", [128, KC], F32); self.gain_b = Buf()
        self.adab_sb = kb.sb("adab_sb", [128, 24], F32); self.adab_b = Buf()
        self.aw_i = 0

    def carve(self, shape, dt, p0=0):
        n = int(np.prod(shape[1:]))
        words = n if dt == F32 or dt == I32 else (n + 1) // 2
        assert self.scr_off + words <= self.SCR, ("scratch overflow", self.scr_off, words)
        v = self.scratch[:, self.scr_off:self.scr_off + words]
        self.scr_off += words
        if dt != F32:
            v = v.bitcast(dt)
        if len(shape) == 3:
            v = v.rearrange("p (a b) -> p a b", b=shape[2])
        elif len(shape) == 4:
            v = v.rearrange("p (a b c) -> p a b c", b=shape[2], c=shape[3])
        return v[p0:p0 + shape[0]] if shape[0] < 128 else v

    def phase_begin(self):
        self.barrier()
        self.kb.phase_reset()
        self.scr_off = 0

    def norm_scratch(self):
        self.h = self.carve([128, KC, T], BF16)
        self.hb = [Buf() for g in range(NG)]
        self.sq = self.carve([128, KC, TG], BF16); self.sq_b = Buf()
        self.sd = self.carve([128, TG], F32); self.sd_b = Buf()
        self.rstd = self.carve([128, TG], F32); self.rstd_b = Buf()
        self.tmp = [(self.carve([128, TG], F32), Buf()) for i in range(2)]

    def dbg(self, name, ap, bufs, dt=F32):
        if not getattr(self, "debug", False):
            return
        shp = list(ap.shape)
        d = self.dout("dbg_" + name, shp, dt)
        self.dbg_toks.append(self.kb.dma("sp", d, ap, reads=bufs))

    def dsel(self, out, in0, in1, reads=(), writes=()):
        if not hasattr(self, "cond0"):
            rr = self.nc.sync.partition_id() % 2
            self.cond0 = (rr == 0)
        return self.kb.dma_if("sp", self.cond0, out, in0, in1, reads=reads, writes=writes)

    def rot(self, lo=0, n=2):
        i = lo + (self.rot_i % n)
        self.rot_i += 1
        return self.pb[i]

    def load_x(self, xT):
        for k in range(KC):
            self.kb.dma("sp", self.x[:, k, :], xT[k * 128:(k + 1) * 128, :], writes=self.xb[k])

    def store_x(self, yT):
        toks = []
        for k in range(KC):
            toks.append(self.kb.dma("sp", yT[k * 128:(k + 1) * 128, :], self.x[:, k, :], reads=self.xb[k]))
        return toks

    def barrier(self):
        kb = self.kb
        toks = [(k, v) for k, v in kb.cnt.items() if v > 0]
        for e in ENGS:
            kb.wait_all(e, toks)

    def setup_global_mods(self):
        kb = self.kb
        self.global_mods = True
        self.phase_begin()
        C = self.carve
        adawc = self.din("adawc", [4 * D, 1152])
        c4T_d = self.din("c4T", [128, 32])
        adab_d = self.din("adab_all", [128, 288])
        gain_d = self.din("gain_all", [128, 96])
        esel_d = self.din("esel", [128, 4])
        self.shift_all = kb.sb("shift_all", [128, 96], F32)
        self.A_all = kb.sb("A_all", [128, 96], F32)
        self.hg_all = kb.sb("hg_all", [128, 96], F32)
        self.gm_b = Buf()
        shift_all, A_all, hg_all = self.shift_all, self.A_all, self.hg_all
        c4 = C([128, 8, 4], F32); c4_b = Buf()
        sc4 = C([128, 8, 4], F32); sc4_b = Buf()
        adab = C([128, 288], F32); adab_b = Buf()
        gains = C([128, 96], F32); gains_b = Buf()
        esel = C([128, 4], F32); esel_b = Buf()
        gsc = C([128, 96], F32); gsc_b = Buf()
        aw = [(C([128, KC, 384], F32), Buf()) for i in range(2)]
        mp = C([128, 4, 36], F32); mp_b = Buf()
        gsb = C([128, 32, 36], F32); gsb_b = Buf()
        acc = C([128, 8, 36], F32); acc_b = Buf()
        mfull = C([128, 4, 72], F32); mfull_b = Buf()
        kb.dma("sp", c4[:], c4T_d.rearrange("p (k b) -> p k b", b=4), writes=[c4_b])
        kb.dma("sp", adab[:], adab_d, writes=[adab_b])
        kb.dma("sp", gains[:], gain_d, writes=[gains_b])
        kb.dma("sp", esel[:], esel_d, writes=[esel_b])
        kb.op("act", lambda e: e.activation(sc4[:], c4[:], AF.Silu), reads=[c4_b], writes=[sc4_b])
        ps, psb = self.pb[7]
        wv = adawc.rearrange("(l k p) n -> p l k n", l=4, p=128)
        bi = 0
        for l in range(4):
            for blk in range(3):
                wt, wb = aw[bi % 2]; bi += 1
                kb.dma("sp", wt[:], wv[:, l, :, blk * 384:(blk + 1) * 384], writes=[wb])
                for j3 in range(3):
                    col = (l * 9 + blk * 3 + j3) * 4
                    for k in range(KC):
                        kb.op("pe", lambda e, col=col, k=k, j3=j3, wt=wt: e.matmul(
                            ps[:, col:col + 4], lhsT=wt[:, k, j3 * 128:(j3 + 1) * 128], rhs=sc4[:, k, :], start=(k == 0), stop=(k == KC - 1)),
                            reads=[wb, sc4_b], writes=[psb], inc=(k == KC - 1 and j3 == 2))
        kb.op("dve", lambda e: e.tensor_copy(mp[:], ps[:, 0:144].rearrange("p (a b) -> p b a", b=4)), reads=[psb], writes=[mp_b])
        snd = self.dint("gm_snd", [512, 36], F32)
        gat = self.dint("gm_gat", [8 * 512, 36], F32)
        t0 = kb.dma("sp", snd.rearrange("(b p) a -> p b a", p=128), mp[:], reads=[mp_b])
        kb.cc("AllGather", [snd], [gat], [[0, 1, 2, 3, 4, 5, 6, 7]], extra=[t0])
        cct = ("CC", kb.cnt["CC"])
        kb.dma("sp", gsb[:], gat.rearrange("(cb p) a -> p cb a", p=128), writes=[gsb_b], extra=[cct])
        g4 = gsb[:].rearrange("p (c b) a -> p c b a", b=4)
        kb.op("dve", lambda e: e.tensor_scalar(acc[:], g4[:, :, 0, :], esel[:, 0:1], None, op0=ALU.mult), reads=[gsb_b, esel_b], writes=[acc_b])
        for b_ in range(1, 4):
            kb.op("dve", lambda e, b_=b_: e.scalar_tensor_tensor(acc[:], g4[:, :, b_, :], esel[:, b_:b_ + 1], acc[:], op0=ALU.mult, op1=ALU.add),
                  reads=[gsb_b, esel_b, acc_b], writes=[acc_b])
        kb.op("dve", lambda e: e.tensor_tensor(mfull[:].rearrange("p l (c j) -> p l c j", j=9), acc[:].rearrange("p c (l j) -> p l c j", j=9),
                                               adab[:].rearrange("p (l c j) -> p l c j", l=4, j=9), op=ALU.add),
              reads=[acc_b, adab_b], writes=[mfull_b])
        m4 = mfull[:].rearrange("p l (s t) -> p l s t", t=24)
        v96 = lambda t_: t_[:].rearrange("p (l s k) -> p l s k", l=4, s=3)
        kb.op("dve", lambda e: e.tensor_copy(v96(shift_all), m4[:, :, :, 0:8]), reads=[mfull_b], writes=[self.gm_b])
        kb.op("dve", lambda e: e.tensor_scalar(v96(A_all), m4[:, :, :, 8:16], 1.0, None, op0=ALU.add), reads=[mfull_b], writes=[self.gm_b])
        kb.op("dve", lambda e: e.tensor_tensor(A_all[:], A_all[:], gains[:], op=ALU.mult), reads=[self.gm_b, gains_b], writes=[self.gm_b])
        kb.op("pool", lambda e: e.memset(gsc[:], 0.5), writes=[gsc_b])
        kb.op("pool", lambda e: e.memset(v96(gsc)[:, :, 1, :], 1.0), reads=[gsc_b], writes=[gsc_b])
        kb.op("dve", lambda e: e.tensor_tensor(v96(hg_all), m4[:, :, :, 16:24], v96(gsc), op=ALU.mult), reads=[mfull_b, gsc_b], writes=[self.gm_b])

    def emit_mods(self, tag, gate_scale):
        kb = self.kb
        if getattr(self, "global_mods", False):
            idx = int(tag[1]) * 3 + {"F0": 0, "M": 1, "F1": 2}[tag[2:]]
            self.mods = self.shift_all[:, idx * 8:(idx + 1) * 8]
            self.A = self.A_all[:, idx * 8:(idx + 1) * 8]
            self.hg = self.hg_all[:, idx * 8:(idx + 1) * 8]
            self.mods_b = self.A_b = self.hg_b = self.gm_b
            return
        save_off = self.scr_off
        self.aw = [(self.carve([128, KC, 256], F32), Buf()) for i in range(2)]
        adaw = self.din(tag + "_adaw", [D, 3 * D])
        adab = self.din(tag + "_adab", [128, 24])
        gain = self.din(tag + "_gain", [128, KC])
        kb.dma("sp", self.gain_sb[:], gain, writes=[self.gain_b])
        kb.dma("sp", self.adab_sb[:], adab, writes=[self.adab_b])
        ps_mod, ps_mod_b = self.pb[7]
        wv = adaw.rearrange("(k p) n -> p k n", p=128)
        sc = self.sc
        for blk in range(12):
            wt, wb = self.aw[self.aw_i % 2]
            slot = self.aw_i % 2
            self.aw_i += 1
            kb.dma("sp", wt[:], wv[:, :, blk * 256:(blk + 1) * 256], writes=[wb], key="aw%d" % slot)
            for j in range(2):
                col = blk * 2 + j
                for k in range(KC):
                    kb.op("pe", lambda e, col=col, k=k, j=j, wt=wt: e.matmul(
                        ps_mod[:, col:col + 1], lhsT=wt[:, k, j * 128:(j + 1) * 128], rhs=sc[:, k:k + 1],
                        start=(k == 0), stop=(k == KC - 1)),
                        reads=[wb, self.sc_b], writes=[ps_mod_b], inc=(k == KC - 1 and j == 1))
        mods, A, hg = self.mods, self.A, self.hg
        kb.op("dve", lambda e: e.tensor_tensor(mods[:], ps_mod[:, 0:24], self.adab_sb[:], op=ALU.add),
              reads=[ps_mod_b, self.adab_b], writes=[self.mods_b])
        kb.op("dve", lambda e: e.scalar_tensor_tensor(A[:], mods[:, 8:16], 1.0, self.gain_sb[:], op0=ALU.add, op1=ALU.mult),
              reads=[self.mods_b, self.gain_b], writes=[self.A_b])
        kb.op("dve", lambda e: e.tensor_scalar(hg[:], mods[:, 16:24], gate_scale, None, op0=ALU.mult),
              reads=[self.mods_b], writes=[self.hg_b])
        self.barrier()
        self.scr_off = save_off

    def rstd_from_ps(self, ps, psb, npart, dim, out, outb, sdt=None):
        kb = self.kb
        sd, sd_b = sdt if sdt is not None else (self.sd, self.sd_b)
        kb.op("dve", lambda e: e.tensor_scalar(sd[0:npart, :], ps[0:npart, :], 1.0 / dim, EPS, op0=ALU.mult, op1=ALU.add),
              reads=[psb], writes=[sd_b])
        kb.op("act", lambda e: e.activation(sd[0:npart, :], sd[0:npart, :], AF.Sqrt), reads=[sd_b], writes=[sd_b])
        kb.op("dve", lambda e: e.reciprocal(out[0:npart, :], sd[0:npart, :]), reads=[sd_b], writes=[outb])

    def emit_h(self, groups=None):
        kb = self.kb
        x, h, sq, ones = self.x, self.h, self.sq, self.ones
        rstd, A, mods = self.rstd, self.A, self.mods
        ps_ssq, ps_ssq_b = self.pb[6]
        for g in (range(NG) if groups is None else groups):
            gs = slice(g * TG, (g + 1) * TG)
            kb.op("act", lambda e, gs=gs: e.activation(sq[:], x[:, :, gs], AF.Square),
                  reads=[self.xb[k][g] for k in range(KC)], writes=[self.sq_b])
            for k in range(KC):
                kb.op("pe", lambda e, k=k: e.matmul(ps_ssq[:], lhsT=ones[:], rhs=sq[:, k, :], start=(k == 0), stop=(k == KC - 1)),
                      reads=[self.ones_b, self.sq_b], writes=[ps_ssq_b], inc=(k == KC - 1))
            self.rstd_from_ps(ps_ssq, ps_ssq_b, 128, D, self.rstd, self.rstd_b)
            for k in range(KC):
                tt, tb = self.tmp[k % 2]
                kb.op("dve", lambda e, k=k, gs=gs, tt=tt: e.scalar_tensor_tensor(
                    tt[:], x[:, k, gs], A[:, k:k + 1], rstd[:], op0=ALU.mult, op1=ALU.mult),
                    reads=[self.xb[k][g], self.A_b, self.rstd_b], writes=[tb])
                kb.op("act", lambda e, k=k, gs=gs, tt=tt: e.activation(
                    h[:, k, gs], tt[:], AF.Identity, bias=mods[:, k:k + 1], scale=1.0),
                    reads=[tb, self.mods_b], writes=[self.hb[g]])

    def ffn(self, tag):
        kb = self.kb
        self.phase_begin()
        self.emit_mods(tag, 0.5)
        self.norm_scratch()
        self.emit_h([0])
        wgu = self.din(tag + "_wgu", [D, 2 * DFF])
        wdn = self.din(tag + "_wdn", [DFF, D])
        x, h, hg = self.x, self.h, self.hg
        if True:
            sbl = lambda n, s, d: self.carve(s, d)
            blocks = [(0, 4), (4, 4), (8, 4), (12, 4), (16, 4), (20, 2)]
            NBMAX = 4
            wg = [(sbl(tag + "wg%d" % i, [128, KC, NBMAX * 128], BF16), Buf()) for i in range(2)]
            wu = [(sbl(tag + "wu%d" % i, [128, KC, NBMAX * 128], BF16), Buf()) for i in range(2)]
            wd = [(sbl(tag + "wd%d" % i, [128, NBMAX, D], BF16), Buf()) for i in range(2)]
            act = [(sbl(tag + "act%d" % i, [128, NBMAX, TG], BF16), Buf()) for i in range(2)]
            sg = [(sbl(tag + "sg%d" % i, [128, TG], F32), Buf()) for i in range(2)]
            wguv = wgu.rearrange("(k p) n -> p k n", p=128)
            wdnv = wdn.rearrange("(j p) n -> p j n", p=128)
            it = 0
            ci = 0
            for bi, (j0, nb) in enumerate(blocks):
                s = bi % 2
                wgt, wgb = wg[s]; wut, wub = wu[s]; wdt, wdb = wd[s]
                kb.dma("pool", wgt[:, :, 0:nb * 128], wguv[:, :, j0 * 128:(j0 + nb) * 128], writes=[wgb], key=tag + "wg%d" % s)
                kb.dma("pool", wut[:, :, 0:nb * 128], wguv[:, :, DFF + j0 * 128:DFF + (j0 + nb) * 128], writes=[wub], key=tag + "wu%d" % s)
                kb.dma("pool", wdt[:, 0:nb, :], wdnv[:, j0:j0 + nb, :], writes=[wdb], key=tag + "wd%d" % s)
                for g in range(NG):
                    gs = slice(g * TG, (g + 1) * TG)
                    at, ab = act[it % 2]
                    it += 1
                    if bi == 0 and g + 1 < NG:
                        self.emit_h([g + 1])
                    for jj in range(nb):
                        pg, pgb = self.pb[ci % 2]; pu, pub = self.pb[2 + ci % 2]
                        sgt, sgb = sg[ci % 2]
                        ci += 1
                        for k in range(KC):
                            kb.op("pe", lambda e, k=k, jj=jj, pg=pg, wgt=wgt, gs=gs: e.matmul(
                                pg[:], lhsT=wgt[:, k, jj * 128:(jj + 1) * 128], rhs=h[:, k, gs], start=(k == 0), stop=(k == KC - 1)),
                                reads=[wgb, self.hb[g]], writes=[pgb], inc=(k == KC - 1))
                        for k in range(KC):
                            kb.op("pe", lambda e, k=k, jj=jj, pu=pu, wut=wut, gs=gs: e.matmul(
                                pu[:], lhsT=wut[:, k, jj * 128:(jj + 1) * 128], rhs=h[:, k, gs], start=(k == 0), stop=(k == KC - 1)),
                                reads=[wub, self.hb[g]], writes=[pub], inc=(k == KC - 1))
                        kb.op("act", lambda e, pg=pg, sgt=sgt: e.activation(sgt[:], pg[:], AF.Silu), reads=[pgb], writes=[sgb])
                        kb.op("dve", lambda e, jj=jj, at=at, sgt=sgt, pu=pu: e.tensor_tensor(at[:, jj, :], sgt[:], pu[:], op=ALU.mult),
                              reads=[sgb, pub], writes=[ab])
                    for m in range(KC):
                        py, pyb = self.pb[4 + m % 2]
                        for jj in range(nb):
                            kb.op("pe", lambda e, m=m, jj=jj, py=py, wdt=wdt, at=at: e.matmul(
                                py[:], lhsT=wdt[:, jj, m * 128:(m + 1) * 128], rhs=at[:, jj, :], start=(jj == 0), stop=(jj == nb - 1)),
                                reads=[wdb, ab], writes=[pyb], inc=(jj == nb - 1))
                        kb.op("dve", lambda e, m=m, gs=gs, py=py: e.scalar_tensor_tensor(
                            x[:, m, gs], py[:], hg[:, m:m + 1], x[:, m, gs], op0=ALU.mult, op1=ALU.add),
                            reads=[pyb, self.hg_b, self.xb[m][g]], writes=[self.xb[m][g]])

    def rope_tables(self, ropec_sb, ropec_b, pos_d, gs, cos2, sinS, cs_b, W):
        kb = self.kb
        TWO_PI = float(2 * np.pi)
        C1 = 6.28125
        C2 = float(np.float32(2 * np.pi - 6.28125))
        pi_f = float(np.pi)
        pos_i, ang, kf, ki, r, m = W["pos_i"], W["ang"], W["kf"], W["ki"], W["r"], W["m"]
        wb = W["b"]
        kb.dma("sp", pos_i[:], pos_d[0:1, gs].broadcast_to([32, TG]), writes=[wb])
        kb.op("dve", lambda e: e.tensor_copy(ang[:], pos_i[:]), reads=[wb], writes=[wb])
        kb.op("dve", lambda e: e.tensor_scalar(ang[:], ang[:], ropec_sb[64:96, 0:1], None, op0=ALU.mult), reads=[wb, ropec_b], writes=[wb])
        kb.op("dve", lambda e: e.tensor_scalar(kf[:], ang[:], 1.0 / TWO_PI, None, op0=ALU.mult), reads=[wb], writes=[wb])
        kb.op("dve", lambda e: e.tensor_copy(ki[:], kf[:]), reads=[wb], writes=[wb])
        kb.op("dve", lambda e: e.tensor_copy(kf[:], ki[:]), reads=[wb], writes=[wb])
        kb.op("dve", lambda e: e.scalar_tensor_tensor(r[:], kf[:], -C1, ang[:], op0=ALU.mult, op1=ALU.add), reads=[wb], writes=[wb])
        kb.op("dve", lambda e: e.scalar_tensor_tensor(r[:], kf[:], -C2, r[:], op0=ALU.mult, op1=ALU.add), reads=[wb], writes=[wb])
        kb.op("dve", lambda e: e.tensor_scalar(m[:], r[:], pi_f, TWO_PI, op0=ALU.is_gt, op1=ALU.mult), reads=[wb], writes=[wb])
        kb.op("dve", lambda e: e.tensor_tensor(r[:], r[:], m[:], op=ALU.subtract), reads=[wb], writes=[wb])
        kb.op("dve", lambda e: e.tensor_scalar(m[:], r[:], -pi_f, TWO_PI, op0=ALU.is_lt, op1=ALU.mult), reads=[wb], writes=[wb])
        kb.op("dve", lambda e: e.tensor_tensor(r[:], r[:], m[:], op=ALU.add), reads=[wb], writes=[wb])
        kb.op("act", lambda e: e.activation(sinS[:], r[:], AF.Sin), reads=[wb], writes=[cs_b])
        kb.op("dve", lambda e: e.tensor_scalar(sinS[:], sinS[:], ropec_sb[64:96, 1:2], None, op0=ALU.mult), reads=[cs_b, ropec_b], writes=[cs_b])
        kb.op("dve", lambda e: e.tensor_scalar(r[:], r[:], pi_f / 2, None, op0=ALU.add), reads=[wb, cs_b], writes=[wb])
        kb.op("dve", lambda e: e.tensor_scalar(m[:], r[:], pi_f, TWO_PI, op0=ALU.is_gt, op1=ALU.mult), reads=[wb], writes=[wb])
        kb.op("dve", lambda e: e.tensor_tensor(r[:], r[:], m[:], op=ALU.subtract), reads=[wb], writes=[wb])
        kb.op("act", lambda e: e.activation(cos2[:], r[:], AF.Sin), reads=[wb], writes=[cs_b])

    def mla_p1(self, tag, snd):
        kb = self.kb
        self.phase_begin()
        self.emit_mods(tag, 1.0)
        self.norm_scratch()
        self.emit_h()
        x, h, ones = self.x, self.h, self.ones
        wa_d = self.din(tag + "_wa", [D, 704])
        wq_d = self.din(tag + "_wq", [QL, 16 * 128])
        wkn_d = self.din(tag + "_wkn", [KVL, 1024])
        wv_d = self.din(tag + "_wv", [KVL, 1024])
        mg_d = self.din(tag + "_mg", [128, 12])
        if "ropec" not in self.inputs:
            self.ropec_d = self.din("ropec", [32, 2])
            self.pos_d = self.din("pos", [1, T], I32)
        C = self.carve
        wa = C([128, KC, 704], BF16); wa_b = Buf()
        wq = C([128, 3, 2048], BF16); wq_b = Buf()
        wkn = C([128, 2, 1024], BF16); wkn_b = Buf()
        wv = C([128, 2, 1024], BF16); wv_b = Buf()
        mg = C([128, 12], F32); mg_b = Buf()
        ropec = C([128, 2], F32); ropec_b = Buf()
        sel2 = C([128, 2], BF16); sel2_b = Buf()
        alat = C([128, 3, TG], F32); alat_b = Buf()
        sqa = C([128, 3, TG], BF16); sqa_b = Buf()
        qn = C([128, 3, TG], BF16); qn_b = Buf()
        kvn = C([128, 2, TG], BF16); kvn_b = Buf()
        cos2 = C([32, TG], F32, 64); sinS = C([32, TG], F32, 64); cs_b = Buf()
        RW = {"pos_i": C([32, TG], I32, 64), "ang": C([32, TG], F32, 64), "kf": C([32, TG], F32, 64),
              "r": C([32, TG], F32, 64), "b": Buf()}
        RW["ki"] = RW["pos_i"]
        RW["m"] = RW["kf"]
        krsq = C([32, TG], BF16, 64); krsq_b = Buf()
        t1 = C([32, TG], F32, 64); t1_b = Buf()
        t2 = C([32, TG], F32, 64); t2_b = Buf()
        krf = C([32, TG], BF16, 64); krf_b = Buf()
        kst = [(C([128, TG], BF16), Buf()) for i in range(2)]
        ksq = [(C([128, TG], BF16), Buf()) for i in range(2)]
        v_sbs = [(C([128, 1024], BF16), Buf()) for i in range(2)]
        qsqs = [(C([96, TG], BF16), Buf()) for i in range(2)]
        rq96s = [(C([96, TG], F32), Buf()) for i in range(2)]
        sds = [(C([96, TG], F32), Buf()) for i in range(2)]
        t1s = [(t1, t1_b), (C([32, TG], F32, 64), Buf())]
        t2s = [(t2, t2_b), (C([32, TG], F32, 64), Buf())]
        qst = [(C([96, TG], BF16), Buf()) for i in range(2)]
        rsum = C([128, 4], F32); rsum_b = Buf()
        s1 = C([128, 4, 16], F32); s1_b = Buf()
        rstd, rstd_b = self.rstd, self.rstd_b
        kb.dma("pool", wa[:], wa_d.rearrange("(k p) n -> p k n", p=128), writes=[wa_b])
        kb.dma("pool", wq[:], wq_d.rearrange("(k p) n -> p k n", p=128), writes=[wq_b])
        kb.dma("pool", wkn[:], wkn_d.rearrange("(k p) n -> p k n", p=128), writes=[wkn_b])
        kb.dma("pool", wv[:], wv_d.rearrange("(k p) n -> p k n", p=128), writes=[wv_b])
        kb.dma("sp", mg[:], mg_d, writes=[mg_b])
        kb.dma("sp", ropec[64:96, :], self.ropec_d, writes=[ropec_b])
        kb.op("pool", lambda e: e.memset(sel2[:], 0.0), writes=[sel2_b])
        kb.op("pool", lambda e: e.memset(sel2[0:64, 0:1], 1.0), writes=[sel2_b])
        kb.op("pool", lambda e: e.memset(sel2[64:128, 1:2], 1.0), writes=[sel2_b])
        QT, KT, V, RK = snd["QT"], snd["KT"], snd["V"], snd["RK"]
        pb = self.pb
        p_ssq, p_ssq_b = pb[6]
        p_kr, p_kr_b = pb[3]
        p_krs, p_krs_b = pb[4]
        p_st, p_st_b = pb[5]
        out_toks = []
        qi = 0
        for g in range(NG):
            gs = slice(g * TG, (g + 1) * TG)
            self.rope_tables(ropec, ropec_b, self.pos_d, gs, cos2, sinS, cs_b, RW)

            def a_chunks(c0, n):
                for ci in range(n):
                    ps, psb = self.rot()
                    for k in range(KC):
                        kb.op("pe", lambda e, k=k, ci=ci, ps=ps, gs=gs, c0=c0: e.matmul(
                            ps[:], lhsT=wa[:, k, (c0 + ci) * 128:(c0 + ci + 1) * 128], rhs=h[:, k, gs], start=(k == 0), stop=(k == KC - 1)),
                            reads=[wa_b, self.hb[g]], writes=[psb], inc=(k == KC - 1))
                    kb.op("act", lambda e, ci=ci, ps=ps: e.activation(alat[:, ci, :], ps[:], AF.Copy), reads=[psb], writes=[alat_b])
                    kb.op("act", lambda e, ci=ci, ps=ps: e.activation(sqa[:, ci, :], ps[:], AF.Square), reads=[psb], writes=[sqa_b])
                for ci in range(n):
                    kb.op("pe", lambda e, ci=ci, n=n: e.matmul(p_ssq[:], lhsT=ones[:], rhs=sqa[:, ci, :], start=(ci == 0), stop=(ci == n - 1)),
                          reads=[self.ones_b, sqa_b], writes=[p_ssq_b], inc=(ci == n - 1))
                self.rstd_from_ps(p_ssq, p_ssq_b, 128, n * 128, rstd, rstd_b)
            a_chunks(0, 3)
            for ci in range(3):
                kb.op("dve", lambda e, ci=ci: e.scalar_tensor_tensor(qn[:, ci, :], alat[:, ci, :], mg[:, 5 + ci:6 + ci], rstd[:], op0=ALU.mult, op1=ALU.mult),
                      reads=[alat_b, mg_b, rstd_b], writes=[qn_b])
            if g == 0:
                self.dbg("h", h[:, :, 0:TG], self.hb, BF16)
                self.dbg("mods", self.mods[:], [self.mods_b])
                self.dbg("qn", qn[:], [qn_b], BF16)
                self.dbg("alat_q", alat[:], [alat_b])
            a_chunks(3, 2)
            for ci in range(2):
                kb.op("dve", lambda e, ci=ci: e.scalar_tensor_tensor(kvn[:, ci, :], alat[:, ci, :], mg[:, 3 + ci:4 + ci], rstd[:], op0=ALU.mult, op1=ALU.mult),
                      reads=[alat_b, mg_b, rstd_b], writes=[kvn_b])
            if g == 0:
                self.dbg("kvn", kvn[:], [kvn_b], BF16)
                self.dbg("cos2", cos2[:], [cs_b])
                self.dbg("sinS", sinS[:], [cs_b])
            for k in range(KC):
                kb.op("pe", lambda e, k=k, gs=gs: e.matmul(p_kr[64:96, :], lhsT=wa[:, k, 640:672], rhs=h[:, k, gs], start=(k == 0), stop=(k == KC - 1)),
                      reads=[wa_b, self.hb[g]], writes=[p_kr_b], inc=(k == KC - 1))
            for k in range(KC):
                kb.op("pe", lambda e, k=k, gs=gs: e.matmul(p_krs[64:96, :], lhsT=wa[:, k, 672:704], rhs=h[:, k, gs], start=(k == 0), stop=(k == KC - 1)),
                      reads=[wa_b, self.hb[g]], writes=[p_krs_b], inc=(k == KC - 1))
            kb.op("act", lambda e: e.activation(krsq[:], p_kr[64:96, :], AF.Square), reads=[p_kr_b], writes=[krsq_b])
            kb.op("dve", lambda e: e.scalar_tensor_tensor(t1[:], p_kr[64:96, :], mg[64:96, 1:2], cos2[:], op0=ALU.mult, op1=ALU.mult),
                  reads=[p_kr_b, mg_b, cs_b], writes=[t1_b])
            kb.op("dve", lambda e: e.scalar_tensor_tensor(t2[:], p_krs[64:96, :], mg[64:96, 2:3], sinS[:], op0=ALU.mult, op1=ALU.mult),
                  reads=[p_krs_b, mg_b, cs_b], writes=[t2_b])
            kb.op("pool", lambda e: e.tensor_tensor(krf[:], t1[:], t2[:], op=ALU.add), reads=[t1_b, t2_b], writes=[krf_b])
            KTv = KT.rearrange("(h p) t -> p h t", p=DH)
            out_toks.append(kb.dma("sp", KTv[64:96, :, gs], krf[:].unsqueeze(1).broadcast_to([32, NH, TG]), reads=[krf_b]))
            for tt in range(4):
                kb.op("pe", lambda e, tt=tt: e.matmul(p_st[:, 64 + tt:65 + tt], lhsT=krsq[:, tt * 128:(tt + 1) * 128], rhs=ones[64:96, 0:1],
                                                      start=True, stop=True),
                      reads=[krsq_b, self.ones_b], writes=[p_st_b], inc=(tt == 3))
            for j in range(8):
                ps, psb = self.rot()
                for kc in range(2):
                    kb.op("pe", lambda e, kc=kc, j=j, ps=ps: e.matmul(ps[:], lhsT=wkn[:, kc, j * 128:(j + 1) * 128], rhs=kvn[:, kc, :],
                                                                  start=(kc == 0), stop=(kc == 1)),
                          reads=[wkn_b, kvn_b], writes=[psb], inc=(kc == 1))
                kt_, ktb = kst[j % 2]
                kq_, kqb = ksq[j % 2]
                kb.op("act", lambda e, ps=ps, kt_=kt_: e.activation(kt_[:], ps[:], AF.Identity, scale=mg[:, 0:1]), reads=[psb, mg_b], writes=[ktb])
                kb.op("act", lambda e, ps=ps, kq_=kq_: e.activation(kq_[:], ps[:], AF.Square), reads=[psb], writes=[kqb])
                for hh in range(2):
                    hq = 2 * j + hh
                    out_toks.append(kb.dma("sp", KT[hq * DH:hq * DH + 64, gs], kt_[hh * 64:(hh + 1) * 64, :], reads=[ktb]))
                for tt in range(4):
                    kb.op("pe", lambda e, tt=tt, j=j, kq_=kq_: e.matmul(p_st[:, tt * 16 + 2 * j:tt * 16 + 2 * j + 2],
                                                                     lhsT=kq_[:, tt * 128:(tt + 1) * 128], rhs=sel2[:], start=True, stop=True),
                          reads=[kqb, sel2_b], writes=[p_st_b], inc=(tt == 3))
            kb.op("act", lambda e: e.activation(rsum[:], p_st[:, 64:68], AF.Copy), reads=[p_st_b], writes=[rsum_b])
            kb.op("dve", lambda e: e.tensor_tensor(s1[:], p_st[:, 0:64].rearrange("p (a b) -> p a b", b=16),
                                                   rsum[:].unsqueeze(2).broadcast_to([128, 4, 16]), op=ALU.add),
                  reads=[p_st_b, rsum_b], writes=[s1_b])
            kb.op("dve", lambda e: e.tensor_scalar(s1[:], s1[:], 1.0 / DH, EPS, op0=ALU.mult, op1=ALU.add), reads=[s1_b], writes=[s1_b])
            kb.op("act", lambda e: e.activation(s1[:], s1[:], AF.Sqrt), reads=[s1_b], writes=[s1_b])
            kb.op("dve", lambda e: e.reciprocal(s1[:], s1[:]), reads=[s1_b], writes=[s1_b])
            kb.op("dve", lambda e: e.tensor_scalar(s1[:], s1[:], SC96, None, op0=ALU.mult), reads=[s1_b], writes=[s1_b])
            out_toks.append(kb.dma("sp", RK[g * TG:(g + 1) * TG, :].rearrange("(a p) h -> p a h", p=128), s1[:], reads=[s1_b]))
            for tt in range(4):
                v_sb, v_b = v_sbs[tt % 2]
                for half in range(2):
                    ps, psb = self.rot()
                    for kc in range(2):
                        kb.op("pe", lambda e, kc=kc, tt=tt, half=half, ps=ps: e.matmul(
                            ps[:], lhsT=kvn[:, kc, tt * 128:(tt + 1) * 128], rhs=wv[:, kc, half * 512:(half + 1) * 512],
                            start=(kc == 0), stop=(kc == 1)),
                            reads=[wv_b, kvn_b], writes=[psb], inc=(kc == 1))
                    if half == 0:
                        kb.op("act", lambda e, ps=ps, v_sb=v_sb: e.activation(v_sb[:, 0:512], ps[:], AF.Copy), reads=[psb], writes=[v_b])
                    else:
                        kb.op("dve", lambda e, ps=ps, v_sb=v_sb: e.tensor_copy(v_sb[:, 512:1024], ps[:]), reads=[psb], writes=[v_b])
                r0 = g * TG + tt * 128
                out_toks.append(kb.dma("sp", V[r0:r0 + 128, :], v_sb[:], reads=[v_b]))
            for hq in range(NH):
                ps, psb = self.rot()
                pks, pks_b = pb[3 + hq % 2]
                pss, pss_b = pb[6 + hq % 2]
                qsq, qsq_b = qsqs[hq % 2]
                rq96, rq96_b = rq96s[hq % 2]
                t1q, t1q_b = t1s[hq % 2]
                t2q, t2q_b = t2s[hq % 2]
                for kc in range(3):
                    kb.op("pe", lambda e, kc=kc, hq=hq, ps=ps: e.matmul(ps[0:96, :], lhsT=wq[:, kc, hq * 128:hq * 128 + 96], rhs=qn[:, kc, :],
                                                                    start=(kc == 0), stop=(kc == 2)),
                          reads=[wq_b, qn_b], writes=[psb], inc=(kc == 2))
                for kc in range(3):
                    kb.op("pe", lambda e, kc=kc, hq=hq, pks=pks: e.matmul(pks[64:96, :], lhsT=wq[:, kc, hq * 128 + 96:hq * 128 + 128], rhs=qn[:, kc, :],
                                                                       start=(kc == 0), stop=(kc == 2)),
                          reads=[wq_b, qn_b], writes=[pks_b], inc=(kc == 2))
                kb.op("act", lambda e, ps=ps, qsq=qsq: e.activation(qsq[:], ps[0:96, :], AF.Square), reads=[psb], writes=[qsq_b])
                kb.op("pe", lambda e, pss=pss, qsq=qsq: e.matmul(pss[0:96, :], lhsT=ones[0:96, 0:96], rhs=qsq[:], start=True, stop=True),
                      reads=[self.ones_b, qsq_b], writes=[pss_b])
                self.rstd_from_ps(pss, pss_b, 96, DH, rq96, rq96_b, sds[hq % 2])
                qs_, qsb = qst[qi % 2]
                qi += 1
                kb.op("dve", lambda e, ps=ps, qs_=qs_, rq96=rq96: e.scalar_tensor_tensor(qs_[0:64, :], ps[0:64, :], mg[0:64, 8:9], rq96[0:64, :],
                                                                                       op0=ALU.mult, op1=ALU.mult),
                      reads=[psb, mg_b, rq96_b], writes=[qsb])
                kb.op("dve", lambda e, ps=ps, t1q=t1q: e.scalar_tensor_tensor(t1q[:], ps[64:96, :], mg[64:96, 8:9], cos2[:], op0=ALU.mult, op1=ALU.mult),
                      reads=[psb, mg_b, cs_b], writes=[t1q_b])
                kb.op("dve", lambda e, pks=pks, t2q=t2q: e.scalar_tensor_tensor(t2q[:], pks[64:96, :], mg[64:96, 9:10], sinS[:], op0=ALU.mult, op1=ALU.mult),
                      reads=[pks_b, mg_b, cs_b], writes=[t2q_b])
                kb.op("pool", lambda e, t1q=t1q, t2q=t2q: e.tensor_tensor(t1q[:], t1q[:], t2q[:], op=ALU.add), reads=[t1q_b, t2q_b], writes=[t1q_b])
                kb.op("pool", lambda e, qs_=qs_, t1q=t1q, rq96=rq96: e.tensor_tensor(qs_[64:96, :], t1q[:], rq96[64:96, :], op=ALU.mult),
                      reads=[t1q_b, rq96_b], writes=[qsb])
                out_toks.append(kb.dma("sp", QT[hq * DH:(hq + 1) * DH, gs], qs_[:], reads=[qsb]))
        return out_toks

    def mla_p2(self, tag, gat, snd_o):
        kb = self.kb
        nc = self.nc
        self.phase_begin()
        C = self.carve
        GQ, GK, GV, GRK = gat["QT"], gat["KT"], gat["V"], gat["RK"]
        rk_all = C([128, 32, 8], F32); rk_b = Buf()
        v_all = C([128, 32, 512], BF16); v_b = Buf()
        qT = [(C([96, S], BF16), Buf()) for i in range(2)]
        kT = [(C([96, S], BF16), Buf()) for i in range(2)]
        pt = [(C([128, TG], BF16), Buf()) for i in range(6)]
        ost = [(C([64, S], BF16), Buf()) for i in range(2)]
        rec = C([64, TG], F32); rec_b = Buf()
        ones = self.ones
        for s in range(2):
            rkr = GRK.rows(s, 0, T)
            self.dsel(rk_all[:, s * 16:(s + 1) * 16, :],
                      rkr[:, 0:8].rearrange("(a p) h -> p a h", p=128),
                      rkr[:, 8:16].rearrange("(a p) h -> p a h", p=128), writes=[rk_b])
            for a in range(2):
                vr = GV.rows(s, a * 1024, 1024)
                self.dsel(v_all[:, s * 16 + a * 8:s * 16 + (a + 1) * 8, :],
                          vr[:, 0:512].rearrange("(a p) n -> p a n", p=128),
                          vr[:, 512:1024].rearrange("(a p) n -> p a n", p=128), writes=[v_b])
        pb = self.pb
        out_toks = []
        LAG = 2
        NPT = len(pt)
        pend = []
        ti = 0
        gi = 0
        for hh in range(8):
            qt_, qb = qT[hh % 2]
            kt_, kbf = kT[hh % 2]
            for s in range(2):
                self.dsel(qt_[:, s * T:(s + 1) * T], GQ.rows(s, hh * DH, DH), GQ.rows(s, (8 + hh) * DH, DH), writes=[qb])
                self.dsel(kt_[:, s * T:(s + 1) * T], GK.rows(s, hh * DH, DH), GK.rows(s, (8 + hh) * DH, DH), writes=[kbf])
            os_, osb = ost[hh % 2]
            for gq in range(8):
                qs = slice(gq * TG, (gq + 1) * TG)
                po, pob = pb[4 + gi % 2]
                pd, pdb = pb[6 + gi % 2]
                gi += 1
                nkt = 4 * (gq + 1)
                for kt in range(nkt):
                    ps, psb = pb[ti % 4]
                    p_, p_b = pt[ti % NPT]
                    ti += 1
                    kb.op("pe", lambda e, kt=kt, ps=ps, kt_=kt_, qt_=qt_, qs=qs: e.matmul(
                        ps[:], lhsT=kt_[:, kt * 128:(kt + 1) * 128], rhs=qt_[:, qs], start=True, stop=True),
                        reads=[kbf, qb], writes=[psb])
                    kb.op("act", lambda e, kt=kt, hh=hh, ps=ps, p_=p_: e.activation(p_[:], ps[:], AF.Exp, scale=rk_all[:, kt, hh:hh + 1]),
                          reads=[psb, rk_b], writes=[p_b])
                    if kt >= 4 * gq:
                        base = gq * TG - kt * 128
                        kb.op("pool", lambda e, p_=p_, base=base: e.affine_select(
                            out=p_[:], in_=p_[:], pattern=[[1, TG]], compare_op=ALU.is_ge, fill=0.0, base=base, channel_multiplier=-1),
                            reads=[p_b], writes=[p_b])

                    def emit_pv(kt=kt, hh=hh, po=po, pob=pob, pd=pd, pdb=pdb, p_=p_, p_b=p_b, nkt=nkt, os_=os_, osb=osb, qs=qs):
                        kb.op("pe", lambda e: e.matmul(po[0:64, :], lhsT=v_all[:, kt, hh * 64:(hh + 1) * 64], rhs=p_[:],
                                                       start=(kt == 0), stop=(kt == nkt - 1)),
                              reads=[v_b, p_b], writes=[pob], inc=(kt == nkt - 1))
                        kb.op("pe", lambda e: e.matmul(pd[0:64, :], lhsT=ones[:, 0:64], rhs=p_[:], start=(kt == 0), stop=(kt == nkt - 1)),
                              reads=[self.ones_b, p_b], writes=[pdb], inc=True)
                        if kt == nkt - 1:
                            kb.op("dve", lambda e: e.reciprocal(rec[:], pd[0:64, :]), reads=[pdb], writes=[rec_b])
                            kb.op("dve", lambda e: e.tensor_tensor(os_[:, qs], po[0:64, :], rec[:], op=ALU.mult),
                                  reads=[pob, rec_b], writes=[osb])
                            if qs.stop == S:
                                out_toks.append(kb.dma("sp", snd_o[hh * 64:(hh + 1) * 64, :], os_[:], reads=[osb]))
                    pend.append(emit_pv)
                    if len(pend) > LAG:
                        pend.pop(0)()
        while pend:
            pend.pop(0)()
        return out_toks

    def mixer_p3(self, tag, gat_o, nk, recompute_mods, wname, gate_scale=1.0):
        kb = self.kb
        nc = self.nc
        self.phase_begin()
        if recompute_mods:
            self.emit_mods(tag, gate_scale)
        C = self.carve
        wo_d = self.din(tag + wname, [nk * 128, D])
        wo = C([128, nk, D], BF16); wo_b = Buf()
        kb.dma("pool", wo[:], wo_d.rearrange("(k p) n -> p k n", p=128), writes=[wo_b])
        osb = [(C([128, nk, TG], BF16), Buf()) for i in range(2)]
        x, hg = self.x, self.hg
        for g in range(NG):
            gs = slice(g * TG, (g + 1) * TG)
            o_, ob = osb[g % 2]
            R = gat_o.R
            for s_ in range(2):
                for kk in range(gat_o.nrows // R):
                    rr_ = gat_o.rows(s_, kk * R, R)
                    c0_ = (s_ * gat_o.nrows + kk * R) // 128
                    self.dsel(o_[:, c0_:c0_ + R // 128, :], rr_[:, g * TG:(g + 1) * TG].rearrange("(k p) t -> p k t", p=128),
                              rr_[:, T + g * TG:T + (g + 1) * TG].rearrange("(k p) t -> p k t", p=128), writes=[ob])
            for m in range(KC):
                py, pyb = self.pb[m % 2]
                for k in range(nk):
                    kb.op("pe", lambda e, m=m, k=k, py=py, o_=o_: e.matmul(py[:], lhsT=wo[:, k, m * 128:(m + 1) * 128], rhs=o_[:, k, :],
                                                                       start=(k == 0), stop=(k == nk - 1)),
                          reads=[wo_b, ob], writes=[pyb], inc=(k == nk - 1))
                kb.op("dve", lambda e, m=m, gs=gs, py=py: e.scalar_tensor_tensor(
                    x[:, m, gs], py[:], hg[:, m:m + 1], x[:, m, gs], op0=ALU.mult, op1=ALU.add),
                    reads=[pyb, self.hg_b, self.xb[m][g]], writes=[self.xb[m][g]])

    def ssd_p1(self, tag, snd):
        kb = self.kb
        self.phase_begin()
        self.emit_mods(tag, 1.0)
        self.norm_scratch()
        self.emit_h()
        h = self.h
        win_d = self.din(tag + "_win", [D, 5152])
        wv = win_d.rearrange("(k p) n -> p k n", p=128)
        C = self.carve
        wblk = [(C([128, KC, 512], BF16), Buf()) for i in range(2)]
        wdt = C([128, KC, 32], BF16); wdt_b = Buf()
        stg = [(C([128, TG], BF16), Buf()) for i in range(4)]
        dts = C([128, 16, 32], F32); dts_b = Buf()
        XBC, Z, DT = snd["XBC"], snd["Z"], snd["DT"]
        out_toks = []
        kb.dma("pool", wdt[:], wv[:, :, 5120:5152], writes=[wdt_b])
        bi = 0
        si = 0
        for blk in range(6):
            wt, wb = wblk[bi % 2]; bi += 1
            kb.dma("pool", wt[:], wv[:, :, 2048 + blk * 512:2048 + (blk + 1) * 512], writes=[wb])
            for cc in range(4):
                ch = blk * 4 + cc
                for g in range(NG):
                    gs = slice(g * TG, (g + 1) * TG)
                    ps, psb = self.rot(0, 4)
                    for k in range(KC):
                        kb.op("pe", lambda e, k=k, cc=cc, ps=ps, wt=wt, gs=gs: e.matmul(
                            ps[:], lhsT=wt[:, k, cc * 128:(cc + 1) * 128], rhs=h[:, k, gs], start=(k == 0), stop=(k == KC - 1)),
                            reads=[wb, self.hb[g]], writes=[psb], inc=(k == KC - 1))
                    st, stb = stg[si % 4]; si += 1
                    if si % 2 == 0:
                        kb.op("act", lambda e, ps=ps, st=st: e.activation(st[:], ps[:], AF.Copy), reads=[psb], writes=[stb])
                    else:
                        kb.op("dve", lambda e, ps=ps, st=st: e.tensor_copy(st[:], ps[:]), reads=[psb], writes=[stb])
                    out_toks.append(kb.dma("sp", XBC[ch * 128:(ch + 1) * 128, gs], st[:], reads=[stb]))
        for blk in range(4):
            wt, wb = wblk[bi % 2]; bi += 1
            kb.dma("pool", wt[:], wv[:, :, blk * 512:(blk + 1) * 512], writes=[wb])
            for tt in range(16):
                ps, psb = self.rot(0, 4)
                for k in range(KC):
                    kb.op("pe", lambda e, k=k, tt=tt, ps=ps, wt=wt: e.matmul(
                        ps[:], lhsT=h[:, k, tt * 128:(tt + 1) * 128], rhs=wt[:, k, :], start=(k == 0), stop=(k == KC - 1)),
                        reads=[wb, self.hb[tt // 4]], writes=[psb], inc=(k == KC - 1))
                st, stb = stg[si % 4]; si += 1
                if si % 2 == 0:
                    kb.op("act", lambda e, ps=ps, st=st: e.activation(st[:], ps[:], AF.Copy), reads=[psb], writes=[stb])
                else:
                    kb.op("dve", lambda e, ps=ps, st=st: e.tensor_copy(st[:], ps[:]), reads=[psb], writes=[stb])
                out_toks.append(kb.dma("sp", Z[tt * 128:(tt + 1) * 128, blk * 512:(blk + 1) * 512], st[:], reads=[stb]))
        pdt, pdt_b = self.pb[4]
        for tt in range(16):
            for k in range(KC):
                kb.op("pe", lambda e, k=k, tt=tt: e.matmul(pdt[:, tt * 32:(tt + 1) * 32], lhsT=h[:, k, tt * 128:(tt + 1) * 128], rhs=wdt[:, k, :],
                                                        start=(k == 0), stop=(k == KC - 1)),
                      reads=[wdt_b, self.hb[tt // 4]], writes=[pdt_b], inc=(k == KC - 1))
        kb.op("dve", lambda e: e.tensor_copy(dts[:], pdt[:].rearrange("p (a b) -> p a b", b=32)), reads=[pdt_b], writes=[dts_b])
        out_toks.append(kb.dma("sp", DT.rearrange("(a p) h -> p a h", p=128), dts[:], reads=[dts_b]))
        return out_toks

    def ssd_p2(self, tag, gat, snd_g):
        kb = self.kb
        nc = self.nc
        self.phase_begin()
        C = self.carve
        GX, GZ, GDT = gat["XBC"], gat["Z"], gat["DT"]
        cw_d = self.din(tag + "_cw", [128, 48])
        cb_d = self.din(tag + "_cb", [128, 12])
        cbrow_d = self.din(tag + "_cbrow", [1, 1280])
        hp_d = self.din(tag + "_hp", [128, 48])
        ng_d = self.din(tag + "_ng", [128, 1024])
        cw = C([128, 48], F32); cw_b = Buf()
        cb = C([128, 12], F32); cb_b = Buf()
        cbrow = C([1, 1280], F32); cbrow_b = Buf()
        cbhi = C([1, 1280], BF16); cblo = C([1, 1280], BF16); cbf = C([1, 1280], F32); cbhl_b = Buf()
        hp = C([128, 48], F32); hp_b = Buf()
        ng = C([128, 1024], F32); ng_b = Buf()
        identb = C([128, 128], BF16); Tb = C([128, 128], BF16)
        U = C([128, 128], F32); Tm = C([128, 128], F32); cst_b = Buf()
        diag = C([128, 48, 128], BF16); diag_b = Buf()
        Aneg = C([128, 16], F32); Aneg_b = Buf()
        u = C([128, 12, TG + 4], BF16); u_b = Buf()
        zt = C([128, 4, 1024], BF16); zt_b = Buf()
        dtr = C([128, 4, 16], F32); dtr_b = Buf()
        xs = C([128, 4, 1024], F32); xs_b = Buf()
        Btok = C([128, 4, 256], BF16); Btok_b = Buf()
        BT = C([128, 2, TG], BF16); CT = C([128, 2, TG], BF16); bct_b = Buf()
        dtv = C([128, 4, 16], F32); av = C([128, 4, 16], F32); dtv_b = Buf()
        acum = C([128, 16], F32); ea = C([128, 16], F32); dte = C([128, 16], F32); cd = C([128, 16], F32); sm_b = Buf()
        aU = C([128, 16, 128], F32); aU_b = Buf()
        dec = [(C([128, 8, 128], BF16), Buf()) for i in range(2)]
        cbm = [(C([128, 128], BF16), Buf()) for i in range(2)]
        MT = [(C([128, 8, 128], BF16), Buf()) for i in range(2)]
        xdt = C([128, 1024], BF16); xdt_b = Buf()
        Bdec = C([128, 16, 128], BF16); Bdec_b = Buf()
        Sf = C([128, 1024], F32); Sf_b = Buf()
        Sb = C([128, 1024], BF16); Sb_b = Buf()
        t1 = C([128, 1024], F32); t1_b = Buf()
        t3 = C([128, 1024], F32); t3_b = Buf()
        yv = C([128, 1024], F32); yv_b = Buf()
        sz = t3; sz_b = t3_b
        ssq = C([128, 2], F32); ssq_b = Buf()
        junk = C([128, 512], BF16); junk_b = Buf()
        gn = C([128, 1024], BF16); gn_b = Buf()
        gT = [(C([128, 8, TG], BF16), Buf()) for i in range(1)]
        ones, onesf = self.ones, self.onesf
        pb = self.pb
        kb.dma("sp", cw[:], cw_d, writes=[cw_b])
        kb.dma("sp", cb[:], cb_d, writes=[cb_b])
        kb.dma("sp", cbrow[:], cbrow_d, writes=[cbrow_b])
        kb.dma("sp", hp[:], hp_d, writes=[hp_b])
        kb.dma("sp", ng[:], ng_d, writes=[ng_b])
        kb.op("dve", lambda e: e.tensor_copy(cbhi[:], cbrow[:]), reads=[cbrow_b], writes=[cbhl_b])
        kb.op("dve", lambda e: e.tensor_copy(cbf[:], cbhi[:]), reads=[cbhl_b], writes=[cbhl_b])
        kb.op("dve", lambda e: e.tensor_tensor(cbf[:], cbrow[:], cbf[:], op=ALU.subtract), reads=[cbhl_b, cbrow_b], writes=[cbhl_b])
        kb.op("dve", lambda e: e.tensor_copy(cblo[:], cbf[:]), reads=[cbhl_b], writes=[cbhl_b])
        for (tile_, pat, base, cm, cmp_) in ((Tm, [[1, 128]], 0, -1, ALU.is_ge), (U, [[-1, 128]], -1, 1, ALU.is_ge)):
            kb.op("pool", lambda e, tile_=tile_: e.memset(tile_[:], 1.0), writes=[cst_b])
            kb.op("pool", lambda e, tile_=tile_, pat=pat, base=base, cm=cm, cmp_=cmp_: e.affine_select(
                out=tile_[:], in_=tile_[:], pattern=pat, compare_op=cmp_, fill=0.0, base=base, channel_multiplier=cm),
                reads=[cst_b], writes=[cst_b])
        kb.op("pool", lambda e: e.memset(identb[:], 1.0), writes=[cst_b])
        kb.op("pool", lambda e: e.affine_select(out=identb[:], in_=identb[:], pattern=[[-1, 128]], compare_op=ALU.is_equal, fill=0.0,
                                                base=0, channel_multiplier=1), reads=[cst_b], writes=[cst_b])
        kb.op("dve", lambda e: e.tensor_copy(Tb[:], Tm[:]), reads=[cst_b], writes=[cst_b])
        for c in range(12):
            for j in range(4):
                kb.op("dve" if (c + j) % 2 else "pool", lambda e, c=c, j=j: e.tensor_scalar(
                    diag[:, c * 4 + j, :], identb[:], cw[:, c * 4 + j:c * 4 + j + 1], None, op0=ALU.mult),
                    reads=[cst_b, cw_b], writes=[diag_b])
        kb.op("act", lambda e: e.activation(Aneg[:], hp[:, 16:32], AF.Exp), reads=[hp_b], writes=[Aneg_b])
        kb.op("dve", lambda e: e.tensor_scalar(Aneg[:], Aneg[:], -1.0, None, op0=ALU.mult), reads=[Aneg_b], writes=[Aneg_b])
        kb.op("pool", lambda e: e.memset(Sf[:], 0.0), writes=[Sf_b])
        kb.op("pool", lambda e: e.memset(u[:, :, 0:4], 0.0), writes=[u_b])
        out_toks = []
        for G in range(8):
            s = G // 4
            gl = G % 4
            t0 = gl * TG
            if G > 0:
                kb.op("dve", lambda e: e.tensor_copy(u[:, :, 0:4], u[:, :, TG:TG + 4]), reads=[u_b], writes=[u_b])
            for (c0, nch, r0, r1) in ((0, 4, 0, 1024), (4, 4, 512, 1536), (8, 2, 2048, 2048 + 256), (10, 2, 2560, 2560 + 256)):
                self.dsel(u[:, c0:c0 + nch, 4:4 + TG],
                          GX.rows(s, r0, nch * 128)[:, t0:t0 + TG].rearrange("(c p) t -> p c t", p=128),
                          GX.rows(s, r1, nch * 128)[:, t0:t0 + TG].rearrange("(c p) t -> p c t", p=128), writes=[u_b])
            zr = GZ.rows(s, t0, TG)
            self.dsel(zt[:], zr[:, 0:1024].rearrange("(a p) n -> p a n", p=128),
                      zr[:, 1024:2048].rearrange("(a p) n -> p a n", p=128), writes=[zt_b])
            dr = GDT.rows(s, t0, TG)
            self.dsel(dtr[:], dr[:, 0:16].rearrange("(a p) h -> p a h", p=128),
                      dr[:, 16:32].rearrange("(a p) h -> p a h", p=128), writes=[dtr_b])
            for c in range(8, 12):
                ps, psb = pb[7]
                for j in range(4):
                    kb.op("pe", lambda e, c=c, j=j, ps=ps: e.matmul(ps[:], lhsT=diag[:, c * 4 + j, :], rhs=u[:, c, 1 + j:1 + j + TG],
                                                              start=(j == 0), stop=(j == 3)),
                          reads=[diag_b, u_b], writes=[psb], inc=(j == 3))
                dst = BT if c < 10 else CT
                kb.op("act", lambda e, c=c, ps=ps, dst=dst: e.activation(dst[:, c % 2, :], ps[:], AF.Silu, bias=cb[:, c:c + 1]),
                      reads=[psb, cb_b], writes=[bct_b])
            for tt in range(4):
                for half in range(2):
                    ps, psb = pb[4 + half]
                    for cc in range(4):
                        c = half * 4 + cc
                        for j in range(4):
                            kb.op("pe", lambda e, c=c, cc=cc, j=j, tt=tt, ps=ps: e.matmul(
                                ps[:, cc * 128:(cc + 1) * 128], lhsT=u[:, c, 1 + j + tt * 128:1 + j + tt * 128 + 128], rhs=diag[:, c * 4 + j, :],
                                start=(cc == 0 and j == 0), stop=False, skip_group_check=True),
                                reads=[diag_b, u_b], writes=[psb], inc=False)
                    kb.op("pe", lambda e, half=half, ps=ps: e.matmul(ps[:], lhsT=ones[0:1, 0:128], rhs=cbhi[0:1, half * 512:(half + 1) * 512],
                                                                  start=False, stop=False, skip_group_check=True),
                          reads=[self.ones_b, cbhl_b], writes=[psb], inc=False)
                    kb.op("pe", lambda e, half=half, ps=ps: e.matmul(ps[:], lhsT=ones[0:1, 0:128], rhs=cblo[0:1, half * 512:(half + 1) * 512],
                                                                  start=False, stop=True, skip_group_check=True),
                          reads=[self.ones_b, cbhl_b], writes=[psb], inc=True)
                    kb.op("act", lambda e, half=half, tt=tt, ps=ps: e.activation(xs[:, tt, half * 512:(half + 1) * 512], ps[:], AF.Silu),
                          reads=[psb], writes=[xs_b])
                ps, psb = pb[7]
                for cc in range(2):
                    c = 8 + cc
                    for j in range(4):
                        kb.op("pe", lambda e, c=c, cc=cc, j=j, tt=tt, ps=ps: e.matmul(
                            ps[:, cc * 128:(cc + 1) * 128], lhsT=u[:, c, 1 + j + tt * 128:1 + j + tt * 128 + 128], rhs=diag[:, c * 4 + j, :],
                            start=(cc == 0 and j == 0), stop=False, skip_group_check=True),
                            reads=[diag_b, u_b], writes=[psb], inc=False)
                kb.op("pe", lambda e, ps=ps: e.matmul(ps[:, 0:256], lhsT=ones[0:1, 0:128], rhs=cbhi[0:1, 1024:1280], start=False, stop=False,
                                                      skip_group_check=True), reads=[self.ones_b, cbhl_b], writes=[psb], inc=False)
                kb.op("pe", lambda e, ps=ps: e.matmul(ps[:, 0:256], lhsT=ones[0:1, 0:128], rhs=cblo[0:1, 1024:1280], start=False, stop=True,
                                                      skip_group_check=True), reads=[self.ones_b, cbhl_b], writes=[psb], inc=True)
                kb.op("act", lambda e, tt=tt, ps=ps: e.activation(Btok[:, tt, :], ps[:, 0:256], AF.Silu), reads=[psb], writes=[Btok_b])
            kb.op("act", lambda e: e.activation(zt[:], zt[:], AF.Silu), reads=[zt_b], writes=[zt_b])
            kb.op("dve", lambda e: e.tensor_tensor(dtv[:], dtr[:], hp[:, 0:16].unsqueeze(1).broadcast_to([128, 4, 16]), op=ALU.add),
                  reads=[dtr_b, hp_b], writes=[dtv_b])
            kb.op("act", lambda e: e.activation(dtv[:], dtv[:], AF.Exp), reads=[dtv_b], writes=[dtv_b])
            kb.op("act", lambda e: e.activation(dtv[:], dtv[:], AF.Ln, bias=1.0), reads=[dtv_b], writes=[dtv_b])
            kb.op("dve", lambda e: e.tensor_tensor(av[:], dtv[:], Aneg[:].unsqueeze(1).broadcast_to([128, 4, 16]), op=ALU.mult),
                  reads=[dtv_b, Aneg_b], writes=[dtv_b])
            gt_, gtb = gT[0]
            for tt in range(4):
                ts_ = slice(tt * 128, (tt + 1) * 128)
                pst, pstb = pb[6]
                kb.op("pe", lambda e, tt=tt: e.matmul(pst[:, 0:16], lhsT=Tm[:], rhs=av[:, tt, :], start=True, stop=True),
                      reads=[cst_b, dtv_b], writes=[pstb])
                kb.op("pe", lambda e, tt=tt: e.matmul(pst[:, 16:32], lhsT=onesf[:], rhs=av[:, tt, :], start=True, stop=True),
                      reads=[self.onesf_b, dtv_b], writes=[pstb])
                kb.op("act", lambda e: e.activation(ea[:], pst[:, 0:16], AF.Exp), reads=[pstb], writes=[sm_b])
                kb.op("act", lambda e: e.activation(cd[:], pst[:, 16:32], AF.Exp), reads=[pstb], writes=[sm_b])
                kb.op("act", lambda e: e.activation(acum[:], pst[:, 0:16], AF.Copy), reads=[pstb], writes=[sm_b])
                kb.op("dve", lambda e: e.tensor_tensor(dte[:], pst[:, 16:32], acum[:], op=ALU.subtract), reads=[pstb, sm_b], writes=[sm_b])
                kb.op("act", lambda e: e.activation(dte[:], dte[:], AF.Exp), reads=[sm_b], writes=[sm_b])
                kb.op("dve", lambda e, tt=tt: e.tensor_tensor(xdt[:].rearrange("p (h d) -> p h d", d=64), xs[:, tt, :].rearrange("p (h d) -> p h d", d=64),
                                                            dtv[:, tt, :].unsqueeze(2).broadcast_to([128, 16, 64]), op=ALU.mult),
                      reads=[xs_b, dtv_b], writes=[xdt_b])
                kb.op("pool", lambda e, tt=tt: e.tensor_tensor(aU[:], U[:].unsqueeze(1).broadcast_to([128, 16, 128]),
                                                             av[:, tt, :].unsqueeze(2).broadcast_to([128, 16, 128]), op=ALU.mult),
                      reads=[cst_b, dtv_b], writes=[aU_b])
                for gg in range(2):
                    kb.op("pool", lambda e, tt=tt, gg=gg: e.tensor_tensor(
                        Bdec[:, gg * 8:(gg + 1) * 8, :], Btok[:, tt, gg * 128:(gg + 1) * 128].unsqueeze(1).broadcast_to([128, 8, 128]),
                        dte[:, gg * 8:(gg + 1) * 8].unsqueeze(2).broadcast_to([128, 8, 128]), op=ALU.mult),
                        reads=[Btok_b, sm_b], writes=[Bdec_b])
                kb.op("dve", lambda e: e.tensor_copy(Sb[:], Sf[:]), reads=[Sf_b], writes=[Sb_b])
                py0, py0b = pb[2]
                py1, py1b = pb[3]
                pys = [(py0, py0b), (py1, py1b)]
                for gg in range(2):
                    cm_, cmb = cbm[gg]
                    kb.op("pe", lambda e, gg=gg, ts_=ts_: e.matmul(pst[:, 32 + gg * 128:32 + (gg + 1) * 128], lhsT=BT[:, gg, ts_], rhs=CT[:, gg, ts_],
                                                                start=True, stop=True), reads=[bct_b], writes=[pstb])
                    kb.op("dve", lambda e, gg=gg, cm_=cm_: e.tensor_tensor(cm_[:], pst[:, 32 + gg * 128:32 + (gg + 1) * 128], Tb[:], op=ALU.mult),
                          reads=[pstb, cst_b], writes=[cmb])
                    for half in range(2):
                        ps, psb = pb[half]
                        for hh in range(4):
                            hd = gg * 8 + half * 4 + hh
                            kb.op("pe", lambda e, hd=hd, hh=hh, ps=ps: e.matmul(ps[:, hh * 128:(hh + 1) * 128], lhsT=aU[:, hd, :], rhs=Tm[:],
                                                                             start=True, stop=True),
                                  reads=[aU_b, cst_b], writes=[psb])
                    dc, dcb = dec[gg]
                    for half in range(2):
                        ps, psb = pb[half]
                        kb.op("act", lambda e, half=half, ps=ps, dc=dc: e.activation(
                            dc[:, half * 4:(half + 1) * 4, :], ps[:].rearrange("p (a b) -> p a b", b=128), AF.Exp), reads=[psb], writes=[dcb])
                    mt, mtb = MT[gg]
                    kb.op("dve" if gg == 0 else "pool", lambda e, mt=mt, dc=dc, cm_=cm_: e.tensor_tensor(
                        mt[:], dc[:], cm_[:].unsqueeze(1).broadcast_to([128, 8, 128]), op=ALU.mult), reads=[dcb, cmb], writes=[mtb])
                    py, pyb = pys[gg]
                    for hh in range(8):
                        hd = gg * 8 + hh
                        kb.op("pe", lambda e, hd=hd, hh=hh, py=py, mt=mt: e.matmul(py[:, hh * 64:(hh + 1) * 64], lhsT=mt[:, hh, :],
                                                                              rhs=xdt[:, hd * 64:(hd + 1) * 64], start=True, stop=True),
                              reads=[mtb, xdt_b], writes=[pyb], inc=(hh == 7))
                for gg in range(2):
                    ps, psb = pb[gg]
                    kb.op("pe", lambda e, gg=gg, ps=ps, ts_=ts_: e.matmul(ps[:], lhsT=CT[:, gg, ts_], rhs=Sb[:, gg * 512:(gg + 1) * 512], start=True, stop=True),
                          reads=[bct_b, Sb_b], writes=[psb])
                    kb.op("dve", lambda e, gg=gg, ps=ps: e.tensor_tensor(
                        t1[:, gg * 512:(gg + 1) * 512].rearrange("p (h d) -> p h d", d=64), ps[:].rearrange("p (h d) -> p h d", d=64),
                        ea[:, gg * 8:(gg + 1) * 8].unsqueeze(2).broadcast_to([128, 8, 64]), op=ALU.mult),
                        reads=[psb, sm_b], writes=[t1_b])
                kb.op("pool", lambda e, tt=tt: e.tensor_tensor(t3[:].rearrange("p (h d) -> p h d", d=64), xs[:, tt, :].rearrange("p (h d) -> p h d", d=64),
                                                             hp[:, 32:48].unsqueeze(2).broadcast_to([128, 16, 64]), op=ALU.mult),
                      reads=[xs_b, hp_b], writes=[t3_b])
                kb.op("pool", lambda e: e.tensor_tensor(t1[:], t1[:], t3[:], op=ALU.add), reads=[t1_b, t3_b], writes=[t1_b])
                for gg in range(2):
                    py, pyb = pys[gg]
                    kb.op("dve", lambda e, gg=gg, py=py: e.tensor_tensor(yv[:, gg * 512:(gg + 1) * 512], py[:], t1[:, gg * 512:(gg + 1) * 512], op=ALU.add),
                          reads=[pyb, t1_b], writes=[yv_b])
                kb.op("dve", lambda e, tt=tt: e.tensor_tensor(yv[:], yv[:], zt[:, tt, :], op=ALU.mult), reads=[yv_b, zt_b], writes=[yv_b])
                for gg in range(2):
                    kb.op("act", lambda e, gg=gg: e.activation(junk[:], yv[:, gg * 512:(gg + 1) * 512], AF.Square, accum_out=ssq[:, gg:gg + 1]),
                          reads=[yv_b], writes=[junk_b, ssq_b])
                kb.op("dve", lambda e: e.tensor_scalar(ssq[:], ssq[:], 1.0 / 512, EPS, op0=ALU.mult, op1=ALU.add), reads=[ssq_b], writes=[ssq_b])
                kb.op("act", lambda e: e.activation(ssq[:], ssq[:], AF.Ln), reads=[ssq_b], writes=[ssq_b])
                kb.op("act", lambda e: e.activation(ssq[:], ssq[:], AF.Exp, scale=-0.5), reads=[ssq_b], writes=[ssq_b])
                for gg in range(2):
                    kb.op("dve", lambda e, gg=gg: e.scalar_tensor_tensor(gn[:, gg * 512:(gg + 1) * 512], yv[:, gg * 512:(gg + 1) * 512], ssq[:, gg:gg + 1],
                                                                       ng[:, gg * 512:(gg + 1) * 512], op0=ALU.mult, op1=ALU.mult),
                          reads=[yv_b, ssq_b, ng_b], writes=[gn_b])
                for half in range(2):
                    ps, psb = pb[4 + half]
                    for fc in range(4):
                        f = half * 4 + fc
                        kb.op("pe", lambda e, f=f, fc=fc, ps=ps: e.matmul(ps[:, fc * 128:(fc + 1) * 128], lhsT=gn[:, f * 128:(f + 1) * 128], rhs=identb[:],
                                                                       start=True, stop=True), reads=[gn_b, cst_b], writes=[psb], inc=(fc == 3))
                    kb.op("act" if half == 0 else "dve", (lambda e, half=half, ps=ps, gt_=gt_, ts_=ts_: e.activation(
                        gt_[:, half * 4:(half + 1) * 4, ts_], ps[:].rearrange("p (a b) -> p a b", b=128), AF.Copy)) if half == 0 else
                        (lambda e, half=half, ps=ps, gt_=gt_, ts_=ts_: e.tensor_copy(gt_[:, half * 4:(half + 1) * 4, ts_], ps[:].rearrange("p (a b) -> p a b", b=128))),
                        reads=[psb], writes=[gtb])
                for gg in range(2):
                    ps, psb = pb[gg]
                    for hh in range(8):
                        hd = gg * 8 + hh
                        kb.op("pe", lambda e, hd=hd, hh=hh, ps=ps: e.matmul(ps[:, hh * 64:(hh + 1) * 64], lhsT=Bdec[:, hd, :], rhs=xdt[:, hd * 64:(hd + 1) * 64],
                                                                         start=True, stop=True), reads=[Bdec_b, xdt_b], writes=[psb], inc=(hh == 7))
                kb.op("pool", lambda e: e.tensor_tensor(Sf[:].rearrange("p (h d) -> p h d", d=64), Sf[:].rearrange("p (h d) -> p h d", d=64),
                                                        cd[:].unsqueeze(2).broadcast_to([128, 16, 64]), op=ALU.mult), reads=[Sf_b, sm_b, Sb_b], writes=[Sf_b])
                for gg in range(2):
                    ps, psb = pb[gg]
                    kb.op("dve", lambda e, gg=gg, ps=ps: e.tensor_tensor(Sf[:, gg * 512:(gg + 1) * 512], Sf[:, gg * 512:(gg + 1) * 512], ps[:], op=ALU.add),
                          reads=[psb, Sf_b], writes=[Sf_b])
            col0 = s * T + t0
            out_toks.append(kb.dma("sp", snd_g[:, col0:col0 + TG].rearrange("(c p) t -> p c t", p=128), gt_[:], reads=[gtb]))
        return out_toks


D = 1024; T = 2048; S = 4096; DFF = 2816


def fm(v):
    v = np.asarray(v, np.float32)
    return np.ascontiguousarray(v.reshape(-1, 128).T)


def ropec():
    inv = (1.0 / (10000.0 ** (np.arange(0, 32, 2, dtype=np.float32) / 32))).astype(np.float32)
    c = np.zeros((32, 2), np.float32)
    c[:16, 0] = inv; c[16:, 0] = inv
    c[:16, 1] = -1.0; c[16:, 1] = 1.0
    return c


def prep_mods(I, i, sub, tag):
    return {tag + "_adaw": np.ascontiguousarray(I["ada_w"][i][:, sub * 3072:(sub + 1) * 3072]),
            tag + "_adab": fm(I["ada_b"][i][sub * 3072:(sub + 1) * 3072]),
            tag + "_gain": fm(I["norm_gain"][i, sub])}


def prep_ffn(I, i, which, tag):
    d = prep_mods(I, i, 0 if which == 0 else 2, tag)
    d[tag + "_wgu"] = I["ffn_w_gu"][i, which]
    d[tag + "_wdn"] = I["ffn_w_down"][i, which]
    return d


def prep_mla(I, i, tag):
    j = i // 2
    d = prep_mods(I, i, 1, tag)
    wa = I["mla_w_a"][j]
    kr = wa[:, 640:672]
    d[tag + "_wa"] = np.ascontiguousarray(np.concatenate([wa, kr[:, 16:], kr[:, :16]], 1))
    wqb = I["mla_w_qb"][j].reshape(384, 16, 96)
    nope, rp = wqb[:, :, :64], wqb[:, :, 64:]
    wq = np.concatenate([nope, rp, rp[:, :, 16:], rp[:, :, :16]], 2)
    d[tag + "_wq"] = np.ascontiguousarray(wq.reshape(384, 2048))
    wkv = I["mla_w_kvb"][j].reshape(256, 16, 128)
    d[tag + "_wkn"] = np.ascontiguousarray(wkv[:, :, :64].reshape(256, 1024))
    d[tag + "_wv"] = np.ascontiguousarray(wkv[:, :, 64:].reshape(256, 1024))
    mg = np.zeros((128, 12), np.float32)
    gk = I["mla_k_gain"][j]; gq = I["mla_q_gain"][j]
    mg[:64, 0] = gk[:64]; mg[64:, 0] = gk[:64]
    mg[64:96, 1] = gk[64:]
    mg[64:80, 2] = gk[80:]; mg[80:96, 2] = gk[64:80]
    mg[:, 3:5] = fm(I["mla_kv_a_gain"][j])
    mg[:, 5:8] = fm(I["mla_q_a_gain"][j])
    mg[:96, 8] = gq
    mg[64:80, 9] = gq[80:]; mg[80:96, 9] = gq[64:80]
    d[tag + "_mg"] = mg
    d[tag + "_wo"] = I["mla_w_o"][j]
    return d


def prep_ssd(I, i, tag):
    j = i // 2
    d = prep_mods(I, i, 1, tag)
    d[tag + "_win"] = I["ssd_w_in"][j]
    d[tag + "_wout"] = I["ssd_w_out"][j]
    return d


def prep_ssd_rank(I, i, tag, r):
    j = i // 2
    cwf = I["ssd_conv_w"][j]
    cbf = I["ssd_conv_b"][j]
    chans = np.concatenate([np.arange(1024 * r, 1024 * r + 1024), 2048 + 256 * r + np.arange(256), 2560 + 256 * r + np.arange(256)])
    cw = cwf[:, chans].reshape(4, 12, 128).transpose(2, 1, 0).reshape(128, 48)
    cb = cbf[chans].reshape(12, 128).T
    cbrow = cbf[chans[:1280]][None, :]
    hp = np.concatenate([I["ssd_dt_bias"][j][16 * r:16 * r + 16], I["ssd_a_log"][j][16 * r:16 * r + 16], I["ssd_d"][j][16 * r:16 * r + 16]])
    hp = np.broadcast_to(hp[None, :], (128, 48))
    ng = np.broadcast_to(I["ssd_norm_gain"][j][1024 * r:1024 * r + 1024][None, :], (128, 1024))
    f = lambda a: np.ascontiguousarray(a, dtype=np.float32)
    return {tag + "_cw": f(cw), tag + "_cb": f(cb), tag + "_cbrow": f(cbrow), tag + "_hp": f(hp), tag + "_ng": f(ng)}


import ml_dtypes
_BF = ml_dtypes.bfloat16
_PROGS = {}


def _build(kind):
    if kind in _PROGS:
        return _PROGS[kind]
    ph, mixer = kind
    P = Prog(None)
    P.setup()
    toks = []
    if ph == "A":
        xT = P.din("xT", [D, T]); P.load_x(xT)
        P.ffn("F0")
        yT = P.dout("yT", [D, T])
        toks += P.store_x(yT)
        if mixer == "mla":
            snd = {"QT": P.dout("QT", [16 * 96, T], BF16), "KT": P.dout("KT", [16 * 96, T], BF16),
                   "V": P.dout("V", [T, 1024], BF16), "RK": P.dout("RK", [T, 16], F32)}
            toks += P.mla_p1("M", snd)
        else:
            snd = {"XBC": P.dout("XBC", [3072, T], BF16), "Z": P.dout("Z", [T, 2048], BF16), "DT": P.dout("DT", [T, 32], F32)}
            toks += P.ssd_p1("M", snd)
    elif ph == "B":
        if mixer == "mla":
            g_ap = {"QT": GBuf(P.din("gQT", [2 * 16 * 96, T], BF16), 1536, 1536), "KT": GBuf(P.din("gKT", [2 * 16 * 96, T], BF16), 1536, 1536),
                    "V": GBuf(P.din("gV", [2 * T, 1024], BF16), T, T), "RK": GBuf(P.din("gRK", [2 * T, 16], F32), T, T)}
            snd_o = P.dout("O", [512, S], BF16)
            toks += P.mla_p2("M", g_ap, snd_o)
        else:
            g_ap = {"XBC": GBuf(P.din("gXBC", [2 * 3072, T], BF16), 3072, 3072), "Z": GBuf(P.din("gZ", [2 * T, 2048], BF16), T, T),
                    "DT": GBuf(P.din("gDT", [2 * T, 32], F32), T, T)}
            snd_g = P.dout("O", [1024, S], BF16)
            toks += P.ssd_p2("M", g_ap, snd_g)
    else:
        xT = P.din("xT", [D, T]); P.load_x(xT)
        if mixer == "mla":
            gO = GBuf(P.din("gO", [1024, S], BF16), 512, 512)
            P.mixer_p3("M", gO, 8, True, "_wo")
        else:
            gO = GBuf(P.din("gO", [2048, S], BF16), 1024, 1024)
            P.mixer_p3("M", gO, 16, True, "_wout")
        P.ffn("F1")
        yT = P.dout("yT", [D, T])
        toks += P.store_x(yT)
    P.kb.wait_all("sp", toks)
    P.kb.emit_all()
    _PROGS[kind] = P
    return P


def _launch(P, cands):
    ims = []
    for c in range(8):
        d = {}
        for n in P.inputs:
            for src in cands[c]:
                if n in src:
                    d[n] = src[n]
                    break
            else:
                raise KeyError(n)
        ims.append(d)
    res = run_bass_kernel_spmd(P.nc, ims, core_ids=list(range(8)))
    return res.results


def kernel(**I):
    I = {k: np.asarray(v) for k, v in I.items()}
    x = I["x"].astype(np.float32)
    B = x.shape[0]
    rc = ropec()
    core_common = []
    for c in range(8):
        b, r = c // 2, c % 2
        core_common.append({"cT": fm(I["c"][b]), "ropec": rc,
                            "pos": np.ascontiguousarray(I["positions"][b][None, r * T:(r + 1) * T]).astype(np.int32)})
    xs = [np.ascontiguousarray(x[c // 2, (c % 2) * T:(c % 2 + 1) * T].T) for c in range(8)]
    for i in range(4):
        mixer = "mla" if i % 2 == 0 else "ssd"
        W0 = prep_ffn(I, i, 0, "F0")
        W1 = prep_ffn(I, i, 1, "F1")
        WM = prep_mla(I, i, "M") if mixer == "mla" else prep_ssd(I, i, "M")
        WR = [prep_ssd_rank(I, i, "M", r) for r in range(2)] if mixer == "ssd" else [{}, {}]
        P = _build(("A", mixer))
        res = _launch(P, [[{"xT": xs[c]}, core_common[c], W0, WM] for c in range(8)])
        xs = [res[c]["yT"] for c in range(8)]
        names = ["QT", "KT", "V", "RK"] if mixer == "mla" else ["XBC", "Z", "DT"]
        gat = []
        for pr in range(4):
            g = {"g" + n: np.concatenate([res[2 * pr][n], res[2 * pr + 1][n]], 0) for n in names}
            gat += [g, g]
        del res
        P = _build(("B", mixer))
        res = _launch(P, [[gat[c], core_common[c], WM, WR[c % 2]] for c in range(8)])
        gO = []
        for pr in range(4):
            g = {"gO": np.concatenate([res[2 * pr]["O"], res[2 * pr + 1]["O"]], 0)}
            gO += [g, g]
        del res, gat
        P = _build(("C", mixer))
        res = _launch(P, [[{"xT": xs[c]}, gO[c], core_common[c], W1, WM] for c in range(8)])
        xs = [res[c]["yT"] for c in range(8)]
        del res
    out = np.empty_like(x)
    for c in range(8):
        out[c // 2, (c % 2) * T:(c % 2 + 1) * T] = xs[c].T
    return out


def _build_fused():
    if "fused" in _PROGS:
        return _PROGS["fused"]
    P = Prog(None)
    P.setup()
    kb = P.kb
    xT = P.din("xT", [D, T]); P.load_x(xT)
    P.setup_global_mods()
    for i in range(4):
        mixer = "mla" if i % 2 == 0 else "ssd"
        L = "L%d" % i
        P.ffn(L + "F0")
        if mixer == "mla":
            shapes = {"QT": ([16 * 96, T], BF16, 384), "KT": ([16 * 96, T], BF16, 384), "V": ([T, 1024], BF16, 1024), "RK": ([T, 16], F32, T)}
        else:
            shapes = {"XBC": ([3072, T], BF16, 512), "Z": ([T, 2048], BF16, 512), "DT": ([T, 32], F32, T)}
        snd = {n: P.dint(L + "s" + n, shp, dt) for n, (shp, dt, R) in shapes.items()}
        gat = {n: GBuf(P.dint(L + "g" + n, [2 * shp[0], shp[1]], dt), shp[0], R, snd[n]) for n, (shp, dt, R) in shapes.items()}
        toks = P.mla_p1(L + "M", snd) if mixer == "mla" else P.ssd_p1(L + "M", snd)
        for n in shapes:
            gat[n].gather(kb, toks, GROUPS)
        if mixer == "mla":
            snd_o = P.dint(L + "sO", [512, S], BF16)
            gat_o = GBuf(P.dint(L + "gO", [1024, S], BF16), 512, 256, snd_o)
            toks = P.mla_p2(L + "M", gat, snd_o)
        else:
            snd_o = P.dint(L + "sO", [1024, S], BF16)
            gat_o = GBuf(P.dint(L + "gO", [2048, S], BF16), 1024, 256, snd_o)
            toks = P.ssd_p2(L + "M", gat, snd_o)
        gat_o.gather(kb, toks, GROUPS)
        if mixer == "mla":
            P.mixer_p3(L + "M", gat_o, 8, False, "_wo")
        else:
            P.mixer_p3(L + "M", gat_o, 16, False, "_wout")
        P.ffn(L + "F1")
    yT = P.dout("yT", [D, T])
    toks = P.store_x(yT)
    kb.wait_all("sp", toks)
    kb.emit_all()
    _PROGS["fused"] = P
    return P


def kernel_fused(**I):
    I = {k: np.asarray(v) for k, v in I.items()}
    x = I["x"].astype(np.float32)
    rc = ropec()
    P = _build_fused()
    Wall = {}
    WR = [{}, {}]
    for i in range(4):
        L = "L%d" % i
        Wall.update(prep_ffn(I, i, 0, L + "F0"))
        Wall.update(prep_ffn(I, i, 1, L + "F1"))
        if i % 2 == 0:
            Wall.update(prep_mla(I, i, L + "M"))
        else:
            Wall.update(prep_ssd(I, i, L + "M"))
            for r in range(2):
                WR[r].update(prep_ssd_rank(I, i, L + "M", r))
    c4T = np.ascontiguousarray(np.stack([fm(I["c"][b_]) for b_ in range(4)], axis=2).reshape(128, 32))
    adab_all = np.ascontiguousarray(np.concatenate([fm(I["ada_b"][l]) for l in range(4)], axis=1))
    gain_all = np.ascontiguousarray(np.concatenate([fm(I["norm_gain"][l, s_]) for l in range(4) for s_ in range(3)], axis=1))
    gm = {"c4T": c4T, "adab_all": adab_all, "gain_all": gain_all}
    cands = []
    for c in range(8):
        b, r = c // 2, c % 2
        es = np.zeros((128, 4), np.float32); es[:, b] = 1.0
        cc = {"cT": fm(I["c"][b]), "ropec": rc, "esel": es,
              "adawc": np.ascontiguousarray(I["ada_w"][:, :, c * 1152:(c + 1) * 1152].reshape(4 * D, 1152)), "pos": np.ascontiguousarray(I["positions"][b][None, r * T:(r + 1) * T]).astype(np.int32),
              "xT": np.ascontiguousarray(x[b, r * T:(r + 1) * T].T)}
        cands.append([cc, gm, Wall, WR[r]])
    res = _launch(P, cands)
    out = np.empty_like(x)
    for c in range(8):
        out[c // 2, (c % 2) * T:(c % 2 + 1) * T] = res[c]["yT"].T
    return out


kernel_multi = kernel
kernel = kernel_fused
```

```python
import numpy as np
from contextlib import ExitStack
import concourse.bass as bass
import concourse.mybir as mybir
from concourse.bass_utils import run_bass_kernel_spmd


F32 = mybir.dt.float32
BF16 = mybir.dt.bfloat16
I32 = mybir.dt.int32
AF = mybir.ActivationFunctionType
ALU = mybir.AluOpType
AX = mybir.AxisListType

ENGS = ("pe", "act", "dve", "pool", "sp")


class Buf:
    __slots__ = ("w", "r", "name")

    def __init__(self, name=""):
        self.w = None
        self.r = []
        self.name = name


class KB:
    def __init__(self, nc, ctx):
        self.nc = nc
        self.ctx = ctx
        self.q = {e: [] for e in ENGS}
        self.cnt = {}
        self.semh = {}
        self.seen = {e: {} for e in ENGS}
        self.pe_pending = []
        for e in ENGS:
            self.new_sem("E_" + e)
        self.ndma = 0
        self.NPOOL = 64
        self.buf_key = {}
        self.pool_i = 0

    def phase_reset(self):
        self.buf_key = {}
        self.pool_i = 0

    def _dma_key(self, reads, writes):
        prim = (list(writes) + list(reads))[0]
        k = self.buf_key.get(id(prim))
        if k is None:
            assert self.pool_i < self.NPOOL, "DMA semaphore pool exhausted in this phase"
            k = "DS%d" % self.pool_i
            self.pool_i += 1
            self.buf_key[id(prim)] = k
        return k

    def new_sem(self, key):
        self.semh[key] = self.ctx.enter_context(self.nc.semaphore(key))
        self.cnt[key] = 0
        return key

    def sb(self, name, shape, dtype):
        return self.ctx.enter_context(self.nc.sbuf_tensor(name, list(shape), dtype))

    def ps(self, name, shape, dtype=F32):
        return self.ctx.enter_context(self.nc.psum_tensor(name, list(shape), dtype))

    def _waits(self, eng, toks):
        ws = []
        best = {}
        for t in toks:
            if t is None:
                continue
            s, v = t
            if self.seen[eng].get(s, 0) >= v:
                continue
            if best.get(s, 0) < v:
                best[s] = v
        for s, v in best.items():
            self.seen[eng][s] = v
            ws.append((s, v))
        return ws

    def op(self, eng, fn, reads=(), writes=(), inc=True, extra=()):
        toks = list(extra)
        for b in reads:
            toks.append(b.w)
        for b in writes:
            toks.append(b.w)
            toks.extend(b.r)
        if eng == "pe":
            toks = [t for t in toks if t is not None and t[0] != "E_pe"]
        ws = self._waits(eng, toks)
        key = "E_" + eng
        if eng == "pe" and not inc:
            self.pe_pending.append((tuple(reads), tuple(writes)))
            tok = None
        else:
            self.cnt[key] += 1
            tok = (key, self.cnt[key])
            allrw = [(tuple(reads), tuple(writes))]
            if eng == "pe":
                allrw += self.pe_pending
                self.pe_pending = []
            for rs, wr in allrw:
                for b in rs:
                    b.r.append(tok)
                for b in wr:
                    b.w = tok
                    b.r = []
        semh = self.semh

        def emit(e, fn=fn, ws=ws, tok=tok):
            for s, v in ws:
                e.wait_ge(semh[s], v)
            ins = fn(e)
            if tok is not None:
                ins.then_inc(semh[tok[0]], 1)
        self.q[eng].append(emit)
        return tok

    def dma(self, eng, out, in_, reads=(), writes=(), extra=(), key=None, **kw):
        toks = list(extra)
        for b in reads:
            toks.append(b.w)
        for b in writes:
            toks.append(b.w)
            toks.extend(b.r)
        ws = self._waits(eng, toks)
        key = self._dma_key(reads, writes)
        if key not in self.semh:
            self.new_sem(key)
        self.cnt[key] += 16
        tok = (key, self.cnt[key])
        for b in reads:
            b.r.append(tok)
        for b in writes:
            b.w = tok
            b.r = []
        semh = self.semh

        def emit(e, ws=ws, tok=tok, out=out, in_=in_, kw=kw):
            for s, v in ws:
                e.wait_ge(semh[s], v)
            try:
                e.dma_start(out=out, in_=in_, **kw).then_inc(semh[tok[0]], 16)
            except Exception:
                print("DMA FAIL out", out.shape, out.ap, "in", in_.shape, in_.ap, flush=True)
                raise
        self.q[eng].append(emit)
        return tok

    def wait_all(self, eng, toks):
        ws = self._waits(eng, toks)
        semh = self.semh

        def emit(e, ws=ws):
            for s, v in ws:
                e.wait_ge(semh[s], v)
        self.q[eng].append(emit)

    def emit_all(self):
        nc = self.nc
        q = self.q
        with nc.Block() as block:
            @block.sync
            def _(e):
                for f in q["sp"]:
                    f(e)

            @block.tensor
            def _(e):
                for f in q["pe"]:
                    f(e)

            @block.scalar
            def _(e):
                for f in q["act"]:
                    f(e)

            @block.vector
            def _(e):
                for f in q["dve"]:
                    f(e)

            @block.gpsimd
            def _(e):
                for f in q["pool"]:
                    f(e)


CC_INC = 1


def _kb_cc(self, kind, ins, outs, groups, reads=(), writes=(), extra=()):
    toks = list(extra)
    for b in reads:
        toks.append(b.w)
    for b in writes:
        toks.append(b.w)
        toks.extend(b.r)
    ws = self._waits("pool", toks)
    key = "CC"
    if key not in self.semh:
        self.new_sem(key)
    self.cnt[key] += CC_INC
    tok = (key, self.cnt[key])
    for b in reads:
        b.r.append(tok)
    for b in writes:
        b.w = tok
        b.r = []
    semh = self.semh

    def emit(e, ws=ws, tok=tok):
        for s, v in ws:
            e.wait_ge(semh[s], v)
        e.collective_compute(kind, ALU.bypass, replica_groups=groups, ins=ins, outs=outs).then_inc(semh[tok[0]], CC_INC)
    self.q["pool"].append(emit)
    return tok


KB.cc = _kb_cc


def _kb_dma_if(self, eng, cond, out, in_true, in_false, reads=(), writes=(), extra=()):
    toks = list(extra)
    for b in reads:
        toks.append(b.w)
    for b in writes:
        toks.append(b.w)
        toks.extend(b.r)
    ws = self._waits(eng, toks)
    key = self._dma_key(reads, writes)
    if key not in self.semh:
        self.new_sem(key)
    self.cnt[key] += 16
    tok = (key, self.cnt[key])
    for b in reads:
        b.r.append(tok)
    for b in writes:
        b.w = tok
        b.r = []
    semh = self.semh

    def emit(e, ws=ws, tok=tok):
        for s, v in ws:
            e.wait_ge(semh[s], v)
        with e.If(cond):
            e.dma_start(out=out, in_=in_true).then_inc(semh[tok[0]], 16)
        with e.Else():
            e.dma_start(out=out, in_=in_false).then_inc(semh[tok[0]], 16)
    self.q[eng].append(emit)
    return tok


KB.dma_if = _kb_dma_if


D = 1024
KC = 8
T = 2048
S = 4096
TG = 512
NG = T // TG
DFF = 2816
EPS = 1e-6
NH = 16
DH = 96
QL = 384
KVL = 256
SC96 = 96 ** -0.5
GROUPS = [[0, 1], [2, 3], [4, 5], [6, 7]]


class GBuf:
    def __init__(self, gat_ap, rows, R, snd_ap=None):
        self.gat, self.nrows, self.R, self.snd = gat_ap, rows, R, snd_ap

    def rows(self, s, q0, n):
        R = self.R
        k = q0 // R
        assert (q0 + n - 1) // R == k, (q0, n, R)
        base = k * 2 * R + s * R + q0 % R
        return self.gat[base:base + n, :]

    def gather_chunk(self, kb, k, toks, groups):
        R = self.R
        self.done = getattr(self, "done", set())
        self.done.add(k)
        kb.cc("AllGather", [self.snd[k * R:(k + 1) * R, :]], [self.gat[k * 2 * R:(k + 1) * 2 * R, :]], groups, extra=toks)

    def gather(self, kb, toks, groups):
        R = self.R
        for k in range(self.nrows // R):
            if k in getattr(self, "done", set()):
                continue
            kb.cc("AllGather", [self.snd[k * R:(k + 1) * R, :]], [self.gat[k * 2 * R:(k + 1) * 2 * R, :]], groups, extra=toks)


class Prog:
    def __init__(self, steps, name="p"):
        self.steps = steps
        self.nc = bass.Bass("TRN2", target_bir_lowering=False)
        self.ctx = ExitStack()
        self.kb = KB(self.nc, self.ctx)
        self.debug = False
        self.dbg_toks = []
        self.inputs = {}
        self.outputs = {}
        self.rot_i = 0

    def din(self, name, shape, dt=F32):
        self.inputs[name] = (tuple(shape), dt)
        return self.nc.dram_tensor(name, list(shape), dt, kind="ExternalInput").ap()

    def dout(self, name, shape, dt=F32):
        self.outputs[name] = (tuple(shape), dt)
        return self.nc.dram_tensor(name, list(shape), dt, kind="ExternalOutput").ap()

    def dint(self, name, shape, dt=F32):
        return self.nc.dram_tensor(name, list(shape), dt).ap()

    def setup(self):
        kb = self.kb
        self.x = kb.sb("x", [128, KC, T], F32)
        self.xb = [[Buf() for g in range(NG)] for k in range(KC)]
        self.SCR = 35328
        self.scratch = kb.sb("scratch", [128, self.SCR], F32)
        self.scr_off = 0
        self.ones = kb.sb("ones", [128, 128], BF16); self.ones_b = Buf()
        self.onesf = kb.sb("onesf", [128, 128], F32); self.onesf_b = Buf()
        kb.op("pool", lambda e: e.memset(self.ones[:], 1.0), writes=[self.ones_b])
        kb.op("pool", lambda e: e.memset(self.onesf[:], 1.0), writes=[self.onesf_b])
        self.c_sb = kb.sb("c_sb", [128, KC], F32); self.c_b = Buf()
        self.sc = kb.sb("sc", [128, KC], F32); self.sc_b = Buf()
        cT = self.din("cT", [128, KC])
        kb.dma("sp", self.c_sb[:], cT, writes=[self.c_b])
        kb.op("act", lambda e: e.activation(self.sc[:], self.c_sb[:], AF.Silu), reads=[self.c_b], writes=[self.sc_b])
        self.pb = [(kb.ps("pb%d" % i, [128, 512]), Buf()) for i in range(8)]
        self.mods = kb.sb("mods", [128, 24], F32); self.mods_b = Buf()
        self.A = kb.sb("A", [128, KC], F32); self.A_b = Buf()
        self.hg = kb.sb("hg", [128, KC], F32); self.hg_b = Buf()
        self.gain_sb = kb.sb("gain_sb", [128, KC], F32); self.gain_b = Buf()
        self.adab_sb = kb.sb("adab_sb", [128, 24], F32); self.adab_b = Buf()
        self.aw_i = 0

    def carve(self, shape, dt, p0=0):
        n = int(np.prod(shape[1:]))
        words = n if dt == F32 or dt == I32 else (n + 1) // 2
        assert self.scr_off + words <= self.SCR, ("scratch overflow", self.scr_off, words)
        v = self.scratch[:, self.scr_off:self.scr_off + words]
        self.scr_off += words
        if dt != F32:
            v = v.bitcast(dt)
        if len(shape) == 3:
            v = v.rearrange("p (a b) -> p a b", b=shape[2])
        elif len(shape) == 4:
            v = v.rearrange("p (a b c) -> p a b c", b=shape[2], c=shape[3])
        return v[p0:p0 + shape[0]] if shape[0] < 128 else v

    def phase_begin(self):
        self.barrier()
        self.kb.phase_reset()
        self.scr_off = 0

    def norm_scratch(self):
        self.h = self.carve([128, KC, T], BF16)
        self.hb = [Buf() for g in range(NG)]
        self.sq = self.carve([128, KC, TG], BF16); self.sq_b = Buf()
        self.sd = self.carve([128, TG], F32); self.sd_b = Buf()
        self.rstd = self.carve([128, TG], F32); self.rstd_b = Buf()
        self.tmp = [(self.carve([128, TG], F32), Buf()) for i in range(2)]

    def dbg(self, name, ap, bufs, dt=F32):
        if not getattr(self, "debug", False):
            return
        shp = list(ap.shape)
        d = self.dout("dbg_" + name, shp, dt)
        self.dbg_toks.append(self.kb.dma("sp", d, ap, reads=bufs))

    def dsel(self, out, in0, in1, reads=(), writes=()):
        if not hasattr(self, "cond0"):
            rr = self.nc.sync.partition_id() % 2
            self.cond0 = (rr == 0)
        return self.kb.dma_if("sp", self.cond0, out, in0, in1, reads=reads, writes=writes)

    def rot(self, lo=0, n=2):
        i = lo + (self.rot_i % n)
        self.rot_i += 1
        return self.pb[i]

    def load_x(self, xT):
        for k in range(KC):
            self.kb.dma("sp", self.x[:, k, :], xT[k * 128:(k + 1) * 128, :], writes=self.xb[k])

    def store_x(self, yT):
        toks = []
        for k in range(KC):
            toks.append(self.kb.dma("sp", yT[k * 128:(k + 1) * 128, :], self.x[:, k, :], reads=self.xb[k]))
        return toks

    def barrier(self):
        kb = self.kb
        toks = [(k, v) for k, v in kb.cnt.items() if v > 0]
        for e in ENGS:
            kb.wait_all(e, toks)

    def setup_global_mods(self):
        kb = self.kb
        self.global_mods = True
        self.phase_begin()
        C = self.carve
        adawc = self.din("adawc", [4 * D, 1152])
        c4T_d = self.din("c4T", [128, 32])
        adab_d = self.din("adab_all", [128, 288])
        gain_d = self.din("gain_all", [128, 96])
        esel_d = self.din("esel", [128, 4])
        self.shift_all = kb.sb("shift_all", [128, 96], F32)
        self.A_all = kb.sb("A_all", [128, 96], F32)
        self.hg_all = kb.sb("hg_all", [128, 96], F32)
        self.gm_b = Buf()
        shift_all, A_all, hg_all = self.shift_all, self.A_all, self.hg_all
        c4 = C([128, 8, 4], F32); c4_b = Buf()
        sc4 = C([128, 8, 4], F32); sc4_b = Buf()
        adab = C([128, 288], F32); adab_b = Buf()
        gains = C([128, 96], F32); gains_b = Buf()
        esel = C([128, 4], F32); esel_b = Buf()
        gsc = C([128, 96], F32); gsc_b = Buf()
        aw = [(C([128, KC, 384], F32), Buf()) for i in range(2)]
        mp = C([128, 4, 36], F32); mp_b = Buf()
        gsb = C([128, 32, 36], F32); gsb_b = Buf()
        acc = C([128, 8, 36], F32); acc_b = Buf()
        mfull = C([128, 4, 72], F32); mfull_b = Buf()
        kb.dma("sp", c4[:], c4T_d.rearrange("p (k b) -> p k b", b=4), writes=[c4_b])
        kb.dma("sp", adab[:], adab_d, writes=[adab_b])
        kb.dma("sp", gains[:], gain_d, writes=[gains_b])
        kb.dma("sp", esel[:], esel_d, writes=[esel_b])
        kb.op("act", lambda e: e.activation(sc4[:], c4[:], AF.Silu), reads=[c4_b], writes=[sc4_b])
        ps, psb = self.pb[7]
        wv = adawc.rearrange("(l k p) n -> p l k n", l=4, p=128)
        bi = 0
        for l in range(4):
            for blk in range(3):
                wt, wb = aw[bi % 2]; bi += 1
                kb.dma("sp", wt[:], wv[:, l, :, blk * 384:(blk + 1) * 384], writes=[wb])
                for j3 in range(3):
                    col = (l * 9 + blk * 3 + j3) * 4
                    for k in range(KC):
                        kb.op("pe", lambda e, col=col, k=k, j3=j3, wt=wt: e.matmul(
                            ps[:, col:col + 4], lhsT=wt[:, k, j3 * 128:(j3 + 1) * 128], rhs=sc4[:, k, :], start=(k == 0), stop=(k == KC - 1)),
                            reads=[wb, sc4_b], writes=[psb], inc=(k == KC - 1 and j3 == 2))
        kb.op("dve", lambda e: e.tensor_copy(mp[:], ps[:, 0:144].rearrange("p (a b) -> p b a", b=4)), reads=[psb], writes=[mp_b])
        snd = self.dint("gm_snd", [512, 36], F32)
        gat = self.dint("gm_gat", [8 * 512, 36], F32)
        t0 = kb.dma("sp", snd.rearrange("(b p) a -> p b a", p=128), mp[:], reads=[mp_b])
        kb.cc("AllGather", [snd], [gat], [[0, 1, 2, 3, 4, 5, 6, 7]], extra=[t0])
        cct = ("CC", kb.cnt["CC"])
        kb.dma("sp", gsb[:], gat.rearrange("(cb p) a -> p cb a", p=128), writes=[gsb_b], extra=[cct])
        g4 = gsb[:].rearrange("p (c b) a -> p c b a", b=4)
        kb.op("dve", lambda e: e.tensor_scalar(acc[:], g4[:, :, 0, :], esel[:, 0:1], None, op0=ALU.mult), reads=[gsb_b, esel_b], writes=[acc_b])
        for b_ in range(1, 4):
            kb.op("dve", lambda e, b_=b_: e.scalar_tensor_tensor(acc[:], g4[:, :, b_, :], esel[:, b_:b_ + 1], acc[:], op0=ALU.mult, op1=ALU.add),
                  reads=[gsb_b, esel_b, acc_b], writes=[acc_b])
        kb.op("dve", lambda e: e.tensor_tensor(mfull[:].rearrange("p l (c j) -> p l c j", j=9), acc[:].rearrange("p c (l j) -> p l c j", j=9),
                                               adab[:].rearrange("p (l c j) -> p l c j", l=4, j=9), op=ALU.add),
              reads=[acc_b, adab_b], writes=[mfull_b])
        m4 = mfull[:].rearrange("p l (s t) -> p l s t", t=24)
        v96 = lambda t_: t_[:].rearrange("p (l s k) -> p l s k", l=4, s=3)
        kb.op("dve", lambda e: e.tensor_copy(v96(shift_all), m4[:, :, :, 0:8]), reads=[mfull_b], writes=[self.gm_b])
        kb.op("dve", lambda e: e.tensor_scalar(v96(A_all), m4[:, :, :, 8:16], 1.0, None, op0=ALU.add), reads=[mfull_b], writes=[self.gm_b])
        kb.op("dve", lambda e: e.tensor_tensor(A_all[:], A_all[:], gains[:], op=ALU.mult), reads=[self.gm_b, gains_b], writes=[self.gm_b])
        kb.op("pool", lambda e: e.memset(gsc[:], 0.5), writes=[gsc_b])
        kb.op("pool", lambda e: e.memset(v96(gsc)[:, :, 1, :], 1.0), reads=[gsc_b], writes=[gsc_b])
        kb.op("dve", lambda e: e.tensor_tensor(v96(hg_all), m4[:, :, :, 16:24], v96(gsc), op=ALU.mult), reads=[mfull_b, gsc_b], writes=[self.gm_b])

    def emit_mods(self, tag, gate_scale):
        kb = self.kb
        if getattr(self, "global_mods", False):
            idx = int(tag[1]) * 3 + {"F0": 0, "M": 1, "F1": 2}[tag[2:]]
            self.mods = self.shift_all[:, idx * 8:(idx + 1) * 8]
            self.A = self.A_all[:, idx * 8:(idx + 1) * 8]
            self.hg = self.hg_all[:, idx * 8:(idx + 1) * 8]
            self.mods_b = self.A_b = self.hg_b = self.gm_b
            return
        save_off = self.scr_off
        self.aw = [(self.carve([128, KC, 256], F32), Buf()) for i in range(2)]
        adaw = self.din(tag + "_adaw", [D, 3 * D])
        adab = self.din(tag + "_adab", [128, 24])
        gain = self.din(tag + "_gain", [128, KC])
        kb.dma("sp", self.gain_sb[:], gain, writes=[self.gain_b])
        kb.dma("sp", self.adab_sb[:], adab, writes=[self.adab_b])
        ps_mod, ps_mod_b = self.pb[7]
        wv = adaw.rearrange("(k p) n -> p k n", p=128)
        sc = self.sc
        for blk in range(12):
            wt, wb = self.aw[self.aw_i % 2]
            slot = self.aw_i % 2
            self.aw_i += 1
            kb.dma("sp", wt[:], wv[:, :, blk * 256:(blk + 1) * 256], writes=[wb], key="aw%d" % slot)
            for j in range(2):
                col = blk * 2 + j
                for k in range(KC):
                    kb.op("pe", lambda e, col=col, k=k, j=j, wt=wt: e.matmul(
                        ps_mod[:, col:col + 1], lhsT=wt[:, k, j * 128:(j + 1) * 128], rhs=sc[:, k:k + 1],
                        start=(k == 0), stop=(k == KC - 1)),
                        reads=[wb, self.sc_b], writes=[ps_mod_b], inc=(k == KC - 1 and j == 1))
        mods, A, hg = self.mods, self.A, self.hg
        kb.op("dve", lambda e: e.tensor_tensor(mods[:], ps_mod[:, 0:24], self.adab_sb[:], op=ALU.add),
              reads=[ps_mod_b, self.adab_b], writes=[self.mods_b])
        kb.op("dve", lambda e: e.scalar_tensor_tensor(A[:], mods[:, 8:16], 1.0, self.gain_sb[:], op0=ALU.add, op1=ALU.mult),
              reads=[self.mods_b, self.gain_b], writes=[self.A_b])
        kb.op("dve", lambda e: e.tensor_scalar(hg[:], mods[:, 16:24], gate_scale, None, op0=ALU.mult),
              reads=[self.mods_b], writes=[self.hg_b])
        self.barrier()
        self.scr_off = save_off

    def rstd_from_ps(self, ps, psb, npart, dim, out, outb, sdt=None):
        kb = self.kb
        sd, sd_b = sdt if sdt is not None else (self.sd, self.sd_b)
        kb.op("dve", lambda e: e.tensor_scalar(sd[0:npart, :], ps[0:npart, :], 1.0 / dim, EPS, op0=ALU.mult, op1=ALU.add),
              reads=[psb], writes=[sd_b])
        kb.op("act", lambda e: e.activation(sd[0:npart, :], sd[0:npart, :], AF.Sqrt), reads=[sd_b], writes=[sd_b])
        kb.op("dve", lambda e: e.reciprocal(out[0:npart, :], sd[0:npart, :]), reads=[sd_b], writes=[outb])

    def emit_h(self, groups=None):
        kb = self.kb
        x, h, sq, ones = self.x, self.h, self.sq, self.ones
        rstd, A, mods = self.rstd, self.A, self.mods
        ps_ssq, ps_ssq_b = self.pb[6]
        for g in (range(NG) if groups is None else groups):
            gs = slice(g * TG, (g + 1) * TG)
            kb.op("act", lambda e, gs=gs: e.activation(sq[:], x[:, :, gs], AF.Square),
                  reads=[self.xb[k][g] for k in range(KC)], writes=[self.sq_b])
            for k in range(KC):
                kb.op("pe", lambda e, k=k: e.matmul(ps_ssq[:], lhsT=ones[:], rhs=sq[:, k, :], start=(k == 0), stop=(k == KC - 1)),
                      reads=[self.ones_b, self.sq_b], writes=[ps_ssq_b], inc=(k == KC - 1))
            self.rstd_from_ps(ps_ssq, ps_ssq_b, 128, D, self.rstd, self.rstd_b)
            for k in range(KC):
                tt, tb = self.tmp[k % 2]
                kb.op("dve", lambda e, k=k, gs=gs, tt=tt: e.scalar_tensor_tensor(
                    tt[:], x[:, k, gs], A[:, k:k + 1], rstd[:], op0=ALU.mult, op1=ALU.mult),
                    reads=[self.xb[k][g], self.A_b, self.rstd_b], writes=[tb])
                kb.op("act", lambda e, k=k, gs=gs, tt=tt: e.activation(
                    h[:, k, gs], tt[:], AF.Identity, bias=mods[:, k:k + 1], scale=1.0),
                    reads=[tb, self.mods_b], writes=[self.hb[g]])

    def ffn(self, tag):
        kb = self.kb
        self.phase_begin()
        self.emit_mods(tag, 0.5)
        self.norm_scratch()
        self.emit_h([0])
        wgu = self.din(tag + "_wgu", [D, 2 * DFF])
        wdn = self.din(tag + "_wdn", [DFF, D])
        x, h, hg = self.x, self.h, self.hg
        if True:
            sbl = lambda n, s, d: self.carve(s, d)
            blocks = [(0, 6), (6, 6), (12, 5), (17, 5)]
            NBMAX = 6
            wg = [(sbl(tag + "wg%d" % i, [128, KC, NBMAX * 128], BF16), Buf()) for i in range(2)]
            wu = [(sbl(tag + "wu%d" % i, [128, KC, NBMAX * 128], BF16), Buf()) for i in range(2)]
            wd = [(sbl(tag + "wd%d" % i, [128, NBMAX, D], BF16), Buf()) for i in range(2)]
            act = [(sbl(tag + "act%d" % i, [128, NBMAX, TG], BF16), Buf()) for i in range(2)]
            sg = [(sbl(tag + "sg%d" % i, [128, TG], F32), Buf()) for i in range(2)]
            wguv = wgu.rearrange("(k p) n -> p k n", p=128)
            wdnv = wdn.rearrange("(j p) n -> p j n", p=128)
            it = 0
            ci = 0
            for bi, (j0, nb) in enumerate(blocks):
                s = bi % 2
                wgt, wgb = wg[s]; wut, wub = wu[s]; wdt, wdb = wd[s]
                kb.dma("pool", wgt[:, :, 0:nb * 128], wguv[:, :, j0 * 128:(j0 + nb) * 128], writes=[wgb], key=tag + "wg%d" % s)
                kb.dma("pool", wut[:, :, 0:nb * 128], wguv[:, :, DFF + j0 * 128:DFF + (j0 + nb) * 128], writes=[wub], key=tag + "wu%d" % s)
                kb.dma("pool", wdt[:, 0:nb, :], wdnv[:, j0:j0 + nb, :], writes=[wdb], key=tag + "wd%d" % s)
                for g in range(NG):
                    gs = slice(g * TG, (g + 1) * TG)
                    at, ab = act[it % 2]
                    it += 1
                    if bi == 0 and g + 1 < NG:
                        self.emit_h([g + 1])
                    for jj in range(nb):
                        pg, pgb = self.pb[ci % 2]; pu, pub = self.pb[2 + ci % 2]
                        sgt, sgb = sg[ci % 2]
                        ci += 1
                        for k in range(KC):
                            kb.op("pe", lambda e, k=k, jj=jj, pg=pg, wgt=wgt, gs=gs: e.matmul(
                                pg[:], lhsT=wgt[:, k, jj * 128:(jj + 1) * 128], rhs=h[:, k, gs], start=(k == 0), stop=(k == KC - 1)),
                                reads=[wgb, self.hb[g]], writes=[pgb], inc=(k == KC - 1))
                        for k in range(KC):
                            kb.op("pe", lambda e, k=k, jj=jj, pu=pu, wut=wut, gs=gs: e.matmul(
                                pu[:], lhsT=wut[:, k, jj * 128:(jj + 1) * 128], rhs=h[:, k, gs], start=(k == 0), stop=(k == KC - 1)),
                                reads=[wub, self.hb[g]], writes=[pub], inc=(k == KC - 1))
                        kb.op("act", lambda e, pg=pg, sgt=sgt: e.activation(sgt[:], pg[:], AF.Silu), reads=[pgb], writes=[sgb])
                        kb.op("dve", lambda e, jj=jj, at=at, sgt=sgt, pu=pu: e.tensor_tensor(at[:, jj, :], sgt[:], pu[:], op=ALU.mult),
                              reads=[sgb, pub], writes=[ab])
                    for m in range(KC):
                        py, pyb = self.pb[4 + m % 2]
                        for jj in range(nb):
                            kb.op("pe", lambda e, m=m, jj=jj, py=py, wdt=wdt, at=at: e.matmul(
                                py[:], lhsT=wdt[:, jj, m * 128:(m + 1) * 128], rhs=at[:, jj, :], start=(jj == 0), stop=(jj == nb - 1)),
                                reads=[wdb, ab], writes=[pyb], inc=(jj == nb - 1))
                        kb.op("dve", lambda e, m=m, gs=gs, py=py: e.scalar_tensor_tensor(
                            x[:, m, gs], py[:], hg[:, m:m + 1], x[:, m, gs], op0=ALU.mult, op1=ALU.add),
                            reads=[pyb, self.hg_b, self.xb[m][g]], writes=[self.xb[m][g]])

    def rope_tables(self, ropec_sb, ropec_b, pos_d, gs, cos2, sinS, cs_b, W):
        kb = self.kb
        TWO_PI = float(2 * np.pi)
        C1 = 6.28125
        C2 = float(np.float32(2 * np.pi - 6.28125))
        pi_f = float(np.pi)
        pos_i, ang, kf, ki, r, m = W["pos_i"], W["ang"], W["kf"], W["ki"], W["r"], W["m"]
        wb = W["b"]
        kb.dma("sp", pos_i[:], pos_d[0:1, gs].broadcast_to([32, TG]), writes=[wb])
        kb.op("dve", lambda e: e.tensor_copy(ang[:], pos_i[:]), reads=[wb], writes=[wb])
        kb.op("dve", lambda e: e.tensor_scalar(ang[:], ang[:], ropec_sb[64:96, 0:1], None, op0=ALU.mult), reads=[wb, ropec_b], writes=[wb])
        kb.op("dve", lambda e: e.tensor_scalar(kf[:], ang[:], 1.0 / TWO_PI, None, op0=ALU.mult), reads=[wb], writes=[wb])
        kb.op("dve", lambda e: e.tensor_copy(ki[:], kf[:]), reads=[wb], writes=[wb])
        kb.op("dve", lambda e: e.tensor_copy(kf[:], ki[:]), reads=[wb], writes=[wb])
        kb.op("dve", lambda e: e.scalar_tensor_tensor(r[:], kf[:], -C1, ang[:], op0=ALU.mult, op1=ALU.add), reads=[wb], writes=[wb])
        kb.op("dve", lambda e: e.scalar_tensor_tensor(r[:], kf[:], -C2, r[:], op0=ALU.mult, op1=ALU.add), reads=[wb], writes=[wb])
        kb.op("dve", lambda e: e.tensor_scalar(m[:], r[:], pi_f, TWO_PI, op0=ALU.is_gt, op1=ALU.mult), reads=[wb], writes=[wb])
        kb.op("dve", lambda e: e.tensor_tensor(r[:], r[:], m[:], op=ALU.subtract), reads=[wb], writes=[wb])
        kb.op("dve", lambda e: e.tensor_scalar(m[:], r[:], -pi_f, TWO_PI, op0=ALU.is_lt, op1=ALU.mult), reads=[wb], writes=[wb])
        kb.op("dve", lambda e: e.tensor_tensor(r[:], r[:], m[:], op=ALU.add), reads=[wb], writes=[wb])
        kb.op("act", lambda e: e.activation(sinS[:], r[:], AF.Sin), reads=[wb], writes=[cs_b])
        kb.op("dve", lambda e: e.tensor_scalar(sinS[:], sinS[:], ropec_sb[64:96, 1:2], None, op0=ALU.mult), reads=[cs_b, ropec_b], writes=[cs_b])
        kb.op("dve", lambda e: e.tensor_scalar(r[:], r[:], pi_f / 2, None, op0=ALU.add), reads=[wb, cs_b], writes=[wb])
        kb.op("dve", lambda e: e.tensor_scalar(m[:], r[:], pi_f, TWO_PI, op0=ALU.is_gt, op1=ALU.mult), reads=[wb], writes=[wb])
        kb.op("dve", lambda e: e.tensor_tensor(r[:], r[:], m[:], op=ALU.subtract), reads=[wb], writes=[wb])
        kb.op("act", lambda e: e.activation(cos2[:], r[:], AF.Sin), reads=[wb], writes=[cs_b])

    def mla_p1(self, tag, snd):
        kb = self.kb
        self.phase_begin()
        self.emit_mods(tag, 1.0)
        self.norm_scratch()
        self.emit_h()
        x, h, ones = self.x, self.h, self.ones
        wa_d = self.din(tag + "_wa", [D, 704])
        wq_d = self.din(tag + "_wq", [QL, 16 * 128])
        wkn_d = self.din(tag + "_wkn", [KVL, 1024])
        wv_d = self.din(tag + "_wv", [KVL, 1024])
        mg_d = self.din(tag + "_mg", [128, 12])
        if "ropec" not in self.inputs:
            self.ropec_d = self.din("ropec", [32, 2])
            self.pos_d = self.din("pos", [1, T], I32)
        C = self.carve
        wa = C([128, KC, 704], BF16); wa_b = Buf()
        wq = C([128, 3, 2048], BF16); wq_b = Buf()
        wkn = C([128, 2, 1024], BF16); wkn_b = Buf()
        wv = C([128, 2, 1024], BF16); wv_b = Buf()
        mg = C([128, 12], F32); mg_b = Buf()
        ropec = C([128, 2], F32); ropec_b = Buf()
        sel2 = C([128, 2], BF16); sel2_b = Buf()
        alat = C([128, 3, TG], F32); alat_b = Buf()
        sqa = C([128, 3, TG], BF16); sqa_b = Buf()
        qn = C([128, 3, TG], BF16); qn_b = Buf()
        kvn = C([128, 2, TG], BF16); kvn_b = Buf()
        cos2 = C([32, TG], F32, 64); sinS = C([32, TG], F32, 64); cs_b = Buf()
        RW = {"pos_i": C([32, TG], I32, 64), "ang": C([32, TG], F32, 64), "kf": C([32, TG], F32, 64),
              "r": C([32, TG], F32, 64), "b": Buf()}
        RW["ki"] = RW["pos_i"]
        RW["m"] = RW["kf"]
        krsq = C([32, TG], BF16, 64); krsq_b = Buf()
        t1 = C([32, TG], F32, 64); t1_b = Buf()
        t2 = C([32, TG], F32, 64); t2_b = Buf()
        krf = C([32, TG], BF16, 64); krf_b = Buf()
        kst = [(C([128, TG], BF16), Buf()) for i in range(2)]
        ksq = [(C([128, TG], BF16), Buf()) for i in range(2)]
        v_sbs = [(C([128, 1024], BF16), Buf()) for i in range(2)]
        qsqs = [(C([96, TG], BF16), Buf()) for i in range(2)]
        rq96s = [(C([96, TG], F32), Buf()) for i in range(2)]
        sds = [(C([96, TG], F32), Buf()) for i in range(2)]
        t1s = [(t1, t1_b), (C([32, TG], F32, 64), Buf())]
        t2s = [(t2, t2_b), (C([32, TG], F32, 64), Buf())]
        qst = [(C([96, TG], BF16), Buf()) for i in range(2)]
        rsum = C([128, 4], F32); rsum_b = Buf()
        s1 = C([128, 4, 16], F32); s1_b = Buf()
        rstd, rstd_b = self.rstd, self.rstd_b
        kb.dma("pool", wa[:], wa_d.rearrange("(k p) n -> p k n", p=128), writes=[wa_b])
        kb.dma("pool", wq[:], wq_d.rearrange("(k p) n -> p k n", p=128), writes=[wq_b])
        kb.dma("pool", wkn[:], wkn_d.rearrange("(k p) n -> p k n", p=128), writes=[wkn_b])
        kb.dma("pool", wv[:], wv_d.rearrange("(k p) n -> p k n", p=128), writes=[wv_b])
        kb.dma("sp", mg[:], mg_d, writes=[mg_b])
        kb.dma("sp", ropec[64:96, :], self.ropec_d, writes=[ropec_b])
        kb.op("pool", lambda e: e.memset(sel2[:], 0.0), writes=[sel2_b])
        kb.op("pool", lambda e: e.memset(sel2[0:64, 0:1], 1.0), writes=[sel2_b])
        kb.op("pool", lambda e: e.memset(sel2[64:128, 1:2], 1.0), writes=[sel2_b])
        QT, KT, V, RK = snd["QT"], snd["KT"], snd["V"], snd["RK"]
        pb = self.pb
        p_ssq, p_ssq_b = pb[6]
        p_kr, p_kr_b = pb[3]
        p_krs, p_krs_b = pb[4]
        p_st, p_st_b = pb[5]
        out_toks = []
        qi = 0
        for g in range(NG):
            gs = slice(g * TG, (g + 1) * TG)
            self.rope_tables(ropec, ropec_b, self.pos_d, gs, cos2, sinS, cs_b, RW)

            def a_chunks(c0, n):
                for ci in range(n):
                    ps, psb = self.rot()
                    for k in range(KC):
                        kb.op("pe", lambda e, k=k, ci=ci, ps=ps, gs=gs, c0=c0: e.matmul(
                            ps[:], lhsT=wa[:, k, (c0 + ci) * 128:(c0 + ci + 1) * 128], rhs=h[:, k, gs], start=(k == 0), stop=(k == KC - 1)),
                            reads=[wa_b, self.hb[g]], writes=[psb], inc=(k == KC - 1))
                    kb.op("act", lambda e, ci=ci, ps=ps: e.activation(alat[:, ci, :], ps[:], AF.Copy), reads=[psb], writes=[alat_b])
                    kb.op("act", lambda e, ci=ci, ps=ps: e.activation(sqa[:, ci, :], ps[:], AF.Square), reads=[psb], writes=[sqa_b])
                for ci in range(n):
                    kb.op("pe", lambda e, ci=ci, n=n: e.matmul(p_ssq[:], lhsT=ones[:], rhs=sqa[:, ci, :], start=(ci == 0), stop=(ci == n - 1)),
                          reads=[self.ones_b, sqa_b], writes=[p_ssq_b], inc=(ci == n - 1))
                self.rstd_from_ps(p_ssq, p_ssq_b, 128, n * 128, rstd, rstd_b)
            a_chunks(0, 3)
            for ci in range(3):
                kb.op("dve", lambda e, ci=ci: e.scalar_tensor_tensor(qn[:, ci, :], alat[:, ci, :], mg[:, 5 + ci:6 + ci], rstd[:], op0=ALU.mult, op1=ALU.mult),
                      reads=[alat_b, mg_b, rstd_b], writes=[qn_b])
            if g == 0:
                self.dbg("h", h[:, :, 0:TG], self.hb, BF16)
                self.dbg("mods", self.mods[:], [self.mods_b])
                self.dbg("qn", qn[:], [qn_b], BF16)
                self.dbg("alat_q", alat[:], [alat_b])
            a_chunks(3, 2)
            for ci in range(2):
                kb.op("dve", lambda e, ci=ci: e.scalar_tensor_tensor(kvn[:, ci, :], alat[:, ci, :], mg[:, 3 + ci:4 + ci], rstd[:], op0=ALU.mult, op1=ALU.mult),
                      reads=[alat_b, mg_b, rstd_b], writes=[kvn_b])
            if g == 0:
                self.dbg("kvn", kvn[:], [kvn_b], BF16)
                self.dbg("cos2", cos2[:], [cs_b])
                self.dbg("sinS", sinS[:], [cs_b])
            for k in range(KC):
                kb.op("pe", lambda e, k=k, gs=gs: e.matmul(p_kr[64:96, :], lhsT=wa[:, k, 640:672], rhs=h[:, k, gs], start=(k == 0), stop=(k == KC - 1)),
                      reads=[wa_b, self.hb[g]], writes=[p_kr_b], inc=(k == KC - 1))
            for k in range(KC):
                kb.op("pe", lambda e, k=k, gs=gs: e.matmul(p_krs[64:96, :], lhsT=wa[:, k, 672:704], rhs=h[:, k, gs], start=(k == 0), stop=(k == KC - 1)),
                      reads=[wa_b, self.hb[g]], writes=[p_krs_b], inc=(k == KC - 1))
            kb.op("act", lambda e: e.activation(krsq[:], p_kr[64:96, :], AF.Square), reads=[p_kr_b], writes=[krsq_b])
            kb.op("dve", lambda e: e.scalar_tensor_tensor(t1[:], p_kr[64:96, :], mg[64:96, 1:2], cos2[:], op0=ALU.mult, op1=ALU.mult),
                  reads=[p_kr_b, mg_b, cs_b], writes=[t1_b])
            kb.op("dve", lambda e: e.scalar_tensor_tensor(t2[:], p_krs[64:96, :], mg[64:96, 2:3], sinS[:], op0=ALU.mult, op1=ALU.mult),
                  reads=[p_krs_b, mg_b, cs_b], writes=[t2_b])
            kb.op("pool", lambda e: e.tensor_tensor(krf[:], t1[:], t2[:], op=ALU.add), reads=[t1_b, t2_b], writes=[krf_b])
            KTv = KT.rearrange("(h p) t -> p h t", p=DH)
            out_toks.append(kb.dma("sp", KTv[64:96, :, gs], krf[:].unsqueeze(1).broadcast_to([32, NH, TG]), reads=[krf_b]))
            for tt in range(4):
                kb.op("pe", lambda e, tt=tt: e.matmul(p_st[:, 64 + tt:65 + tt], lhsT=krsq[:, tt * 128:(tt + 1) * 128], rhs=ones[64:96, 0:1],
                                                      start=True, stop=True),
                      reads=[krsq_b, self.ones_b], writes=[p_st_b], inc=(tt == 3))
            for j in range(8):
                ps, psb = self.rot()
                for kc in range(2):
                    kb.op("pe", lambda e, kc=kc, j=j, ps=ps: e.matmul(ps[:], lhsT=wkn[:, kc, j * 128:(j + 1) * 128], rhs=kvn[:, kc, :],
                                                                  start=(kc == 0), stop=(kc == 1)),
                          reads=[wkn_b, kvn_b], writes=[psb], inc=(kc == 1))
                kt_, ktb = kst[j % 2]
                kq_, kqb = ksq[j % 2]
                kb.op("act", lambda e, ps=ps, kt_=kt_: e.activation(kt_[:], ps[:], AF.Identity, scale=mg[:, 0:1]), reads=[psb, mg_b], writes=[ktb])
                kb.op("act", lambda e, ps=ps, kq_=kq_: e.activation(kq_[:], ps[:], AF.Square), reads=[psb], writes=[kqb])
                for hh in range(2):
                    hq = 2 * j + hh
                    out_toks.append(kb.dma("sp", KT[hq * DH:hq * DH + 64, gs], kt_[hh * 64:(hh + 1) * 64, :], reads=[ktb]))
                for tt in range(4):
                    kb.op("pe", lambda e, tt=tt, j=j, kq_=kq_: e.matmul(p_st[:, tt * 16 + 2 * j:tt * 16 + 2 * j + 2],
                                                                     lhsT=kq_[:, tt * 128:(tt + 1) * 128], rhs=sel2[:], start=True, stop=True),
                          reads=[kqb, sel2_b], writes=[p_st_b], inc=(tt == 3))
            kb.op("act", lambda e: e.activation(rsum[:], p_st[:, 64:68], AF.Copy), reads=[p_st_b], writes=[rsum_b])
            kb.op("dve", lambda e: e.tensor_tensor(s1[:], p_st[:, 0:64].rearrange("p (a b) -> p a b", b=16),
                                                   rsum[:].unsqueeze(2).broadcast_to([128, 4, 16]), op=ALU.add),
                  reads=[p_st_b, rsum_b], writes=[s1_b])
            kb.op("dve", lambda e: e.tensor_scalar(s1[:], s1[:], 1.0 / DH, EPS, op0=ALU.mult, op1=ALU.add), reads=[s1_b], writes=[s1_b])
            kb.op("act", lambda e: e.activation(s1[:], s1[:], AF.Sqrt), reads=[s1_b], writes=[s1_b])
            kb.op("dve", lambda e: e.reciprocal(s1[:], s1[:]), reads=[s1_b], writes=[s1_b])
            kb.op("dve", lambda e: e.tensor_scalar(s1[:], s1[:], SC96, None, op0=ALU.mult), reads=[s1_b], writes=[s1_b])
            out_toks.append(kb.dma("sp", RK[g * TG:(g + 1) * TG, :].rearrange("(a p) h -> p a h", p=128), s1[:], reads=[s1_b]))
            for tt in range(4):
                v_sb, v_b = v_sbs[tt % 2]
                for half in range(2):
                    ps, psb = self.rot()
                    for kc in range(2):
                        kb.op("pe", lambda e, kc=kc, tt=tt, half=half, ps=ps: e.matmul(
                            ps[:], lhsT=kvn[:, kc, tt * 128:(tt + 1) * 128], rhs=wv[:, kc, half * 512:(half + 1) * 512],
                            start=(kc == 0), stop=(kc == 1)),
                            reads=[wv_b, kvn_b], writes=[psb], inc=(kc == 1))
                    if half == 0:
                        kb.op("act", lambda e, ps=ps, v_sb=v_sb: e.activation(v_sb[:, 0:512], ps[:], AF.Copy), reads=[psb], writes=[v_b])
                    else:
                        kb.op("dve", lambda e, ps=ps, v_sb=v_sb: e.tensor_copy(v_sb[:, 512:1024], ps[:]), reads=[psb], writes=[v_b])
                r0 = g * TG + tt * 128
                out_toks.append(kb.dma("sp", V[r0:r0 + 128, :], v_sb[:], reads=[v_b]))
            for hq in range(NH):
                ps, psb = self.rot()
                pks, pks_b = pb[3 + hq % 2]
                pss, pss_b = pb[6 + hq % 2]
                qsq, qsq_b = qsqs[hq % 2]
                rq96, rq96_b = rq96s[hq % 2]
                t1q, t1q_b = t1s[hq % 2]
                t2q, t2q_b = t2s[hq % 2]
                for kc in range(3):
                    kb.op("pe", lambda e, kc=kc, hq=hq, ps=ps: e.matmul(ps[0:96, :], lhsT=wq[:, kc, hq * 128:hq * 128 + 96], rhs=qn[:, kc, :],
                                                                    start=(kc == 0), stop=(kc == 2)),
                          reads=[wq_b, qn_b], writes=[psb], inc=(kc == 2))
                for kc in range(3):
                    kb.op("pe", lambda e, kc=kc, hq=hq, pks=pks: e.matmul(pks[64:96, :], lhsT=wq[:, kc, hq * 128 + 96:hq * 128 + 128], rhs=qn[:, kc, :],
                                                                       start=(kc == 0), stop=(kc == 2)),
                          reads=[wq_b, qn_b], writes=[pks_b], inc=(kc == 2))
                kb.op("act", lambda e, ps=ps, qsq=qsq: e.activation(qsq[:], ps[0:96, :], AF.Square), reads=[psb], writes=[qsq_b])
                kb.op("pe", lambda e, pss=pss, qsq=qsq: e.matmul(pss[0:96, :], lhsT=ones[0:96, 0:96], rhs=qsq[:], start=True, stop=True),
                      reads=[self.ones_b, qsq_b], writes=[pss_b])
                self.rstd_from_ps(pss, pss_b, 96, DH, rq96, rq96_b, sds[hq % 2])
                qs_, qsb = qst[qi % 2]
                qi += 1
                kb.op("dve", lambda e, ps=ps, qs_=qs_, rq96=rq96: e.scalar_tensor_tensor(qs_[0:64, :], ps[0:64, :], mg[0:64, 8:9], rq96[0:64, :],
                                                                                       op0=ALU.mult, op1=ALU.mult),
                      reads=[psb, mg_b, rq96_b], writes=[qsb])
                kb.op("dve", lambda e, ps=ps, t1q=t1q: e.scalar_tensor_tensor(t1q[:], ps[64:96, :], mg[64:96, 8:9], cos2[:], op0=ALU.mult, op1=ALU.mult),
                      reads=[psb, mg_b, cs_b], writes=[t1q_b])
                kb.op("dve", lambda e, pks=pks, t2q=t2q: e.scalar_tensor_tensor(t2q[:], pks[64:96, :], mg[64:96, 9:10], sinS[:], op0=ALU.mult, op1=ALU.mult),
                      reads=[pks_b, mg_b, cs_b], writes=[t2q_b])
                kb.op("pool", lambda e, t1q=t1q, t2q=t2q: e.tensor_tensor(t1q[:], t1q[:], t2q[:], op=ALU.add), reads=[t1q_b, t2q_b], writes=[t1q_b])
                kb.op("pool", lambda e, qs_=qs_, t1q=t1q, rq96=rq96: e.tensor_tensor(qs_[64:96, :], t1q[:], rq96[64:96, :], op=ALU.mult),
                      reads=[t1q_b, rq96_b], writes=[qsb])
                out_toks.append(kb.dma("sp", QT[hq * DH:(hq + 1) * DH, gs], qs_[:], reads=[qsb]))
        return out_toks

    def mla_p2(self, tag, gat, snd_o):
        kb = self.kb
        nc = self.nc
        self.phase_begin()
        C = self.carve
        GQ, GK, GV, GRK = gat["QT"], gat["KT"], gat["V"], gat["RK"]
        rk_all = C([128, 32, 8], F32); rk_b = Buf()
        v_all = C([128, 32, 512], BF16); v_b = Buf()
        qT = [(C([96, S], BF16), Buf()) for i in range(2)]
        kT = [(C([96, S], BF16), Buf()) for i in range(2)]
        pt = [(C([128, TG], BF16), Buf()) for i in range(6)]
        ost = [(C([64, S], BF16), Buf()) for i in range(2)]
        rec = C([64, TG], F32); rec_b = Buf()
        ones = self.ones
        for s in range(2):
            rkr = GRK.rows(s, 0, T)
            self.dsel(rk_all[:, s * 16:(s + 1) * 16, :],
                      rkr[:, 0:8].rearrange("(a p) h -> p a h", p=128),
                      rkr[:, 8:16].rearrange("(a p) h -> p a h", p=128), writes=[rk_b])
            for a in range(2):
                vr = GV.rows(s, a * 1024, 1024)
                self.dsel(v_all[:, s * 16 + a * 8:s * 16 + (a + 1) * 8, :],
                          vr[:, 0:512].rearrange("(a p) n -> p a n", p=128),
                          vr[:, 512:1024].rearrange("(a p) n -> p a n", p=128), writes=[v_b])
        pb = self.pb
        out_toks = []
        LAG = 2
        NPT = len(pt)
        pend = []
        ti = 0
        gi = 0
        for hh in range(8):
            qt_, qb = qT[hh % 2]
            kt_, kbf = kT[hh % 2]
            for s in range(2):
                self.dsel(qt_[:, s * T:(s + 1) * T], GQ.rows(s, hh * DH, DH), GQ.rows(s, (8 + hh) * DH, DH), writes=[qb])
                self.dsel(kt_[:, s * T:(s + 1) * T], GK.rows(s, hh * DH, DH), GK.rows(s, (8 + hh) * DH, DH), writes=[kbf])
            os_, osb = ost[hh % 2]
            for gq in range(8):
                qs = slice(gq * TG, (gq + 1) * TG)
                po, pob = pb[4 + gi % 2]
                pd, pdb = pb[6 + gi % 2]
                gi += 1
                nkt = 4 * (gq + 1)
                for kt in range(nkt):
                    ps, psb = pb[ti % 4]
                    p_, p_b = pt[ti % NPT]
                    ti += 1
                    kb.op("pe", lambda e, kt=kt, ps=ps, kt_=kt_, qt_=qt_, qs=qs: e.matmul(
                        ps[:], lhsT=kt_[:, kt * 128:(kt + 1) * 128], rhs=qt_[:, qs], start=True, stop=True),
                        reads=[kbf, qb], writes=[psb])
                    kb.op("act", lambda e, kt=kt, hh=hh, ps=ps, p_=p_: e.activation(p_[:], ps[:], AF.Exp, scale=rk_all[:, kt, hh:hh + 1]),
                          reads=[psb, rk_b], writes=[p_b])
                    if kt >= 4 * gq:
                        base = gq * TG - kt * 128
                        kb.op("pool", lambda e, p_=p_, base=base: e.affine_select(
                            out=p_[:], in_=p_[:], pattern=[[1, TG]], compare_op=ALU.is_ge, fill=0.0, base=base, channel_multiplier=-1),
                            reads=[p_b], writes=[p_b])

                    def emit_pv(kt=kt, hh=hh, po=po, pob=pob, pd=pd, pdb=pdb, p_=p_, p_b=p_b, nkt=nkt, os_=os_, osb=osb, qs=qs):
                        kb.op("pe", lambda e: e.matmul(po[0:64, :], lhsT=v_all[:, kt, hh * 64:(hh + 1) * 64], rhs=p_[:],
                                                       start=(kt == 0), stop=(kt == nkt - 1)),
                              reads=[v_b, p_b], writes=[pob], inc=(kt == nkt - 1))
                        kb.op("pe", lambda e: e.matmul(pd[0:64, :], lhsT=ones[:, 0:64], rhs=p_[:], start=(kt == 0), stop=(kt == nkt - 1)),
                              reads=[self.ones_b, p_b], writes=[pdb], inc=True)
                        if kt == nkt - 1:
                            kb.op("dve", lambda e: e.reciprocal(rec[:], pd[0:64, :]), reads=[pdb], writes=[rec_b])
                            kb.op("dve", lambda e: e.tensor_tensor(os_[:, qs], po[0:64, :], rec[:], op=ALU.mult),
                                  reads=[pob, rec_b], writes=[osb])
                            if qs.stop == S:
                                out_toks.append(kb.dma("sp", snd_o[hh * 64:(hh + 1) * 64, :], os_[:], reads=[osb]))
                    pend.append(emit_pv)
                    if len(pend) > LAG:
                        pend.pop(0)()
        while pend:
            pend.pop(0)()
        return out_toks

    def mixer_p3(self, tag, gat_o, nk, recompute_mods, wname, gate_scale=1.0):
        kb = self.kb
        nc = self.nc
        self.phase_begin()
        if recompute_mods:
            self.emit_mods(tag, gate_scale)
        C = self.carve
        wo_d = self.din(tag + wname, [nk * 128, D])
        wo = C([128, nk, D], BF16); wo_b = Buf()
        kb.dma("pool", wo[:], wo_d.rearrange("(k p) n -> p k n", p=128), writes=[wo_b])
        osb = [(C([128, nk, TG], BF16), Buf()) for i in range(2)]
        x, hg = self.x, self.hg
        for g in range(NG):
            gs = slice(g * TG, (g + 1) * TG)
            o_, ob = osb[g % 2]
            R = gat_o.R
            for s_ in range(2):
                for kk in range(gat_o.nrows // R):
                    rr_ = gat_o.rows(s_, kk * R, R)
                    c0_ = (s_ * gat_o.nrows + kk * R) // 128
                    self.dsel(o_[:, c0_:c0_ + R // 128, :], rr_[:, g * TG:(g + 1) * TG].rearrange("(k p) t -> p k t", p=128),
                              rr_[:, T + g * TG:T + (g + 1) * TG].rearrange("(k p) t -> p k t", p=128), writes=[ob])
            for m in range(KC):
                py, pyb = self.pb[m % 2]
                for k in range(nk):
                    kb.op("pe", lambda e, m=m, k=k, py=py, o_=o_: e.matmul(py[:], lhsT=wo[:, k, m * 128:(m + 1) * 128], rhs=o_[:, k, :],
                                                                       start=(k == 0), stop=(k == nk - 1)),
                          reads=[wo_b, ob], writes=[pyb], inc=(k == nk - 1))
                kb.op("dve", lambda e, m=m, gs=gs, py=py: e.scalar_tensor_tensor(
                    x[:, m, gs], py[:], hg[:, m:m + 1], x[:, m, gs], op0=ALU.mult, op1=ALU.add),
                    reads=[pyb, self.hg_b, self.xb[m][g]], writes=[self.xb[m][g]])

    def ssd_p1(self, tag, snd):
        kb = self.kb
        self.phase_begin()
        self.emit_mods(tag, 1.0)
        self.norm_scratch()
        self.emit_h()
        h = self.h
        win_d = self.din(tag + "_win", [D, 5152])
        wv = win_d.rearrange("(k p) n -> p k n", p=128)
        C = self.carve
        wblk = [(C([128, KC, 512], BF16), Buf()) for i in range(2)]
        wdt = C([128, KC, 32], BF16); wdt_b = Buf()
        stg = [(C([128, TG], BF16), Buf()) for i in range(4)]
        dts = C([128, 16, 32], F32); dts_b = Buf()
        XBC, Z, DT = snd["XBC"], snd["Z"], snd["DT"]
        out_toks = []
        kb.dma("pool", wdt[:], wv[:, :, 5120:5152], writes=[wdt_b])
        si = 0
        wcols = [2048 + blk * 512 for blk in range(6)] + [blk * 512 for blk in range(4)]

        def wload(i):
            wt, wb = wblk[i % 2]
            kb.dma("pool", wt[:], wv[:, :, wcols[i]:wcols[i] + 512], writes=[wb])
        wload(0)
        for blk in range(6):
            wt, wb = wblk[blk % 2]
            blk_toks = []
            for cc in range(4):
                ch = blk * 4 + cc
                for g in range(NG):
                    gs = slice(g * TG, (g + 1) * TG)
                    ps, psb = self.rot(0, 4)
                    for k in range(KC):
                        kb.op("pe", lambda e, k=k, cc=cc, ps=ps, wt=wt, gs=gs: e.matmul(
                            ps[:], lhsT=wt[:, k, cc * 128:(cc + 1) * 128], rhs=h[:, k, gs], start=(k == 0), stop=(k == KC - 1)),
                            reads=[wb, self.hb[g]], writes=[psb], inc=(k == KC - 1))
                    st, stb = stg[si % 4]; si += 1
                    if si % 2 == 0:
                        kb.op("act", lambda e, ps=ps, st=st: e.activation(st[:], ps[:], AF.Copy), reads=[psb], writes=[stb])
                    else:
                        kb.op("dve", lambda e, ps=ps, st=st: e.tensor_copy(st[:], ps[:]), reads=[psb], writes=[stb])
                    out_toks.append(kb.dma("sp", XBC[ch * 128:(ch + 1) * 128, gs], st[:], reads=[stb]))
                    blk_toks.append(out_toks[-1])
                if cc == 0:
                    wload(blk + 1)
            if snd.get("XBC_g") is not None:
                snd["XBC_g"].gather_chunk(kb, blk, blk_toks, snd["groups"])
        for blk in range(4):
            wt, wb = wblk[(6 + blk) % 2]
            for tt in range(16):
                ps, psb = self.rot(0, 4)
                for k in range(KC):
                    kb.op("pe", lambda e, k=k, tt=tt, ps=ps, wt=wt: e.matmul(
                        ps[:], lhsT=h[:, k, tt * 128:(tt + 1) * 128], rhs=wt[:, k, :], start=(k == 0), stop=(k == KC - 1)),
                        reads=[wb, self.hb[tt // 4]], writes=[psb], inc=(k == KC - 1))
                st, stb = stg[si % 4]; si += 1
                if si % 2 == 0:
                    kb.op("act", lambda e, ps=ps, st=st: e.activation(st[:], ps[:], AF.Copy), reads=[psb], writes=[stb])
                else:
                    kb.op("dve", lambda e, ps=ps, st=st: e.tensor_copy(st[:], ps[:]), reads=[psb], writes=[stb])
                out_toks.append(kb.dma("sp", Z[tt * 128:(tt + 1) * 128, blk * 512:(blk + 1) * 512], st[:], reads=[stb]))
                if tt == 0 and blk < 3:
                    wload(6 + blk + 1)
        pdt, pdt_b = self.pb[4]
        for tt in range(16):
            for k in range(KC):
                kb.op("pe", lambda e, k=k, tt=tt: e.matmul(pdt[:, tt * 32:(tt + 1) * 32], lhsT=h[:, k, tt * 128:(tt + 1) * 128], rhs=wdt[:, k, :],
                                                        start=(k == 0), stop=(k == KC - 1)),
                      reads=[wdt_b, self.hb[tt // 4]], writes=[pdt_b], inc=(k == KC - 1))
        kb.op("dve", lambda e: e.tensor_copy(dts[:], pdt[:].rearrange("p (a b) -> p a b", b=32)), reads=[pdt_b], writes=[dts_b])
        out_toks.append(kb.dma("sp", DT.rearrange("(a p) h -> p a h", p=128), dts[:], reads=[dts_b]))
        return out_toks

    def ssd_p2(self, tag, gat, snd_g):
        kb = self.kb
        nc = self.nc
        self.phase_begin()
        C = self.carve
        GX, GZ, GDT = gat["XBC"], gat["Z"], gat["DT"]
        cw_d = self.din(tag + "_cw", [128, 48])
        cb_d = self.din(tag + "_cb", [128, 12])
        cbrow_d = self.din(tag + "_cbrow", [1, 1280])
        hp_d = self.din(tag + "_hp", [128, 48])
        ng_d = self.din(tag + "_ng", [128, 1024])
        cw = C([128, 48], F32); cw_b = Buf()
        cb = C([128, 12], F32); cb_b = Buf()
        cbrow = C([1, 1280], F32); cbrow_b = Buf()
        cbhi = C([1, 1280], BF16); cblo = C([1, 1280], BF16); cbf = C([1, 1280], F32); cbhl_b = Buf()
        hp = C([128, 48], F32); hp_b = Buf()
        ng = C([128, 1024], F32); ng_b = Buf()
        identb = C([128, 128], BF16); Tb = C([128, 128], BF16)
        U = C([128, 128], F32); Tm = C([128, 128], F32); cst_b = Buf()
        diag = C([128, 48, 128], BF16); diag_b = Buf()
        Aneg = C([128, 16], F32); Aneg_b = Buf()
        u = C([128, 12, TG + 4], BF16); u_b = Buf()
        zt = C([128, 4, 1024], BF16); zt_b = Buf()
        dtr = C([128, 4, 16], F32); dtr_b = Buf()
        xs = C([128, 4, 1024], F32); xs_b = Buf()
        Btok = C([128, 4, 256], BF16); Btok_b = Buf()
        BT = C([128, 2, TG], BF16); CT = C([128, 2, TG], BF16); bct_b = Buf()
        dtv = C([128, 4, 16], F32); av = C([128, 4, 16], F32); dtv_b = Buf()
        acum = C([128, 16], F32); ea = C([128, 16], F32); dte = C([128, 16], F32); cd = C([128, 16], F32); sm_b = Buf()
        aU = C([128, 16, 128], F32); aU_b = Buf()
        dec = [(C([128, 8, 128], BF16), Buf()) for i in range(2)]
        cbm = [(C([128, 128], BF16), Buf()) for i in range(2)]
        MT = [(C([128, 8, 128], BF16), Buf()) for i in range(2)]
        xdt = C([128, 1024], BF16); xdt_b = Buf()
        Bdec = C([128, 16, 128], BF16); Bdec_b = Buf()
        Sf = C([128, 1024], F32); Sf_b = Buf()
        Sb = C([128, 1024], BF16); Sb_b = Buf()
        t1 = C([128, 1024], F32); t1_b = Buf()
        t3 = C([128, 1024], F32); t3_b = Buf()
        yv = C([128, 1024], F32); yv_b = Buf()
        sz = t3; sz_b = t3_b
        ssq = C([128, 2], F32); ssq_b = Buf()
        junk = C([128, 512], BF16); junk_b = Buf()
        gn = C([128, 1024], BF16); gn_b = Buf()
        gT = [(C([128, 8, TG], BF16), Buf()) for i in range(1)]
        ones, onesf = self.ones, self.onesf
        pb = self.pb
        kb.dma("sp", cw[:], cw_d, writes=[cw_b])
        kb.dma("sp", cb[:], cb_d, writes=[cb_b])
        kb.dma("sp", cbrow[:], cbrow_d, writes=[cbrow_b])
        kb.dma("sp", hp[:], hp_d, writes=[hp_b])
        kb.dma("sp", ng[:], ng_d, writes=[ng_b])
        kb.op("dve", lambda e: e.tensor_copy(cbhi[:], cbrow[:]), reads=[cbrow_b], writes=[cbhl_b])
        kb.op("dve", lambda e: e.tensor_copy(cbf[:], cbhi[:]), reads=[cbhl_b], writes=[cbhl_b])
        kb.op("dve", lambda e: e.tensor_tensor(cbf[:], cbrow[:], cbf[:], op=ALU.subtract), reads=[cbhl_b, cbrow_b], writes=[cbhl_b])
        kb.op("dve", lambda e: e.tensor_copy(cblo[:], cbf[:]), reads=[cbhl_b], writes=[cbhl_b])
        for (tile_, pat, base, cm, cmp_) in ((Tm, [[1, 128]], 0, -1, ALU.is_ge), (U, [[-1, 128]], -1, 1, ALU.is_ge)):
            kb.op("pool", lambda e, tile_=tile_: e.memset(tile_[:], 1.0), writes=[cst_b])
            kb.op("pool", lambda e, tile_=tile_, pat=pat, base=base, cm=cm, cmp_=cmp_: e.affine_select(
                out=tile_[:], in_=tile_[:], pattern=pat, compare_op=cmp_, fill=0.0, base=base, channel_multiplier=cm),
                reads=[cst_b], writes=[cst_b])
        kb.op("pool", lambda e: e.memset(identb[:], 1.0), writes=[cst_b])
        kb.op("pool", lambda e: e.affine_select(out=identb[:], in_=identb[:], pattern=[[-1, 128]], compare_op=ALU.is_equal, fill=0.0,
                                                base=0, channel_multiplier=1), reads=[cst_b], writes=[cst_b])
        kb.op("dve", lambda e: e.tensor_copy(Tb[:], Tm[:]), reads=[cst_b], writes=[cst_b])
        for c in range(12):
            for j in range(4):
                kb.op("dve" if (c + j) % 2 else "pool", lambda e, c=c, j=j: e.tensor_scalar(
                    diag[:, c * 4 + j, :], identb[:], cw[:, c * 4 + j:c * 4 + j + 1], None, op0=ALU.mult),
                    reads=[cst_b, cw_b], writes=[diag_b])
        kb.op("act", lambda e: e.activation(Aneg[:], hp[:, 16:32], AF.Exp), reads=[hp_b], writes=[Aneg_b])
        kb.op("dve", lambda e: e.tensor_scalar(Aneg[:], Aneg[:], -1.0, None, op0=ALU.mult), reads=[Aneg_b], writes=[Aneg_b])
        kb.op("pool", lambda e: e.memset(Sf[:], 0.0), writes=[Sf_b])
        kb.op("pool", lambda e: e.memset(u[:, :, 0:4], 0.0), writes=[u_b])
        out_toks = []
        for G in range(8):
            s = G // 4
            gl = G % 4
            t0 = gl * TG
            if G > 0:
                kb.op("dve", lambda e: e.tensor_copy(u[:, :, 0:4], u[:, :, TG:TG + 4]), reads=[u_b], writes=[u_b])
            for (c0, nch, r0, r1) in ((0, 4, 0, 1024), (4, 4, 512, 1536), (8, 2, 2048, 2048 + 256), (10, 2, 2560, 2560 + 256)):
                self.dsel(u[:, c0:c0 + nch, 4:4 + TG],
                          GX.rows(s, r0, nch * 128)[:, t0:t0 + TG].rearrange("(c p) t -> p c t", p=128),
                          GX.rows(s, r1, nch * 128)[:, t0:t0 + TG].rearrange("(c p) t -> p c t", p=128), writes=[u_b])
            zr = GZ.rows(s, t0, TG)
            self.dsel(zt[:], zr[:, 0:1024].rearrange("(a p) n -> p a n", p=128),
                      zr[:, 1024:2048].rearrange("(a p) n -> p a n", p=128), writes=[zt_b])
            dr = GDT.rows(s, t0, TG)
            self.dsel(dtr[:], dr[:, 0:16].rearrange("(a p) h -> p a h", p=128),
                      dr[:, 16:32].rearrange("(a p) h -> p a h", p=128), writes=[dtr_b])
            for c in range(8, 12):
                ps, psb = pb[7]
                for j in range(4):
                    kb.op("pe", lambda e, c=c, j=j, ps=ps: e.matmul(ps[:], lhsT=diag[:, c * 4 + j, :], rhs=u[:, c, 1 + j:1 + j + TG],
                                                              start=(j == 0), stop=(j == 3)),
                          reads=[diag_b, u_b], writes=[psb], inc=(j == 3))
                dst = BT if c < 10 else CT
                kb.op("act", lambda e, c=c, ps=ps, dst=dst: e.activation(dst[:, c % 2, :], ps[:], AF.Silu, bias=cb[:, c:c + 1]),
                      reads=[psb, cb_b], writes=[bct_b])
            for tt in range(4):
                for half in range(2):
                    ps, psb = pb[4 + half]
                    for cc in range(4):
                        c = half * 4 + cc
                        for j in range(4):
                            kb.op("pe", lambda e, c=c, cc=cc, j=j, tt=tt, ps=ps: e.matmul(
                                ps[:, cc * 128:(cc + 1) * 128], lhsT=u[:, c, 1 + j + tt * 128:1 + j + tt * 128 + 128], rhs=diag[:, c * 4 + j, :],
                                start=(cc == 0 and j == 0), stop=False, skip_group_check=True),
                                reads=[diag_b, u_b], writes=[psb], inc=False)
                    kb.op("pe", lambda e, half=half, ps=ps: e.matmul(ps[:], lhsT=ones[0:1, 0:128], rhs=cbhi[0:1, half * 512:(half + 1) * 512],
                                                                  start=False, stop=False, skip_group_check=True),
                          reads=[self.ones_b, cbhl_b], writes=[psb], inc=False)
                    kb.op("pe", lambda e, half=half, ps=ps: e.matmul(ps[:], lhsT=ones[0:1, 0:128], rhs=cblo[0:1, half * 512:(half + 1) * 512],
                                                                  start=False, stop=True, skip_group_check=True),
                          reads=[self.ones_b, cbhl_b], writes=[psb], inc=True)
                    kb.op("act", lambda e, half=half, tt=tt, ps=ps: e.activation(xs[:, tt, half * 512:(half + 1) * 512], ps[:], AF.Silu),
                          reads=[psb], writes=[xs_b])
                ps, psb = pb[7]
                for cc in range(2):
                    c = 8 + cc
                    for j in range(4):
                        kb.op("pe", lambda e, c=c, cc=cc, j=j, tt=tt, ps=ps: e.matmul(
                            ps[:, cc * 128:(cc + 1) * 128], lhsT=u[:, c, 1 + j + tt * 128:1 + j + tt * 128 + 128], rhs=diag[:, c * 4 + j, :],
                            start=(cc == 0 and j == 0), stop=False, skip_group_check=True),
                            reads=[diag_b, u_b], writes=[psb], inc=False)
                kb.op("pe", lambda e, ps=ps: e.matmul(ps[:, 0:256], lhsT=ones[0:1, 0:128], rhs=cbhi[0:1, 1024:1280], start=False, stop=False,
                                                      skip_group_check=True), reads=[self.ones_b, cbhl_b], writes=[psb], inc=False)
                kb.op("pe", lambda e, ps=ps: e.matmul(ps[:, 0:256], lhsT=ones[0:1, 0:128], rhs=cblo[0:1, 1024:1280], start=False, stop=True,
                                                      skip_group_check=True), reads=[self.ones_b, cbhl_b], writes=[psb], inc=True)
                kb.op("act", lambda e, tt=tt, ps=ps: e.activation(Btok[:, tt, :], ps[:, 0:256], AF.Silu), reads=[psb], writes=[Btok_b])
            kb.op("act", lambda e: e.activation(zt[:], zt[:], AF.Silu), reads=[zt_b], writes=[zt_b])
            kb.op("dve", lambda e: e.tensor_tensor(dtv[:], dtr[:], hp[:, 0:16].unsqueeze(1).broadcast_to([128, 4, 16]), op=ALU.add),
                  reads=[dtr_b, hp_b], writes=[dtv_b])
            kb.op("act", lambda e: e.activation(dtv[:], dtv[:], AF.Exp), reads=[dtv_b], writes=[dtv_b])
            kb.op("act", lambda e: e.activation(dtv[:], dtv[:], AF.Ln, bias=1.0), reads=[dtv_b], writes=[dtv_b])
            kb.op("dve", lambda e: e.tensor_tensor(av[:], dtv[:], Aneg[:].unsqueeze(1).broadcast_to([128, 4, 16]), op=ALU.mult),
                  reads=[dtv_b, Aneg_b], writes=[dtv_b])
            gt_, gtb = gT[0]
            for tt in range(4):
                ts_ = slice(tt * 128, (tt + 1) * 128)
                pst, pstb = pb[6]
                kb.op("pe", lambda e, tt=tt: e.matmul(pst[:, 0:16], lhsT=Tm[:], rhs=av[:, tt, :], start=True, stop=True),
                      reads=[cst_b, dtv_b], writes=[pstb])
                kb.op("pe", lambda e, tt=tt: e.matmul(pst[:, 16:32], lhsT=onesf[:], rhs=av[:, tt, :], start=True, stop=True),
                      reads=[self.onesf_b, dtv_b], writes=[pstb])
                kb.op("act", lambda e: e.activation(ea[:], pst[:, 0:16], AF.Exp), reads=[pstb], writes=[sm_b])
                kb.op("act", lambda e: e.activation(cd[:], pst[:, 16:32], AF.Exp), reads=[pstb], writes=[sm_b])
                kb.op("act", lambda e: e.activation(acum[:], pst[:, 0:16], AF.Copy), reads=[pstb], writes=[sm_b])
                kb.op("dve", lambda e: e.tensor_tensor(dte[:], pst[:, 16:32], acum[:], op=ALU.subtract), reads=[pstb, sm_b], writes=[sm_b])
                kb.op("act", lambda e: e.activation(dte[:], dte[:], AF.Exp), reads=[sm_b], writes=[sm_b])
                kb.op("dve", lambda e, tt=tt: e.tensor_tensor(xdt[:].rearrange("p (h d) -> p h d", d=64), xs[:, tt, :].rearrange("p (h d) -> p h d", d=64),
                                                            dtv[:, tt, :].unsqueeze(2).broadcast_to([128, 16, 64]), op=ALU.mult),
                      reads=[xs_b, dtv_b], writes=[xdt_b])
                kb.op("pool", lambda e, tt=tt: e.tensor_tensor(aU[:], U[:].unsqueeze(1).broadcast_to([128, 16, 128]),
                                                             av[:, tt, :].unsqueeze(2).broadcast_to([128, 16, 128]), op=ALU.mult),
                      reads=[cst_b, dtv_b], writes=[aU_b])
                for gg in range(2):
                    kb.op("pool", lambda e, tt=tt, gg=gg: e.tensor_tensor(
                        Bdec[:, gg * 8:(gg + 1) * 8, :], Btok[:, tt, gg * 128:(gg + 1) * 128].unsqueeze(1).broadcast_to([128, 8, 128]),
                        dte[:, gg * 8:(gg + 1) * 8].unsqueeze(2).broadcast_to([128, 8, 128]), op=ALU.mult),
                        reads=[Btok_b, sm_b], writes=[Bdec_b])
                kb.op("dve", lambda e: e.tensor_copy(Sb[:], Sf[:]), reads=[Sf_b], writes=[Sb_b])
                py0, py0b = pb[2]
                py1, py1b = pb[3]
                pys = [(py0, py0b), (py1, py1b)]
                for gg in range(2):
                    cm_, cmb = cbm[gg]
                    kb.op("pe", lambda e, gg=gg, ts_=ts_: e.matmul(pst[:, 32 + gg * 128:32 + (gg + 1) * 128], lhsT=BT[:, gg, ts_], rhs=CT[:, gg, ts_],
                                                                start=True, stop=True), reads=[bct_b], writes=[pstb])
                    kb.op("dve", lambda e, gg=gg, cm_=cm_: e.tensor_tensor(cm_[:], pst[:, 32 + gg * 128:32 + (gg + 1) * 128], Tb[:], op=ALU.mult),
                          reads=[pstb, cst_b], writes=[cmb])
                    for half in range(2):
                        ps, psb = pb[half]
                        for hh in range(4):
                            hd = gg * 8 + half * 4 + hh
                            kb.op("pe", lambda e, hd=hd, hh=hh, ps=ps: e.matmul(ps[:, hh * 128:(hh + 1) * 128], lhsT=aU[:, hd, :], rhs=Tm[:],
                                                                             start=True, stop=True),
                                  reads=[aU_b, cst_b], writes=[psb])
                    dc, dcb = dec[gg]
                    for half in range(2):
                        ps, psb = pb[half]
                        kb.op("act", lambda e, half=half, ps=ps, dc=dc: e.activation(
                            dc[:, half * 4:(half + 1) * 4, :], ps[:].rearrange("p (a b) -> p a b", b=128), AF.Exp), reads=[psb], writes=[dcb])
                    mt, mtb = MT[gg]
                    kb.op("dve" if gg == 0 else "pool", lambda e, mt=mt, dc=dc, cm_=cm_: e.tensor_tensor(
                        mt[:], dc[:], cm_[:].unsqueeze(1).broadcast_to([128, 8, 128]), op=ALU.mult), reads=[dcb, cmb], writes=[mtb])
                    py, pyb = pys[gg]
                    for hh in range(8):
                        hd = gg * 8 + hh
                        kb.op("pe", lambda e, hd=hd, hh=hh, py=py, mt=mt: e.matmul(py[:, hh * 64:(hh + 1) * 64], lhsT=mt[:, hh, :],
                                                                              rhs=xdt[:, hd * 64:(hd + 1) * 64], start=True, stop=True),
                              reads=[mtb, xdt_b], writes=[pyb], inc=(hh == 7))
                for gg in range(2):
                    ps, psb = pb[gg]
                    kb.op("pe", lambda e, gg=gg, ps=ps, ts_=ts_: e.matmul(ps[:], lhsT=CT[:, gg, ts_], rhs=Sb[:, gg * 512:(gg + 1) * 512], start=True, stop=True),
                          reads=[bct_b, Sb_b], writes=[psb])
                    kb.op("dve", lambda e, gg=gg, ps=ps: e.tensor_tensor(
                        t1[:, gg * 512:(gg + 1) * 512].rearrange("p (h d) -> p h d", d=64), ps[:].rearrange("p (h d) -> p h d", d=64),
                        ea[:, gg * 8:(gg + 1) * 8].unsqueeze(2).broadcast_to([128, 8, 64]), op=ALU.mult),
                        reads=[psb, sm_b], writes=[t1_b])
                kb.op("pool", lambda e, tt=tt: e.tensor_tensor(t3[:].rearrange("p (h d) -> p h d", d=64), xs[:, tt, :].rearrange("p (h d) -> p h d", d=64),
                                                             hp[:, 32:48].unsqueeze(2).broadcast_to([128, 16, 64]), op=ALU.mult),
                      reads=[xs_b, hp_b], writes=[t3_b])
                kb.op("pool", lambda e: e.tensor_tensor(t1[:], t1[:], t3[:], op=ALU.add), reads=[t1_b, t3_b], writes=[t1_b])
                for gg in range(2):
                    py, pyb = pys[gg]
                    kb.op("dve", lambda e, gg=gg, py=py: e.tensor_tensor(yv[:, gg * 512:(gg + 1) * 512], py[:], t1[:, gg * 512:(gg + 1) * 512], op=ALU.add),
                          reads=[pyb, t1_b], writes=[yv_b])
                kb.op("dve", lambda e, tt=tt: e.tensor_tensor(yv[:], yv[:], zt[:, tt, :], op=ALU.mult), reads=[yv_b, zt_b], writes=[yv_b])
                for gg in range(2):
                    kb.op("act", lambda e, gg=gg: e.activation(junk[:], yv[:, gg * 512:(gg + 1) * 512], AF.Square, accum_out=ssq[:, gg:gg + 1]),
                          reads=[yv_b], writes=[junk_b, ssq_b])
                kb.op("dve", lambda e: e.tensor_scalar(ssq[:], ssq[:], 1.0 / 512, EPS, op0=ALU.mult, op1=ALU.add), reads=[ssq_b], writes=[ssq_b])
                kb.op("act", lambda e: e.activation(ssq[:], ssq[:], AF.Ln), reads=[ssq_b], writes=[ssq_b])
                kb.op("act", lambda e: e.activation(ssq[:], ssq[:], AF.Exp, scale=-0.5), reads=[ssq_b], writes=[ssq_b])
                for gg in range(2):
                    kb.op("dve", lambda e, gg=gg: e.scalar_tensor_tensor(gn[:, gg * 512:(gg + 1) * 512], yv[:, gg * 512:(gg + 1) * 512], ssq[:, gg:gg + 1],
                                                                       ng[:, gg * 512:(gg + 1) * 512], op0=ALU.mult, op1=ALU.mult),
                          reads=[yv_b, ssq_b, ng_b], writes=[gn_b])
                for half in range(2):
                    ps, psb = pb[4 + half]
                    for fc in range(4):
                        f = half * 4 + fc
                        kb.op("pe", lambda e, f=f, fc=fc, ps=ps: e.matmul(ps[:, fc * 128:(fc + 1) * 128], lhsT=gn[:, f * 128:(f + 1) * 128], rhs=identb[:],
                                                                       start=True, stop=True), reads=[gn_b, cst_b], writes=[psb], inc=(fc == 3))
                    kb.op("act" if half == 0 else "dve", (lambda e, half=half, ps=ps, gt_=gt_, ts_=ts_: e.activation(
                        gt_[:, half * 4:(half + 1) * 4, ts_], ps[:].rearrange("p (a b) -> p a b", b=128), AF.Copy)) if half == 0 else
                        (lambda e, half=half, ps=ps, gt_=gt_, ts_=ts_: e.tensor_copy(gt_[:, half * 4:(half + 1) * 4, ts_], ps[:].rearrange("p (a b) -> p a b", b=128))),
                        reads=[psb], writes=[gtb])
                for gg in range(2):
                    ps, psb = pb[gg]
                    for hh in range(8):
                        hd = gg * 8 + hh
                        kb.op("pe", lambda e, hd=hd, hh=hh, ps=ps: e.matmul(ps[:, hh * 64:(hh + 1) * 64], lhsT=Bdec[:, hd, :], rhs=xdt[:, hd * 64:(hd + 1) * 64],
                                                                         start=True, stop=True), reads=[Bdec_b, xdt_b], writes=[psb], inc=(hh == 7))
                kb.op("pool", lambda e: e.tensor_tensor(Sf[:].rearrange("p (h d) -> p h d", d=64), Sf[:].rearrange("p (h d) -> p h d", d=64),
                                                        cd[:].unsqueeze(2).broadcast_to([128, 16, 64]), op=ALU.mult), reads=[Sf_b, sm_b, Sb_b], writes=[Sf_b])
                for gg in range(2):
                    ps, psb = pb[gg]
                    kb.op("dve", lambda e, gg=gg, ps=ps: e.tensor_tensor(Sf[:, gg * 512:(gg + 1) * 512], Sf[:, gg * 512:(gg + 1) * 512], ps[:], op=ALU.add),
                          reads=[psb, Sf_b], writes=[Sf_b])
            col0 = s * T + t0
            out_toks.append(kb.dma("sp", snd_g[:, col0:col0 + TG].rearrange("(c p) t -> p c t", p=128), gt_[:], reads=[gtb]))
        return out_toks


D = 1024; T = 2048; S = 4096; DFF = 2816


def fm(v):
    v = np.asarray(v, np.float32)
    return np.ascontiguousarray(v.reshape(-1, 128).T)


def ropec():
    inv = (1.0 / (10000.0 ** (np.arange(0, 32, 2, dtype=np.float32) / 32))).astype(np.float32)
    c = np.zeros((32, 2), np.float32)
    c[:16, 0] = inv; c[16:, 0] = inv
    c[:16, 1] = -1.0; c[16:, 1] = 1.0
    return c


def prep_mods(I, i, sub, tag):
    return {tag + "_adaw": np.ascontiguousarray(I["ada_w"][i][:, sub * 3072:(sub + 1) * 3072]),
            tag + "_adab": fm(I["ada_b"][i][sub * 3072:(sub + 1) * 3072]),
            tag + "_gain": fm(I["norm_gain"][i, sub])}


def prep_ffn(I, i, which, tag):
    d = prep_mods(I, i, 0 if which == 0 else 2, tag)
    d[tag + "_wgu"] = I["ffn_w_gu"][i, which]
    d[tag + "_wdn"] = I["ffn_w_down"][i, which]
    return d


def prep_mla(I, i, tag):
    j = i // 2
    d = prep_mods(I, i, 1, tag)
    wa = I["mla_w_a"][j]
    kr = wa[:, 640:672]
    d[tag + "_wa"] = np.ascontiguousarray(np.concatenate([wa, kr[:, 16:], kr[:, :16]], 1))
    wqb = I["mla_w_qb"][j].reshape(384, 16, 96)
    nope, rp = wqb[:, :, :64], wqb[:, :, 64:]
    wq = np.concatenate([nope, rp, rp[:, :, 16:], rp[:, :, :16]], 2)
    d[tag + "_wq"] = np.ascontiguousarray(wq.reshape(384, 2048))
    wkv = I["mla_w_kvb"][j].reshape(256, 16, 128)
    d[tag + "_wkn"] = np.ascontiguousarray(wkv[:, :, :64].reshape(256, 1024))
    d[tag + "_wv"] = np.ascontiguousarray(wkv[:, :, 64:].reshape(256, 1024))
    mg = np.zeros((128, 12), np.float32)
    gk = I["mla_k_gain"][j]; gq = I["mla_q_gain"][j]
    mg[:64, 0] = gk[:64]; mg[64:, 0] = gk[:64]
    mg[64:96, 1] = gk[64:]
    mg[64:80, 2] = gk[80:]; mg[80:96, 2] = gk[64:80]
    mg[:, 3:5] = fm(I["mla_kv_a_gain"][j])
    mg[:, 5:8] = fm(I["mla_q_a_gain"][j])
    mg[:96, 8] = gq
    mg[64:80, 9] = gq[80:]; mg[80:96, 9] = gq[64:80]
    d[tag + "_mg"] = mg
    d[tag + "_wo"] = I["mla_w_o"][j]
    return d


def prep_ssd(I, i, tag):
    j = i // 2
    d = prep_mods(I, i, 1, tag)
    d[tag + "_win"] = I["ssd_w_in"][j]
    d[tag + "_wout"] = I["ssd_w_out"][j]
    return d


def prep_ssd_rank(I, i, tag, r):
    j = i // 2
    cwf = I["ssd_conv_w"][j]
    cbf = I["ssd_conv_b"][j]
    chans = np.concatenate([np.arange(1024 * r, 1024 * r + 1024), 2048 + 256 * r + np.arange(256), 2560 + 256 * r + np.arange(256)])
    cw = cwf[:, chans].reshape(4, 12, 128).transpose(2, 1, 0).reshape(128, 48)
    cb = cbf[chans].reshape(12, 128).T
    cbrow = cbf[chans[:1280]][None, :]
    hp = np.concatenate([I["ssd_dt_bias"][j][16 * r:16 * r + 16], I["ssd_a_log"][j][16 * r:16 * r + 16], I["ssd_d"][j][16 * r:16 * r + 16]])
    hp = np.broadcast_to(hp[None, :], (128, 48))
    ng = np.broadcast_to(I["ssd_norm_gain"][j][1024 * r:1024 * r + 1024][None, :], (128, 1024))
    f = lambda a: np.ascontiguousarray(a, dtype=np.float32)
    return {tag + "_cw": f(cw), tag + "_cb": f(cb), tag + "_cbrow": f(cbrow), tag + "_hp": f(hp), tag + "_ng": f(ng)}


import ml_dtypes
_BF = ml_dtypes.bfloat16
_PROGS = {}


def _build(kind):
    if kind in _PROGS:
        return _PROGS[kind]
    ph, mixer = kind
    P = Prog(None)
    P.setup()
    toks = []
    if ph == "A":
        xT = P.din("xT", [D, T]); P.load_x(xT)
        P.ffn("F0")
        yT = P.dout("yT", [D, T])
        toks += P.store_x(yT)
        if mixer == "mla":
            snd = {"QT": P.dout("QT", [16 * 96, T], BF16), "KT": P.dout("KT", [16 * 96, T], BF16),
                   "V": P.dout("V", [T, 1024], BF16), "RK": P.dout("RK", [T, 16], F32)}
            toks += P.mla_p1("M", snd)
        else:
            snd = {"XBC": P.dout("XBC", [3072, T], BF16), "Z": P.dout("Z", [T, 2048], BF16), "DT": P.dout("DT", [T, 32], F32)}
            toks += P.ssd_p1("M", snd)
    elif ph == "B":
        if mixer == "mla":
            g_ap = {"QT": GBuf(P.din("gQT", [2 * 16 * 96, T], BF16), 1536, 1536), "KT": GBuf(P.din("gKT", [2 * 16 * 96, T], BF16), 1536, 1536),
                    "V": GBuf(P.din("gV", [2 * T, 1024], BF16), T, T), "RK": GBuf(P.din("gRK", [2 * T, 16], F32), T, T)}
            snd_o = P.dout("O", [512, S], BF16)
            toks += P.mla_p2("M", g_ap, snd_o)
        else:
            g_ap = {"XBC": GBuf(P.din("gXBC", [2 * 3072, T], BF16), 3072, 3072), "Z": GBuf(P.din("gZ", [2 * T, 2048], BF16), T, T),
                    "DT": GBuf(P.din("gDT", [2 * T, 32], F32), T, T)}
            snd_g = P.dout("O", [1024, S], BF16)
            toks += P.ssd_p2("M", g_ap, snd_g)
    else:
        xT = P.din("xT", [D, T]); P.load_x(xT)
        if mixer == "mla":
            gO = GBuf(P.din("gO", [1024, S], BF16), 512, 512)
            P.mixer_p3("M", gO, 8, True, "_wo")
        else:
            gO = GBuf(P.din("gO", [2048, S], BF16), 1024, 1024)
            P.mixer_p3("M", gO, 16, True, "_wout")
        P.ffn("F1")
        yT = P.dout("yT", [D, T])
        toks += P.store_x(yT)
    P.kb.wait_all("sp", toks)
    P.kb.emit_all()
    _PROGS[kind] = P
    return P


def _launch(P, cands):
    ims = []
    for c in range(8):
        d = {}
        for n in P.inputs:
            for src in cands[c]:
                if n in src:
                    d[n] = src[n]
                    break
            else:
                raise KeyError(n)
        ims.append(d)
    res = run_bass_kernel_spmd(P.nc, ims, core_ids=list(range(8)))
    return res.results


def kernel(**I):
    I = {k: np.asarray(v) for k, v in I.items()}
    x = I["x"].astype(np.float32)
    B = x.shape[0]
    rc = ropec()
    core_common = []
    for c in range(8):
        b, r = c // 2, c % 2
        core_common.append({"cT": fm(I["c"][b]), "ropec": rc,
                            "pos": np.ascontiguousarray(I["positions"][b][None, r * T:(r + 1) * T]).astype(np.int32)})
    xs = [np.ascontiguousarray(x[c // 2, (c % 2) * T:(c % 2 + 1) * T].T) for c in range(8)]
    for i in range(4):
        mixer = "mla" if i % 2 == 0 else "ssd"
        W0 = prep_ffn(I, i, 0, "F0")
        W1 = prep_ffn(I, i, 1, "F1")
        WM = prep_mla(I, i, "M") if mixer == "mla" else prep_ssd(I, i, "M")
        WR = [prep_ssd_rank(I, i, "M", r) for r in range(2)] if mixer == "ssd" else [{}, {}]
        P = _build(("A", mixer))
        res = _launch(P, [[{"xT": xs[c]}, core_common[c], W0, WM] for c in range(8)])
        xs = [res[c]["yT"] for c in range(8)]
        names = ["QT", "KT", "V", "RK"] if mixer == "mla" else ["XBC", "Z", "DT"]
        gat = []
        for pr in range(4):
            g = {"g" + n: np.concatenate([res[2 * pr][n], res[2 * pr + 1][n]], 0) for n in names}
            gat += [g, g]
        del res
        P = _build(("B", mixer))
        res = _launch(P, [[gat[c], core_common[c], WM, WR[c % 2]] for c in range(8)])
        gO = []
        for pr in range(4):
            g = {"gO": np.concatenate([res[2 * pr]["O"], res[2 * pr + 1]["O"]], 0)}
            gO += [g, g]
        del res, gat
        P = _build(("C", mixer))
        res = _launch(P, [[{"xT": xs[c]}, gO[c], core_common[c], W1, WM] for c in range(8)])
        xs = [res[c]["yT"] for c in range(8)]
        del res
    out = np.empty_like(x)
    for c in range(8):
        out[c // 2, (c % 2) * T:(c % 2 + 1) * T] = xs[c].T
    return out


def _build_fused():
    if "fused" in _PROGS:
        return _PROGS["fused"]
    P = Prog(None)
    P.setup()
    kb = P.kb
    xT = P.din("xT", [D, T]); P.load_x(xT)
    P.setup_global_mods()
    for i in range(4):
        mixer = "mla" if i % 2 == 0 else "ssd"
        L = "L%d" % i
        P.ffn(L + "F0")
        if mixer == "mla":
            shapes = {"QT": ([16 * 96, T], BF16, 384), "KT": ([16 * 96, T], BF16, 384), "V": ([T, 1024], BF16, 1024), "RK": ([T, 16], F32, T)}
        else:
            shapes = {"XBC": ([3072, T], BF16, 512), "Z": ([T, 2048], BF16, 512), "DT": ([T, 32], F32, T)}
        snd = {n: P.dint(L + "s" + n, shp, dt) for n, (shp, dt, R) in shapes.items()}
        gat = {n: GBuf(P.dint(L + "g" + n, [2 * shp[0], shp[1]], dt), shp[0], R, snd[n]) for n, (shp, dt, R) in shapes.items()}
        if mixer == "ssd":
            snd["XBC_g"] = gat["XBC"]
            snd["groups"] = GROUPS
        toks = P.mla_p1(L + "M", snd) if mixer == "mla" else P.ssd_p1(L + "M", snd)
        for n in shapes:
            gat[n].gather(kb, toks, GROUPS)
        if mixer == "mla":
            snd_o = P.dint(L + "sO", [512, S], BF16)
            gat_o = GBuf(P.dint(L + "gO", [1024, S], BF16), 512, 256, snd_o)
            toks = P.mla_p2(L + "M", gat, snd_o)
        else:
            snd_o = P.dint(L + "sO", [1024, S], BF16)
            gat_o = GBuf(P.dint(L + "gO", [2048, S], BF16), 1024, 256, snd_o)
            toks = P.ssd_p2(L + "M", gat, snd_o)
        gat_o.gather(kb, toks, GROUPS)
        if mixer == "mla":
            P.mixer_p3(L + "M", gat_o, 8, False, "_wo")
        else:
            P.mixer_p3(L + "M", gat_o, 16, False, "_wout")
        P.ffn(L + "F1")
    yT = P.dout("yT", [D, T])
    toks = P.store_x(yT)
    kb.wait_all("sp", toks)
    kb.emit_all()
    _PROGS["fused"] = P
    return P


def kernel_fused(**I):
    I = {k: np.asarray(v) for k, v in I.items()}
    x = I["x"].astype(np.float32)
    rc = ropec()
    P = _build_fused()
    Wall = {}
    WR = [{}, {}]
    for i in range(4):
        L = "L%d" % i
        Wall.update(prep_ffn(I, i, 0, L + "F0"))
        Wall.update(prep_ffn(I, i, 1, L + "F1"))
        if i % 2 == 0:
            Wall.update(prep_mla(I, i, L + "M"))
        else:
            Wall.update(prep_ssd(I, i, L + "M"))
            for r in range(2):
                WR[r].update(prep_ssd_rank(I, i, L + "M", r))
    c4T = np.ascontiguousarray(np.stack([fm(I["c"][b_]) for b_ in range(4)], axis=2).reshape(128, 32))
    adab_all = np.ascontiguousarray(np.concatenate([fm(I["ada_b"][l]) for l in range(4)], axis=1))
    gain_all = np.ascontiguousarray(np.concatenate([fm(I["norm_gain"][l, s_]) for l in range(4) for s_ in range(3)], axis=1))
    gm = {"c4T": c4T, "adab_all": adab_all, "gain_all": gain_all}
    cands = []
    for c in range(8):
        b, r = c // 2, c % 2
        es = np.zeros((128, 4), np.float32); es[:, b] = 1.0
        cc = {"cT": fm(I["c"][b]), "ropec": rc, "esel": es,
              "adawc": np.ascontiguousarray(I["ada_w"][:, :, c * 1152:(c + 1) * 1152].reshape(4 * D, 1152)), "pos": np.ascontiguousarray(I["positions"][b][None, r * T:(r + 1) * T]).astype(np.int32),
              "xT": np.ascontiguousarray(x[b, r * T:(r + 1) * T].T)}
        cands.append([cc, gm, Wall, WR[r]])
    res = _launch(P, cands)
    out = np.empty_like(x)
    for c in range(8):
        out[c // 2, (c % 2) * T:(c % 2 + 1) * T] = res[c]["yT"].T
    return out


kernel_multi = kernel
kernel = kernel_fused
```

```python
import numpy as np
from contextlib import ExitStack
import concourse.bass as bass
import concourse.mybir as mybir
from concourse.bass_utils import run_bass_kernel_spmd


F32 = mybir.dt.float32
BF16 = mybir.dt.bfloat16
I32 = mybir.dt.int32
AF = mybir.ActivationFunctionType
ALU = mybir.AluOpType
AX = mybir.AxisListType

ENGS = ("pe", "act", "dve", "pool", "sp")


class Buf:
    __slots__ = ("w", "r", "name")

    def __init__(self, name=""):
        self.w = None
        self.r = []
        self.name = name


class KB:
    def __init__(self, nc, ctx):
        self.nc = nc
        self.ctx = ctx
        self.q = {e: [] for e in ENGS}
        self.cnt = {}
        self.semh = {}
        self.seen = {e: {} for e in ENGS}
        self.pe_pending = []
        for e in ENGS:
            self.new_sem("E_" + e)
        self.ndma = 0
        self.NPOOL = 64
        self.buf_key = {}
        self.pool_i = 0

    def phase_reset(self):
        self.buf_key = {}
        self.pool_i = 0

    def _dma_key(self, reads, writes):
        prim = (list(writes) + list(reads))[0]
        k = self.buf_key.get(id(prim))
        if k is None:
            assert self.pool_i < self.NPOOL, "DMA semaphore pool exhausted in this phase"
            k = "DS%d" % self.pool_i
            self.pool_i += 1
            self.buf_key[id(prim)] = k
        return k

    def new_sem(self, key):
        self.semh[key] = self.ctx.enter_context(self.nc.semaphore(key))
        self.cnt[key] = 0
        return key

    def sb(self, name, shape, dtype):
        return self.ctx.enter_context(self.nc.sbuf_tensor(name, list(shape), dtype))

    def ps(self, name, shape, dtype=F32):
        return self.ctx.enter_context(self.nc.psum_tensor(name, list(shape), dtype))

    def _waits(self, eng, toks):
        ws = []
        best = {}
        for t in toks:
            if t is None:
                continue
            s, v = t
            if self.seen[eng].get(s, 0) >= v:
                continue
            if best.get(s, 0) < v:
                best[s] = v
        for s, v in best.items():
            self.seen[eng][s] = v
            ws.append((s, v))
        return ws

    def op(self, eng, fn, reads=(), writes=(), inc=True, extra=()):
        toks = list(extra)
        for b in reads:
            toks.append(b.w)
        for b in writes:
            toks.append(b.w)
            toks.extend(b.r)
        if eng == "pe":
            toks = [t for t in toks if t is not None and t[0] != "E_pe"]
        ws = self._waits(eng, toks)
        key = "E_" + eng
        if eng == "pe" and not inc:
            self.pe_pending.append((tuple(reads), tuple(writes)))
            tok = None
        else:
            self.cnt[key] += 1
            tok = (key, self.cnt[key])
            allrw = [(tuple(reads), tuple(writes))]
            if eng == "pe":
                allrw += self.pe_pending
                self.pe_pending = []
            for rs, wr in allrw:
                for b in rs:
                    b.r.append(tok)
                for b in wr:
                    b.w = tok
                    b.r = []
        semh = self.semh

        def emit(e, fn=fn, ws=ws, tok=tok):
            for s, v in ws:
                e.wait_ge(semh[s], v)
            ins = fn(e)
            if tok is not None:
                ins.then_inc(semh[tok[0]], 1)
        self.q[eng].append(emit)
        return tok

    def dma(self, eng, out, in_, reads=(), writes=(), extra=(), key=None, **kw):
        toks = list(extra)
        for b in reads:
            toks.append(b.w)
        for b in writes:
            toks.append(b.w)
            toks.extend(b.r)
        ws = self._waits(eng, toks)
        key = self._dma_key(reads, writes)
        if key not in self.semh:
            self.new_sem(key)
        self.cnt[key] += 16
        tok = (key, self.cnt[key])
        for b in reads:
            b.r.append(tok)
        for b in writes:
            b.w = tok
            b.r = []
        semh = self.semh

        def emit(e, ws=ws, tok=tok, out=out, in_=in_, kw=kw):
            for s, v in ws:
                e.wait_ge(semh[s], v)
            try:
                e.dma_start(out=out, in_=in_, **kw).then_inc(semh[tok[0]], 16)
            except Exception:
                print("DMA FAIL out", out.shape, out.ap, "in", in_.shape, in_.ap, flush=True)
                raise
        self.q[eng].append(emit)
        return tok

    def wait_all(self, eng, toks):
        ws = self._waits(eng, toks)
        semh = self.semh

        def emit(e, ws=ws):
            for s, v in ws:
                e.wait_ge(semh[s], v)
        self.q[eng].append(emit)

    def emit_all(self):
        nc = self.nc
        q = self.q
        with nc.Block() as block:
            @block.sync
            def _(e):
                for f in q["sp"]:
                    f(e)

            @block.tensor
            def _(e):
                for f in q["pe"]:
                    f(e)

            @block.scalar
            def _(e):
                for f in q["act"]:
                    f(e)

            @block.vector
            def _(e):
                for f in q["dve"]:
                    f(e)

            @block.gpsimd
            def _(e):
                for f in q["pool"]:
                    f(e)


CC_INC = 1


def _kb_cc(self, kind, ins, outs, groups, reads=(), writes=(), extra=()):
    toks = list(extra)
    for b in reads:
        toks.append(b.w)
    for b in writes:
        toks.append(b.w)
        toks.extend(b.r)
    ws = self._waits("pool", toks)
    key = "CC"
    if key not in self.semh:
        self.new_sem(key)
    self.cnt[key] += CC_INC
    tok = (key, self.cnt[key])
    for b in reads:
        b.r.append(tok)
    for b in writes:
        b.w = tok
        b.r = []
    semh = self.semh

    def emit(e, ws=ws, tok=tok):
        for s, v in ws:
            e.wait_ge(semh[s], v)
        e.collective_compute(kind, ALU.bypass, replica_groups=groups, ins=ins, outs=outs).then_inc(semh[tok[0]], CC_INC)
    self.q["pool"].append(emit)
    return tok


KB.cc = _kb_cc


def _kb_dma_if(self, eng, cond, out, in_true, in_false, reads=(), writes=(), extra=()):
    toks = list(extra)
    for b in reads:
        toks.append(b.w)
    for b in writes:
        toks.append(b.w)
        toks.extend(b.r)
    ws = self._waits(eng, toks)
    key = self._dma_key(reads, writes)
    if key not in self.semh:
        self.new_sem(key)
    self.cnt[key] += 16
    tok = (key, self.cnt[key])
    for b in reads:
        b.r.append(tok)
    for b in writes:
        b.w = tok
        b.r = []
    semh = self.semh

    def emit(e, ws=ws, tok=tok):
        for s, v in ws:
            e.wait_ge(semh[s], v)
        with e.If(cond):
            e.dma_start(out=out, in_=in_true).then_inc(semh[tok[0]], 16)
        with e.Else():
            e.dma_start(out=out, in_=in_false).then_inc(semh[tok[0]], 16)
    self.q[eng].append(emit)
    return tok


KB.dma_if = _kb_dma_if


D = 1024
KC = 8
T = 2048
S = 4096
TG = 512
NG = T // TG
DFF = 2816
EPS = 1e-6
NH = 16
DH = 96
QL = 384
KVL = 256
SC96 = 96 ** -0.5
GROUPS = [[0, 1], [2, 3], [4, 5], [6, 7]]


class GBuf:
    def __init__(self, gat_ap, rows, R, snd_ap=None):
        self.gat, self.nrows, self.R, self.snd = gat_ap, rows, R, snd_ap

    def rows(self, s, q0, n):
        R = self.R
        k = q0 // R
        assert (q0 + n - 1) // R == k, (q0, n, R)
        base = k * 2 * R + s * R + q0 % R
        return self.gat[base:base + n, :]

    def gather_chunk(self, kb, k, toks, groups):
        R = self.R
        self.done = getattr(self, "done", set())
        self.done.add(k)
        kb.cc("AllGather", [self.snd[k * R:(k + 1) * R, :]], [self.gat[k * 2 * R:(k + 1) * 2 * R, :]], groups, extra=toks)

    def gather(self, kb, toks, groups):
        R = self.R
        for k in range(self.nrows // R):
            if k in getattr(self, "done", set()):
                continue
            kb.cc("AllGather", [self.snd[k * R:(k + 1) * R, :]], [self.gat[k * 2 * R:(k + 1) * 2 * R, :]], groups, extra=toks)


class Prog:
    def __init__(self, steps, name="p"):
        self.steps = steps
        self.nc = bass.Bass("TRN2", target_bir_lowering=False)
        self.ctx = ExitStack()
        self.kb = KB(self.nc, self.ctx)
        self.debug = False
        self.dbg_toks = []
        self.inputs = {}
        self.outputs = {}
        self.rot_i = 0

    def din(self, name, shape, dt=F32):
        self.inputs[name] = (tuple(shape), dt)
        return self.nc.dram_tensor(name, list(shape), dt, kind="ExternalInput").ap()

    def dout(self, name, shape, dt=F32):
        self.outputs[name] = (tuple(shape), dt)
        return self.nc.dram_tensor(name, list(shape), dt, kind="ExternalOutput").ap()

    def dint(self, name, shape, dt=F32):
        return self.nc.dram_tensor(name, list(shape), dt).ap()

    def setup(self):
        kb = self.kb
        self.x = kb.sb("x", [128, KC, T], F32)
        self.xb = [[Buf() for g in range(NG)] for k in range(KC)]
        self.SCR = 35328
        self.scratch = kb.sb("scratch", [128, self.SCR], F32)
        self.scr_off = 0
        self.ones = kb.sb("ones", [128, 128], BF16); self.ones_b = Buf()
        self.onesf = kb.sb("onesf", [128, 128], F32); self.onesf_b = Buf()
        kb.op("pool", lambda e: e.memset(self.ones[:], 1.0), writes=[self.ones_b])
        kb.op("pool", lambda e: e.memset(self.onesf[:], 1.0), writes=[self.onesf_b])
        self.c_sb = kb.sb("c_sb", [128, KC], F32); self.c_b = Buf()
        self.sc = kb.sb("sc", [128, KC], F32); self.sc_b = Buf()
        cT = self.din("cT", [128, KC])
        kb.dma("sp", self.c_sb[:], cT, writes=[self.c_b])
        kb.op("act", lambda e: e.activation(self.sc[:], self.c_sb[:], AF.Silu), reads=[self.c_b], writes=[self.sc_b])
        self.pb = [(kb.ps("pb%d" % i, [128, 512]), Buf()) for i in range(8)]
        self.mods = kb.sb("mods", [128, 24], F32); self.mods_b = Buf()
        self.A = kb.sb("A", [128, KC], F32); self.A_b = Buf()
        self.hg = kb.sb("hg", [128, KC], F32); self.hg_b = Buf()
        self.gain_sb = kb.sb("gain_sb", [128, KC], F32); self.gain_b = Buf()
        self.adab_sb = kb.sb("adab_sb", [128, 24], F32); self.adab_b = Buf()
        self.aw_i = 0

    def carve(self, shape, dt, p0=0):
        n = int(np.prod(shape[1:]))
        words = n if dt == F32 or dt == I32 else (n + 1) // 2
        assert self.scr_off + words <= self.SCR, ("scratch overflow", self.scr_off, words)
        v = self.scratch[:, self.scr_off:self.scr_off + words]
        self.scr_off += words
        if dt != F32:
            v = v.bitcast(dt)
        if len(shape) == 3:
            v = v.rearrange("p (a b) -> p a b", b=shape[2])
        elif len(shape) == 4:
            v = v.rearrange("p (a b c) -> p a b c", b=shape[2], c=shape[3])
        return v[p0:p0 + shape[0]] if shape[0] < 128 else v

    def phase_begin(self):
        self.barrier()
        self.kb.phase_reset()
        self.scr_off = 0

    def norm_scratch(self):
        self.h = self.carve([128, KC, T], BF16)
        self.hb = [Buf() for g in range(NG)]
        self.sq = self.carve([128, KC, TG], BF16); self.sq_b = Buf()
        self.sd = self.carve([128, TG], F32); self.sd_b = Buf()
        self.rstd = self.carve([128, TG], F32); self.rstd_b = Buf()
        self.tmp = [(self.carve([128, TG], F32), Buf()) for i in range(2)]

    def dbg(self, name, ap, bufs, dt=F32):
        if not getattr(self, "debug", False):
            return
        shp = list(ap.shape)
        d = self.dout("dbg_" + name, shp, dt)
        self.dbg_toks.append(self.kb.dma("sp", d, ap, reads=bufs))

    def dsel(self, out, in0, in1, reads=(), writes=()):
        if not hasattr(self, "cond0"):
            rr = self.nc.sync.partition_id() % 2
            self.cond0 = (rr == 0)
        return self.kb.dma_if("sp", self.cond0, out, in0, in1, reads=reads, writes=writes)

    def rot(self, lo=0, n=2):
        i = lo + (self.rot_i % n)
        self.rot_i += 1
        return self.pb[i]

    def load_x(self, xT):
        for k in range(KC):
            self.kb.dma("sp", self.x[:, k, :], xT[k * 128:(k + 1) * 128, :], writes=self.xb[k])

    def store_x(self, yT):
        toks = []
        for k in range(KC):
            toks.append(self.kb.dma("sp", yT[k * 128:(k + 1) * 128, :], self.x[:, k, :], reads=self.xb[k]))
        return toks

    def barrier(self):
        kb = self.kb
        toks = [(k, v) for k, v in kb.cnt.items() if v > 0]
        for e in ENGS:
            kb.wait_all(e, toks)

    def setup_global_mods(self):
        kb = self.kb
        self.global_mods = True
        self.phase_begin()
        C = self.carve
        adawc = self.din("adawc", [4 * D, 1152])
        c4T_d = self.din("c4T", [128, 32])
        adab_d = self.din("adab_all", [128, 288])
        gain_d = self.din("gain_all", [128, 96])
        esel_d = self.din("esel", [128, 4])
        self.shift_all = kb.sb("shift_all", [128, 96], F32)
        self.A_all = kb.sb("A_all", [128, 96], F32)
        self.hg_all = kb.sb("hg_all", [128, 96], F32)
        self.gm_b = Buf()
        shift_all, A_all, hg_all = self.shift_all, self.A_all, self.hg_all
        c4 = C([128, 8, 4], F32); c4_b = Buf()
        sc4 = C([128, 8, 4], F32); sc4_b = Buf()
        adab = C([128, 288], F32); adab_b = Buf()
        gains = C([128, 96], F32); gains_b = Buf()
        esel = C([128, 4], F32); esel_b = Buf()
        gsc = C([128, 96], F32); gsc_b = Buf()
        aw = [(C([128, KC, 384], F32), Buf()) for i in range(2)]
        mp = C([128, 4, 36], F32); mp_b = Buf()
        gsb = C([128, 32, 36], F32); gsb_b = Buf()
        acc = C([128, 8, 36], F32); acc_b = Buf()
        mfull = C([128, 4, 72], F32); mfull_b = Buf()
        kb.dma("sp", c4[:], c4T_d.rearrange("p (k b) -> p k b", b=4), writes=[c4_b])
        kb.dma("sp", adab[:], adab_d, writes=[adab_b])
        kb.dma("sp", gains[:], gain_d, writes=[gains_b])
        kb.dma("sp", esel[:], esel_d, writes=[esel_b])
        kb.op("act", lambda e: e.activation(sc4[:], c4[:], AF.Silu), reads=[c4_b], writes=[sc4_b])
        ps, psb = self.pb[7]
        wv = adawc.rearrange("(l k p) n -> p l k n", l=4, p=128)
        bi = 0
        for l in range(4):
            for blk in range(3):
                wt, wb = aw[bi % 2]; bi += 1
                kb.dma("sp", wt[:], wv[:, l, :, blk * 384:(blk + 1) * 384], writes=[wb])
                for j3 in range(3):
                    col = (l * 9 + blk * 3 + j3) * 4
                    for k in range(KC):
                        kb.op("pe", lambda e, col=col, k=k, j3=j3, wt=wt: e.matmul(
                            ps[:, col:col + 4], lhsT=wt[:, k, j3 * 128:(j3 + 1) * 128], rhs=sc4[:, k, :], start=(k == 0), stop=(k == KC - 1)),
                            reads=[wb, sc4_b], writes=[psb], inc=(k == KC - 1 and j3 == 2))
        kb.op("dve", lambda e: e.tensor_copy(mp[:], ps[:, 0:144].rearrange("p (a b) -> p b a", b=4)), reads=[psb], writes=[mp_b])
        snd = self.dint("gm_snd", [512, 36], F32)
        gat = self.dint("gm_gat", [8 * 512, 36], F32)
        t0 = kb.dma("sp", snd.rearrange("(b p) a -> p b a", p=128), mp[:], reads=[mp_b])
        kb.cc("AllGather", [snd], [gat], [[0, 1, 2, 3, 4, 5, 6, 7]], extra=[t0])
        cct = ("CC", kb.cnt["CC"])
        kb.dma("sp", gsb[:], gat.rearrange("(cb p) a -> p cb a", p=128), writes=[gsb_b], extra=[cct])
        g4 = gsb[:].rearrange("p (c b) a -> p c b a", b=4)
        kb.op("dve", lambda e: e.tensor_scalar(acc[:], g4[:, :, 0, :], esel[:, 0:1], None, op0=ALU.mult), reads=[gsb_b, esel_b], writes=[acc_b])
        for b_ in range(1, 4):
            kb.op("dve", lambda e, b_=b_: e.scalar_tensor_tensor(acc[:], g4[:, :, b_, :], esel[:, b_:b_ + 1], acc[:], op0=ALU.mult, op1=ALU.add),
                  reads=[gsb_b, esel_b, acc_b], writes=[acc_b])
        kb.op("dve", lambda e: e.tensor_tensor(mfull[:].rearrange("p l (c j) -> p l c j", j=9), acc[:].rearrange("p c (l j) -> p l c j", j=9),
                                               adab[:].rearrange("p (l c j) -> p l c j", l=4, j=9), op=ALU.add),
              reads=[acc_b, adab_b], writes=[mfull_b])
        m4 = mfull[:].rearrange("p l (s t) -> p l s t", t=24)
        v96 = lambda t_: t_[:].rearrange("p (l s k) -> p l s k", l=4, s=3)
        kb.op("dve", lambda e: e.tensor_copy(v96(shift_all), m4[:, :, :, 0:8]), reads=[mfull_b], writes=[self.gm_b])
        kb.op("dve", lambda e: e.tensor_scalar(v96(A_all), m4[:, :, :, 8:16], 1.0, None, op0=ALU.add), reads=[mfull_b], writes=[self.gm_b])
        kb.op("dve", lambda e: e.tensor_tensor(A_all[:], A_all[:], gains[:], op=ALU.mult), reads=[self.gm_b, gains_b], writes=[self.gm_b])
        kb.op("pool", lambda e: e.memset(gsc[:], 0.5), writes=[gsc_b])
        kb.op("pool", lambda e: e.memset(v96(gsc)[:, :, 1, :], 1.0), reads=[gsc_b], writes=[gsc_b])
        kb.op("dve", lambda e: e.tensor_tensor(v96(hg_all), m4[:, :, :, 16:24], v96(gsc), op=ALU.mult), reads=[mfull_b, gsc_b], writes=[self.gm_b])

    def emit_mods(self, tag, gate_scale):
        kb = self.kb
        if getattr(self, "global_mods", False):
            idx = int(tag[1]) * 3 + {"F0": 0, "M": 1, "F1": 2}[tag[2:]]
            self.mods = self.shift_all[:, idx * 8:(idx + 1) * 8]
            self.A = self.A_all[:, idx * 8:(idx + 1) * 8]
            self.hg = self.hg_all[:, idx * 8:(idx + 1) * 8]
            self.mods_b = self.A_b = self.hg_b = self.gm_b
            return
        save_off = self.scr_off
        self.aw = [(self.carve([128, KC, 256], F32), Buf()) for i in range(2)]
        adaw = self.din(tag + "_adaw", [D, 3 * D])
        adab = self.din(tag + "_adab", [128, 24])
        gain = self.din(tag + "_gain", [128, KC])
        kb.dma("sp", self.gain_sb[:], gain, writes=[self.gain_b])
        kb.dma("sp", self.adab_sb[:], adab, writes=[self.adab_b])
        ps_mod, ps_mod_b = self.pb[7]
        wv = adaw.rearrange("(k p) n -> p k n", p=128)
        sc = self.sc
        for blk in range(12):
            wt, wb = self.aw[self.aw_i % 2]
            slot = self.aw_i % 2
            self.aw_i += 1
            kb.dma("sp", wt[:], wv[:, :, blk * 256:(blk + 1) * 256], writes=[wb], key="aw%d" % slot)
            for j in range(2):
                col = blk * 2 + j
                for k in range(KC):
                    kb.op("pe", lambda e, col=col, k=k, j=j, wt=wt: e.matmul(
                        ps_mod[:, col:col + 1], lhsT=wt[:, k, j * 128:(j + 1) * 128], rhs=sc[:, k:k + 1],
                        start=(k == 0), stop=(k == KC - 1)),
                        reads=[wb, self.sc_b], writes=[ps_mod_b], inc=(k == KC - 1 and j == 1))
        mods, A, hg = self.mods, self.A, self.hg
        kb.op("dve", lambda e: e.tensor_tensor(mods[:], ps_mod[:, 0:24], self.adab_sb[:], op=ALU.add),
              reads=[ps_mod_b, self.adab_b], writes=[self.mods_b])
        kb.op("dve", lambda e: e.scalar_tensor_tensor(A[:], mods[:, 8:16], 1.0, self.gain_sb[:], op0=ALU.add, op1=ALU.mult),
              reads=[self.mods_b, self.gain_b], writes=[self.A_b])
        kb.op("dve", lambda e: e.tensor_scalar(hg[:], mods[:, 16:24], gate_scale, None, op0=ALU.mult),
              reads=[self.mods_b], writes=[self.hg_b])
        self.barrier()
        self.scr_off = save_off

    def rstd_from_ps(self, ps, psb, npart, dim, out, outb, sdt=None):
        kb = self.kb
        sd, sd_b = sdt if sdt is not None else (self.sd, self.sd_b)
        kb.op("dve", lambda e: e.tensor_scalar(sd[0:npart, :], ps[0:npart, :], 1.0 / dim, EPS, op0=ALU.mult, op1=ALU.add),
              reads=[psb], writes=[sd_b])
        kb.op("act", lambda e: e.activation(sd[0:npart, :], sd[0:npart, :], AF.Sqrt), reads=[sd_b], writes=[sd_b])
        kb.op("dve", lambda e: e.reciprocal(out[0:npart, :], sd[0:npart, :]), reads=[sd_b], writes=[outb])

    def emit_h(self, groups=None):
        kb = self.kb
        x, h, sq, ones = self.x, self.h, self.sq, self.ones
        rstd, A, mods = self.rstd, self.A, self.mods
        ps_ssq, ps_ssq_b = self.pb[6]
        for g in (range(NG) if groups is None else groups):
            gs = slice(g * TG, (g + 1) * TG)
            kb.op("act", lambda e, gs=gs: e.activation(sq[:], x[:, :, gs], AF.Square),
                  reads=[self.xb[k][g] for k in range(KC)], writes=[self.sq_b])
            for k in range(KC):
                kb.op("pe", lambda e, k=k: e.matmul(ps_ssq[:], lhsT=ones[:], rhs=sq[:, k, :], start=(k == 0), stop=(k == KC - 1)),
                      reads=[self.ones_b, self.sq_b], writes=[ps_ssq_b], inc=(k == KC - 1))
            self.rstd_from_ps(ps_ssq, ps_ssq_b, 128, D, self.rstd, self.rstd_b)
            for k in range(KC):
                tt, tb = self.tmp[k % 2]
                kb.op("dve", lambda e, k=k, gs=gs, tt=tt: e.scalar_tensor_tensor(
                    tt[:], x[:, k, gs], A[:, k:k + 1], rstd[:], op0=ALU.mult, op1=ALU.mult),
                    reads=[self.xb[k][g], self.A_b, self.rstd_b], writes=[tb])
                kb.op("act", lambda e, k=k, gs=gs, tt=tt: e.activation(
                    h[:, k, gs], tt[:], AF.Identity, bias=mods[:, k:k + 1], scale=1.0),
                    reads=[tb, self.mods_b], writes=[self.hb[g]])

    def ffn(self, tag):
        kb = self.kb
        self.phase_begin()
        self.emit_mods(tag, 0.5)
        self.norm_scratch()
        self.emit_h([0])
        wgu = self.din(tag + "_wgu", [D, 2 * DFF])
        wdn = self.din(tag + "_wdn", [DFF, D])
        x, h, hg = self.x, self.h, self.hg
        if True:
            sbl = lambda n, s, d: self.carve(s, d)
            blocks = [(0, 2), (2, 5), (7, 5), (12, 5), (17, 5)]
            NBMAX = 5
            wg = [(sbl(tag + "wg%d" % i, [128, KC, NBMAX * 128], BF16), Buf()) for i in range(2)]
            wu = [(sbl(tag + "wu%d" % i, [128, KC, NBMAX * 128], BF16), Buf()) for i in range(2)]
            wd = [(sbl(tag + "wd%d" % i, [128, NBMAX, D], BF16), Buf()) for i in range(2)]
            act = [(sbl(tag + "act%d" % i, [128, NBMAX, TG], BF16), Buf()) for i in range(2)]
            sg = [(sbl(tag + "sg%d" % i, [128, TG], F32), Buf()) for i in range(2)]
            wguv = wgu.rearrange("(k p) n -> p k n", p=128)
            wdnv = wdn.rearrange("(j p) n -> p j n", p=128)
            it = 0
            ci = 0
            for bi, (j0, nb) in enumerate(blocks):
                s = bi % 2
                wgt, wgb = wg[s]; wut, wub = wu[s]; wdt, wdb = wd[s]
                kb.dma("pool", wgt[:, :, 0:nb * 128], wguv[:, :, j0 * 128:(j0 + nb) * 128], writes=[wgb], key=tag + "wg%d" % s)
                kb.dma("pool", wut[:, :, 0:nb * 128], wguv[:, :, DFF + j0 * 128:DFF + (j0 + nb) * 128], writes=[wub], key=tag + "wu%d" % s)
                kb.dma("pool", wdt[:, 0:nb, :], wdnv[:, j0:j0 + nb, :], writes=[wdb], key=tag + "wd%d" % s)
                for g in range(NG):
                    gs = slice(g * TG, (g + 1) * TG)
                    at, ab = act[it % 2]
                    it += 1
                    if bi == 0 and g + 1 < NG:
                        self.emit_h([g + 1])
                    for jj in range(nb):
                        pg, pgb = self.pb[ci % 2]; pu, pub = self.pb[2 + ci % 2]
                        sgt, sgb = sg[ci % 2]
                        ci += 1
                        for k in range(KC):
                            kb.op("pe", lambda e, k=k, jj=jj, pg=pg, wgt=wgt, gs=gs: e.matmul(
                                pg[:], lhsT=wgt[:, k, jj * 128:(jj + 1) * 128], rhs=h[:, k, gs], start=(k == 0), stop=(k == KC - 1)),
                                reads=[wgb, self.hb[g]], writes=[pgb], inc=(k == KC - 1))
                        for k in range(KC):
                            kb.op("pe", lambda e, k=k, jj=jj, pu=pu, wut=wut, gs=gs: e.matmul(
                                pu[:], lhsT=wut[:, k, jj * 128:(jj + 1) * 128], rhs=h[:, k, gs], start=(k == 0), stop=(k == KC - 1)),
                                reads=[wub, self.hb[g]], writes=[pub], inc=(k == KC - 1))
                        kb.op("act", lambda e, pg=pg, sgt=sgt: e.activation(sgt[:], pg[:], AF.Silu), reads=[pgb], writes=[sgb])
                        kb.op("dve", lambda e, jj=jj, at=at, sgt=sgt, pu=pu: e.tensor_tensor(at[:, jj, :], sgt[:], pu[:], op=ALU.mult),
                              reads=[sgb, pub], writes=[ab])
                    for m in range(KC):
                        py, pyb = self.pb[4 + m % 2]
                        for jj in range(nb):
                            kb.op("pe", lambda e, m=m, jj=jj, py=py, wdt=wdt, at=at: e.matmul(
                                py[:], lhsT=wdt[:, jj, m * 128:(m + 1) * 128], rhs=at[:, jj, :], start=(jj == 0), stop=(jj == nb - 1)),
                                reads=[wdb, ab], writes=[pyb], inc=(jj == nb - 1))
                        kb.op("dve", lambda e, m=m, gs=gs, py=py: e.scalar_tensor_tensor(
                            x[:, m, gs], py[:], hg[:, m:m + 1], x[:, m, gs], op0=ALU.mult, op1=ALU.add),
                            reads=[pyb, self.hg_b, self.xb[m][g]], writes=[self.xb[m][g]])

    def rope_tables(self, ropec_sb, ropec_b, pos_d, gs, cos2, sinS, cs_b, W):
        kb = self.kb
        TWO_PI = float(2 * np.pi)
        C1 = 6.28125
        C2 = float(np.float32(2 * np.pi - 6.28125))
        pi_f = float(np.pi)
        pos_i, ang, kf, ki, r, m = W["pos_i"], W["ang"], W["kf"], W["ki"], W["r"], W["m"]
        wb = W["b"]
        kb.dma("sp", pos_i[:], pos_d[0:1, gs].broadcast_to([32, TG]), writes=[wb])
        kb.op("dve", lambda e: e.tensor_copy(ang[:], pos_i[:]), reads=[wb], writes=[wb])
        kb.op("dve", lambda e: e.tensor_scalar(ang[:], ang[:], ropec_sb[64:96, 0:1], None, op0=ALU.mult), reads=[wb, ropec_b], writes=[wb])
        kb.op("dve", lambda e: e.tensor_scalar(kf[:], ang[:], 1.0 / TWO_PI, None, op0=ALU.mult), reads=[wb], writes=[wb])
        kb.op("dve", lambda e: e.tensor_copy(ki[:], kf[:]), reads=[wb], writes=[wb])
        kb.op("dve", lambda e: e.tensor_copy(kf[:], ki[:]), reads=[wb], writes=[wb])
        kb.op("dve", lambda e: e.scalar_tensor_tensor(r[:], kf[:], -C1, ang[:], op0=ALU.mult, op1=ALU.add), reads=[wb], writes=[wb])
        kb.op("dve", lambda e: e.scalar_tensor_tensor(r[:], kf[:], -C2, r[:], op0=ALU.mult, op1=ALU.add), reads=[wb], writes=[wb])
        kb.op("dve", lambda e: e.tensor_scalar(m[:], r[:], pi_f, TWO_PI, op0=ALU.is_gt, op1=ALU.mult), reads=[wb], writes=[wb])
        kb.op("dve", lambda e: e.tensor_tensor(r[:], r[:], m[:], op=ALU.subtract), reads=[wb], writes=[wb])
        kb.op("dve", lambda e: e.tensor_scalar(m[:], r[:], -pi_f, TWO_PI, op0=ALU.is_lt, op1=ALU.mult), reads=[wb], writes=[wb])
        kb.op("dve", lambda e: e.tensor_tensor(r[:], r[:], m[:], op=ALU.add), reads=[wb], writes=[wb])
        kb.op("act", lambda e: e.activation(sinS[:], r[:], AF.Sin), reads=[wb], writes=[cs_b])
        kb.op("dve", lambda e: e.tensor_scalar(sinS[:], sinS[:], ropec_sb[64:96, 1:2], None, op0=ALU.mult), reads=[cs_b, ropec_b], writes=[cs_b])
        kb.op("dve", lambda e: e.tensor_scalar(r[:], r[:], pi_f / 2, None, op0=ALU.add), reads=[wb, cs_b], writes=[wb])
        kb.op("dve", lambda e: e.tensor_scalar(m[:], r[:], pi_f, TWO_PI, op0=ALU.is_gt, op1=ALU.mult), reads=[wb], writes=[wb])
        kb.op("dve", lambda e: e.tensor_tensor(r[:], r[:], m[:], op=ALU.subtract), reads=[wb], writes=[wb])
        kb.op("act", lambda e: e.activation(cos2[:], r[:], AF.Sin), reads=[wb], writes=[cs_b])

    def mla_p1(self, tag, snd):
        kb = self.kb
        self.phase_begin()
        self.emit_mods(tag, 1.0)
        self.norm_scratch()
        self.emit_h()
        x, h, ones = self.x, self.h, self.ones
        wa_d = self.din(tag + "_wa", [D, 704])
        wq_d = self.din(tag + "_wq", [QL, 16 * 128])
        wkn_d = self.din(tag + "_wkn", [KVL, 1024])
        wv_d = self.din(tag + "_wv", [KVL, 1024])
        mg_d = self.din(tag + "_mg", [128, 12])
        if "ropec" not in self.inputs:
            self.ropec_d = self.din("ropec", [32, 2])
            self.pos_d = self.din("pos", [1, T], I32)
        C = self.carve
        wa = C([128, KC, 704], BF16); wa_b = Buf()
        wq = C([128, 3, 2048], BF16); wq_b = Buf()
        wkn = C([128, 2, 1024], BF16); wkn_b = Buf()
        wv = C([128, 2, 1024], BF16); wv_b = Buf()
        mg = C([128, 12], F32); mg_b = Buf()
        ropec = C([128, 2], F32); ropec_b = Buf()
        sel2 = C([128, 2], BF16); sel2_b = Buf()
        alat = C([128, 3, TG], F32); alat_b = Buf()
        sqa = C([128, 3, TG], BF16); sqa_b = Buf()
        qn = C([128, 3, TG], BF16); qn_b = Buf()
        kvn = C([128, 2, TG], BF16); kvn_b = Buf()
        cos2 = C([32, TG], F32, 64); sinS = C([32, TG], F32, 64); cs_b = Buf()
        RW = {"pos_i": C([32, TG], I32, 64), "ang": C([32, TG], F32, 64), "kf": C([32, TG], F32, 64),
              "r": C([32, TG], F32, 64), "b": Buf()}
        RW["ki"] = RW["pos_i"]
        RW["m"] = RW["kf"]
        krsq = C([32, TG], BF16, 64); krsq_b = Buf()
        t1 = C([32, TG], F32, 64); t1_b = Buf()
        t2 = C([32, TG], F32, 64); t2_b = Buf()
        krf = C([32, TG], BF16, 64); krf_b = Buf()
        kst = [(C([128, TG], BF16), Buf()) for i in range(2)]
        ksq = [(C([128, TG], BF16), Buf()) for i in range(2)]
        v_sbs = [(C([128, 1024], BF16), Buf()) for i in range(2)]
        qsqs = [(C([96, TG], BF16), Buf()) for i in range(2)]
        rq96s = [(C([96, TG], F32), Buf()) for i in range(2)]
        sds = [(C([96, TG], F32), Buf()) for i in range(2)]
        t1s = [(t1, t1_b), (C([32, TG], F32, 64), Buf())]
        t2s = [(t2, t2_b), (C([32, TG], F32, 64), Buf())]
        qst = [(C([96, TG], BF16), Buf()) for i in range(2)]
        rsum = C([128, 4], F32); rsum_b = Buf()
        s1 = C([128, 4, 16], F32); s1_b = Buf()
        rstd, rstd_b = self.rstd, self.rstd_b
        kb.dma("pool", wa[:], wa_d.rearrange("(k p) n -> p k n", p=128), writes=[wa_b])
        kb.dma("pool", wq[:], wq_d.rearrange("(k p) n -> p k n", p=128), writes=[wq_b])
        kb.dma("pool", wkn[:], wkn_d.rearrange("(k p) n -> p k n", p=128), writes=[wkn_b])
        kb.dma("pool", wv[:], wv_d.rearrange("(k p) n -> p k n", p=128), writes=[wv_b])
        kb.dma("sp", mg[:], mg_d, writes=[mg_b])
        kb.dma("sp", ropec[64:96, :], self.ropec_d, writes=[ropec_b])
        kb.op("pool", lambda e: e.memset(sel2[:], 0.0), writes=[sel2_b])
        kb.op("pool", lambda e: e.memset(sel2[0:64, 0:1], 1.0), writes=[sel2_b])
        kb.op("pool", lambda e: e.memset(sel2[64:128, 1:2], 1.0), writes=[sel2_b])
        QT, KT, V, RK = snd["QT"], snd["KT"], snd["V"], snd["RK"]
        pb = self.pb
        p_ssq, p_ssq_b = pb[6]
        p_kr, p_kr_b = pb[3]
        p_krs, p_krs_b = pb[4]
        p_st, p_st_b = pb[5]
        out_toks = []
        qi = 0
        for g in range(NG):
            gs = slice(g * TG, (g + 1) * TG)
            self.rope_tables(ropec, ropec_b, self.pos_d, gs, cos2, sinS, cs_b, RW)

            def a_chunks(c0, n):
                for ci in range(n):
                    ps, psb = self.rot()
                    for k in range(KC):
                        kb.op("pe", lambda e, k=k, ci=ci, ps=ps, gs=gs, c0=c0: e.matmul(
                            ps[:], lhsT=wa[:, k, (c0 + ci) * 128:(c0 + ci + 1) * 128], rhs=h[:, k, gs], start=(k == 0), stop=(k == KC - 1)),
                            reads=[wa_b, self.hb[g]], writes=[psb], inc=(k == KC - 1))
                    kb.op("act", lambda e, ci=ci, ps=ps: e.activation(alat[:, ci, :], ps[:], AF.Copy), reads=[psb], writes=[alat_b])
                    kb.op("act", lambda e, ci=ci, ps=ps: e.activation(sqa[:, ci, :], ps[:], AF.Square), reads=[psb], writes=[sqa_b])
                for ci in range(n):
                    kb.op("pe", lambda e, ci=ci, n=n: e.matmul(p_ssq[:], lhsT=ones[:], rhs=sqa[:, ci, :], start=(ci == 0), stop=(ci == n - 1)),
                          reads=[self.ones_b, sqa_b], writes=[p_ssq_b], inc=(ci == n - 1))
                self.rstd_from_ps(p_ssq, p_ssq_b, 128, n * 128, rstd, rstd_b)
            a_chunks(0, 3)
            for ci in range(3):
                kb.op("dve", lambda e, ci=ci: e.scalar_tensor_tensor(qn[:, ci, :], alat[:, ci, :], mg[:, 5 + ci:6 + ci], rstd[:], op0=ALU.mult, op1=ALU.mult),
                      reads=[alat_b, mg_b, rstd_b], writes=[qn_b])
            if g == 0:
                self.dbg("h", h[:, :, 0:TG], self.hb, BF16)
                self.dbg("mods", self.mods[:], [self.mods_b])
                self.dbg("qn", qn[:], [qn_b], BF16)
                self.dbg("alat_q", alat[:], [alat_b])
            a_chunks(3, 2)
            for ci in range(2):
                kb.op("dve", lambda e, ci=ci: e.scalar_tensor_tensor(kvn[:, ci, :], alat[:, ci, :], mg[:, 3 + ci:4 + ci], rstd[:], op0=ALU.mult, op1=ALU.mult),
                      reads=[alat_b, mg_b, rstd_b], writes=[kvn_b])
            if g == 0:
                self.dbg("kvn", kvn[:], [kvn_b], BF16)
                self.dbg("cos2", cos2[:], [cs_b])
                self.dbg("sinS", sinS[:], [cs_b])
            for k in range(KC):
                kb.op("pe", lambda e, k=k, gs=gs: e.matmul(p_kr[64:96, :], lhsT=wa[:, k, 640:672], rhs=h[:, k, gs], start=(k == 0), stop=(k == KC - 1)),
                      reads=[wa_b, self.hb[g]], writes=[p_kr_b], inc=(k == KC - 1))
            for k in range(KC):
                kb.op("pe", lambda e, k=k, gs=gs: e.matmul(p_krs[64:96, :], lhsT=wa[:, k, 672:704], rhs=h[:, k, gs], start=(k == 0), stop=(k == KC - 1)),
                      reads=[wa_b, self.hb[g]], writes=[p_krs_b], inc=(k == KC - 1))
            kb.op("act", lambda e: e.activation(krsq[:], p_kr[64:96, :], AF.Square), reads=[p_kr_b], writes=[krsq_b])
            kb.op("dve", lambda e: e.scalar_tensor_tensor(t1[:], p_kr[64:96, :], mg[64:96, 1:2], cos2[:], op0=ALU.mult, op1=ALU.mult),
                  reads=[p_kr_b, mg_b, cs_b], writes=[t1_b])
            kb.op("dve", lambda e: e.scalar_tensor_tensor(t2[:], p_krs[64:96, :], mg[64:96, 2:3], sinS[:], op0=ALU.mult, op1=ALU.mult),
                  reads=[p_krs_b, mg_b, cs_b], writes=[t2_b])
            kb.op("pool", lambda e: e.tensor_tensor(krf[:], t1[:], t2[:], op=ALU.add), reads=[t1_b, t2_b], writes=[krf_b])
            KTv = KT.rearrange("(h p) t -> p h t", p=DH)
            out_toks.append(kb.dma("sp", KTv[64:96, :, gs], krf[:].unsqueeze(1).broadcast_to([32, NH, TG]), reads=[krf_b]))
            for tt in range(4):
                kb.op("pe", lambda e, tt=tt: e.matmul(p_st[:, 64 + tt:65 + tt], lhsT=krsq[:, tt * 128:(tt + 1) * 128], rhs=ones[64:96, 0:1],
                                                      start=True, stop=True),
                      reads=[krsq_b, self.ones_b], writes=[p_st_b], inc=(tt == 3))
            for j in range(8):
                ps, psb = self.rot()
                for kc in range(2):
                    kb.op("pe", lambda e, kc=kc, j=j, ps=ps: e.matmul(ps[:], lhsT=wkn[:, kc, j * 128:(j + 1) * 128], rhs=kvn[:, kc, :],
                                                                  start=(kc == 0), stop=(kc == 1)),
                          reads=[wkn_b, kvn_b], writes=[psb], inc=(kc == 1))
                kt_, ktb = kst[j % 2]
                kq_, kqb = ksq[j % 2]
                kb.op("act", lambda e, ps=ps, kt_=kt_: e.activation(kt_[:], ps[:], AF.Identity, scale=mg[:, 0:1]), reads=[psb, mg_b], writes=[ktb])
                kb.op("act", lambda e, ps=ps, kq_=kq_: e.activation(kq_[:], ps[:], AF.Square), reads=[psb], writes=[kqb])
                for hh in range(2):
                    hq = 2 * j + hh
                    out_toks.append(kb.dma("sp", KT[hq * DH:hq * DH + 64, gs], kt_[hh * 64:(hh + 1) * 64, :], reads=[ktb]))
                for tt in range(4):
                    kb.op("pe", lambda e, tt=tt, j=j, kq_=kq_: e.matmul(p_st[:, tt * 16 + 2 * j:tt * 16 + 2 * j + 2],
                                                                     lhsT=kq_[:, tt * 128:(tt + 1) * 128], rhs=sel2[:], start=True, stop=True),
                          reads=[kqb, sel2_b], writes=[p_st_b], inc=(tt == 3))
            kb.op("act", lambda e: e.activation(rsum[:], p_st[:, 64:68], AF.Copy), reads=[p_st_b], writes=[rsum_b])
            kb.op("dve", lambda e: e.tensor_tensor(s1[:], p_st[:, 0:64].rearrange("p (a b) -> p a b", b=16),
                                                   rsum[:].unsqueeze(2).broadcast_to([128, 4, 16]), op=ALU.add),
                  reads=[p_st_b, rsum_b], writes=[s1_b])
            kb.op("dve", lambda e: e.tensor_scalar(s1[:], s1[:], 1.0 / DH, EPS, op0=ALU.mult, op1=ALU.add), reads=[s1_b], writes=[s1_b])
            kb.op("act", lambda e: e.activation(s1[:], s1[:], AF.Sqrt), reads=[s1_b], writes=[s1_b])
            kb.op("dve", lambda e: e.reciprocal(s1[:], s1[:]), reads=[s1_b], writes=[s1_b])
            kb.op("dve", lambda e: e.tensor_scalar(s1[:], s1[:], SC96, None, op0=ALU.mult), reads=[s1_b], writes=[s1_b])
            out_toks.append(kb.dma("sp", RK[g * TG:(g + 1) * TG, :].rearrange("(a p) h -> p a h", p=128), s1[:], reads=[s1_b]))
            for tt in range(4):
                v_sb, v_b = v_sbs[tt % 2]
                for half in range(2):
                    ps, psb = self.rot()
                    for kc in range(2):
                        kb.op("pe", lambda e, kc=kc, tt=tt, half=half, ps=ps: e.matmul(
                            ps[:], lhsT=kvn[:, kc, tt * 128:(tt + 1) * 128], rhs=wv[:, kc, half * 512:(half + 1) * 512],
                            start=(kc == 0), stop=(kc == 1)),
                            reads=[wv_b, kvn_b], writes=[psb], inc=(kc == 1))
                    if half == 0:
                        kb.op("act", lambda e, ps=ps, v_sb=v_sb: e.activation(v_sb[:, 0:512], ps[:], AF.Copy), reads=[psb], writes=[v_b])
                    else:
                        kb.op("dve", lambda e, ps=ps, v_sb=v_sb: e.tensor_copy(v_sb[:, 512:1024], ps[:]), reads=[psb], writes=[v_b])
                r0 = g * TG + tt * 128
                out_toks.append(kb.dma("sp", V[r0:r0 + 128, :], v_sb[:], reads=[v_b]))
            for hq in range(NH):
                ps, psb = self.rot()
                pks, pks_b = pb[3 + hq % 2]
                pss, pss_b = pb[6 + hq % 2]
                qsq, qsq_b = qsqs[hq % 2]
                rq96, rq96_b = rq96s[hq % 2]
                t1q, t1q_b = t1s[hq % 2]
                t2q, t2q_b = t2s[hq % 2]
                for kc in range(3):
                    kb.op("pe", lambda e, kc=kc, hq=hq, ps=ps: e.matmul(ps[0:96, :], lhsT=wq[:, kc, hq * 128:hq * 128 + 96], rhs=qn[:, kc, :],
                                                                    start=(kc == 0), stop=(kc == 2)),
                          reads=[wq_b, qn_b], writes=[psb], inc=(kc == 2))
                for kc in range(3):
                    kb.op("pe", lambda e, kc=kc, hq=hq, pks=pks: e.matmul(pks[64:96, :], lhsT=wq[:, kc, hq * 128 + 96:hq * 128 + 128], rhs=qn[:, kc, :],
                                                                       start=(kc == 0), stop=(kc == 2)),
                          reads=[wq_b, qn_b], writes=[pks_b], inc=(kc == 2))
                kb.op("act", lambda e, ps=ps, qsq=qsq: e.activation(qsq[:], ps[0:96, :], AF.Square), reads=[psb], writes=[qsq_b])
                kb.op("pe", lambda e, pss=pss, qsq=qsq: e.matmul(pss[0:96, :], lhsT=ones[0:96, 0:96], rhs=qsq[:], start=True, stop=True),
                      reads=[self.ones_b, qsq_b], writes=[pss_b])
                self.rstd_from_ps(pss, pss_b, 96, DH, rq96, rq96_b, sds[hq % 2])
                qs_, qsb = qst[qi % 2]
                qi += 1
                kb.op("dve", lambda e, ps=ps, qs_=qs_, rq96=rq96: e.scalar_tensor_tensor(qs_[0:64, :], ps[0:64, :], mg[0:64, 8:9], rq96[0:64, :],
                                                                                       op0=ALU.mult, op1=ALU.mult),
                      reads=[psb, mg_b, rq96_b], writes=[qsb])
                kb.op("dve", lambda e, ps=ps, t1q=t1q: e.scalar_tensor_tensor(t1q[:], ps[64:96, :], mg[64:96, 8:9], cos2[:], op0=ALU.mult, op1=ALU.mult),
                      reads=[psb, mg_b, cs_b], writes=[t1q_b])
                kb.op("dve", lambda e, pks=pks, t2q=t2q: e.scalar_tensor_tensor(t2q[:], pks[64:96, :], mg[64:96, 9:10], sinS[:], op0=ALU.mult, op1=ALU.mult),
                      reads=[pks_b, mg_b, cs_b], writes=[t2q_b])
                kb.op("pool", lambda e, t1q=t1q, t2q=t2q: e.tensor_tensor(t1q[:], t1q[:], t2q[:], op=ALU.add), reads=[t1q_b, t2q_b], writes=[t1q_b])
                kb.op("pool", lambda e, qs_=qs_, t1q=t1q, rq96=rq96: e.tensor_tensor(qs_[64:96, :], t1q[:], rq96[64:96, :], op=ALU.mult),
                      reads=[t1q_b, rq96_b], writes=[qsb])
                out_toks.append(kb.dma("sp", QT[hq * DH:(hq + 1) * DH, gs], qs_[:], reads=[qsb]))
        return out_toks

    def mla_p2(self, tag, gat, snd_o):
        kb = self.kb
        nc = self.nc
        self.phase_begin()
        C = self.carve
        GQ, GK, GV, GRK = gat["QT"], gat["KT"], gat["V"], gat["RK"]
        rk_all = C([128, 32, 8], F32); rk_b = Buf()
        v_all = C([128, 32, 512], BF16); v_b = Buf()
        qT = [(C([96, S], BF16), Buf()) for i in range(2)]
        kT = [(C([96, S], BF16), Buf()) for i in range(2)]
        pt = [(C([128, TG], BF16), Buf()) for i in range(6)]
        ost = [(C([64, S], BF16), Buf()) for i in range(2)]
        rec = C([64, TG], F32); rec_b = Buf()
        ones = self.ones
        for s in range(2):
            rkr = GRK.rows(s, 0, T)
            self.dsel(rk_all[:, s * 16:(s + 1) * 16, :],
                      rkr[:, 0:8].rearrange("(a p) h -> p a h", p=128),
                      rkr[:, 8:16].rearrange("(a p) h -> p a h", p=128), writes=[rk_b])
            for a in range(2):
                vr = GV.rows(s, a * 1024, 1024)
                self.dsel(v_all[:, s * 16 + a * 8:s * 16 + (a + 1) * 8, :],
                          vr[:, 0:512].rearrange("(a p) n -> p a n", p=128),
                          vr[:, 512:1024].rearrange("(a p) n -> p a n", p=128), writes=[v_b])
        pb = self.pb
        out_toks = []
        LAG = 2
        NPT = len(pt)
        pend = []
        ti = 0
        gi = 0
        for hh in range(8):
            qt_, qb = qT[hh % 2]
            kt_, kbf = kT[hh % 2]
            for s in range(2):
                self.dsel(qt_[:, s * T:(s + 1) * T], GQ.rows(s, hh * DH, DH), GQ.rows(s, (8 + hh) * DH, DH), writes=[qb])
                self.dsel(kt_[:, s * T:(s + 1) * T], GK.rows(s, hh * DH, DH), GK.rows(s, (8 + hh) * DH, DH), writes=[kbf])
            os_, osb = ost[hh % 2]
            for gq in range(8):
                qs = slice(gq * TG, (gq + 1) * TG)
                po, pob = pb[4 + gi % 2]
                pd, pdb = pb[6 + gi % 2]
                gi += 1
                nkt = 4 * (gq + 1)
                for kt in range(nkt):
                    ps, psb = pb[ti % 4]
                    p_, p_b = pt[ti % NPT]
                    ti += 1
                    kb.op("pe", lambda e, kt=kt, ps=ps, kt_=kt_, qt_=qt_, qs=qs: e.matmul(
                        ps[:], lhsT=kt_[:, kt * 128:(kt + 1) * 128], rhs=qt_[:, qs], start=True, stop=True),
                        reads=[kbf, qb], writes=[psb])
                    kb.op("act", lambda e, kt=kt, hh=hh, ps=ps, p_=p_: e.activation(p_[:], ps[:], AF.Exp, scale=rk_all[:, kt, hh:hh + 1]),
                          reads=[psb, rk_b], writes=[p_b])
                    if kt >= 4 * gq:
                        base = gq * TG - kt * 128
                        kb.op("pool", lambda e, p_=p_, base=base: e.affine_select(
                            out=p_[:], in_=p_[:], pattern=[[1, TG]], compare_op=ALU.is_ge, fill=0.0, base=base, channel_multiplier=-1),
                            reads=[p_b], writes=[p_b])

                    def emit_pv(kt=kt, hh=hh, po=po, pob=pob, pd=pd, pdb=pdb, p_=p_, p_b=p_b, nkt=nkt, os_=os_, osb=osb, qs=qs):
                        kb.op("pe", lambda e: e.matmul(po[0:64, :], lhsT=v_all[:, kt, hh * 64:(hh + 1) * 64], rhs=p_[:],
                                                       start=(kt == 0), stop=(kt == nkt - 1)),
                              reads=[v_b, p_b], writes=[pob], inc=(kt == nkt - 1))
                        kb.op("pe", lambda e: e.matmul(pd[0:64, :], lhsT=ones[:, 0:64], rhs=p_[:], start=(kt == 0), stop=(kt == nkt - 1)),
                              reads=[self.ones_b, p_b], writes=[pdb], inc=True)
                        if kt == nkt - 1:
                            kb.op("dve", lambda e: e.reciprocal(rec[:], pd[0:64, :]), reads=[pdb], writes=[rec_b])
                            kb.op("dve", lambda e: e.tensor_tensor(os_[:, qs], po[0:64, :], rec[:], op=ALU.mult),
                                  reads=[pob, rec_b], writes=[osb])
                            if qs.stop == S:
                                out_toks.append(kb.dma("sp", snd_o[hh * 64:(hh + 1) * 64, :], os_[:], reads=[osb]))
                    pend.append(emit_pv)
                    if len(pend) > LAG:
                        pend.pop(0)()
        while pend:
            pend.pop(0)()
        return out_toks

    def mixer_p3(self, tag, gat_o, nk, recompute_mods, wname, gate_scale=1.0):
        kb = self.kb
        nc = self.nc
        self.phase_begin()
        if recompute_mods:
            self.emit_mods(tag, gate_scale)
        C = self.carve
        wo_d = self.din(tag + wname, [nk * 128, D])
        wo = C([128, nk, D], BF16); wo_b = Buf()
        kb.dma("pool", wo[:], wo_d.rearrange("(k p) n -> p k n", p=128), writes=[wo_b])
        osb = [(C([128, nk, TG], BF16), Buf()) for i in range(2)]
        x, hg = self.x, self.hg
        for g in range(NG):
            gs = slice(g * TG, (g + 1) * TG)
            o_, ob = osb[g % 2]
            R = gat_o.R
            for s_ in range(2):
                for kk in range(gat_o.nrows // R):
                    rr_ = gat_o.rows(s_, kk * R, R)
                    c0_ = (s_ * gat_o.nrows + kk * R) // 128
                    self.dsel(o_[:, c0_:c0_ + R // 128, :], rr_[:, g * TG:(g + 1) * TG].rearrange("(k p) t -> p k t", p=128),
                              rr_[:, T + g * TG:T + (g + 1) * TG].rearrange("(k p) t -> p k t", p=128), writes=[ob])
            for m in range(KC):
                py, pyb = self.pb[m % 2]
                for k in range(nk):
                    kb.op("pe", lambda e, m=m, k=k, py=py, o_=o_: e.matmul(py[:], lhsT=wo[:, k, m * 128:(m + 1) * 128], rhs=o_[:, k, :],
                                                                       start=(k == 0), stop=(k == nk - 1)),
                          reads=[wo_b, ob], writes=[pyb], inc=(k == nk - 1))
                kb.op("dve", lambda e, m=m, gs=gs, py=py: e.scalar_tensor_tensor(
                    x[:, m, gs], py[:], hg[:, m:m + 1], x[:, m, gs], op0=ALU.mult, op1=ALU.add),
                    reads=[pyb, self.hg_b, self.xb[m][g]], writes=[self.xb[m][g]])

    def ssd_p1(self, tag, snd):
        kb = self.kb
        self.phase_begin()
        self.emit_mods(tag, 1.0)
        self.norm_scratch()
        self.emit_h()
        h = self.h
        win_d = self.din(tag + "_win", [D, 5152])
        wv = win_d.rearrange("(k p) n -> p k n", p=128)
        C = self.carve
        wblk = [(C([128, KC, 512], BF16), Buf()) for i in range(2)]
        wdt = C([128, KC, 32], BF16); wdt_b = Buf()
        stg = [(C([128, TG], BF16), Buf()) for i in range(4)]
        dts = C([128, 16, 32], F32); dts_b = Buf()
        XBC, Z, DT = snd["XBC"], snd["Z"], snd["DT"]
        out_toks = []
        kb.dma("pool", wdt[:], wv[:, :, 5120:5152], writes=[wdt_b])
        si = 0
        wcols = [2048 + blk * 512 for blk in range(6)] + [blk * 512 for blk in range(4)]

        def wload(i):
            wt, wb = wblk[i % 2]
            kb.dma("pool", wt[:], wv[:, :, wcols[i]:wcols[i] + 512], writes=[wb])
        wload(0)
        for blk in range(6):
            wt, wb = wblk[blk % 2]
            blk_toks = []
            for cc in range(4):
                ch = blk * 4 + cc
                for g in range(NG):
                    gs = slice(g * TG, (g + 1) * TG)
                    ps, psb = self.rot(0, 4)
                    for k in range(KC):
                        kb.op("pe", lambda e, k=k, cc=cc, ps=ps, wt=wt, gs=gs: e.matmul(
                            ps[:], lhsT=wt[:, k, cc * 128:(cc + 1) * 128], rhs=h[:, k, gs], start=(k == 0), stop=(k == KC - 1)),
                            reads=[wb, self.hb[g]], writes=[psb], inc=(k == KC - 1))
                    st, stb = stg[si % 4]; si += 1
                    if si % 2 == 0:
                        kb.op("act", lambda e, ps=ps, st=st: e.activation(st[:], ps[:], AF.Copy), reads=[psb], writes=[stb])
                    else:
                        kb.op("dve", lambda e, ps=ps, st=st: e.tensor_copy(st[:], ps[:]), reads=[psb], writes=[stb])
                    out_toks.append(kb.dma("sp", XBC[ch * 128:(ch + 1) * 128, gs], st[:], reads=[stb]))
                    blk_toks.append(out_toks[-1])
                if cc == 0:
                    wload(blk + 1)
            if snd.get("XBC_g") is not None:
                snd["XBC_g"].gather_chunk(kb, blk, blk_toks, snd["groups"])
        for blk in range(4):
            wt, wb = wblk[(6 + blk) % 2]
            for tt in range(16):
                ps, psb = self.rot(0, 4)
                for k in range(KC):
                    kb.op("pe", lambda e, k=k, tt=tt, ps=ps, wt=wt: e.matmul(
                        ps[:], lhsT=h[:, k, tt * 128:(tt + 1) * 128], rhs=wt[:, k, :], start=(k == 0), stop=(k == KC - 1)),
                        reads=[wb, self.hb[tt // 4]], writes=[psb], inc=(k == KC - 1))
                st, stb = stg[si % 4]; si += 1
                if si % 2 == 0:
                    kb.op("act", lambda e, ps=ps, st=st: e.activation(st[:], ps[:], AF.Copy), reads=[psb], writes=[stb])
                else:
                    kb.op("dve", lambda e, ps=ps, st=st: e.tensor_copy(st[:], ps[:]), reads=[psb], writes=[stb])
                out_toks.append(kb.dma("sp", Z[tt * 128:(tt + 1) * 128, blk * 512:(blk + 1) * 512], st[:], reads=[stb]))
                if tt == 0 and blk < 3:
                    wload(6 + blk + 1)
        pdt, pdt_b = self.pb[4]
        for tt in range(16):
            for k in range(KC):
                kb.op("pe", lambda e, k=k, tt=tt: e.matmul(pdt[:, tt * 32:(tt + 1) * 32], lhsT=h[:, k, tt * 128:(tt + 1) * 128], rhs=wdt[:, k, :],
                                                        start=(k == 0), stop=(k == KC - 1)),
                      reads=[wdt_b, self.hb[tt // 4]], writes=[pdt_b], inc=(k == KC - 1))
        kb.op("dve", lambda e: e.tensor_copy(dts[:], pdt[:].rearrange("p (a b) -> p a b", b=32)), reads=[pdt_b], writes=[dts_b])
        out_toks.append(kb.dma("sp", DT.rearrange("(a p) h -> p a h", p=128), dts[:], reads=[dts_b]))
        return out_toks

    def ssd_p2(self, tag, gat, snd_g):
        kb = self.kb
        nc = self.nc
        self.phase_begin()
        C = self.carve
        GX, GZ, GDT = gat["XBC"], gat["Z"], gat["DT"]
        cw_d = self.din(tag + "_cw", [128, 48])
        cb_d = self.din(tag + "_cb", [128, 12])
        cbrow_d = self.din(tag + "_cbrow", [1, 1280])
        hp_d = self.din(tag + "_hp", [128, 48])
        ng_d = self.din(tag + "_ng", [128, 1024])
        cw = C([128, 48], F32); cw_b = Buf()
        cb = C([128, 12], F32); cb_b = Buf()
        cbrow = C([1, 1280], F32); cbrow_b = Buf()
        cbhi = C([1, 1280], BF16); cblo = C([1, 1280], BF16); cbf = C([1, 1280], F32); cbhl_b = Buf()
        hp = C([128, 48], F32); hp_b = Buf()
        ng = C([128, 1024], F32); ng_b = Buf()
        identb = C([128, 128], BF16); Tb = C([128, 128], BF16)
        U = C([128, 128], F32); Tm = C([128, 128], F32); cst_b = Buf()
        diag = C([128, 48, 128], BF16); diag_b = Buf()
        Aneg = C([128, 16], F32); Aneg_b = Buf()
        u = C([128, 12, TG + 4], BF16); u_b = Buf()
        zt = C([128, 4, 1024], BF16); zt_b = Buf()
        dtr = C([128, 4, 16], F32); dtr_b = Buf()
        xs = C([128, 4, 1024], F32); xs_b = Buf()
        Btok = C([128, 4, 256], BF16); Btok_b = Buf()
        BT = C([128, 2, TG], BF16); CT = C([128, 2, TG], BF16); bct_b = Buf()
        dtv = C([128, 4, 16], F32); av = C([128, 4, 16], F32); dtv_b = Buf()
        acum = C([128, 16], F32); ea = C([128, 16], F32); dte = C([128, 16], F32); cd = C([128, 16], F32); sm_b = Buf()
        aU = C([128, 16, 128], F32); aU_b = Buf()
        dec = [(C([128, 8, 128], BF16), Buf()) for i in range(2)]
        cbm = [(C([128, 128], BF16), Buf()) for i in range(2)]
        MT = [(C([128, 8, 128], BF16), Buf()) for i in range(2)]
        xdt = C([128, 1024], BF16); xdt_b = Buf()
        Bdec = C([128, 16, 128], BF16); Bdec_b = Buf()
        Sf = C([128, 1024], F32); Sf_b = Buf()
        Sb = C([128, 1024], BF16); Sb_b = Buf()
        t1 = C([128, 1024], F32); t1_b = Buf()
        t3 = C([128, 1024], F32); t3_b = Buf()
        yv = C([128, 1024], F32); yv_b = Buf()
        sz = t3; sz_b = t3_b
        ssq = C([128, 2], F32); ssq_b = Buf()
        junk = C([128, 512], BF16); junk_b = Buf()
        gn = C([128, 1024], BF16); gn_b = Buf()
        gT = [(C([128, 8, TG], BF16), Buf()) for i in range(1)]
        ones, onesf = self.ones, self.onesf
        pb = self.pb
        kb.dma("sp", cw[:], cw_d, writes=[cw_b])
        kb.dma("sp", cb[:], cb_d, writes=[cb_b])
        kb.dma("sp", cbrow[:], cbrow_d, writes=[cbrow_b])
        kb.dma("sp", hp[:], hp_d, writes=[hp_b])
        kb.dma("sp", ng[:], ng_d, writes=[ng_b])
        kb.op("dve", lambda e: e.tensor_copy(cbhi[:], cbrow[:]), reads=[cbrow_b], writes=[cbhl_b])
        kb.op("dve", lambda e: e.tensor_copy(cbf[:], cbhi[:]), reads=[cbhl_b], writes=[cbhl_b])
        kb.op("dve", lambda e: e.tensor_tensor(cbf[:], cbrow[:], cbf[:], op=ALU.subtract), reads=[cbhl_b, cbrow_b], writes=[cbhl_b])
        kb.op("dve", lambda e: e.tensor_copy(cblo[:], cbf[:]), reads=[cbhl_b], writes=[cbhl_b])
        for (tile_, pat, base, cm, cmp_) in ((Tm, [[1, 128]], 0, -1, ALU.is_ge), (U, [[-1, 128]], -1, 1, ALU.is_ge)):
            kb.op("pool", lambda e, tile_=tile_: e.memset(tile_[:], 1.0), writes=[cst_b])
            kb.op("pool", lambda e, tile_=tile_, pat=pat, base=base, cm=cm, cmp_=cmp_: e.affine_select(
                out=tile_[:], in_=tile_[:], pattern=pat, compare_op=cmp_, fill=0.0, base=base, channel_multiplier=cm),
                reads=[cst_b], writes=[cst_b])
        kb.op("pool", lambda e: e.memset(identb[:], 1.0), writes=[cst_b])
        kb.op("pool", lambda e: e.affine_select(out=identb[:], in_=identb[:], pattern=[[-1, 128]], compare_op=ALU.is_equal, fill=0.0,
                                                base=0, channel_multiplier=1), reads=[cst_b], writes=[cst_b])
        kb.op("dve", lambda e: e.tensor_copy(Tb[:], Tm[:]), reads=[cst_b], writes=[cst_b])
        for c in range(12):
            for j in range(4):
                kb.op("dve" if (c + j) % 2 else "pool", lambda e, c=c, j=j: e.tensor_scalar(
                    diag[:, c * 4 + j, :], identb[:], cw[:, c * 4 + j:c * 4 + j + 1], None, op0=ALU.mult),
                    reads=[cst_b, cw_b], writes=[diag_b])
        kb.op("act", lambda e: e.activation(Aneg[:], hp[:, 16:32], AF.Exp), reads=[hp_b], writes=[Aneg_b])
        kb.op("dve", lambda e: e.tensor_scalar(Aneg[:], Aneg[:], -1.0, None, op0=ALU.mult), reads=[Aneg_b], writes=[Aneg_b])
        kb.op("pool", lambda e: e.memset(Sf[:], 0.0), writes=[Sf_b])
        kb.op("pool", lambda e: e.memset(u[:, :, 0:4], 0.0), writes=[u_b])
        out_toks = []
        for G in range(8):
            s = G // 4
            gl = G % 4
            t0 = gl * TG
            if G > 0:
                kb.op("dve", lambda e: e.tensor_copy(u[:, :, 0:4], u[:, :, TG:TG + 4]), reads=[u_b], writes=[u_b])
            for (c0, nch, r0, r1) in ((0, 4, 0, 1024), (4, 4, 512, 1536), (8, 2, 2048, 2048 + 256), (10, 2, 2560, 2560 + 256)):
                self.dsel(u[:, c0:c0 + nch, 4:4 + TG],
                          GX.rows(s, r0, nch * 128)[:, t0:t0 + TG].rearrange("(c p) t -> p c t", p=128),
                          GX.rows(s, r1, nch * 128)[:, t0:t0 + TG].rearrange("(c p) t -> p c t", p=128), writes=[u_b])
            zr = GZ.rows(s, t0, TG)
            self.dsel(zt[:], zr[:, 0:1024].rearrange("(a p) n -> p a n", p=128),
                      zr[:, 1024:2048].rearrange("(a p) n -> p a n", p=128), writes=[zt_b])
            dr = GDT.rows(s, t0, TG)
            self.dsel(dtr[:], dr[:, 0:16].rearrange("(a p) h -> p a h", p=128),
                      dr[:, 16:32].rearrange("(a p) h -> p a h", p=128), writes=[dtr_b])
            for c in range(8, 12):
                ps, psb = pb[7]
                for j in range(4):
                    kb.op("pe", lambda e, c=c, j=j, ps=ps: e.matmul(ps[:], lhsT=diag[:, c * 4 + j, :], rhs=u[:, c, 1 + j:1 + j + TG],
                                                              start=(j == 0), stop=(j == 3)),
                          reads=[diag_b, u_b], writes=[psb], inc=(j == 3))
                dst = BT if c < 10 else CT
                kb.op("act", lambda e, c=c, ps=ps, dst=dst: e.activation(dst[:, c % 2, :], ps[:], AF.Silu, bias=cb[:, c:c + 1]),
                      reads=[psb, cb_b], writes=[bct_b])
            for tt in range(4):
                for half in range(2):
                    ps, psb = pb[4 + half]
                    for cc in range(4):
                        c = half * 4 + cc
                        for j in range(4):
                            kb.op("pe", lambda e, c=c, cc=cc, j=j, tt=tt, ps=ps: e.matmul(
                                ps[:, cc * 128:(cc + 1) * 128], lhsT=u[:, c, 1 + j + tt * 128:1 + j + tt * 128 + 128], rhs=diag[:, c * 4 + j, :],
                                start=(cc == 0 and j == 0), stop=False, skip_group_check=True),
                                reads=[diag_b, u_b], writes=[psb], inc=False)
                    kb.op("pe", lambda e, half=half, ps=ps: e.matmul(ps[:], lhsT=ones[0:1, 0:128], rhs=cbhi[0:1, half * 512:(half + 1) * 512],
                                                                  start=False, stop=False, skip_group_check=True),
                          reads=[self.ones_b, cbhl_b], writes=[psb], inc=False)
                    kb.op("pe", lambda e, half=half, ps=ps: e.matmul(ps[:], lhsT=ones[0:1, 0:128], rhs=cblo[0:1, half * 512:(half + 1) * 512],
                                                                  start=False, stop=True, skip_group_check=True),
                          reads=[self.ones_b, cbhl_b], writes=[psb], inc=True)
                    kb.op("act", lambda e, half=half, tt=tt, ps=ps: e.activation(xs[:, tt, half * 512:(half + 1) * 512], ps[:], AF.Silu),
                          reads=[psb], writes=[xs_b])
                ps, psb = pb[7]
                for cc in range(2):
                    c = 8 + cc
                    for j in range(4):
                        kb.op("pe", lambda e, c=c, cc=cc, j=j, tt=tt, ps=ps: e.matmul(
                            ps[:, cc * 128:(cc + 1) * 128], lhsT=u[:, c, 1 + j + tt * 128:1 + j + tt * 128 + 128], rhs=diag[:, c * 4 + j, :],
                            start=(cc == 0 and j == 0), stop=False, skip_group_check=True),
                            reads=[diag_b, u_b], writes=[psb], inc=False)
                kb.op("pe", lambda e, ps=ps: e.matmul(ps[:, 0:256], lhsT=ones[0:1, 0:128], rhs=cbhi[0:1, 1024:1280], start=False, stop=False,
                                                      skip_group_check=True), reads=[self.ones_b, cbhl_b], writes=[psb], inc=False)
                kb.op("pe", lambda e, ps=ps: e.matmul(ps[:, 0:256], lhsT=ones[0:1, 0:128], rhs=cblo[0:1, 1024:1280], start=False, stop=True,
                                                      skip_group_check=True), reads=[self.ones_b, cbhl_b], writes=[psb], inc=True)
                kb.op("act", lambda e, tt=tt, ps=ps: e.activation(Btok[:, tt, :], ps[:, 0:256], AF.Silu), reads=[psb], writes=[Btok_b])
            kb.op("act", lambda e: e.activation(zt[:], zt[:], AF.Silu), reads=[zt_b], writes=[zt_b])
            kb.op("dve", lambda e: e.tensor_tensor(dtv[:], dtr[:], hp[:, 0:16].unsqueeze(1).broadcast_to([128, 4, 16]), op=ALU.add),
                  reads=[dtr_b, hp_b], writes=[dtv_b])
            kb.op("act", lambda e: e.activation(dtv[:], dtv[:], AF.Exp), reads=[dtv_b], writes=[dtv_b])
            kb.op("act", lambda e: e.activation(dtv[:], dtv[:], AF.Ln, bias=1.0), reads=[dtv_b], writes=[dtv_b])
            kb.op("dve", lambda e: e.tensor_tensor(av[:], dtv[:], Aneg[:].unsqueeze(1).broadcast_to([128, 4, 16]), op=ALU.mult),
                  reads=[dtv_b, Aneg_b], writes=[dtv_b])
            gt_, gtb = gT[0]
            for tt in range(4):
                ts_ = slice(tt * 128, (tt + 1) * 128)
                pst, pstb = pb[6]
                kb.op("pe", lambda e, tt=tt: e.matmul(pst[:, 0:16], lhsT=Tm[:], rhs=av[:, tt, :], start=True, stop=True),
                      reads=[cst_b, dtv_b], writes=[pstb])
                kb.op("pe", lambda e, tt=tt: e.matmul(pst[:, 16:32], lhsT=onesf[:], rhs=av[:, tt, :], start=True, stop=True),
                      reads=[self.onesf_b, dtv_b], writes=[pstb])
                kb.op("act", lambda e: e.activation(ea[:], pst[:, 0:16], AF.Exp), reads=[pstb], writes=[sm_b])
                kb.op("act", lambda e: e.activation(cd[:], pst[:, 16:32], AF.Exp), reads=[pstb], writes=[sm_b])
                kb.op("act", lambda e: e.activation(acum[:], pst[:, 0:16], AF.Copy), reads=[pstb], writes=[sm_b])
                kb.op("dve", lambda e: e.tensor_tensor(dte[:], pst[:, 16:32], acum[:], op=ALU.subtract), reads=[pstb, sm_b], writes=[sm_b])
                kb.op("act", lambda e: e.activation(dte[:], dte[:], AF.Exp), reads=[sm_b], writes=[sm_b])
                kb.op("dve", lambda e, tt=tt: e.tensor_tensor(xdt[:].rearrange("p (h d) -> p h d", d=64), xs[:, tt, :].rearrange("p (h d) -> p h d", d=64),
                                                            dtv[:, tt, :].unsqueeze(2).broadcast_to([128, 16, 64]), op=ALU.mult),
                      reads=[xs_b, dtv_b], writes=[xdt_b])
                kb.op("pool", lambda e, tt=tt: e.tensor_tensor(aU[:], U[:].unsqueeze(1).broadcast_to([128, 16, 128]),
                                                             av[:, tt, :].unsqueeze(2).broadcast_to([128, 16, 128]), op=ALU.mult),
                      reads=[cst_b, dtv_b], writes=[aU_b])
                for gg in range(2):
                    kb.op("pool", lambda e, tt=tt, gg=gg: e.tensor_tensor(
                        Bdec[:, gg * 8:(gg + 1) * 8, :], Btok[:, tt, gg * 128:(gg + 1) * 128].unsqueeze(1).broadcast_to([128, 8, 128]),
                        dte[:, gg * 8:(gg + 1) * 8].unsqueeze(2).broadcast_to([128, 8, 128]), op=ALU.mult),
                        reads=[Btok_b, sm_b], writes=[Bdec_b])
                kb.op("dve", lambda e: e.tensor_copy(Sb[:], Sf[:]), reads=[Sf_b], writes=[Sb_b])
                py0, py0b = pb[2]
                py1, py1b = pb[3]
                pys = [(py0, py0b), (py1, py1b)]
                for gg in range(2):
                    cm_, cmb = cbm[gg]
                    kb.op("pe", lambda e, gg=gg, ts_=ts_: e.matmul(pst[:, 32 + gg * 128:32 + (gg + 1) * 128], lhsT=BT[:, gg, ts_], rhs=CT[:, gg, ts_],
                                                                start=True, stop=True), reads=[bct_b], writes=[pstb])
                    kb.op("dve", lambda e, gg=gg, cm_=cm_: e.tensor_tensor(cm_[:], pst[:, 32 + gg * 128:32 + (gg + 1) * 128], Tb[:], op=ALU.mult),
                          reads=[pstb, cst_b], writes=[cmb])
                    for half in range(2):
                        ps, psb = pb[half]
                        for hh in range(4):
                            hd = gg * 8 + half * 4 + hh
                            kb.op("pe", lambda e, hd=hd, hh=hh, ps=ps: e.matmul(ps[:, hh * 128:(hh + 1) * 128], lhsT=aU[:, hd, :], rhs=Tm[:],
                                                                             start=True, stop=True),
                                  reads=[aU_b, cst_b], writes=[psb])
                    dc, dcb = dec[gg]
                    for half in range(2):
                        ps, psb = pb[half]
                        kb.op("act", lambda e, half=half, ps=ps, dc=dc: e.activation(
                            dc[:, half * 4:(half + 1) * 4, :], ps[:].rearrange("p (a b) -> p a b", b=128), AF.Exp), reads=[psb], writes=[dcb])
                    mt, mtb = MT[gg]
                    kb.op("dve" if gg == 0 else "pool", lambda e, mt=mt, dc=dc, cm_=cm_: e.tensor_tensor(
                        mt[:], dc[:], cm_[:].unsqueeze(1).broadcast_to([128, 8, 128]), op=ALU.mult), reads=[dcb, cmb], writes=[mtb])
                    py, pyb = pys[gg]
                    for hh in range(8):
                        hd = gg * 8 + hh
                        kb.op("pe", lambda e, hd=hd, hh=hh, py=py, mt=mt: e.matmul(py[:, hh * 64:(hh + 1) * 64], lhsT=mt[:, hh, :],
                                                                              rhs=xdt[:, hd * 64:(hd + 1) * 64], start=True, stop=True),
                              reads=[mtb, xdt_b], writes=[pyb], inc=(hh == 7))
                for gg in range(2):
                    ps, psb = pb[gg]
                    kb.op("pe", lambda e, gg=gg, ps=ps, ts_=ts_: e.matmul(ps[:], lhsT=CT[:, gg, ts_], rhs=Sb[:, gg * 512:(gg + 1) * 512], start=True, stop=True),
                          reads=[bct_b, Sb_b], writes=[psb])
                    kb.op("dve", lambda e, gg=gg, ps=ps: e.tensor_tensor(
                        t1[:, gg * 512:(gg + 1) * 512].rearrange("p (h d) -> p h d", d=64), ps[:].rearrange("p (h d) -> p h d", d=64),
                        ea[:, gg * 8:(gg + 1) * 8].unsqueeze(2).broadcast_to([128, 8, 64]), op=ALU.mult),
                        reads=[psb, sm_b], writes=[t1_b])
                kb.op("pool", lambda e, tt=tt: e.tensor_tensor(t3[:].rearrange("p (h d) -> p h d", d=64), xs[:, tt, :].rearrange("p (h d) -> p h d", d=64),
                                                             hp[:, 32:48].unsqueeze(2).broadcast_to([128, 16, 64]), op=ALU.mult),
                      reads=[xs_b, hp_b], writes=[t3_b])
                kb.op("pool", lambda e: e.tensor_tensor(t1[:], t1[:], t3[:], op=ALU.add), reads=[t1_b, t3_b], writes=[t1_b])
                for gg in range(2):
                    py, pyb = pys[gg]
                    kb.op("dve", lambda e, gg=gg, py=py: e.tensor_tensor(yv[:, gg * 512:(gg + 1) * 512], py[:], t1[:, gg * 512:(gg + 1) * 512], op=ALU.add),
                          reads=[pyb, t1_b], writes=[yv_b])
                kb.op("dve", lambda e, tt=tt: e.tensor_tensor(yv[:], yv[:], zt[:, tt, :], op=ALU.mult), reads=[yv_b, zt_b], writes=[yv_b])
                for gg in range(2):
                    kb.op("act", lambda e, gg=gg: e.activation(junk[:], yv[:, gg * 512:(gg + 1) * 512], AF.Square, accum_out=ssq[:, gg:gg + 1]),
                          reads=[yv_b], writes=[junk_b, ssq_b])
                kb.op("dve", lambda e: e.tensor_scalar(ssq[:], ssq[:], 1.0 / 512, EPS, op0=ALU.mult, op1=ALU.add), reads=[ssq_b], writes=[ssq_b])
                kb.op("act", lambda e: e.activation(ssq[:], ssq[:], AF.Ln), reads=[ssq_b], writes=[ssq_b])
                kb.op("act", lambda e: e.activation(ssq[:], ssq[:], AF.Exp, scale=-0.5), reads=[ssq_b], writes=[ssq_b])
                for gg in range(2):
                    kb.op("dve", lambda e, gg=gg: e.scalar_tensor_tensor(gn[:, gg * 512:(gg + 1) * 512], yv[:, gg * 512:(gg + 1) * 512], ssq[:, gg:gg + 1],
                                                                       ng[:, gg * 512:(gg + 1) * 512], op0=ALU.mult, op1=ALU.mult),
                          reads=[yv_b, ssq_b, ng_b], writes=[gn_b])
                for half in range(2):
                    ps, psb = pb[4 + half]
                    for fc in range(4):
                        f = half * 4 + fc
                        kb.op("pe", lambda e, f=f, fc=fc, ps=ps: e.matmul(ps[:, fc * 128:(fc + 1) * 128], lhsT=gn[:, f * 128:(f + 1) * 128], rhs=identb[:],
                                                                       start=True, stop=True), reads=[gn_b, cst_b], writes=[psb], inc=(fc == 3))
                    kb.op("act" if half == 0 else "dve", (lambda e, half=half, ps=ps, gt_=gt_, ts_=ts_: e.activation(
                        gt_[:, half * 4:(half + 1) * 4, ts_], ps[:].rearrange("p (a b) -> p a b", b=128), AF.Copy)) if half == 0 else
                        (lambda e, half=half, ps=ps, gt_=gt_, ts_=ts_: e.tensor_copy(gt_[:, half * 4:(half + 1) * 4, ts_], ps[:].rearrange("p (a b) -> p a b", b=128))),
                        reads=[psb], writes=[gtb])
                for gg in range(2):
                    ps, psb = pb[gg]
                    for hh in range(8):
                        hd = gg * 8 + hh
                        kb.op("pe", lambda e, hd=hd, hh=hh, ps=ps: e.matmul(ps[:, hh * 64:(hh + 1) * 64], lhsT=Bdec[:, hd, :], rhs=xdt[:, hd * 64:(hd + 1) * 64],
                                                                         start=True, stop=True), reads=[Bdec_b, xdt_b], writes=[psb], inc=(hh == 7))
                kb.op("pool", lambda e: e.tensor_tensor(Sf[:].rearrange("p (h d) -> p h d", d=64), Sf[:].rearrange("p (h d) -> p h d", d=64),
                                                        cd[:].unsqueeze(2).broadcast_to([128, 16, 64]), op=ALU.mult), reads=[Sf_b, sm_b, Sb_b], writes=[Sf_b])
                for gg in range(2):
                    ps, psb = pb[gg]
                    kb.op("dve", lambda e, gg=gg, ps=ps: e.tensor_tensor(Sf[:, gg * 512:(gg + 1) * 512], Sf[:, gg * 512:(gg + 1) * 512], ps[:], op=ALU.add),
                          reads=[psb, Sf_b], writes=[Sf_b])
            col0 = s * T + t0
            out_toks.append(kb.dma("sp", snd_g[:, col0:col0 + TG].rearrange("(c p) t -> p c t", p=128), gt_[:], reads=[gtb]))
        return out_toks


D = 1024; T = 2048; S = 4096; DFF = 2816


def fm(v):
    v = np.asarray(v, np.float32)
    return np.ascontiguousarray(v.reshape(-1, 128).T)


def ropec():
    inv = (1.0 / (10000.0 ** (np.arange(0, 32, 2, dtype=np.float32) / 32))).astype(np.float32)
    c = np.zeros((32, 2), np.float32)
    c[:16, 0] = inv; c[16:, 0] = inv
    c[:16, 1] = -1.0; c[16:, 1] = 1.0
    return c


def prep_mods(I, i, sub, tag):
    return {tag + "_adaw": np.ascontiguousarray(I["ada_w"][i][:, sub * 3072:(sub + 1) * 3072]),
            tag + "_adab": fm(I["ada_b"][i][sub * 3072:(sub + 1) * 3072]),
            tag + "_gain": fm(I["norm_gain"][i, sub])}


def prep_ffn(I, i, which, tag):
    d = prep_mods(I, i, 0 if which == 0 else 2, tag)
    d[tag + "_wgu"] = I["ffn_w_gu"][i, which]
    d[tag + "_wdn"] = I["ffn_w_down"][i, which]
    return d


def prep_mla(I, i, tag):
    j = i // 2
    d = prep_mods(I, i, 1, tag)
    wa = I["mla_w_a"][j]
    kr = wa[:, 640:672]
    d[tag + "_wa"] = np.ascontiguousarray(np.concatenate([wa, kr[:, 16:], kr[:, :16]], 1))
    wqb = I["mla_w_qb"][j].reshape(384, 16, 96)
    nope, rp = wqb[:, :, :64], wqb[:, :, 64:]
    wq = np.concatenate([nope, rp, rp[:, :, 16:], rp[:, :, :16]], 2)
    d[tag + "_wq"] = np.ascontiguousarray(wq.reshape(384, 2048))
    wkv = I["mla_w_kvb"][j].reshape(256, 16, 128)
    d[tag + "_wkn"] = np.ascontiguousarray(wkv[:, :, :64].reshape(256, 1024))
    d[tag + "_wv"] = np.ascontiguousarray(wkv[:, :, 64:].reshape(256, 1024))
    mg = np.zeros((128, 12), np.float32)
    gk = I["mla_k_gain"][j]; gq = I["mla_q_gain"][j]
    mg[:64, 0] = gk[:64]; mg[64:, 0] = gk[:64]
    mg[64:96, 1] = gk[64:]
    mg[64:80, 2] = gk[80:]; mg[80:96, 2] = gk[64:80]
    mg[:, 3:5] = fm(I["mla_kv_a_gain"][j])
    mg[:, 5:8] = fm(I["mla_q_a_gain"][j])
    mg[:96, 8] = gq
    mg[64:80, 9] = gq[80:]; mg[80:96, 9] = gq[64:80]
    d[tag + "_mg"] = mg
    d[tag + "_wo"] = I["mla_w_o"][j]
    return d


def prep_ssd(I, i, tag):
    j = i // 2
    d = prep_mods(I, i, 1, tag)
    d[tag + "_win"] = I["ssd_w_in"][j]
    d[tag + "_wout"] = I["ssd_w_out"][j]
    return d


def prep_ssd_rank(I, i, tag, r):
    j = i // 2
    cwf = I["ssd_conv_w"][j]
    cbf = I["ssd_conv_b"][j]
    chans = np.concatenate([np.arange(1024 * r, 1024 * r + 1024), 2048 + 256 * r + np.arange(256), 2560 + 256 * r + np.arange(256)])
    cw = cwf[:, chans].reshape(4, 12, 128).transpose(2, 1, 0).reshape(128, 48)
    cb = cbf[chans].reshape(12, 128).T
    cbrow = cbf[chans[:1280]][None, :]
    hp = np.concatenate([I["ssd_dt_bias"][j][16 * r:16 * r + 16], I["ssd_a_log"][j][16 * r:16 * r + 16], I["ssd_d"][j][16 * r:16 * r + 16]])
    hp = np.broadcast_to(hp[None, :], (128, 48))
    ng = np.broadcast_to(I["ssd_norm_gain"][j][1024 * r:1024 * r + 1024][None, :], (128, 1024))
    f = lambda a: np.ascontiguousarray(a, dtype=np.float32)
    return {tag + "_cw": f(cw), tag + "_cb": f(cb), tag + "_cbrow": f(cbrow), tag + "_hp": f(hp), tag + "_ng": f(ng)}


import ml_dtypes
_BF = ml_dtypes.bfloat16
_PROGS = {}


def _build(kind):
    if kind in _PROGS:
        return _PROGS[kind]
    ph, mixer = kind
    P = Prog(None)
    P.setup()
    toks = []
    if ph == "A":
        xT = P.din("xT", [D, T]); P.load_x(xT)
        P.ffn("F0")
        yT = P.dout("yT", [D, T])
        toks += P.store_x(yT)
        if mixer == "mla":
            snd = {"QT": P.dout("QT", [16 * 96, T], BF16), "KT": P.dout("KT", [16 * 96, T], BF16),
                   "V": P.dout("V", [T, 1024], BF16), "RK": P.dout("RK", [T, 16], F32)}
            toks += P.mla_p1("M", snd)
        else:
            snd = {"XBC": P.dout("XBC", [3072, T], BF16), "Z": P.dout("Z", [T, 2048], BF16), "DT": P.dout("DT", [T, 32], F32)}
            toks += P.ssd_p1("M", snd)
    elif ph == "B":
        if mixer == "mla":
            g_ap = {"QT": GBuf(P.din("gQT", [2 * 16 * 96, T], BF16), 1536, 1536), "KT": GBuf(P.din("gKT", [2 * 16 * 96, T], BF16), 1536, 1536),
                    "V": GBuf(P.din("gV", [2 * T, 1024], BF16), T, T), "RK": GBuf(P.din("gRK", [2 * T, 16], F32), T, T)}
            snd_o = P.dout("O", [512, S], BF16)
            toks += P.mla_p2("M", g_ap, snd_o)
        else:
            g_ap = {"XBC": GBuf(P.din("gXBC", [2 * 3072, T], BF16), 3072, 3072), "Z": GBuf(P.din("gZ", [2 * T, 2048], BF16), T, T),
                    "DT": GBuf(P.din("gDT", [2 * T, 32], F32), T, T)}
            snd_g = P.dout("O", [1024, S], BF16)
            toks += P.ssd_p2("M", g_ap, snd_g)
    else:
        xT = P.din("xT", [D, T]); P.load_x(xT)
        if mixer == "mla":
            gO = GBuf(P.din("gO", [1024, S], BF16), 512, 512)
            P.mixer_p3("M", gO, 8, True, "_wo")
        else:
            gO = GBuf(P.din("gO", [2048, S], BF16), 1024, 1024)
            P.mixer_p3("M", gO, 16, True, "_wout")
        P.ffn("F1")
        yT = P.dout("yT", [D, T])
        toks += P.store_x(yT)
    P.kb.wait_all("sp", toks)
    P.kb.emit_all()
    _PROGS[kind] = P
    return P


def _launch(P, cands):
    ims = []
    for c in range(8):
        d = {}
        for n in P.inputs:
            for src in cands[c]:
                if n in src:
                    d[n] = src[n]
                    break
            else:
                raise KeyError(n)
        ims.append(d)
    res = run_bass_kernel_spmd(P.nc, ims, core_ids=list(range(8)))
    return res.results


def kernel(**I):
    I = {k: np.asarray(v) for k, v in I.items()}
    x = I["x"].astype(np.float32)
    B = x.shape[0]
    rc = ropec()
    core_common = []
    for c in range(8):
        b, r = c // 2, c % 2
        core_common.append({"cT": fm(I["c"][b]), "ropec": rc,
                            "pos": np.ascontiguousarray(I["positions"][b][None, r * T:(r + 1) * T]).astype(np.int32)})
    xs = [np.ascontiguousarray(x[c // 2, (c % 2) * T:(c % 2 + 1) * T].T) for c in range(8)]
    for i in range(4):
        mixer = "mla" if i % 2 == 0 else "ssd"
        W0 = prep_ffn(I, i, 0, "F0")
        W1 = prep_ffn(I, i, 1, "F1")
        WM = prep_mla(I, i, "M") if mixer == "mla" else prep_ssd(I, i, "M")
        WR = [prep_ssd_rank(I, i, "M", r) for r in range(2)] if mixer == "ssd" else [{}, {}]
        P = _build(("A", mixer))
        res = _launch(P, [[{"xT": xs[c]}, core_common[c], W0, WM] for c in range(8)])
        xs = [res[c]["yT"] for c in range(8)]
        names = ["QT", "KT", "V", "RK"] if mixer == "mla" else ["XBC", "Z", "DT"]
        gat = []
        for pr in range(4):
            g = {"g" + n: np.concatenate([res[2 * pr][n], res[2 * pr + 1][n]], 0) for n in names}
            gat += [g, g]
        del res
        P = _build(("B", mixer))
        res = _launch(P, [[gat[c], core_common[c], WM, WR[c % 2]] for c in range(8)])
        gO = []
        for pr in range(4):
            g = {"gO": np.concatenate([res[2 * pr]["O"], res[2 * pr + 1]["O"]], 0)}
            gO += [g, g]
        del res, gat
        P = _build(("C", mixer))
        res = _launch(P, [[{"xT": xs[c]}, gO[c], core_common[c], W1, WM] for c in range(8)])
        xs = [res[c]["yT"] for c in range(8)]
        del res
    out = np.empty_like(x)
    for c in range(8):
        out[c // 2, (c % 2) * T:(c % 2 + 1) * T] = xs[c].T
    return out


def _build_fused():
    if "fused" in _PROGS:
        return _PROGS["fused"]
    P = Prog(None)
    P.setup()
    kb = P.kb
    xT = P.din("xT", [D, T]); P.load_x(xT)
    P.setup_global_mods()
    for i in range(4):
        mixer = "mla" if i % 2 == 0 else "ssd"
        L = "L%d" % i
        P.ffn(L + "F0")
        if mixer == "mla":
            shapes = {"QT": ([16 * 96, T], BF16, 384), "KT": ([16 * 96, T], BF16, 384), "V": ([T, 1024], BF16, 1024), "RK": ([T, 16], F32, T)}
        else:
            shapes = {"XBC": ([3072, T], BF16, 512), "Z": ([T, 2048], BF16, 512), "DT": ([T, 32], F32, T)}
        snd = {n: P.dint(L + "s" + n, shp, dt) for n, (shp, dt, R) in shapes.items()}
        gat = {n: GBuf(P.dint(L + "g" + n, [2 * shp[0], shp[1]], dt), shp[0], R, snd[n]) for n, (shp, dt, R) in shapes.items()}
        if mixer == "ssd":
            snd["XBC_g"] = gat["XBC"]
            snd["groups"] = GROUPS
        toks = P.mla_p1(L + "M", snd) if mixer == "mla" else P.ssd_p1(L + "M", snd)
        for n in shapes:
            gat[n].gather(kb, toks, GROUPS)
        if mixer == "mla":
            snd_o = P.dint(L + "sO", [512, S], BF16)
            gat_o = GBuf(P.dint(L + "gO", [1024, S], BF16), 512, 256, snd_o)
            toks = P.mla_p2(L + "M", gat, snd_o)
        else:
            snd_o = P.dint(L + "sO", [1024, S], BF16)
            gat_o = GBuf(P.dint(L + "gO", [2048, S], BF16), 1024, 256, snd_o)
            toks = P.ssd_p2(L + "M", gat, snd_o)
        gat_o.gather(kb, toks, GROUPS)
        if mixer == "mla":
            P.mixer_p3(L + "M", gat_o, 8, False, "_wo")
        else:
            P.mixer_p3(L + "M", gat_o, 16, False, "_wout")
        P.ffn(L + "F1")
    yT = P.dout("yT", [D, T])
    toks = P.store_x(yT)
    kb.wait_all("sp", toks)
    kb.emit_all()
    _PROGS["fused"] = P
    return P


def kernel_fused(**I):
    I = {k: np.asarray(v) for k, v in I.items()}
    x = I["x"].astype(np.float32)
    rc = ropec()
    P = _build_fused()
    Wall = {}
    WR = [{}, {}]
    for i in range(4):
        L = "L%d" % i
        Wall.update(prep_ffn(I, i, 0, L + "F0"))
        Wall.update(prep_ffn(I, i, 1, L + "F1"))
        if i % 2 == 0:
            Wall.update(prep_mla(I, i, L + "M"))
        else:
            Wall.update(prep_ssd(I, i, L + "M"))
            for r in range(2):
                WR[r].update(prep_ssd_rank(I, i, L + "M", r))
    c4T = np.ascontiguousarray(np.stack([fm(I["c"][b_]) for b_ in range(4)], axis=2).reshape(128, 32))
    adab_all = np.ascontiguousarray(np.concatenate([fm(I["ada_b"][l]) for l in range(4)], axis=1))
    gain_all = np.ascontiguousarray(np.concatenate([fm(I["norm_gain"][l, s_]) for l in range(4) for s_ in range(3)], axis=1))
    gm = {"c4T": c4T, "adab_all": adab_all, "gain_all": gain_all}
    cands = []
    for c in range(8):
        b, r = c // 2, c % 2
        es = np.zeros((128, 4), np.float32); es[:, b] = 1.0
        cc = {"cT": fm(I["c"][b]), "ropec": rc, "esel": es,
              "adawc": np.ascontiguousarray(I["ada_w"][:, :, c * 1152:(c + 1) * 1152].reshape(4 * D, 1152)), "pos": np.ascontiguousarray(I["positions"][b][None, r * T:(r + 1) * T]).astype(np.int32),
              "xT": np.ascontiguousarray(x[b, r * T:(r + 1) * T].T)}
        cands.append([cc, gm, Wall, WR[r]])
    res = _launch(P, cands)
    out = np.empty_like(x)
    for c in range(8):
        out[c // 2, (c % 2) * T:(c % 2 + 1) * T] = res[c]["yT"].T
    return out


kernel_multi = kernel
kernel = kernel_fused
```
